# Optimizing a Trainium2 kernel written in Bass

```python
import math
import jax, jax.numpy as jnp
from jax import lax
import numpy as np

D_MODEL = 4096
BATCH = 2
SEQ = 4096
DEPTH = 2
DEC_BATCH = 4
DEC_SEQ = 4096
PAST_LEN = 128

GLA_HEADS = 4
GLA_DK = D_MODEL // 16
GLA_DV = D_MODEL // 8
GLA_QK = GLA_HEADS * GLA_DK
GLA_V = GLA_HEADS * GLA_DV
GLA_GATE_RANK = 16
GLA_GATE_NORMALIZER = 16.0
GLA_CHUNK = 64
HYENA_WIDTH = D_MODEL // 2
HYENA_SHORT_CONV = 3
HYENA_BANDS = 16
HYENA_EMB_DIM = 2 * HYENA_BANDS + 1
HYENA_FILTER_HIDDEN = 64
HYENA_SIN_FREQ = 1.0
HYENA_STRONG_DECAY_PCT = 0.3
HYENA_WEAK_DECAY_PCT = 1.5
HYENA_DECAY_TARGET = 1e-2
EVEN_SPLITS = (GLA_QK, GLA_QK, GLA_V, GLA_V, GLA_GATE_RANK, GLA_GATE_RANK, 3 * HYENA_WIDTH)
EVEN_IN = sum(EVEN_SPLITS)
EVEN_MIX = GLA_V + HYENA_WIDTH
SSD_INNER = D_MODEL // 2
SSD_HEADDIM = 64
SSD_HEADS = SSD_INNER // SSD_HEADDIM
SSD_GROUPS = 4
SSD_STATE = 128
SSD_BC = SSD_GROUPS * SSD_STATE
SSD_CONV = 4
SSD_CONV_DIM = SSD_INNER + 2 * SSD_BC
SSD_CHUNK = 128
LRU_WIDTH = D_MODEL // 2
LRU_BLOCKS = 16
LRU_BLOCK = LRU_WIDTH // LRU_BLOCKS
LRU_CONV = 4
LRU_C = 8.0
ODD_SPLITS = (SSD_INNER, SSD_CONV_DIM, 2 * SSD_HEADS, LRU_WIDTH, LRU_WIDTH)
ODD_IN = sum(ODD_SPLITS)
ODD_MIX = SSD_INNER + LRU_WIDTH
FFN_HIDDEN = 11008
FFN_CONV = 3
LN_EPS = 1e-5
RMS_EPS = 1e-6
DEEPNORM_ALPHA = (2.0 * DEPTH) ** 0.25
DEEPNORM_BETA = (8.0 * DEPTH) ** -0.25
N_EVEN = (DEPTH + 1) // 2
N_ODD = DEPTH // 2

kernel_name = "hybrid_bidir_gla_hyena_ssd_rglru_encoder"


def split_cols(t, sizes):
    idx = [int(v) for v in np.cumsum(sizes)[:-1]]
    return jnp.split(t, idx, axis=-1)


def flip_seq(t):
    return jnp.flip(t, axis=1)


def layer_norm(x, g, b):
    xf = x.astype(jnp.float32)
    mu = jnp.mean(xf, -1, keepdims=True)
    var = jnp.mean(jnp.square(xf - mu), -1, keepdims=True)
    return ((xf - mu) * lax.rsqrt(var + LN_EPS) * g + b).astype(x.dtype)


def rms_norm(x, g):
    xf = x.astype(jnp.float32)
    return xf * lax.rsqrt(jnp.mean(xf * xf, -1, keepdims=True) + RMS_EPS) * g.astype(jnp.float32)


def dwconv_centred(u, w, b):
    width = w.shape[0]
    left = (width - 1) // 2
    right = width - 1 - left
    L = u.shape[1]
    up = jnp.pad(u, ((0, 0), (left, right), (0, 0)))
    out = up[:, 0:L] * w[0]
    for j in range(1, width):
        out = out + up[:, j:j + L] * w[j]
    return out + b


def gla_chunked(q, k, v, log_a):
    Bsz, L, H, dk = q.shape
    dv = v.shape[-1]
    n = L // GLA_CHUNK
    def chunk(t):
        return t.reshape(Bsz, n, GLA_CHUNK, H, t.shape[-1])
    q, k, v, log_a = chunk(q), chunk(k), chunk(v), chunk(log_a)
    b = jnp.cumsum(log_a, axis=2)
    b_last = b[:, :, -1:]
    q_dec = q * jnp.exp(b)
    k_inv = k * jnp.exp(-b)
    mask = jnp.tril(jnp.ones((GLA_CHUNK, GLA_CHUNK), bool))
    scores = jnp.where(mask, jnp.einsum('bnchd,bnshd->bnhcs', q_dec, k_inv), 0.0)
    o_intra = jnp.einsum('bnhcs,bnshe->bnche', scores, v)
    kv = jnp.einsum('bnshd,bnshe->nbhde', k * jnp.exp(b_last - b), v)
    decay = jnp.exp(b_last[:, :, 0]).transpose(1, 0, 2, 3)
    def step(S, inp):
        dec, kv_n = inp
        return dec[..., None] * S + kv_n, S
    S0 = jnp.zeros((Bsz, H, dk, dv), jnp.float32)
    _, S_in = lax.scan(step, S0, (decay, kv))
    o_inter = jnp.einsum('bnchd,nbhde->bnche', q_dec, S_in)
    return (o_intra + o_inter).reshape(Bsz, L, H, dv)


def gla_mixer(q, k, v, og, lr_f, lr_b, wg_f, bg_f, wg_b, bg_b, norm_g):
    f32 = jnp.float32
    Bsz, L, _ = q.shape
    def heads(t, d):
        return t.astype(f32).reshape(Bsz, L, GLA_HEADS, d)
    qh = heads(q, GLA_DK) * (GLA_DK ** -0.5)
    kh = heads(k, GLA_DK)
    vh = heads(v, GLA_DV)
    la_f = heads(jax.nn.log_sigmoid(lr_f.astype(f32) @ wg_f.astype(f32) + bg_f.astype(f32)) / GLA_GATE_NORMALIZER, GLA_DK)
    la_b = heads(jax.nn.log_sigmoid(lr_b.astype(f32) @ wg_b.astype(f32) + bg_b.astype(f32)) / GLA_GATE_NORMALIZER, GLA_DK)
    o_f = gla_chunked(qh, kh, vh, la_f)
    o_b = flip_seq(gla_chunked(flip_seq(qh), flip_seq(kh), flip_seq(vh), flip_seq(la_b)))
    o = rms_norm(o_f + o_b, norm_g).reshape(Bsz, L, GLA_V)
    return o * jax.nn.silu(og.astype(f32))


def hyena_filters(L, w1, b1, w2, b2, w3):
    f32 = jnp.float32
    t = jnp.linspace(0.0, 1.0, L, dtype=f32)[:, None]
    bands = jnp.linspace(1e-4, HYENA_BANDS - 1, HYENA_BANDS, dtype=f32)
    w = 2.0 * math.pi * jnp.arange(L, dtype=f32)[:, None] / L
    z = jnp.concatenate([t, jnp.cos(bands * w), -jnp.sin(bands * w)], -1)
    h = jnp.sin(HYENA_SIN_FREQ * (z @ w1.astype(f32) + b1.astype(f32)))
    h = jnp.sin(HYENA_SIN_FREQ * (h @ w2.astype(f32) + b2.astype(f32)))
    h = h @ w3.astype(f32)
    max_decay = math.log(HYENA_DECAY_TARGET) / HYENA_STRONG_DECAY_PCT
    min_decay = math.log(HYENA_DECAY_TARGET) / HYENA_WEAK_DECAY_PCT
    deltas = jnp.linspace(min_decay, max_decay, HYENA_WIDTH, dtype=f32)
    window = jnp.exp(-t * jnp.abs(deltas))
    return h[:, :HYENA_WIDTH] * window, h[:, HYENA_WIDTH:] * window


def hyena_mixer(hy, conv_w, conv_b, w1, b1, w2, b2, w3, bias):
    f32 = jnp.float32
    L = hy.shape[1]
    uc = dwconv_centred(hy, conv_w, conv_b).astype(f32)
    x0, x1, v = split_cols(uc, (HYENA_WIDTH, HYENA_WIDTH, HYENA_WIDTH))
    h_f, h_b = hyena_filters(L, w1, b1, w2, b2, w3)
    k = jnp.concatenate([h_f, jnp.zeros((1, HYENA_WIDTH), f32), jnp.flip(h_b[1:], 0)], 0)
    k_f = jnp.fft.rfft(k, axis=0)
    u = v * x1
    u_f = jnp.fft.rfft(u, n=2 * L, axis=1)
    y = jnp.fft.irfft(u_f * k_f, n=2 * L, axis=1)[:, :L] + u * bias.astype(f32)
    return y * x0


def ssd_chunked(x, dt, A, bm, cm):
    Bsz, L, G, Hg, P = x.shape
    N = bm.shape[-1]
    n = L // SSD_CHUNK
    Q = SSD_CHUNK
    x = x.reshape(Bsz, n, Q, G, Hg, P)
    dt = dt.reshape(Bsz, n, Q, G, Hg)
    bm = bm.reshape(Bsz, n, Q, G, N)
    cm = cm.reshape(Bsz, n, Q, G, N)
    xdt = x * dt[..., None]
    ac = jnp.cumsum((dt * A).transpose(0, 1, 3, 4, 2), axis=-1)
    seg = ac[..., :, None] - ac[..., None, :]
    mask = jnp.tril(jnp.ones((Q, Q), bool))
    lmat = jnp.exp(jnp.where(mask, seg, -jnp.inf))
    cb = jnp.einsum('bnlgd,bnsgd->bngls', cm, bm)
    y_diag = jnp.einsum('bngls,bnghls,bnsghp->bnlghp', cb, lmat, xdt)
    decay_states = jnp.exp(ac[..., -1:] - ac)
    states = jnp.einsum('bnsgd,bnghs,bnsghp->nbghpd', bm, decay_states, xdt)
    chunk_decay = jnp.exp(ac[..., -1]).transpose(1, 0, 2, 3)
    def step(S, inp):
        dec, st = inp
        return dec[..., None, None] * S + st, S
    S0 = jnp.zeros((Bsz, G, Hg, P, N), jnp.float32)
    _, S_in = lax.scan(step, S0, (chunk_decay, states))
    y_off = jnp.einsum('bnlgd,bnghl,nbghpd->bnlghp', cm, jnp.exp(ac), S_in)
    return (y_diag + y_off).reshape(Bsz, L, G, Hg, P)


def ssd_mixer(z, xbc, dt_raw, conv_w, conv_b, dt_bias, a_log, d_skip, norm_g):
    f32 = jnp.float32
    Bsz, L, _ = z.shape
    G, Hg = SSD_GROUPS, SSD_HEADS // SSD_GROUPS
    xbc = jax.nn.silu(dwconv_centred(xbc, conv_w, conv_b).astype(f32))
    xs, bm, cm = split_cols(xbc, (SSD_INNER, SSD_BC, SSD_BC))
    xs = xs.reshape(Bsz, L, G, Hg, SSD_HEADDIM)
    bm = bm.reshape(Bsz, L, G, SSD_STATE)
    cm = cm.reshape(Bsz, L, G, SSD_STATE)
    dt = jax.nn.softplus(dt_raw.astype(f32).reshape(Bsz, L, 2, G, Hg) + dt_bias.astype(f32).reshape(2, G, Hg))
    A = -jnp.exp(a_log.astype(f32)).reshape(2, G, Hg)
    dsk = d_skip.astype(f32).reshape(2, G, Hg, 1)
    y_f = ssd_chunked(xs, dt[:, :, 0], A[0], bm, cm)
    y_b = flip_seq(ssd_chunked(flip_seq(xs), flip_seq(dt[:, :, 1]), A[1], flip_seq(bm), flip_seq(cm)))
    y = (y_f + y_b + xs * (dsk[0] + dsk[1])).reshape(Bsz, L, SSD_INNER)
    y = y * jax.nn.silu(z.astype(f32))
    y = rms_norm(y.reshape(Bsz, L, G, SSD_INNER // G), norm_g.reshape(G, SSD_INNER // G))
    return y.reshape(Bsz, L, SSD_INNER)


def rglru_dir(xc, wa, ba, wx, bx, lam):
    f32 = jnp.float32
    Bsz, L, W = xc.shape
    xb = xc.reshape(Bsz, L, LRU_BLOCKS, LRU_BLOCK)
    r = jax.nn.sigmoid(jnp.einsum('blkc,kcd->blkd', xb, wa.astype(f32)).reshape(Bsz, L, W) + ba.astype(f32))
    i = jax.nn.sigmoid(jnp.einsum('blkc,kcd->blkd', xb, wx.astype(f32)).reshape(Bsz, L, W) + bx.astype(f32))
    log_a = -LRU_C * r * jax.nn.softplus(-lam.astype(f32))
    a = jnp.exp(log_a)
    b = jnp.sqrt(-jnp.expm1(2.0 * log_a)) * (i * xc)
    def combine(e1, e2):
        a1, b1 = e1
        a2, b2 = e2
        return a1 * a2, a2 * b1 + b2
    _, h = lax.associative_scan(combine, (a, b), axis=1)
    return h


def lru_mixer(gate_in, x_in, conv_w, conv_b, wa, ba, wx, bx, lam):
    f32 = jnp.float32
    xc = dwconv_centred(x_in, conv_w, conv_b).astype(f32)
    h_f = rglru_dir(xc, wa[0], ba[0], wx[0], bx[0], lam[0])
    h_b = flip_seq(rglru_dir(flip_seq(xc), wa[1], ba[1], wx[1], bx[1], lam[1]))
    return jax.nn.gelu(gate_in.astype(f32)) * (h_f + h_b)


def conv_ffn(x, w_up, conv_w, conv_b, w_down):
    h = dwconv_centred(x @ w_up, conv_w, conv_b)
    g, u = split_cols(h, (FFN_HIDDEN, FFN_HIDDEN))
    return (jax.nn.silu(g) * u) @ w_down


def setup_inputs(seed: int = 0) -> dict:
    key = jax.random.key(seed)
    ks = iter(jax.random.split(key, 64))
    f32 = jnp.float32
    def nrm(shape, scale):
        return jax.random.normal(next(ks), shape, f32) * scale
    def gain(shape):
        return 1.0 + nrm(shape, 0.02)
    def small(shape):
        return nrm(shape, 0.02)
    E, O, HW = N_EVEN, N_ODD, HYENA_WIDTH
    dt0 = jnp.exp(jax.random.uniform(next(ks), (O, 2, SSD_HEADS), f32, math.log(1e-3), math.log(1e-1)))
    dt_bias = dt0 + jnp.log(-jnp.expm1(-dt0))
    a_log = jnp.log(jax.random.uniform(next(ks), (O, 2, SSD_HEADS), f32, 1.0, 16.0))
    a_c = jax.random.uniform(next(ks), (O, 2, LRU_WIDTH), f32, 0.9, 0.999)
    a_base = a_c ** (1.0 / LRU_C)
    lam = jnp.log(a_base) - jnp.log1p(-a_base)
    return {
        "x_prompt": nrm((BATCH, SEQ, D_MODEL), 1.0),
        "x_sample": nrm((DEC_BATCH, DEC_SEQ, D_MODEL), 1.0),
        "ev_w_in": nrm((E, D_MODEL, EVEN_IN), D_MODEL ** -0.5),
        "ev_gla_wg_f": nrm((E, GLA_GATE_RANK, GLA_QK), GLA_GATE_RANK ** -0.5),
        "ev_gla_bg_f": nrm((E, GLA_QK), 0.1),
        "ev_gla_wg_b": nrm((E, GLA_GATE_RANK, GLA_QK), GLA_GATE_RANK ** -0.5),
        "ev_gla_bg_b": nrm((E, GLA_QK), 0.1),
        "ev_gla_norm": gain((E, GLA_DV)),
        "ev_hy_conv_w": nrm((E, HYENA_SHORT_CONV, 3 * HW), HYENA_SHORT_CONV ** -0.5),
        "ev_hy_conv_b": small((E, 3 * HW)),
        "ev_hy_w1": nrm((E, HYENA_EMB_DIM, HYENA_FILTER_HIDDEN), HYENA_EMB_DIM ** -0.5),
        "ev_hy_b1": nrm((E, HYENA_FILTER_HIDDEN), 0.1),
        "ev_hy_w2": nrm((E, HYENA_FILTER_HIDDEN, HYENA_FILTER_HIDDEN), HYENA_FILTER_HIDDEN ** -0.5),
        "ev_hy_b2": nrm((E, HYENA_FILTER_HIDDEN), 0.1),
        "ev_hy_w3": nrm((E, HYENA_FILTER_HIDDEN, 2 * HW), 0.05 * HYENA_FILTER_HIDDEN ** -0.5),
        "ev_hy_bias": nrm((E, HW), 1.0),
        "ev_w_out": nrm((E, EVEN_MIX, D_MODEL), DEEPNORM_BETA * EVEN_MIX ** -0.5),
        "od_w_in": nrm((O, D_MODEL, ODD_IN), D_MODEL ** -0.5),
        "od_ssd_conv_w": nrm((O, SSD_CONV, SSD_CONV_DIM), SSD_CONV ** -0.5),
        "od_ssd_conv_b": small((O, SSD_CONV_DIM)),
        "od_ssd_dt_bias": dt_bias,
        "od_ssd_a_log": a_log,
        "od_ssd_d": gain((O, 2, SSD_HEADS)),
        "od_ssd_norm": gain((O, SSD_INNER)),
        "od_lru_conv_w": nrm((O, LRU_CONV, LRU_WIDTH), LRU_CONV ** -0.5),
        "od_lru_conv_b": small((O, LRU_WIDTH)),
        "od_lru_wa": nrm((O, 2, LRU_BLOCKS, LRU_BLOCK, LRU_BLOCK), LRU_BLOCK ** -0.5),
        "od_lru_ba": small((O, 2, LRU_WIDTH)),
        "od_lru_wx": nrm((O, 2, LRU_BLOCKS, LRU_BLOCK, LRU_BLOCK), LRU_BLOCK ** -0.5),
        "od_lru_bx": small((O, 2, LRU_WIDTH)),
        "od_lru_lambda": lam,
        "od_w_out": nrm((O, ODD_MIX, D_MODEL), DEEPNORM_BETA * ODD_MIX ** -0.5),
        "ln1_g": gain((DEPTH, D_MODEL)),
        "ln1_b": small((DEPTH, D_MODEL)),
        "ffn_w_up": nrm((DEPTH, D_MODEL, 2 * FFN_HIDDEN), D_MODEL ** -0.5),
        "ffn_conv_w": nrm((DEPTH, FFN_CONV, 2 * FFN_HIDDEN), FFN_CONV ** -0.5),
        "ffn_conv_b": small((DEPTH, 2 * FFN_HIDDEN)),
        "ffn_w_down": nrm((DEPTH, FFN_HIDDEN, D_MODEL), DEEPNORM_BETA * FFN_HIDDEN ** -0.5),
        "ln2_g": gain((DEPTH, D_MODEL)),
        "ln2_b": small((DEPTH, D_MODEL)),
    }


def reference(x_prompt, x_sample,
              ev_w_in, ev_gla_wg_f, ev_gla_bg_f, ev_gla_wg_b, ev_gla_bg_b, ev_gla_norm,
              ev_hy_conv_w, ev_hy_conv_b, ev_hy_w1, ev_hy_b1, ev_hy_w2, ev_hy_b2, ev_hy_w3, ev_hy_bias,
              ev_w_out,
              od_w_in, od_ssd_conv_w, od_ssd_conv_b, od_ssd_dt_bias, od_ssd_a_log, od_ssd_d, od_ssd_norm,
              od_lru_conv_w, od_lru_conv_b, od_lru_wa, od_lru_ba, od_lru_wx, od_lru_bx, od_lru_lambda,
              od_w_out,
              ln1_g, ln1_b, ffn_w_up, ffn_conv_w, ffn_conv_b, ffn_w_down, ln2_g, ln2_b):

    def even_mixer(x, j):
        proj = x @ ev_w_in[j]
        q, k, v, og, lr_f, lr_b, hy = split_cols(proj, EVEN_SPLITS)
        a = gla_mixer(q, k, v, og, lr_f, lr_b, ev_gla_wg_f[j], ev_gla_bg_f[j],
                      ev_gla_wg_b[j], ev_gla_bg_b[j], ev_gla_norm[j])
        h = hyena_mixer(hy, ev_hy_conv_w[j], ev_hy_conv_b[j], ev_hy_w1[j], ev_hy_b1[j],
                        ev_hy_w2[j], ev_hy_b2[j], ev_hy_w3[j], ev_hy_bias[j])
        return jnp.concatenate([a, h], -1).astype(x.dtype) @ ev_w_out[j]

    def odd_mixer(x, j):
        proj = x @ od_w_in[j]
        z, xbc, dt_raw, g_in, r_in = split_cols(proj, ODD_SPLITS)
        s = ssd_mixer(z, xbc, dt_raw, od_ssd_conv_w[j], od_ssd_conv_b[j], od_ssd_dt_bias[j],
                      od_ssd_a_log[j], od_ssd_d[j], od_ssd_norm[j])
        r = lru_mixer(g_in, r_in, od_lru_conv_w[j], od_lru_conv_b[j], od_lru_wa[j], od_lru_ba[j],
                      od_lru_wx[j], od_lru_bx[j], od_lru_lambda[j])
        return jnp.concatenate([s, r], -1).astype(x.dtype) @ od_w_out[j]

    def run(x):
        for i in range(DEPTH):
            j = i // 2
            mix = even_mixer(x, j) if i % 2 == 0 else odd_mixer(x, j)
            x = layer_norm(DEEPNORM_ALPHA * x + mix, ln1_g[i], ln1_b[i])
            ffn = conv_ffn(x, ffn_w_up[i], ffn_conv_w[i], ffn_conv_b[i], ffn_w_down[i])
            x = layer_norm(DEEPNORM_ALPHA * x + ffn, ln2_g[i], ln2_b[i])
        return x

    y_prompt = run(x_prompt)
    y_sample = run(x_sample)
    return (y_prompt, y_sample)
```

```python
import contextlib
import numpy as np
import concourse.bass as bass
import concourse.mybir as mybir
from concourse.bass_utils import run_bass_kernel_spmd

F32 = mybir.dt.float32
BF16 = mybir.dt.bfloat16
AF = mybir.ActivationFunctionType
ALU = mybir.AluOpType

D = 4096
KC = 32
TB = 512
FF = 11008
FC = 86
NCORES = 8
ALPHA = 4.0 ** 0.25
LN_EPS = 1e-5
RMS_EPS = 1e-6


class Tk:
    __slots__ = ("h", "w", "r", "pr")

    def __init__(self, h):
        self.h = h
        self.w = {}
        self.r = {}
        self.pr = {}

    def __getitem__(self, idx):
        return self.h[idx]


class KB:
    SEM_LIMIT = 30000
    ND = 6

    def __init__(self, nc, es):
        self.nc = nc
        self.es = es
        self.eng = {"pe": nc.tensor, "act": nc.scalar, "dve": nc.vector, "pool": nc.gpsimd, "sp": nc.sync}
        self.sem = {}
        self.cnt = {}
        self.pe_sems = set()
        self.known = {e: {} for e in self.eng}
        self.nsem = 0
        self.prev = {}
        self.es2 = None
        for e in ("pe", "act", "dve", "pool"):
            self._new_sem(e)
        self.dq = {}
        for q in ("sp", "act", "pool"):
            self.dq[q] = {"sems": [self._alloc_sem() for _ in range(self.ND)], "vals": [0] * self.ND, "i": 0}
        self.ninst = {e: 0 for e in self.eng}
        self.uid = 0

    def _alloc_sem(self):
        self.nsem += 1
        return self.es.enter_context(self.nc.semaphore("s%d" % self.nsem))

    def _new_sem(self, e):
        if e in self.sem:
            self.prev.setdefault(e, []).append((self.sem[e], self.cnt[e]))
        s = self._alloc_sem()
        self.sem[e] = s
        self.cnt[e] = 0
        if e == "pe":
            self.pe_sems.add(s)

    def sb(self, shape, dt, name=None):
        self.uid += 1
        es = self.es2 if getattr(self, "es2", None) is not None else self.es
        h = es.enter_context(self.nc.sbuf_tensor("%s%d" % (name or "sb", self.uid), list(shape), dt))
        return Tk(h)

    def ps(self, shape=(128, 512), dt=F32, name=None):
        self.uid += 1
        h = self.es.enter_context(self.nc.psum_tensor("%s%d" % (name or "ps", self.uid), list(shape), dt))
        return Tk(h)

    def dram(self, name, shape, dt, kind="Internal"):
        h = self.nc.dram_tensor(name, list(shape), dt, kind=kind).ap()
        return Tk(h)

    def _wait(self, e, deps):
        kn = self.known[e]
        eo = self.eng[e]
        for s, v in deps.items():
            if kn.get(s, 0) < v:
                eo.wait_ge(s, v)
                kn[s] = v
                self.ninst[e] += 1

    @staticmethod
    def _merge(d, o):
        for s, v in o.items():
            if d.get(s, 0) < v:
                d[s] = v

    def op(self, e, fn, reads=(), writes=(), acc=False, last=True):
        deps = {}
        for t in reads:
            self._merge(deps, t.w)
        for t in writes:
            self._merge(deps, t.r)
            if not acc:
                self._merge(deps, t.w)
            else:
                self._merge(deps, t.pr)
        if e == "pe":
            for s in list(deps):
                if s in self.pe_sems:
                    del deps[s]
        self._wait(e, deps)
        ins = fn(self.eng[e])
        self.ninst[e] += 1
        if e == "pe" and not last:
            tok = (self.sem[e], self.cnt[e] + 1)
        else:
            ins.then_inc(self.sem[e], 1)
            self.cnt[e] += 1
            tok = (self.sem[e], self.cnt[e])
            if self.cnt[e] >= self.SEM_LIMIT:
                self._new_sem(e)
        s, v = tok
        for t in reads:
            if t.r.get(s, 0) < v:
                t.r[s] = v
        for t in writes:
            if acc:
                if t.w.get(s, 0) < v:
                    t.w[s] = v
            else:
                pr = dict(t.w)
                self._merge(pr, t.r)
                t.pr = pr
                t.w = {s: v}
                t.r = {}
        return ins

    def dma(self, q, out_ap, in_ap, reads=(), writes=(), acc=False, **kw):
        dq = self.dq[q]
        slot = dq["i"] % self.ND
        dq["i"] += 1
        if dq["vals"][slot] >= 60000:
            dq["sems"][slot] = self._alloc_sem()
            dq["vals"][slot] = 0
        s = dq["sems"][slot]
        deps = {}
        if dq["vals"][slot] > 0:
            deps[s] = dq["vals"][slot]
        for t in reads:
            self._merge(deps, t.w)
        for t in writes:
            self._merge(deps, t.r)
            if not acc:
                self._merge(deps, t.w)
            else:
                self._merge(deps, t.pr)
        self._wait(q, deps)
        ins = self.eng[q].dma_start(out=out_ap, in_=in_ap, **kw)
        ins.then_inc(s, 16)
        self.ninst[q] += 1
        dq["vals"][slot] += 16
        v = dq["vals"][slot]
        for t in reads:
            if t.r.get(s, 0) < v:
                t.r[s] = v
        for t in writes:
            if acc:
                if t.w.get(s, 0) < v:
                    t.w[s] = v
            else:
                pr = dict(t.w)
                self._merge(pr, t.r)
                t.pr = pr
                t.w = {s: v}
                t.r = {}
        return ins

    def barrier(self):
        deps = {}
        for e in ("pe", "act", "dve", "pool"):
            if self.cnt[e] > 0:
                deps[self.sem[e]] = self.cnt[e]
            elif self.prev.get(e):
                ps_, pv_ = self.prev[e][-1]
                deps[ps_] = pv_
        for q in self.dq:
            for s, v in zip(self.dq[q]["sems"], self.dq[q]["vals"]):
                if v > 0:
                    deps[s] = v
        for e in self.eng:
            self._wait(e, dict(deps))

    @contextlib.contextmanager
    def scope(self):
        old = self.es
        with contextlib.ExitStack() as es:
            self.es2 = es
            yield
            self.barrier()
        self.es2 = None

    def finish(self, outs):
        deps = {}
        for t in outs:
            self._merge(deps, t.w)
        self._wait("sp", deps)
        for q in self.dq:
            dd = {}
            for s, v in zip(self.dq[q]["sems"], self.dq[q]["vals"]):
                if v > 0:
                    dd[s] = v
            self._wait("sp", dd)


class Ring:
    def __init__(self, items):
        self.items = items
        self.i = 0

    def next(self):
        t = self.items[self.i % len(self.items)]
        self.i += 1
        return t


def sl(i, n=128):
    return slice(i * n, (i + 1) * n)


def phase_prepass(k, x_in, xT32, xT16, L, ident32):
    nb = L // TB
    with k.scope():
        xin = Ring([[k.sb([128, D], F32, "xin") for _ in range(4)] for _ in range(2)])
        st32 = Ring([k.sb([128, TB], F32, "st32") for _ in range(3)])
        st16 = Ring([k.sb([128, TB], BF16, "st16") for _ in range(3)])
        for b in range(nb):
            tiles = xin.next()
            for tt in range(4):
                t0 = b * TB + tt * 128
                k.dma("sp" if tt % 2 == 0 else "act", tiles[tt][:], x_in[t0:t0 + 128, :], reads=[x_in], writes=[tiles[tt]])
            for dc in range(KC):
                pst = k.psr.next()
                for tt in range(4):
                    k.op("pe", lambda e, tt=tt: e.transpose(pst[:, sl(tt)], tiles[tt][:, sl(dc)], ident32[:]),
                         reads=[tiles[tt], ident32], writes=[pst], acc=(tt > 0), last=(tt == 3))
                s32 = st32.next()
                s16 = st16.next()
                k.op("dve", lambda e: e.tensor_copy(s32[:], pst[:]), reads=[pst], writes=[s32])
                k.op("act", lambda e: e.copy(s16[:], s32[:]), reads=[s32], writes=[s16])
                k.dma("sp", xT32[dc, :, sl(b, TB)], s32[:], reads=[s32], writes=[xT32], acc=True)
                k.dma("act", xT16[dc, :, sl(b, TB)], s16[:], reads=[s16], writes=[xT16], acc=True)


def load_X(k, X, xT16, b, L):
    nb = L // TB
    lo = b * TB - 1
    hi = b * TB + TB + 1
    c0, c1 = 0, TB + 2
    if b == 0:
        lo, c0 = 0, 1
    if b == nb - 1:
        hi, c1 = L, TB + 1
    for g in range(4):
        if b == 0:
            k.op("pool", lambda e: e.memset(X[g][:, :, 0:1], 0.0), writes=[X[g]])
        if b == nb - 1:
            k.op("pool", lambda e: e.memset(X[g][:, :, TB + 1:TB + 2], 0.0), writes=[X[g]], acc=(b == 0))
        src = xT16.h[g * 8:(g + 1) * 8, :, lo:hi].rearrange("c p t -> p c t")
        k.dma("sp" if g % 2 == 0 else "act", X[g][:, :, c0:c1], src, reads=[xT16], writes=[X[g]],
              acc=(b == 0 or b == nb - 1))


def dense_fm(k, X, wring, w_ap, m, epi, reads_w, after_mm=None):
    wt = wring.next()
    k.dma("pool", wt[:, :, 0:m], w_ap, reads=reads_w, writes=[wt], max_dma_last_dim=4096)
    pst = k.psr.next()
    for kc in range(KC):
        k.op("pe", lambda e, kc=kc: e.matmul(pst[0:m, :], wt[:, kc, 0:m], X[kc // 8][:, kc % 8, 1:TB + 1],
                                             start=(kc == 0), stop=(kc == KC - 1)),
             reads=[wt, X[kc // 8]], writes=[pst], acc=(kc > 0), last=(kc == KC - 1))
    if after_mm is not None:
        after_mm()
    epi(pst)


def dense_tm(k, X, wring, w_ap, epi, reads_w, n=512):
    wt = wring.next()
    k.dma("pool", wt[:, 0:16, 0:n], w_ap[:, 0:16, :], reads=reads_w, writes=[wt], max_dma_last_dim=4096)
    k.dma("pool", wt[:, 16:32, 0:n], w_ap[:, 16:32, :], reads=reads_w, writes=[wt], acc=True, max_dma_last_dim=4096)
    for tt in range(4):
        pst = k.psr.next()
        for kc in range(KC):
            k.op("pe", lambda e, kc=kc: e.matmul(pst[:, 0:n], X[kc // 8][:, kc % 8, 1 + tt * 128:1 + (tt + 1) * 128], wt[:, kc, 0:n],
                                                 start=(kc == 0), stop=(kc == KC - 1)),
                 reads=[wt, X[kc // 8]], writes=[pst], acc=(kc > 0), last=(kc == KC - 1))
        epi(pst, tt)


def evac_to_dram(k, pst, m, ncol, stage_ring, dst_ap, dst_t, i, in_ap=None):
    st = stage_ring.next()
    src = pst[0:m, 0:ncol] if in_ap is None else in_ap
    if i % 2 == 0:
        k.op("act", lambda e: e.copy(st[0:m, 0:ncol], src), reads=[pst], writes=[st])
    else:
        k.op("dve", lambda e: e.tensor_copy(st[0:m, 0:ncol], src), reads=[pst], writes=[st])
    k.dma("sp" if i % 2 == 0 else "act", dst_ap, st[0:m, 0:ncol], reads=[st], writes=[dst_t], acc=True)


def phase_even_inproj(k, xT16, L, w_fm, w_lr, w_tm, qT, kT, lrT, hyT, k_tok, v_tok, og_tok):
    nb = L // TB
    with k.scope():
        X = [k.sb([128, 8, TB + 2], BF16, "X") for _ in range(4)]
        wr_fm = Ring([k.sb([128, KC, 128], BF16, "wfm") for _ in range(3)])
        wr_tm = Ring([k.sb([128, KC, 512], BF16, "wtm") for _ in range(2)])
        st32 = Ring([k.sb([128, TB], F32, "st32") for _ in range(4)])
        st16 = Ring([k.sb([128, TB], BF16, "st16") for _ in range(2)])
        for b in range(nb):
            load_X(k, X, xT16, b, L)
            cnt = [0]
            for c in range(64):
                if c < 8:
                    dst_t, dst = qT, qT.h[c, :, sl(b, TB)]
                elif c < 16:
                    dst_t, dst = kT, kT.h[c - 8, :, sl(b, TB)]
                else:
                    dst_t, dst = hyT, hyT.h[c - 16, :, sl(b, TB)]

                def epi(pst, dst=dst, dst_t=dst_t):
                    evac_to_dram(k, pst, 128, TB, st32, dst, dst_t, cnt[0])
                    cnt[0] += 1
                dense_fm(k, X, wr_fm, w_fm.h[c], 128, epi, [w_fm])

            def epi_lr(pst):
                evac_to_dram(k, pst, 32, TB, st32, lrT.h[:, sl(b, TB)], lrT, 0)
            dense_fm(k, X, wr_fm, w_lr.h, 32, epi_lr, [w_lr])
            for n in range(10):
                if n < 2:
                    dst_t, col, ring = k_tok, n * 512, st32
                elif n < 6:
                    dst_t, col, ring = v_tok, (n - 2) * 512, st16
                else:
                    dst_t, col, ring = og_tok, (n - 6) * 512, st32

                def epi(pst, tt, dst_t=dst_t, col=col, ring=ring):
                    t0 = b * TB + tt * 128
                    evac_to_dram(k, pst, 128, 512, ring, dst_t.h[t0:t0 + 128, col:col + 512], dst_t, cnt[0])
                    cnt[0] += 1
                dense_tm(k, X, wr_tm, w_tm.h[n], epi, [w_tm])


def tile_fm(W):
    K, N = W.shape
    return np.ascontiguousarray(W.reshape(K // 128, 128, N // 128, 128).transpose(2, 1, 0, 3))


def tile_tm(W, n=512):
    K, N = W.shape
    return np.ascontiguousarray(W.reshape(K // 128, 128, N // n, n).transpose(2, 1, 0, 3))


def make_consts():
    c = {}
    c["ident32"] = np.eye(128, dtype=np.float32)
    return c


class LNBlock:
    def __init__(self, k, G, S, ones32, eps_t, ps_sum, ps_sq, zT, gtab, btab, ln_i):
        self.k, self.G, self.S = k, G, S
        self.ones32, self.eps_t = ones32, eps_t
        self.ps_sum, self.ps_sq = ps_sum, ps_sq
        self.zT, self.gtab, self.btab, self.ln_i = zT, gtab, btab, ln_i
        self.pending = None

    def prefetch_res(self, xT32_old, dc, b):
        res = self.G.next()
        self.k.dma("sp", res[:, 0:TB], xT32_old.h[dc, :, sl(b, TB)], reads=[xT32_old], writes=[res])
        return res

    def flush(self):
        if self.pending is not None:
            self.pending()
            self.pending = None

    def chunk(self, pst, res, dc, b):
        k = self.k
        z = self.G.next()
        zsq = self.G.next()
        k.op("dve", lambda e: e.scalar_tensor_tensor(out=z[:, 0:TB], in0=res[:, 0:TB], scalar=ALPHA, in1=pst[:, :],
                                                     op0=ALU.mult, op1=ALU.add), reads=[res, pst], writes=[z])
        k.op("act", lambda e: e.activation(out=zsq[:, 0:TB], in_=z[:, 0:TB], func=AF.Square), reads=[z], writes=[zsq])
        k.dma("act", self.zT.h[dc, :, sl(b, TB)], z[:, 0:TB], reads=[z], writes=[self.zT], acc=True)

        def stats(z=z, zsq=zsq, dc=dc):
            k.op("pe", lambda e: e.matmul(self.ps_sum[:, :], self.ones32[:], z[:, 0:TB], start=(dc == 0), stop=(dc == KC - 1)),
                 reads=[self.ones32, z], writes=[self.ps_sum], acc=(dc > 0), last=True)
            k.op("pe", lambda e: e.matmul(self.ps_sq[:, :], self.ones32[:], zsq[:, 0:TB], start=(dc == 0), stop=(dc == KC - 1)),
                 reads=[self.ones32, zsq], writes=[self.ps_sq], acc=(dc > 0), last=True)
        self.pending = stats

    def finish(self, b, xT32_new, xT16_new, out_tok=None, ident32=None):
        k = self.k
        self.flush()
        mean, msq, rstd, nmr = self.S
        k.op("dve", lambda e: e.tensor_scalar(out=mean[:], in0=self.ps_sum[:, :], scalar1=1.0 / D, scalar2=None, op0=ALU.mult),
             reads=[self.ps_sum], writes=[mean])
        k.op("dve", lambda e: e.tensor_tensor(out=msq[:], in0=mean[:], in1=mean[:], op=ALU.mult), reads=[mean], writes=[msq])
        k.op("dve", lambda e: e.scalar_tensor_tensor(out=msq[:], in0=self.ps_sq[:, :], scalar=1.0 / D, in1=msq[:],
                                                     op0=ALU.mult, op1=ALU.subtract), reads=[self.ps_sq, msq], writes=[msq])
        k.op("act", lambda e: e.activation(out=rstd[:], in_=msq[:], func=AF.Sqrt, bias=self.eps_t[:, 0:1], scale=1.0),
             reads=[msq, self.eps_t], writes=[rstd])
        k.op("dve", lambda e: e.reciprocal(out=rstd[:], in_=rstd[:]), reads=[rstd], writes=[rstd])
        k.op("dve", lambda e: e.scalar_tensor_tensor(out=nmr[:], in0=mean[:], scalar=-1.0, in1=rstd[:], op0=ALU.mult, op1=ALU.mult),
             reads=[mean, rstd], writes=[nmr])
        gi = self.ln_i * KC
        for dc in range(KC):
            zl = self.G.next()
            k.dma("sp", zl[:, 0:TB], self.zT.h[dc, :, sl(b, TB)], reads=[self.zT], writes=[zl])
            t1 = self.G.next()
            k.op("dve", lambda e: e.tensor_tensor(out=t1[:, 0:TB], in0=zl[:, 0:TB], in1=rstd[:], op=ALU.mult), reads=[zl, rstd], writes=[t1])
            k.op("pool", lambda e: e.tensor_tensor(out=t1[:, 0:TB], in0=t1[:, 0:TB], in1=nmr[:], op=ALU.add), reads=[t1, nmr], writes=[t1])
            x32 = self.G.next()
            k.op("act", lambda e: e.activation(out=x32[:, 0:TB], in_=t1[:, 0:TB], func=AF.Identity,
                                               bias=self.btab[:, gi + dc:gi + dc + 1], scale=self.gtab[:, gi + dc:gi + dc + 1]),
                 reads=[t1, self.gtab, self.btab], writes=[x32])
            if xT32_new is not None:
                k.dma("sp", xT32_new.h[dc, :, sl(b, TB)], x32[:, 0:TB], reads=[x32], writes=[xT32_new], acc=True)
                x16 = self.G.next()
                x16v = x16.h[:, 0:TB // 2].bitcast(BF16)
                k.op("pool", lambda e: e.tensor_copy(x16v, x32[:, 0:TB]), reads=[x32], writes=[x16])
                k.dma("act", xT16_new.h[dc, :, sl(b, TB)], x16v, reads=[x16], writes=[xT16_new], acc=True)
            if out_tok is not None:
                pst = k.psr.next()
                for tt in range(4):
                    k.op("pe", lambda e, tt=tt: e.transpose(pst[:, sl(tt)], x32[:, sl(tt)], ident32[:]),
                         reads=[x32, ident32], writes=[pst], acc=(tt > 0), last=(tt == 3))
                ot = self.G.next()
                k.op("dve", lambda e: e.tensor_copy(ot[:, 0:TB], pst[:, :]), reads=[pst], writes=[ot])
                for tt in range(4):
                    t0 = b * TB + tt * 128
                    k.dma("sp" if tt % 2 == 0 else "act", out_tok.h[t0:t0 + 128, sl(dc)], ot[:, sl(tt)], reads=[ot], writes=[out_tok], acc=True)


def ln_consts(k, ones32_d, gtab_d, btab_d):
    ones32 = k.sb([128, 128], F32, "ones")
    k.dma("sp", ones32[:], ones32_d.h[:, :], reads=[ones32_d], writes=[ones32])
    eps_t = k.sb([128, 1], F32, "eps")
    k.op("dve", lambda e: e.memset(eps_t[:], LN_EPS), writes=[eps_t])
    gtab = k.sb([128, 4 * KC], F32, "gtab")
    btab = k.sb([128, 4 * KC], F32, "btab")
    k.dma("sp", gtab[:], gtab_d.h[:, :], reads=[gtab_d], writes=[gtab])
    k.dma("sp", btab[:], btab_d.h[:, :], reads=[btab_d], writes=[btab])
    return ones32, eps_t, gtab, btab


def phase_outproj_ln(k, mixT16, L, w_out, xT32_old, xT32_new, xT16_new, zT, ones32_d, gtab_d, btab_d, ln_i):
    nb = L // TB
    with k.scope():
        ones32, eps_t, gtab, btab = ln_consts(k, ones32_d, gtab_d, btab_d)
        X = [k.sb([128, 8, TB + 2], BF16, "X") for _ in range(4)]
        wr = Ring([k.sb([128, KC, 128], BF16, "wfm") for _ in range(3)])
        G = Ring([k.sb([128, TB + 2], F32, "G") for _ in range(12)])
        S = [k.sb([128, TB], F32, "S") for _ in range(4)]
        for b in range(nb):
            load_X(k, X, mixT16, b, L)
            ln = LNBlock(k, G, S, ones32, eps_t, k.ps_sum, k.ps_sq, zT, gtab, btab, ln_i)
            for dc in range(KC):
                res = ln.prefetch_res(xT32_old, dc, b)
                dense_fm(k, X, wr, w_out.h[dc], 128, lambda pst, res=res, dc=dc: ln.chunk(pst, res, dc, b), [w_out], after_mm=ln.flush)
            ln.finish(b, xT32_new, xT16_new)


def phase_ffn(k, xT16, L, w_up, w_dn, cw_d, xT32_old, xT32_new, xT16_new, zT, ones32_d, gtab_d, btab_d, ln_i, layer,
              out_tok=None, ident32_d=None):
    nb = L // TB
    with k.scope():
        ones32, eps_t, gtab, btab = ln_consts(k, ones32_d, gtab_d, btab_d)
        ident32 = None
        if out_tok is not None:
            ident32 = k.sb([128, 128], F32, "ident")
            k.dma("sp", ident32[:], ident32_d.h[:, :], reads=[ident32_d], writes=[ident32])
        cw = k.sb([128, 2 * FC, 4], F32, "cw")
        k.dma("sp", cw[:], cw_d.h[:, layer], reads=[cw_d], writes=[cw])
        X = [k.sb([128, 8, TB + 2], BF16, "X") for _ in range(4)]
        act = [k.sb([128, TB], BF16, "act") for _ in range(FC)]
        wr = Ring([k.sb([128, 43 * 128], BF16, "w") for _ in range(4)])
        G = Ring([k.sb([128, TB + 2], F32, "G") for _ in range(10)])
        S = [k.sb([128, TB], F32, "S") for _ in range(4)]
        for b in range(nb):
            load_X(k, X, xT16, b, L)
            for f in range(FC):
                cs = []
                for half in range(2):
                    fi = half * FC + f
                    wt = wr.next()
                    wv = wt.h[:, 0:KC * 128].rearrange("p (c j) -> p c j", j=128)
                    k.dma("pool", wv, w_up.h[fi], reads=[w_up], writes=[wt], max_dma_last_dim=4096)
                    pst = k.psr.next()
                    hal = k.psh.next()
                    for kc in range(KC):
                        xt = X[kc // 8]
                        k.op("pe", lambda e, kc=kc, xt=xt: e.matmul(pst[:, :], wv[:, kc, :], xt[:, kc % 8, 1:TB + 1],
                                                                   start=(kc == 0), stop=(kc == KC - 1)),
                             reads=[wt, xt], writes=[pst], acc=(kc > 0), last=False)
                        k.op("pe", lambda e, kc=kc, xt=xt: e.matmul(hal[:, 0:2], wv[:, kc, :], xt[:, kc % 8, 0:TB + 2:TB + 1],
                                                                   start=(kc == 0), stop=(kc == KC - 1)),
                             reads=[wt, xt], writes=[hal], acc=(kc > 0), last=(kc == KC - 1))
                    hb = G.next()
                    k.op("act", lambda e: e.copy(hb[:, 1:TB + 1], pst[:, :]), reads=[pst], writes=[hb])
                    k.op("act", lambda e: e.copy(hb[:, 0:TB + 2:TB + 1], hal[:, 0:2]), reads=[hal], writes=[hb], acc=True)
                    c = G.next()
                    k.op("dve", lambda e: e.tensor_scalar(out=c[:, 0:TB], in0=hb[:, 0:TB], scalar1=cw[:, fi, 0:1], scalar2=cw[:, fi, 3:4],
                                                          op0=ALU.mult, op1=ALU.add), reads=[hb, cw], writes=[c])
                    k.op("dve", lambda e: e.scalar_tensor_tensor(out=c[:, 0:TB], in0=hb[:, 1:TB + 1], scalar=cw[:, fi, 1:2], in1=c[:, 0:TB],
                                                                 op0=ALU.mult, op1=ALU.add), reads=[hb, cw, c], writes=[c])
                    k.op("dve", lambda e: e.scalar_tensor_tensor(out=c[:, 0:TB], in0=hb[:, 2:TB + 2], scalar=cw[:, fi, 2:3], in1=c[:, 0:TB],
                                                                 op0=ALU.mult, op1=ALU.add), reads=[hb, cw, c], writes=[c])
                    cs.append(c)
                sg = G.next()
                k.op("act", lambda e: e.activation(out=sg[:, 0:TB], in_=cs[0][:, 0:TB], func=AF.Silu), reads=[cs[0]], writes=[sg])
                k.op("pool", lambda e: e.tensor_tensor(out=act[f][:], in0=sg[:, 0:TB], in1=cs[1][:, 0:TB], op=ALU.mult),
                     reads=[sg, cs[1]], writes=[act[f]])
            ln = LNBlock(k, G, S, ones32, eps_t, k.ps_sum, k.ps_sq, zT, gtab, btab, ln_i)
            for dc in range(KC):
                res = ln.prefetch_res(xT32_old, dc, b)
                wts = []
                for h in range(2):
                    wt = wr.next()
                    k.dma("pool", wt[:, :], w_dn.h[dc, :, h * 43 * 128:(h + 1) * 43 * 128], reads=[w_dn], writes=[wt], max_dma_last_dim=4096)
                    wts.append(wt)
                pst = k.psr.next()
                for fc in range(FC):
                    wt = wts[fc // 43]
                    o = (fc % 43) * 128
                    k.op("pe", lambda e, wt=wt, o=o, fc=fc: e.matmul(pst[:, :], wt[:, o:o + 128], act[fc][:], start=(fc == 0), stop=(fc == FC - 1)),
                         reads=[wt, act[fc]], writes=[pst], acc=(fc > 0), last=(fc == FC - 1))
                ln.flush()
                ln.chunk(pst, res, dc, b)
            ln.finish(b, xT32_new, xT16_new, out_tok=out_tok, ident32=ident32)


def phase_gla(k, L, qT, kT, k_tok, v_tok, og_tok, lrT, wg_d, gnorm_d, cst_d, ident32_d, of_tok, mixT16):
    NCH = L // 128
    with k.scope():
        cst = k.sb([128, 4, 128], F32, "cst")
        k.dma("sp", cst[:], cst_d.h.rearrange("a p c -> p a c"), reads=[cst_d], writes=[cst])
        ident32 = k.sb([128, 128], F32, "ident")
        k.dma("sp", ident32[:], ident32_d.h[:, :], reads=[ident32_d], writes=[ident32])
        gn = k.sb([128, 512], F32, "gn")
        k.dma("sp", gn[:], gnorm_d.h[:, :], reads=[gnorm_d], writes=[gn])
        wg = [k.sb([17, 1024], F32, "wg") for _ in range(2)]
        lra = [k.sb([17, L], F32, "lra") for _ in range(2)]
        for d in range(2):
            k.dma("sp", wg[d][:], wg_d.h[d], reads=[wg_d], writes=[wg[d]])
            k.op("dve", lambda e, d=d: e.memset(lra[d][:], 1.0), writes=[lra[d]])
            k.dma("sp", lra[d][0:16, :], lrT.h[16 * d:16 * d + 16, :], reads=[lrT], writes=[lra[d]])
        epsr = k.sb([128, 1], F32, "epsr")
        k.op("dve", lambda e: e.memset(epsr[:], RMS_EPS), writes=[epsr])
        S32 = [k.sb([128, 512], F32, "S32") for _ in range(2)]
        S16 = [k.sb([128, 512], BF16, "S16") for _ in range(2)]
        qr = Ring([k.sb([128, 2, 128], F32, "q") for _ in range(2)])
        kr = Ring([k.sb([128, 2, 128], F32, "kk") for _ in range(2)])
        ktr = Ring([k.sb([128, 256], F32, "kt") for _ in range(2)])
        vr = Ring([k.sb([128, 512], BF16, "v") for _ in range(2)])
        ogr = Ring([k.sb([128, 512], F32, "og") for _ in range(2)])
        ofr = Ring([k.sb([128, 512], F32, "of") for _ in range(2)])
        A = Ring([k.sb([128, 256], F32, "A") for _ in range(8)])
        Bq = Ring([k.sb([128, 2, 128], BF16, "Bq") for _ in range(4)])
        Bk = Ring([k.sb([128, 256], BF16, "Bk") for _ in range(2)])
        Pr = Ring([k.sb([128, 128], BF16, "P") for _ in range(2)])
        O = Ring([k.sb([128, 512], F32, "O") for _ in range(6)])
        sm = Ring([k.sb([128, 1], F32, "sm") for _ in range(6)])
        tr16 = Ring([k.sb([128, 4, 128], BF16, "tr") for _ in range(2)])
        for h in range(4):
            for d in range(2):
                tri = cst[:, 0 + d, :]
                ust = cst[:, 2 + d, :]
                for j in range(2):
                    k.op("dve", lambda e, j=j: e.memset(S32[j][:], 0.0), writes=[S32[j]])
                    k.op("pool", lambda e, j=j: e.memset(S16[j][:], 0.0), writes=[S16[j]])
                order = range(NCH) if d == 0 else range(NCH - 1, -1, -1)
                for n in order:
                    ts = slice(n * 128, (n + 1) * 128)
                    qt, kt, ktt, vt = qr.next(), kr.next(), ktr.next(), vr.next()
                    k.dma("sp", qt[:], qT.h[2 * h:2 * h + 2, :, ts].rearrange("j p c -> p j c"), reads=[qT], writes=[qt])
                    k.dma("act", kt[:], kT.h[2 * h:2 * h + 2, :, ts].rearrange("j p c -> p j c"), reads=[kT], writes=[kt])
                    k.dma("sp", ktt[:], k_tok.h[ts, h * 256:(h + 1) * 256], reads=[k_tok], writes=[ktt])
                    k.dma("act", vt[:], v_tok.h[ts, h * 512:(h + 1) * 512], reads=[v_tok], writes=[vt])
                    p1 = k.psr.next()
                    k.op("pe", lambda e: e.matmul(p1[:, 0:256], lra[d][0:17, ts], wg[d][0:17, h * 256:(h + 1) * 256], start=True, stop=True),
                         reads=[lra[d], wg[d]], writes=[p1])
                    ex = A.next()
                    k.op("act", lambda e: e.activation(out=ex[:], in_=p1[:, 0:256], func=AF.Exp, scale=-1.0), reads=[p1], writes=[ex])
                    la = A.next()
                    k.op("act", lambda e: e.activation(out=la[:], in_=ex[:], func=AF.Ln, bias=1.0, scale=1.0), reads=[ex], writes=[la])
                    p2 = k.psr.next()
                    k.op("pe", lambda e: e.matmul(p2[:, 0:256], ust, la[:], start=True, stop=True), reads=[cst, la], writes=[p2])
                    kd = A.next()
                    k.op("act", lambda e: e.activation(out=kd[:], in_=p2[:, 0:256], func=AF.Exp, scale=-1.0 / 16), reads=[p2], writes=[kd])
                    kdec = Bk.next()
                    k.op("dve", lambda e: e.tensor_tensor(out=kdec[:], in0=ktt[:], in1=kd[:], op=ALU.mult), reads=[ktt, kd], writes=[kdec])
                    p3 = k.psr.next()
                    for j in range(2):
                        k.op("pe", lambda e, j=j: e.matmul(p3[:, j * 128:(j + 1) * 128], la[:, j * 128:(j + 1) * 128], tri, start=True, stop=True),
                             reads=[la, cst], writes=[p3], acc=(j > 0), last=(j == 1))
                    eb = A.next()
                    ei = A.next()
                    k.op("act", lambda e: e.activation(out=eb[:], in_=p3[:, 0:256], func=AF.Exp, scale=-1.0 / 16), reads=[p3], writes=[eb])
                    k.op("act", lambda e: e.activation(out=ei[:], in_=p3[:, 0:256], func=AF.Exp, scale=1.0 / 16), reads=[p3], writes=[ei])
                    qd = Bq.next()
                    ki = Bq.next()
                    k.op("dve", lambda e: e.scalar_tensor_tensor(out=qd.h[:].rearrange("p j c -> p (j c)"), in0=qt.h[:].rearrange("p j c -> p (j c)"),
                                                                 scalar=1.0 / 16, in1=eb[:], op0=ALU.mult, op1=ALU.mult),
                         reads=[qt, eb], writes=[qd])
                    k.op("pool", lambda e: e.tensor_tensor(out=ki.h[:].rearrange("p j c -> p (j c)"), in0=kt.h[:].rearrange("p j c -> p (j c)"),
                                                           in1=ei[:], op=ALU.mult), reads=[kt, ei], writes=[ki])
                    p4 = k.psr.next()
                    for j in range(2):
                        k.op("pe", lambda e, j=j: e.matmul(p4[:, 0:128], ki[:, j, :], qd[:, j, :], start=(j == 0), stop=(j == 1)),
                             reads=[ki, qd], writes=[p4], acc=(j > 0), last=(j == 1))
                    P = Pr.next()
                    k.op("dve", lambda e: e.tensor_tensor(out=P[:], in0=p4[:, 0:128], in1=tri, op=ALU.mult), reads=[p4, cst], writes=[P])
                    p5 = k.psr.next()
                    k.op("pe", lambda e: e.matmul(p5[:, :], P[:], vt[:], start=True, stop=False), reads=[P, vt], writes=[p5], last=False)
                    for j in range(2):
                        k.op("pe", lambda e, j=j: e.matmul(p5[:, :], qd[:, j, :], S16[j][:], start=False, stop=(j == 1)),
                             reads=[qd, S16[j]], writes=[p5], acc=True, last=(j == 1))
                    lastc = 127 if d == 0 else 0
                    for j in range(2):
                        p6 = k.psr.next()
                        k.op("pe", lambda e, j=j: e.matmul(p6[:, :], kdec[:, j * 128:(j + 1) * 128], vt[:], start=True, stop=True),
                             reads=[kdec, vt], writes=[p6])
                        k.op("dve", lambda e, j=j, p6=p6: e.scalar_tensor_tensor(out=S32[j][:], in0=S32[j][:], scalar=eb[:, j * 128 + lastc:j * 128 + lastc + 1],
                                                                                 in1=p6[:, :], op0=ALU.mult, op1=ALU.add),
                             reads=[S32[j], eb, p6], writes=[S32[j]])
                        k.op("act", lambda e, j=j: e.copy(S16[j][:], S32[j][:]), reads=[S32[j]], writes=[S16[j]])
                    if d == 0:
                        o = O.next()
                        k.op("act", lambda e: e.copy(o[:], p5[:, :]), reads=[p5], writes=[o])
                        k.dma("sp", of_tok.h[ts, h * 512:(h + 1) * 512], o[:], reads=[o], writes=[of_tok], acc=True)
                    else:
                        oft, ogt = ofr.next(), ogr.next()
                        k.dma("sp", oft[:], of_tok.h[ts, h * 512:(h + 1) * 512], reads=[of_tok], writes=[oft])
                        k.dma("act", ogt[:], og_tok.h[ts, h * 512:(h + 1) * 512], reads=[og_tok], writes=[ogt])
                        o = O.next()
                        k.op("dve", lambda e: e.tensor_tensor(out=o[:], in0=oft[:], in1=p5[:, :], op=ALU.add), reads=[oft, p5], writes=[o])
                        sq = O.next()
                        ss = sm.next()
                        k.op("act", lambda e: e.activation(out=sq[:], in_=o[:], func=AF.Square, accum_out=ss[:]), reads=[o], writes=[sq, ss])
                        rs = sm.next()
                        k.op("act", lambda e: e.activation(out=rs[:], in_=ss[:], func=AF.Sqrt, bias=epsr[:, 0:1], scale=1.0 / 512),
                             reads=[ss, epsr], writes=[rs])
                        k.op("dve", lambda e: e.reciprocal(out=rs[:], in_=rs[:]), reads=[rs], writes=[rs])
                        on = O.next()
                        k.op("dve", lambda e: e.scalar_tensor_tensor(out=on[:], in0=o[:], scalar=rs[:, 0:1], in1=gn[:], op0=ALU.mult, op1=ALU.mult),
                             reads=[o, rs, gn], writes=[on])
                        sg = O.next()
                        k.op("act", lambda e: e.activation(out=sg[:], in_=ogt[:], func=AF.Silu), reads=[ogt], writes=[sg])
                        k.op("pool", lambda e: e.tensor_tensor(out=on[:], in0=on[:], in1=sg[:], op=ALU.mult), reads=[on, sg], writes=[on])
                        p7 = k.psr.next()
                        for j in range(4):
                            k.op("pe", lambda e, j=j: e.transpose(p7[:, sl(j)], on[:, sl(j)], ident32[:]),
                                 reads=[on, ident32], writes=[p7], acc=(j > 0), last=(j == 3))
                        t16 = tr16.next()
                        k.op("dve", lambda e: e.tensor_copy(t16.h[:].rearrange("p j c -> p (j c)"), p7[:, :]), reads=[p7], writes=[t16])
                        k.dma("sp", mixT16.h[4 * h:4 * h + 4, :, ts].rearrange("j p c -> p j c"), t16[:], reads=[t16], writes=[mixT16], acc=True)


def gla_consts():
    i = np.arange(128)
    tri = (i[:, None] <= i[None, :]).astype(np.float32)
    ust = (i[:, None] > i[None, :]).astype(np.float32)
    return np.stack([tri, tri.T.copy(), ust, ust.T.copy()])


MAGIC = 12582912.0
TWO_PI = 6.283185307179586


def hy_consts(L):
    N = 2 * L
    N1 = N // 128
    T1 = L // 128
    t = np.linspace(0.0, 1.0, L, dtype=np.float32)[:, None]
    bands = np.linspace(1e-4, 15.0, 16, dtype=np.float32)
    w = (2.0 * np.pi * np.arange(L, dtype=np.float32)[:, None] / L).astype(np.float32)
    z = np.concatenate([t, np.cos(bands * w), -np.sin(bands * w), np.ones((L, 1), np.float32)], -1).astype(np.float32)
    c = {}
    c["hy_zT"] = np.ascontiguousarray(z.T)
    max_decay = np.log(1e-2) / 0.3
    min_decay = np.log(1e-2) / 1.5
    deltas = np.linspace(min_decay, max_decay, 2048, dtype=np.float32)
    c["hy_absd"] = np.tile(np.abs(deltas)[None, :], (128, 1)).astype(np.float32)
    tau = (np.arange(L).reshape(T1, 128).T).astype(np.float64)
    c["hy_negt"] = np.ascontiguousarray((-(tau / (L - 1))).astype(np.float32))
    t1 = np.arange(N1)[:, None, None]
    t2 = np.arange(128)[None, :, None]
    f1 = np.arange(N1)[None, None, :]
    ang = 2.0 * np.pi * ((f1 * (128 * t1 + t2)) % N) / N
    fw = np.stack([np.cos(ang), -np.sin(ang)], 2).astype(np.float32)
    c["hy_fw"] = np.ascontiguousarray(fw[:T1])
    iv = np.stack([np.cos(ang), -np.sin(ang)], 2) / N
    c["hy_iv"] = np.ascontiguousarray(iv[:T1].transpose(3, 1, 2, 0)).astype(np.float32)
    a = 2.0 * np.pi * ((np.arange(128)[:, None] * np.arange(128)[None, :]) % 128) / 128
    c["hy_cs"] = np.stack([np.cos(a), np.sin(a), -np.sin(a)]).astype(np.float32)
    return c


def hy_sin(k, G, ps, bias_ap, bias_t, out_ap, out_t, m):
    xs, n1, xr = G.next(), G.next(), G.next()
    if bias_ap is None:
        k.op("dve", lambda e: e.tensor_copy(xs[0:m, :], ps[0:m, :]), reads=[ps], writes=[xs])
    else:
        k.op("dve", lambda e: e.tensor_scalar(out=xs[0:m, :], in0=ps[0:m, :], scalar1=bias_ap, scalar2=None, op0=ALU.add),
             reads=[ps, bias_t], writes=[xs])
    k.op("dve", lambda e: e.tensor_scalar(out=n1[0:m, :], in0=xs[0:m, :], scalar1=1.0 / TWO_PI, scalar2=MAGIC, op0=ALU.mult, op1=ALU.add),
         reads=[xs], writes=[n1])
    k.op("dve", lambda e: e.tensor_scalar(out=n1[0:m, :], in0=n1[0:m, :], scalar1=-MAGIC, scalar2=None, op0=ALU.add), reads=[n1], writes=[n1])
    k.op("dve", lambda e: e.scalar_tensor_tensor(out=xr[0:m, :], in0=n1[0:m, :], scalar=-TWO_PI, in1=xs[0:m, :], op0=ALU.mult, op1=ALU.add),
         reads=[n1, xs], writes=[xr])
    k.op("act", lambda e: e.activation(out=out_ap, in_=xr[0:m, :], func=AF.Sin), reads=[xr], writes=[out_t])


def hy_filters(k, L, zT_d, w1a_d, w2_d, b2_d, w3_d, absd_d, negt_d, kf_tok, kb_tok):
    T1 = L // 128
    with k.scope():
        zT = k.sb([34, L], F32, "zT")
        k.dma("sp", zT[:], zT_d.h[:, :], reads=[zT_d], writes=[zT])
        w1a = k.sb([34, 64], F32, "w1a")
        k.dma("sp", w1a[:], w1a_d.h[:, :], reads=[w1a_d], writes=[w1a])
        w2 = k.sb([64, 64], F32, "w2")
        k.dma("sp", w2[:], w2_d.h[:, :], reads=[w2_d], writes=[w2])
        b2 = k.sb([64, 1], F32, "b2")
        k.dma("sp", b2[:], b2_d.h[:, :], reads=[b2_d], writes=[b2])
        w3 = k.sb([64, 4096], F32, "w3")
        k.dma("act", w3[:], w3_d.h[:, :], reads=[w3_d], writes=[w3])
        absd = k.sb([128, 2048], F32, "absd")
        k.dma("act", absd[:], absd_d.h[:, :], reads=[absd_d], writes=[absd])
        negt = k.sb([128, T1], F32, "negt")
        k.dma("sp", negt[:], negt_d.h[:, :], reads=[negt_d], writes=[negt])
        h2T = k.sb([64, L], F32, "h2T")
        G = Ring([k.sb([128, 512], F32, "G") for _ in range(10)])
        h1r = Ring([k.sb([64, 512], F32, "h1") for _ in range(2)])
        for b in range(L // 512):
            p = k.psr.next()
            k.op("pe", lambda e: e.matmul(p[0:64, :], w1a[:, :], zT[:, sl(b, 512)], start=True, stop=True), reads=[w1a, zT], writes=[p])
            h1 = h1r.next()
            hy_sin(k, G, p, None, None, h1[:, :], h1, 64)
            p2 = k.psr.next()
            k.op("pe", lambda e: e.matmul(p2[0:64, :], w2[:, :], h1[:, :], start=True, stop=True), reads=[w2, h1], writes=[p2])
            hy_sin(k, G, p2, b2[:, 0:1], b2, h2T[:, sl(b, 512)], h2T, 64)
        i = 0
        for n in range(T1):
            for c4 in range(4):
                win = G.next()
                k.op("act", lambda e: e.activation(out=win[:], in_=absd[:, sl(c4, 512)], func=AF.Exp, scale=negt[:, n:n + 1]),
                     reads=[absd, negt], writes=[win])
                for fb in range(2):
                    p = k.psr.next()
                    col = fb * 2048 + c4 * 512
                    k.op("pe", lambda e: e.matmul(p[:, :], h2T[:, sl(n)], w3[:, col:col + 512], start=True, stop=True), reads=[h2T, w3], writes=[p])
                    st = G.next()
                    k.op("dve", lambda e: e.tensor_tensor(out=st[:], in0=p[:, :], in1=win[:], op=ALU.mult), reads=[p, win], writes=[st])
                    dst = kf_tok if fb == 0 else kb_tok
                    if fb == 1 and n == 0:
                        k.op("dve", lambda e: e.memset(st[0:1, :], 0.0), writes=[st])
                    k.dma("sp" if i % 2 == 0 else "act", dst.h[sl(n), sl(c4, 512)], st[:], reads=[st], writes=[dst], acc=True)
                    i += 1


def hy_prep_u(k, L, hyT, hcw_d, ident32_d, uT, u_tok):
    with k.scope():
        hcw = k.sb([128, 48, 4], F32, "hcw")
        k.dma("sp", hcw[:], hcw_d.h[:, :, :], reads=[hcw_d], writes=[hcw])
        ident32 = k.sb([128, 128], F32, "ident")
        k.dma("sp", ident32[:], ident32_d.h[:, :], reads=[ident32_d], writes=[ident32])
        inr = Ring([k.sb([128, L + 2], F32, "in") for _ in range(3)])
        cr = Ring([k.sb([128, L], F32, "c") for _ in range(3)])
        st = Ring([k.sb([128, 4, 128], F32, "st") for _ in range(3)])
        for cc in range(16):
            cs = []
            for which in (16, 32):
                ch = which + cc
                t = inr.next()
                k.op("pool", lambda e: e.memset(t[:, 0:L + 2:L + 1], 0.0), writes=[t])
                k.dma("sp" if which == 16 else "act", t[:, 1:L + 1], hyT.h[ch], reads=[hyT], writes=[t], acc=True)
                c = cr.next()
                hy_conv(k, c, t, hcw, ch, L)
                cs.append(c)
            u = cs[0]
            k.op("pool", lambda e: e.tensor_tensor(out=u[:], in0=cs[0][:], in1=cs[1][:], op=ALU.mult), reads=[cs[0], cs[1]], writes=[u])
            k.dma("sp", uT.h[cc], u[:], reads=[u], writes=[uT], acc=True)
            for g in range(L // 512):
                p = k.psr.next()
                for j in range(4):
                    k.op("pe", lambda e, j=j: e.transpose(p[:, sl(j)], u[:, g * 512 + j * 128:g * 512 + (j + 1) * 128], ident32[:]),
                         reads=[u, ident32], writes=[p], acc=(j > 0), last=(j == 3))
                s = st.next()
                if g % 2 == 0:
                    k.op("act", lambda e: e.copy(s.h[:].rearrange("p a c -> p (a c)"), p[:, :]), reads=[p], writes=[s])
                else:
                    k.op("dve", lambda e: e.tensor_copy(s.h[:].rearrange("p a c -> p (a c)"), p[:, :]), reads=[p], writes=[s])
                k.dma("act" if g % 2 == 0 else "sp", u_tok.h[g * 512:(g + 1) * 512, sl(cc)].rearrange("(a p) c -> p a c", p=128), s[:],
                      reads=[s], writes=[u_tok], acc=True)


def hy_conv(k, c, t, hcw, ch, L):
    k.op("dve", lambda e: e.tensor_scalar(out=c[:, 0:L], in0=t[:, 0:L], scalar1=hcw[:, ch, 0:1], scalar2=hcw[:, ch, 3:4], op0=ALU.mult, op1=ALU.add),
         reads=[t, hcw], writes=[c])
    k.op("dve", lambda e: e.scalar_tensor_tensor(out=c[:, 0:L], in0=t[:, 1:L + 1], scalar=hcw[:, ch, 1:2], in1=c[:, 0:L], op0=ALU.mult, op1=ALU.add),
         reads=[t, hcw, c], writes=[c])
    k.op("dve", lambda e: e.scalar_tensor_tensor(out=c[:, 0:L], in0=t[:, 2:L + 2], scalar=hcw[:, ch, 2:3], in1=c[:, 0:L], op0=ALU.mult, op1=ALU.add),
         reads=[t, hcw, c], writes=[c])


def hy_fft_a(k, L, src_tok, fw_d, scrA):
    N1, T1 = 2 * L // 128, L // 128
    with k.scope():
        fw = k.sb([T1, 128, 2, N1], F32, "fw")
        k.dma("sp", fw[:, 0:64], fw_d.h[:, 0:64], reads=[fw_d], writes=[fw])
        k.dma("act", fw[:, 64:128], fw_d.h[:, 64:128], reads=[fw_d], writes=[fw], acc=True)
        sr = Ring([k.sb([T1, 2048], F32, "s") for _ in range(3)])
        st = Ring([k.sb([N1, 2048], F32, "st") for _ in range(4)])
        src_v = src_tok.h.rearrange("(a p) c -> p a c", p=128)
        i = 0
        for t2 in range(128):
            s = sr.next()
            k.dma("sp" if t2 % 2 == 0 else "act", s[:], src_v[t2], reads=[src_tok], writes=[s])
            for ri in range(2):
                so = st.next()
                for ct in range(4):
                    p = k.psr.next()
                    k.op("pe", lambda e: e.matmul(p[0:N1, :], fw[:, t2, ri, :], s[:, sl(ct, 512)], start=True, stop=True), reads=[fw, s], writes=[p])
                    if i % 2 == 0:
                        k.op("act", lambda e: e.copy(so[:, sl(ct, 512)], p[0:N1, :]), reads=[p], writes=[so], acc=(ct > 0))
                    else:
                        k.op("dve", lambda e: e.tensor_copy(so[:, sl(ct, 512)], p[0:N1, :]), reads=[p], writes=[so], acc=(ct > 0))
                    i += 1
                k.dma("sp" if ri == 0 else "act", scrA.h[ri, :, t2, :], so[:], reads=[so], writes=[scrA], acc=True)


def hy_fft_b(k, L, scrA, cs_d, epi, mk_extra):
    N1 = 2 * L // 128
    with k.scope():
        cs = k.sb([128, 3, 128], F32, "cs")
        k.dma("sp", cs[:], cs_d.h.rearrange("a p c -> p a c"), reads=[cs_d], writes=[cs])
        ar = Ring([k.sb([128, 2, 2048], F32, "a") for _ in range(2)])
        ctx = mk_extra(cs)
        for f1 in range(N1):
            a = ar.next()
            k.dma("sp", a[:, 0, :], scrA.h[0, f1], reads=[scrA], writes=[a])
            k.dma("act", a[:, 1, :], scrA.h[1, f1], reads=[scrA], writes=[a], acc=True)
            for ct in range(4):
                pr, pi = k.psr.next(), k.psr.next()
                c = slice(ct * 512, (ct + 1) * 512)
                k.op("pe", lambda e: e.matmul(pr[:, :], cs[:, 0, :], a[:, 0, c], start=True, stop=False), reads=[cs, a], writes=[pr], last=False)
                k.op("pe", lambda e: e.matmul(pr[:, :], cs[:, 1, :], a[:, 1, c], start=False, stop=True), reads=[cs, a], writes=[pr], acc=True)
                k.op("pe", lambda e: e.matmul(pi[:, :], cs[:, 0, :], a[:, 1, c], start=True, stop=False), reads=[cs, a], writes=[pi], last=False)
                k.op("pe", lambda e: e.matmul(pi[:, :], cs[:, 2, :], a[:, 0, c], start=False, stop=True), reads=[cs, a], writes=[pi], acc=True)
                epi(f1, ct, pr, pi, ctx)


def hy_spectrum_store(k, scrK, combine):
    def mk(cs):
        return {"st": Ring([k.sb([128, 2, 512], F32, "kst") for _ in range(3)]),
                "ld": Ring([k.sb([128, 2, 512], F32, "kld") for _ in range(3)])}

    def epi(f1, ct, pr, pi, ctx):
        c = slice(ct * 512, (ct + 1) * 512)
        s = ctx["st"].next()
        if not combine:
            k.op("act", lambda e: e.copy(s[:, 0, :], pr[:, :]), reads=[pr], writes=[s])
            k.op("dve", lambda e: e.tensor_copy(s[:, 1, :], pi[:, :]), reads=[pi], writes=[s], acc=True)
        else:
            ld = ctx["ld"].next()
            k.dma("sp", ld[:], scrK.h[:, f1, :, c].rearrange("r p c -> p r c"), reads=[scrK], writes=[ld])
            k.op("dve", lambda e: e.tensor_tensor(out=s[:, 0, :], in0=ld[:, 0, :], in1=pr[:, :], op=ALU.add), reads=[ld, pr], writes=[s])
            k.op("dve", lambda e: e.tensor_tensor(out=s[:, 1, :], in0=ld[:, 1, :], in1=pi[:, :], op=ALU.subtract), reads=[ld, pi], writes=[s], acc=True)
        k.dma("act", scrK.h[:, f1, :, c].rearrange("r p c -> p r c"), s[:], reads=[s], writes=[scrK], acc=True)
    return mk, epi


def hy_mul_inv(k, scrK, scrG):
    def mk(cs):
        return {"cs": cs, "ld": Ring([k.sb([128, 2, 512], F32, "kld") for _ in range(3)]),
                "y": Ring([k.sb([128, 2, 512], F32, "y") for _ in range(2)]),
                "t": Ring([k.sb([128, 512], F32, "t") for _ in range(4)]),
                "g": Ring([k.sb([128, 2, 512], F32, "g") for _ in range(3)])}

    def epi(f1, ct, pr, pi, ctx):
        cs = ctx["cs"]
        c = slice(ct * 512, (ct + 1) * 512)
        ld = ctx["ld"].next()
        k.dma("sp", ld[:], scrK.h[:, f1, :, c].rearrange("r p c -> p r c"), reads=[scrK], writes=[ld])
        y = ctx["y"].next()
        ta, tb = ctx["t"].next(), ctx["t"].next()
        k.op("dve", lambda e: e.tensor_tensor(out=ta[:], in0=pr[:, :], in1=ld[:, 0, :], op=ALU.mult), reads=[pr, ld], writes=[ta])
        k.op("act", lambda e: e.copy(tb[:], pi[:, :]), reads=[pi], writes=[tb])
        k.op("dve", lambda e: e.tensor_tensor(out=y[:, 1, :], in0=pr[:, :], in1=ld[:, 1, :], op=ALU.mult), reads=[pr, ld], writes=[y])
        tc_ = ctx["t"].next()
        k.op("pool", lambda e: e.tensor_tensor(out=tc_[:], in0=tb[:], in1=ld[:, 1, :], op=ALU.mult), reads=[tb, ld], writes=[tc_])
        k.op("pool", lambda e: e.tensor_tensor(out=y[:, 0, :], in0=ta[:], in1=tc_[:], op=ALU.subtract), reads=[ta, tc_], writes=[y], acc=True)
        td = ctx["t"].next()
        k.op("dve", lambda e: e.tensor_tensor(out=td[:], in0=tb[:], in1=ld[:, 0, :], op=ALU.mult), reads=[tb, ld], writes=[td])
        k.op("dve", lambda e: e.tensor_tensor(out=y[:, 1, :], in0=y[:, 1, :], in1=td[:], op=ALU.add), reads=[y, td], writes=[y], acc=True)
        gr, gi = k.psr.next(), k.psr.next()
        k.op("pe", lambda e: e.matmul(gr[:, :], cs[:, 0, :], y[:, 0, :], start=True, stop=False), reads=[cs, y], writes=[gr], last=False)
        k.op("pe", lambda e: e.matmul(gr[:, :], cs[:, 2, :], y[:, 1, :], start=False, stop=True), reads=[cs, y], writes=[gr], acc=True)
        k.op("pe", lambda e: e.matmul(gi[:, :], cs[:, 0, :], y[:, 1, :], start=True, stop=False), reads=[cs, y], writes=[gi], last=False)
        k.op("pe", lambda e: e.matmul(gi[:, :], cs[:, 1, :], y[:, 0, :], start=False, stop=True), reads=[cs, y], writes=[gi], acc=True)
        g = ctx["g"].next()
        k.op("act", lambda e: e.copy(g[:, 0, :], gr[:, :]), reads=[gr], writes=[g])
        k.op("dve", lambda e: e.tensor_copy(g[:, 1, :], gi[:, :]), reads=[gi], writes=[g], acc=True)
        k.dma("act", scrG.h[:, :, f1, c].rearrange("r p c -> p r c"), g[:], reads=[g], writes=[scrG], acc=True)
    return mk, epi


def hy_fft_c(k, L, scrG, iv_d, hyT, uT, hcw_d, hbias_d, mixT16):
    N1, T1 = 2 * L // 128, L // 128
    TPB = 512 // T1
    with k.scope():
        iv = k.sb([N1, 128, 2, T1], F32, "iv")
        k.dma("sp", iv[:, 0:64], iv_d.h[:, 0:64], reads=[iv_d], writes=[iv])
        k.dma("act", iv[:, 64:128], iv_d.h[:, 64:128], reads=[iv_d], writes=[iv], acc=True)
        hcw = k.sb([128, 48, 4], F32, "hcw")
        k.dma("sp", hcw[:], hcw_d.h[:, :, :], reads=[hcw_d], writes=[hcw])
        hb = k.sb([128, 16], F32, "hb")
        k.dma("sp", hb[:], hbias_d.h[:, :], reads=[hbias_d], writes=[hb])
        gr = Ring([k.sb([N1, 2, 512], F32, "g") for _ in range(3)])
        yT = [k.sb([128, L], F32, "yT") for _ in range(4)]
        x0r = Ring([k.sb([128, L + 2], F32, "x0") for _ in range(1)])
        ur = Ring([k.sb([128, L], F32, "u") for _ in range(1)])
        cr = Ring([k.sb([128, L], F32, "c") for _ in range(1)])
        o16 = Ring([k.sb([128, L], BF16, "o16") for _ in range(2)])
        allps = Ring(k.allps)
        for cg in range(4):
            banks = None
            for t2 in range(128):
                if t2 % TPB == 0:
                    banks = [allps.next() for _ in range(4)]
                g = gr.next()
                k.dma("sp" if t2 % 2 == 0 else "act", g[:], scrG.h[:, t2, :, sl(cg, 512)].rearrange("r f c -> f r c"), reads=[scrG], writes=[g])
                o = (t2 % TPB) * T1
                for j in range(4):
                    k.op("pe", lambda e, j=j: e.matmul(banks[j][:, o:o + T1], g[:, 0, sl(j)], iv[:, t2, 0, :], start=True, stop=False),
                         reads=[g, iv], writes=[banks[j]], acc=True, last=False)
                    k.op("pe", lambda e, j=j: e.matmul(banks[j][:, o:o + T1], g[:, 1, sl(j)], iv[:, t2, 1, :], start=False, stop=True),
                         reads=[g, iv], writes=[banks[j]], acc=True, last=True)
                if t2 % TPB == TPB - 1:
                    t2a = t2 - (TPB - 1)
                    for j in range(4):
                        dst = yT[j].h[:, :].rearrange("p (a b) -> p b a", b=128)[:, t2a:t2a + TPB, :]
                        srcp = banks[j].h[:, 0:TPB * T1].rearrange("p (b a) -> p b a", a=T1)
                        if j % 2 == 0:
                            k.op("act", lambda e: e.copy(dst, srcp), reads=[banks[j]], writes=[yT[j]], acc=True)
                        else:
                            k.op("dve", lambda e: e.tensor_copy(dst, srcp), reads=[banks[j]], writes=[yT[j]], acc=True)
            for j in range(4):
                cc = cg * 4 + j
                u = ur.next()
                k.dma("sp", u[:], uT.h[cc], reads=[uT], writes=[u])
                t = x0r.next()
                k.op("pool", lambda e: e.memset(t[:, 0:L + 2:L + 1], 0.0), writes=[t])
                k.dma("act", t[:, 1:L + 1], hyT.h[cc], reads=[hyT], writes=[t], acc=True)
                c = cr.next()
                hy_conv(k, c, t, hcw, cc, L)
                k.op("dve", lambda e: e.scalar_tensor_tensor(out=u[:], in0=u[:], scalar=hb[:, cc:cc + 1], in1=yT[j][:], op0=ALU.mult, op1=ALU.add),
                     reads=[u, hb, yT[j]], writes=[u])
                o = o16.next()
                k.op("pool", lambda e: e.tensor_tensor(out=o[:], in0=u[:], in1=c[:], op=ALU.mult), reads=[u, c], writes=[o])
                k.dma("sp", mixT16.h[16 + cc], o[:], reads=[o], writes=[mixT16], acc=True)


def phase_hyena(k, L, hyT, P, S, mixT16):
    hy_filters(k, L, P["hy_zT"], P["hy_w1a"], P["hy_w2"], P["hy_b2"], P["hy_w3"], P["hy_absd"], P["hy_negt"], S["kf_tok"], S["kb_tok"])
    hy_prep_u(k, L, hyT, P["hy_cw"], P["ident32"], S["uT"], S["u_tok"])
    hy_fft_a(k, L, S["kf_tok"], P["hy_fw"], S["scrA"])
    mk, epi = hy_spectrum_store(k, S["scrK"], False)
    hy_fft_b(k, L, S["scrA"], P["hy_cs"], epi, mk)
    hy_fft_a(k, L, S["kb_tok"], P["hy_fw"], S["scrA"])
    mk, epi = hy_spectrum_store(k, S["scrK"], True)
    hy_fft_b(k, L, S["scrA"], P["hy_cs"], epi, mk)
    hy_fft_a(k, L, S["u_tok"], P["hy_fw"], S["scrA"])
    mk, epi = hy_mul_inv(k, S["scrK"], S["scrG"])
    hy_fft_b(k, L, S["scrA"], P["hy_cs"], epi, mk)
    hy_fft_c(k, L, S["scrG"], P["hy_iv"], hyT, S["uT"], P["hy_cw"], P["hy_bias"], mixT16)


def hy_scratch(k, L, kind="Internal"):
    N1 = 2 * L // 128
    return {"kf_tok": k.dram("kf_tok", [L, 2048], F32, kind=kind), "kb_tok": k.dram("kb_tok", [L, 2048], F32, kind=kind),
            "uT": k.dram("uT", [16, 128, L], F32, kind=kind), "u_tok": k.dram("u_tok", [L, 2048], F32, kind=kind),
            "scrA": k.dram("scrA", [2, N1, 128, 2048], F32, kind=kind), "scrK": k.dram("scrK", [2, N1, 128, 2048], F32, kind=kind),
            "scrG": k.dram("scrG", [2, 128, N1, 2048], F32, kind=kind)}


def phase_odd_inproj(k, xT16, L, w_fm, w_dt, w_tm, xbcT, dtT, ginT, rinT, z_tok):
    nb = L // TB
    with k.scope():
        X = [k.sb([128, 8, TB + 2], BF16, "X") for _ in range(4)]
        wr_fm = Ring([k.sb([128, KC, 128], BF16, "wfm") for _ in range(3)])
        wr_tm = Ring([k.sb([128, KC, 512], BF16, "wtm") for _ in range(2)])
        st32 = Ring([k.sb([128, TB], F32, "st32") for _ in range(4)])
        for b in range(nb):
            load_X(k, X, xT16, b, L)
            cnt = [0]
            for c in range(56):
                if c < 24:
                    dst_t, dst = xbcT, xbcT.h[c, :, sl(b, TB)]
                elif c < 40:
                    dst_t, dst = ginT, ginT.h[c - 24, :, sl(b, TB)]
                else:
                    dst_t, dst = rinT, rinT.h[c - 40, :, sl(b, TB)]

                def epi(pst, dst=dst, dst_t=dst_t):
                    evac_to_dram(k, pst, 128, TB, st32, dst, dst_t, cnt[0])
                    cnt[0] += 1
                dense_fm(k, X, wr_fm, w_fm.h[c], 128, epi, [w_fm])

            def epi_dt(pst):
                evac_to_dram(k, pst, 64, TB, st32, dtT.h[:, sl(b, TB)], dtT, 0)
            dense_fm(k, X, wr_fm, w_dt.h, 64, epi_dt, [w_dt])
            for n in range(4):
                def epi(pst, tt, n=n):
                    t0 = b * TB + tt * 128
                    evac_to_dram(k, pst, 128, 512, st32, z_tok.h[t0:t0 + 128, n * 512:(n + 1) * 512], z_tok, cnt[0])
                    cnt[0] += 1
                dense_tm(k, X, wr_tm, w_tm.h[n], epi, [w_tm])


def conv4(k, c, t, cw, ch, L):
    k.op("dve", lambda e: e.tensor_scalar(out=c[:, 0:L], in0=t[:, 0:L], scalar1=cw[:, ch, 0:1], scalar2=cw[:, ch, 4:5], op0=ALU.mult, op1=ALU.add),
         reads=[t, cw], writes=[c])
    for j in (1, 2, 3):
        k.op("dve", lambda e, j=j: e.scalar_tensor_tensor(out=c[:, 0:L], in0=t[:, j:L + j], scalar=cw[:, ch, j:j + 1], in1=c[:, 0:L],
                                                          op0=ALU.mult, op1=ALU.add), reads=[t, cw, c], writes=[c])


def phase_lru(k, L, rinT, ginT, lcw_d, wax_d, lb_d, lam_d, mixT16):
    NT = L // 512
    with k.scope():
        lcw = k.sb([128, 16, 5], F32, "lcw")
        k.dma("sp", lcw[:], lcw_d.h[:, :, :], reads=[lcw_d], writes=[lcw])
        lb = k.sb([128, 2, 2, 16], F32, "lb")
        k.dma("sp", lb[:], lb_d.h[:, :, :, :], reads=[lb_d], writes=[lb])
        lam = k.sb([128, 32], F32, "lam")
        k.dma("sp", lam[:], lam_d.h.rearrange("p a b -> p (a b)"), reads=[lam_d], writes=[lam])
        nc8 = k.sb([128, 32], F32, "nc8")
        k.op("act", lambda e: e.activation(out=nc8[:], in_=lam[:], func=AF.Exp, scale=-1.0), reads=[lam], writes=[nc8])
        k.op("act", lambda e: e.activation(out=nc8[:], in_=nc8[:], func=AF.Ln, bias=1.0, scale=1.0), reads=[nc8], writes=[nc8])
        k.op("dve", lambda e: e.tensor_scalar(out=nc8[:], in0=nc8[:], scalar1=-8.0, scalar2=None, op0=ALU.mult), reads=[nc8], writes=[nc8])
        wr = Ring([k.sb([128, 2, 128], F32, "w") for _ in range(4)])
        padr = Ring([k.sb([128, L + 3], F32, "pad") for _ in range(2)])
        xcr = Ring([k.sb([128, L], F32, "xc") for _ in range(2)])
        hr = Ring([k.sb([128, L], F32, "h") for _ in range(3)])
        gr = Ring([k.sb([128, L], F32, "g") for _ in range(2)])
        o16 = Ring([k.sb([128, L], BF16, "o16") for _ in range(2)])
        G = Ring([k.sb([128, 512], F32, "G") for _ in range(12)])
        for cc in range(16):
            t = padr.next()
            k.op("pool", lambda e: e.memset(t[:, 0:1], 0.0), writes=[t])
            k.op("pool", lambda e: e.memset(t[:, L + 1:L + 3], 0.0), writes=[t], acc=True)
            k.dma("sp", t[:, 1:L + 1], rinT.h[cc], reads=[rinT], writes=[t], acc=True)
            xc = xcr.next()
            conv4(k, xc, t, lcw, cc, L)
            hs = []
            for d in range(2):
                w = wr.next()
                k.dma("act", w[:], wax_d.h[d, :, cc].rearrange("a p c -> p a c"), reads=[wax_d], writes=[w])
                h = hr.next()
                tiles = range(NT) if d == 0 else range(NT - 1, -1, -1)
                prev = None
                for ti in tiles:
                    c = slice(ti * 512, (ti + 1) * 512)
                    pr, pi = k.psr.next(), k.psr.next()
                    k.op("pe", lambda e: e.matmul(pr[:, :], w[:, 0, :], xc[:, c], start=True, stop=True), reads=[w, xc], writes=[pr])
                    k.op("pe", lambda e: e.matmul(pi[:, :], w[:, 1, :], xc[:, c], start=True, stop=True), reads=[w, xc], writes=[pi])
                    r, ig = G.next(), G.next()
                    k.op("act", lambda e: e.activation(out=r[:], in_=pr[:, :], func=AF.Sigmoid, bias=lb[:, d, 0, cc:cc + 1], scale=1.0), reads=[pr, lb], writes=[r])
                    k.op("act", lambda e: e.activation(out=ig[:], in_=pi[:, :], func=AF.Sigmoid, bias=lb[:, d, 1, cc:cc + 1], scale=1.0), reads=[pi, lb], writes=[ig])
                    a = G.next()
                    k.op("act", lambda e: e.activation(out=a[:], in_=r[:], func=AF.Exp, scale=nc8[:, d * 16 + cc:d * 16 + cc + 1]), reads=[r, nc8], writes=[a])
                    om = G.next()
                    k.op("dve", lambda e: e.tensor_tensor(out=om[:], in0=a[:], in1=a[:], op=ALU.mult), reads=[a], writes=[om])
                    k.op("dve", lambda e: e.tensor_scalar(out=om[:], in0=om[:], scalar1=-1.0, scalar2=1.0, op0=ALU.mult, op1=ALU.add), reads=[om], writes=[om])
                    k.op("act", lambda e: e.activation(out=om[:], in_=om[:], func=AF.Sqrt), reads=[om], writes=[om])
                    k.op("pool", lambda e: e.tensor_tensor(out=ig[:], in0=ig[:], in1=xc[:, c], op=ALU.mult), reads=[ig, xc], writes=[ig])
                    k.op("pool", lambda e: e.tensor_tensor(out=om[:], in0=om[:], in1=ig[:], op=ALU.mult), reads=[om, ig], writes=[om])
                    if d == 0:
                        init = 0.0 if prev is None else h[:, prev * 512 + 511:prev * 512 + 512]
                        k.op("dve", lambda e: e.tensor_tensor_scan(out=h[:, c], data0=a[:], data1=om[:], initial=init, op0=ALU.mult, op1=ALU.add),
                             reads=[a, om, h], writes=[h], acc=(prev is not None))
                    else:
                        init = 0.0 if prev is None else h[:, prev * 512:prev * 512 + 1]
                        lo = ti * 512
                        rs = slice(lo + 511, lo - 1 if lo > 0 else None, -1)
                        k.op("dve", lambda e: e.tensor_tensor_scan(out=h[:, rs], data0=a[:, ::-1], data1=om[:, ::-1], initial=init, op0=ALU.mult, op1=ALU.add),
                             reads=[a, om, h], writes=[h], acc=(prev is not None))
                    prev = ti
                hs.append(h)
            g = gr.next()
            k.dma("sp", g[:], ginT.h[cc], reads=[ginT], writes=[g])
            u = hr.next()
            k.op("pool", lambda e: e.tensor_tensor(out=u[:], in0=g[:], in1=g[:], op=ALU.mult), reads=[g], writes=[u])
            k.op("dve", lambda e: e.tensor_scalar(out=u[:], in0=u[:], scalar1=0.044715, scalar2=1.0, op0=ALU.mult, op1=ALU.add), reads=[u], writes=[u])
            k.op("dve", lambda e: e.tensor_tensor(out=u[:], in0=u[:], in1=g[:], op=ALU.mult), reads=[u, g], writes=[u])
            k.op("act", lambda e: e.activation(out=u[:], in_=u[:], func=AF.Sigmoid, scale=1.5957691216057308), reads=[u], writes=[u])
            k.op("pool", lambda e: e.tensor_tensor(out=u[:], in0=u[:], in1=g[:], op=ALU.mult), reads=[u, g], writes=[u])
            k.op("dve", lambda e: e.tensor_tensor(out=hs[0][:], in0=hs[0][:], in1=hs[1][:], op=ALU.add), reads=[hs[0], hs[1]], writes=[hs[0]])
            o = o16.next()
            k.op("dve", lambda e: e.tensor_tensor(out=o[:], in0=hs[0][:], in1=u[:], op=ALU.mult), reads=[hs[0], u], writes=[o])
            k.dma("act", mixT16.h[16 + cc], o[:], reads=[o], writes=[mixT16], acc=True)


def ssd_consts(L):
    c = {}
    sel = np.zeros((64, 64, 128), np.float32)
    for h in range(64):
        sel[h, h, :] = 1.0
    c["ssd_sel"] = sel
    c["ssd_ones64"] = np.ones((64, 128), np.float32)
    t = np.arange(L)
    m = np.ones((64, L), np.float32)
    m[0:32, t % 128 == 0] = 0.0
    m[32:64, t % 128 == 127] = 0.0
    c["ssd_smask"] = m
    return c


def phase_ssd_prep(k, L, xbcT, scw_d, ident32_d, xs_tok, bm_tok, bmT16, cmT16):
    with k.scope():
        scw = k.sb([128, 24, 5], F32, "scw")
        k.dma("sp", scw[:], scw_d.h[:, :, :], reads=[scw_d], writes=[scw])
        ident32 = k.sb([128, 128], F32, "ident")
        k.dma("sp", ident32[:], ident32_d.h[:, :], reads=[ident32_d], writes=[ident32])
        padr = Ring([k.sb([128, L + 3], F32, "pad") for _ in range(2)])
        cr = Ring([k.sb([128, L], F32, "c") for _ in range(2)])
        c16 = Ring([k.sb([128, L], BF16, "c16") for _ in range(2)])
        st = Ring([k.sb([128, 4, 128], F32, "st") for _ in range(3)])
        for ch in range(24):
            t = padr.next()
            k.op("pool", lambda e: e.memset(t[:, 0:1], 0.0), writes=[t])
            k.op("pool", lambda e: e.memset(t[:, L + 1:L + 3], 0.0), writes=[t], acc=True)
            k.dma("sp" if ch % 2 == 0 else "act", t[:, 1:L + 1], xbcT.h[ch], reads=[xbcT], writes=[t], acc=True)
            c = cr.next()
            conv4(k, c, t, scw, ch, L)
            k.op("act", lambda e: e.activation(out=c[:], in_=c[:], func=AF.Silu), reads=[c], writes=[c])
            if ch >= 16:
                s16 = c16.next()
                k.op("pool", lambda e: e.tensor_copy(s16[:], c[:]), reads=[c], writes=[s16])
                dst = bmT16 if ch < 20 else cmT16
                k.dma("act", dst.h[(ch - 16) % 4], s16[:], reads=[s16], writes=[dst], acc=True)
            if ch < 20:
                dst, col = (xs_tok, ch * 128) if ch < 16 else (bm_tok, (ch - 16) * 128)
                for g in range(L // 512):
                    p = k.psr.next()
                    for j in range(4):
                        k.op("pe", lambda e, j=j: e.transpose(p[:, sl(j)], c[:, g * 512 + j * 128:g * 512 + (j + 1) * 128], ident32[:]),
                             reads=[c, ident32], writes=[p], acc=(j > 0), last=(j == 3))
                    s = st.next()
                    if g % 2 == 0:
                        k.op("act", lambda e: e.copy(s.h[:].rearrange("p a c -> p (a c)"), p[:, :]), reads=[p], writes=[s])
                    else:
                        k.op("dve", lambda e: e.tensor_copy(s.h[:].rearrange("p a c -> p (a c)"), p[:, :]), reads=[p], writes=[s])
                    k.dma("act" if g % 2 == 0 else "sp", dst.h[g * 512:(g + 1) * 512, col:col + 128].rearrange("(a p) c -> p a c", p=128), s[:],
                          reads=[s], writes=[dst], acc=True)


def phase_ssd(k, L, dtT, z_tok, xs_tok, bm_tok, bmT16, cmT16, yf_tok, P, mixT16):
    NCH = L // 128
    with k.scope():
        def ld(name, shape, q="sp", src=None):
            t = k.sb(shape, F32, name)
            k.dma(q, t[:], (P[name].h if src is None else src), reads=[P[name]], writes=[t])
            return t
        sel = ld("ssd_sel", [64, 64, 128])
        ones64 = ld("ssd_ones64", [64, 128])
        smask = ld("ssd_smask", [64, L], "act")
        cst = ld("gla_cst", [128, 4, 128], "act", P["gla_cst"].h.rearrange("a p c -> p a c"))
        ident32 = ld("ident32", [128, 128])
        dtb = ld("ssd_dtb", [64, 1])
        alog = ld("ssd_alog", [64, 1])
        ng = ld("ssd_ng", [128, 2048], "act")
        dsk0 = ld("ssd_dsk", [128, 2048], "sp", P["ssd_dsk"].h[0])
        dsk1 = ld("ssd_dsk", [128, 2048], "act", P["ssd_dsk"].h[1])
        k.op("pool", lambda e: e.tensor_tensor(out=dsk0[:], in0=dsk0[:], in1=dsk1[:], op=ALU.add), reads=[dsk0, dsk1], writes=[dsk0])
        epsr = k.sb([128, 1], F32, "epsr")
        k.op("dve", lambda e: e.memset(epsr[:], RMS_EPS), writes=[epsr])
        dt = k.sb([64, L], F32, "dt")
        k.dma("sp", dt[:], dtT.h[:, :], reads=[dtT], writes=[dt])
        k.op("act", lambda e: e.activation(out=dt[:], in_=dt[:], func=AF.Exp, bias=dtb[:, 0:1], scale=1.0), reads=[dt, dtb], writes=[dt])
        k.op("act", lambda e: e.activation(out=dt[:], in_=dt[:], func=AF.Ln, bias=1.0, scale=1.0), reads=[dt], writes=[dt])
        negA = k.sb([64, 1], F32, "negA")
        k.op("act", lambda e: e.activation(out=negA[:], in_=alog[:], func=AF.Exp), reads=[alog], writes=[negA])
        k.op("dve", lambda e: e.tensor_scalar(out=negA[:], in0=negA[:], scalar1=-1.0, scalar2=None, op0=ALU.mult), reads=[negA], writes=[negA])
        acT = k.sb([64, L], F32, "acT")
        nacT = k.sb([64, L], F32, "nacT")
        k.op("dve", lambda e: e.tensor_scalar(out=nacT[:], in0=dt[:], scalar1=negA[:, 0:1], scalar2=None, op0=ALU.mult), reads=[dt, negA], writes=[nacT])
        k.op("dve", lambda e: e.tensor_tensor_scan(out=acT[0:32, :], data0=smask[0:32, :], data1=nacT[0:32, :], initial=0.0, op0=ALU.mult, op1=ALU.add),
             reads=[smask, nacT], writes=[acT])
        k.op("dve", lambda e: e.tensor_tensor_scan(out=acT[32:64, ::-1], data0=smask[32:64, ::-1], data1=nacT[32:64, ::-1], initial=0.0,
                                                   op0=ALU.mult, op1=ALU.add), reads=[smask, nacT], writes=[acT], acc=True)
        k.op("dve", lambda e: e.tensor_scalar(out=nacT[:], in0=acT[:], scalar1=-1.0, scalar2=None, op0=ALU.mult), reads=[acT], writes=[nacT])
        ac_tok = k.sb([128, NCH, 64], F32, "ac_tok")
        dt_tok = k.sb([128, NCH, 64], F32, "dt_tok")
        eac = k.sb([128, NCH, 64], F32, "eac")
        for src, dst in ((acT, ac_tok), (dt, dt_tok)):
            for n0 in range(0, NCH, 4):
                p = k.psr.next()
                nn = min(4, NCH - n0)
                for j in range(nn):
                    k.op("pe", lambda e, j=j: e.transpose(p[:, j * 64:(j + 1) * 64], src[:, sl(n0 + j)], ident32[0:64, 0:64]),
                         reads=[src, ident32], writes=[p], acc=(j > 0), last=(j == nn - 1))
                k.op("dve", lambda e: e.tensor_copy(dst.h[:, n0:n0 + nn, :].rearrange("p a c -> p (a c)"), p[:, 0:nn * 64]), reads=[p], writes=[dst], acc=(n0 > 0))
        k.op("act", lambda e: e.activation(out=eac.h[:].rearrange("p a c -> p (a c)"), in_=ac_tok.h[:].rearrange("p a c -> p (a c)"), func=AF.Exp),
             reads=[ac_tok], writes=[eac])
        S32 = k.sb([128, 512], F32, "S32")
        S16 = k.sb([128, 512], BF16, "S16")
        xsr = Ring([k.sb([128, 512], F32, "xs") for _ in range(2)])
        bmr = Ring([k.sb([128, 128], F32, "bm") for _ in range(2)])
        bm16r = Ring([k.sb([128, 128], BF16, "bm16") for _ in range(2)])
        bTr = Ring([k.sb([128, 128], BF16, "bT") for _ in range(2)])
        cTr = Ring([k.sb([128, 128], BF16, "cT") for _ in range(2)])
        cbr = Ring([k.sb([128, 128], F32, "cbm") for _ in range(2)])
        Dr = Ring([k.sb([128, 512], F32, "Dm") for _ in range(2)])
        lmr = Ring([k.sb([128, 4, 128], F32, "lm") for _ in range(2)])
        Mr = Ring([k.sb([128, 4, 128], BF16, "M") for _ in range(2)])
        xdtr = Ring([k.sb([128, 512], BF16, "xdt") for _ in range(2)])
        xddr = Ring([k.sb([128, 512], BF16, "xdd") for _ in range(2)])
        O = Ring([k.sb([128, 512], F32, "O") for _ in range(6)])
        sm = Ring([k.sb([128, 8], F32, "sm") for _ in range(6)])
        Xr = Ring([k.sb([64, 8], F32, "X") for _ in range(2)])
        tr16 = Ring([k.sb([128, 4, 128], BF16, "tr") for _ in range(2)])
        zr = Ring([k.sb([128, 512], F32, "z") for _ in range(2)])
        yfr = Ring([k.sb([128, 512], F32, "yf") for _ in range(2)])
        for g in range(4):
            gc = slice(g * 512, (g + 1) * 512)
            for d in range(2):
                row0 = d * 32 + g * 8
                mask = cst[:, d, :]
                lastc = 127 if d == 0 else 0
                k.op("dve", lambda e: e.memset(S32[:], 0.0), writes=[S32])
                k.op("pool", lambda e: e.memset(S16[:], 0.0), writes=[S16])
                order = range(NCH) if d == 0 else range(NCH - 1, -1, -1)
                for n in order:
                    ts = slice(n * 128, (n + 1) * 128)
                    xs, bmt, bT, cT = xsr.next(), bmr.next(), bTr.next(), cTr.next()
                    k.dma("sp", xs[:], xs_tok.h[ts, gc], reads=[xs_tok], writes=[xs])
                    k.dma("act", bmt[:], bm_tok.h[ts, sl(g)], reads=[bm_tok], writes=[bmt])
                    k.dma("sp", bT[:], bmT16.h[g, :, ts], reads=[bmT16], writes=[bT])
                    k.dma("act", cT[:], cmT16.h[g, :, ts], reads=[cmT16], writes=[cT])
                    bm16 = bm16r.next()
                    k.op("pool", lambda e: e.tensor_copy(bm16[:], bmt[:]), reads=[bmt], writes=[bm16])
                    p = k.psr.next()
                    k.op("pe", lambda e: e.matmul(p[:, 0:128], bT[:], cT[:], start=True, stop=True), reads=[bT, cT], writes=[p])
                    cbm = cbr.next()
                    k.op("dve", lambda e: e.tensor_tensor(out=cbm[:], in0=p[:, 0:128], in1=mask, op=ALU.mult), reads=[p, cst], writes=[cbm])
                    lms, Ms = [], []
                    for half in range(2):
                        pD = k.psr.next()
                        for q in range(4):
                            row = row0 + half * 4 + q
                            k.op("pe", lambda e, q=q, row=row: e.matmul(pD[:, sl(q)], sel[:, row, :], acT[:, ts], start=True, stop=False),
                                 reads=[sel, acT], writes=[pD], acc=True, last=False)
                            k.op("pe", lambda e, q=q, row=row: e.matmul(pD[:, sl(q)], nacT[:, ts], sel[:, row, :], start=False, stop=True),
                                 reads=[sel, nacT], writes=[pD], acc=True, last=(q == 3))
                        Dm = Dr.next()
                        k.op("dve", lambda e: e.tensor_scalar(out=Dm[:], in0=pD[:, :], scalar1=0.0, scalar2=None, op0=ALU.min), reads=[pD], writes=[Dm])
                        lm = lmr.next()
                        k.op("act", lambda e: e.activation(out=lm.h[:].rearrange("p a c -> p (a c)"), in_=Dm[:], func=AF.Exp), reads=[Dm], writes=[lm])
                        M = Mr.next()
                        for q in range(4):
                            k.op("pool" if q % 2 else "dve", lambda e, q=q: e.tensor_tensor(out=M[:, q, :], in0=lm[:, q, :], in1=cbm[:], op=ALU.mult),
                                 reads=[lm, cbm], writes=[M], acc=(q > 0))
                        lms.append(lm)
                        Ms.append(M)
                    xdt, xdd = xdtr.next(), xddr.next()
                    for h in range(8):
                        hc = slice(h * 64, (h + 1) * 64)
                        dts = dt_tok[:, n, row0 + h:row0 + h + 1]
                        k.op("pool", lambda e, hc=hc, dts=dts: e.tensor_scalar(out=xdt[:, hc], in0=xs[:, hc], scalar1=dts, scalar2=1.0, op0=ALU.mult, op1=ALU.mult),
                             reads=[xs, dt_tok], writes=[xdt], acc=(h > 0))
                        dss = lms[h // 4][:, h % 4, lastc:lastc + 1]
                        k.op("dve", lambda e, hc=hc, dts=dts, dss=dss: e.tensor_scalar(out=xdd[:, hc], in0=xs[:, hc], scalar1=dts, scalar2=dss, op0=ALU.mult, op1=ALU.mult),
                             reads=[xs, dt_tok, lms[h // 4]], writes=[xdd], acc=(h > 0))
                    pY = k.psr.next()
                    for h in range(8):
                        k.op("pe", lambda e, h=h: e.matmul(pY[:, h * 64:(h + 1) * 64], Ms[h // 4][:, h % 4, :], xdt[:, h * 64:(h + 1) * 64], start=True, stop=True),
                             reads=[Ms[h // 4], xdt], writes=[pY], acc=True, last=(h == 7))
                    pO = k.psr.next()
                    k.op("pe", lambda e: e.matmul(pO[:, :], cT[:], S16[:], start=True, stop=True), reads=[cT, S16], writes=[pO])
                    pS = k.psr.next()
                    k.op("pe", lambda e: e.matmul(pS[:, :], bm16[:], xdd[:], start=True, stop=True), reads=[bm16, xdd], writes=[pS])
                    tl = n * 128 + lastc
                    X = Xr.next()
                    k.op("pool", lambda e: e.tensor_scalar(out=X[:], in0=sel[:, row0:row0 + 8, 0], scalar1=acT[:, tl:tl + 1], scalar2=1.0, op0=ALU.mult, op1=ALU.mult),
                         reads=[sel, acT], writes=[X])
                    pC = k.psr.next()
                    k.op("pe", lambda e: e.matmul(pC[:, 0:8], ones64[:], X[:], start=True, stop=True), reads=[ones64, X], writes=[pC])
                    cd = sm.next()
                    k.op("act", lambda e: e.activation(out=cd[:], in_=pC[:, 0:8], func=AF.Exp), reads=[pC], writes=[cd])
                    yd = O.next()
                    k.op("act", lambda e: e.copy(yd[:], pY[:, :]), reads=[pY], writes=[yd])
                    y = O.next()
                    for h in range(8):
                        hc = slice(h * 64, (h + 1) * 64)
                        k.op("dve", lambda e, hc=hc, h=h: e.scalar_tensor_tensor(out=y[:, hc], in0=pO[:, hc], scalar=eac[:, n, row0 + h:row0 + h + 1], in1=yd[:, hc],
                                                                              op0=ALU.mult, op1=ALU.add), reads=[pO, eac, yd], writes=[y], acc=(h > 0))
                    for h in range(8):
                        hc = slice(h * 64, (h + 1) * 64)
                        k.op("dve", lambda e, hc=hc, h=h: e.scalar_tensor_tensor(out=S32[:, hc], in0=S32[:, hc], scalar=cd[:, h:h + 1], in1=pS[:, hc],
                                                                              op0=ALU.mult, op1=ALU.add), reads=[S32, cd, pS], writes=[S32], acc=(h > 0))
                    k.op("act", lambda e: e.copy(S16[:], S32[:]), reads=[S32], writes=[S16])
                    if d == 0:
                        k.dma("sp", yf_tok.h[ts, gc], y[:], reads=[y], writes=[yf_tok], acc=True)
                    else:
                        yf, zt = yfr.next(), zr.next()
                        k.dma("sp", yf[:], yf_tok.h[ts, gc], reads=[yf_tok], writes=[yf])
                        k.dma("act", zt[:], z_tok.h[ts, gc], reads=[z_tok], writes=[zt])
                        k.op("pool", lambda e: e.tensor_tensor(out=y[:], in0=y[:], in1=yf[:], op=ALU.add), reads=[y, yf], writes=[y])
                        t2 = O.next()
                        k.op("pool", lambda e: e.tensor_tensor(out=t2[:], in0=xs[:], in1=dsk0[:, gc], op=ALU.mult), reads=[xs, dsk0], writes=[t2])
                        k.op("dve", lambda e: e.tensor_tensor(out=y[:], in0=y[:], in1=t2[:], op=ALU.add), reads=[y, t2], writes=[y])
                        sz = O.next()
                        k.op("act", lambda e: e.activation(out=sz[:], in_=zt[:], func=AF.Silu), reads=[zt], writes=[sz])
                        k.op("pool", lambda e: e.tensor_tensor(out=y[:], in0=y[:], in1=sz[:], op=ALU.mult), reads=[y, sz], writes=[y])
                        ss = sm.next()
                        k.op("act", lambda e: e.activation(out=sz[:], in_=y[:], func=AF.Square, accum_out=ss[:, 0:1]), reads=[y], writes=[sz, ss])
                        k.op("act", lambda e: e.activation(out=ss[:, 0:1], in_=ss[:, 0:1], func=AF.Sqrt, bias=epsr[:, 0:1], scale=1.0 / 512), reads=[ss, epsr], writes=[ss])
                        k.op("dve", lambda e: e.reciprocal(out=ss[:, 0:1], in_=ss[:, 0:1]), reads=[ss], writes=[ss])
                        on = O.next()
                        k.op("dve", lambda e: e.scalar_tensor_tensor(out=on[:], in0=y[:], scalar=ss[:, 0:1], in1=ng[:, gc], op0=ALU.mult, op1=ALU.mult),
                             reads=[y, ss, ng], writes=[on])
                        p7 = k.psr.next()
                        for j in range(4):
                            k.op("pe", lambda e, j=j: e.transpose(p7[:, sl(j)], on[:, sl(j)], ident32[:]), reads=[on, ident32], writes=[p7], acc=(j > 0), last=(j == 3))
                        t16 = tr16.next()
                        k.op("dve", lambda e: e.tensor_copy(t16.h[:].rearrange("p j c -> p (j c)"), p7[:, :]), reads=[p7], writes=[t16])
                        k.dma("sp", mixT16.h[4 * g:4 * g + 4, :, ts].rearrange("j p c -> p j c"), t16[:], reads=[t16], writes=[mixT16], acc=True)


L_FULL = 4096


def host_params(inp, L=None):
    L_FULL = L or 4096
    f = lambda a: np.ascontiguousarray(np.asarray(a, dtype=np.float32))
    P = {}
    P["ident32"] = np.eye(128, dtype=np.float32)
    P["ones32"] = np.ones((128, 128), np.float32)
    P["gla_cst"] = gla_consts()
    P.update(hy_consts(L_FULL))
    P.update(ssd_consts(L_FULL))
    W = np.asarray(inp["ev_w_in"][0], np.float32)
    Wq, Wk, Wv, Wog, Wlr, Why = W[:, :1024], W[:, 1024:2048], W[:, 2048:4096], W[:, 4096:6144], W[:, 6144:6176], W[:, 6176:]
    P["ev_fm"] = tile_fm(np.concatenate([Wq, Wk, Why], 1))
    P["ev_lr"] = f(Wlr.reshape(KC, 128, 32).transpose(1, 0, 2))
    P["ev_tm"] = tile_tm(np.concatenate([Wk, Wv, Wog], 1))
    P["gla_wg"] = f(np.stack([np.concatenate([inp["ev_gla_wg_f"][0], inp["ev_gla_bg_f"][0][None]], 0),
                              np.concatenate([inp["ev_gla_wg_b"][0], inp["ev_gla_bg_b"][0][None]], 0)]))
    P["gla_gn"] = f(np.tile(np.asarray(inp["ev_gla_norm"][0])[None], (128, 1)))
    P["hy_w1a"] = f(np.concatenate([inp["ev_hy_w1"][0], inp["ev_hy_b1"][0][None]], 0))
    P["hy_w2"] = f(inp["ev_hy_w2"][0])
    P["hy_b2"] = f(np.asarray(inp["ev_hy_b2"][0])[:, None])
    P["hy_w3"] = f(inp["ev_hy_w3"][0])
    hcw = np.zeros((128, 48, 4), np.float32)
    hcw[:, :, 0:3] = np.asarray(inp["ev_hy_conv_w"][0]).T.reshape(48, 128, 3).transpose(1, 0, 2)
    hcw[:, :, 3] = np.asarray(inp["ev_hy_conv_b"][0]).reshape(48, 128).T
    P["hy_cw"] = hcw
    P["hy_bias"] = f(np.asarray(inp["ev_hy_bias"][0]).reshape(16, 128).T)
    P["ev_w_out"] = tile_fm(np.asarray(inp["ev_w_out"][0], np.float32))
    W = np.asarray(inp["od_w_in"][0], np.float32)
    P["od_fm"] = tile_fm(np.concatenate([W[:, 2048:5120], W[:, 5184:7232], W[:, 7232:9280]], 1))
    P["od_dt"] = f(W[:, 5120:5184].reshape(KC, 128, 64).transpose(1, 0, 2))
    P["od_tm"] = tile_tm(W[:, 0:2048])
    scw = np.zeros((128, 24, 5), np.float32)
    scw[:, :, 0:4] = np.asarray(inp["od_ssd_conv_w"][0]).T.reshape(24, 128, 4).transpose(1, 0, 2)
    scw[:, :, 4] = np.asarray(inp["od_ssd_conv_b"][0]).reshape(24, 128).T
    P["ssd_scw"] = scw
    P["ssd_dtb"] = f(np.asarray(inp["od_ssd_dt_bias"][0]).reshape(64, 1))
    P["ssd_alog"] = f(np.asarray(inp["od_ssd_a_log"][0]).reshape(64, 1))
    P["ssd_dsk"] = f(np.tile(np.repeat(np.asarray(inp["od_ssd_d"][0]), 64, axis=1)[:, None, :], (1, 128, 1)))
    P["ssd_ng"] = f(np.tile(np.asarray(inp["od_ssd_norm"][0])[None], (128, 1)))
    lcw = np.zeros((128, 16, 5), np.float32)
    lcw[:, :, 0:4] = np.asarray(inp["od_lru_conv_w"][0]).T.reshape(16, 128, 4).transpose(1, 0, 2)
    lcw[:, :, 4] = np.asarray(inp["od_lru_conv_b"][0]).reshape(16, 128).T
    P["lru_cw"] = lcw
    P["lru_wax"] = f(np.stack([inp["od_lru_wa"][0], inp["od_lru_wx"][0]], 1))
    P["lru_b"] = f(np.stack([np.asarray(inp["od_lru_ba"][0]).reshape(2, 16, 128), np.asarray(inp["od_lru_bx"][0]).reshape(2, 16, 128)], 1).transpose(3, 0, 1, 2))
    P["lru_lam"] = f(np.asarray(inp["od_lru_lambda"][0]).reshape(2, 16, 128).transpose(2, 0, 1))
    P["od_w_out"] = tile_fm(np.asarray(inp["od_w_out"][0], np.float32))
    cw = np.zeros((128, 2, 2 * FC, 4), np.float32)
    for i in range(2):
        P["w_up%d" % i] = tile_fm(np.asarray(inp["ffn_w_up"][i], np.float32))
        P["w_dn%d" % i] = f(np.asarray(inp["ffn_w_down"][i], np.float32).reshape(FC, 128, KC, 128).transpose(2, 1, 0, 3).reshape(KC, 128, FC * 128))
        cw[:, i, :, 0:3] = np.asarray(inp["ffn_conv_w"][i]).T.reshape(2 * FC, 128, 3).transpose(1, 0, 2)
        cw[:, i, :, 3] = np.asarray(inp["ffn_conv_b"][i]).reshape(2 * FC, 128).T
    P["ffn_cw"] = cw
    g = np.stack([inp["ln1_g"][0], inp["ln2_g"][0], inp["ln1_g"][1], inp["ln2_g"][1]])
    b = np.stack([inp["ln1_b"][0], inp["ln2_b"][0], inp["ln1_b"][1], inp["ln2_b"][1]])
    P["gtab"] = f(np.asarray(g).reshape(4, KC, 128).transpose(2, 0, 1).reshape(128, 4 * KC))
    P["btab"] = f(np.asarray(b).reshape(4, KC, 128).transpose(2, 0, 1).reshape(128, 4 * KC))
    return {n: f(a) for n, a in P.items()}


def build_program(L, pshapes):
    nc = bass.Bass("TRN2", target_bir_lowering=False)
    with contextlib.ExitStack() as es:
        k = KB(nc, es)
        k.allps = [k.ps() for _ in range(8)]
        k.psr8 = Ring(k.allps)
        k.psr4 = Ring(k.allps[0:4])
        k.psh = Ring(k.allps[4:6])
        k.ps_sum, k.ps_sq = k.allps[6], k.allps[7]
        k.psr = k.psr8
        x_in = k.dram("x", [L, D], F32, kind="ExternalInput")
        y_out = k.dram("y", [L, D], F32, kind="ExternalOutput")
        P = {n: k.dram(n, list(s), F32, kind="ExternalInput") for n, s in pshapes.items()}
        xA32, xB32, zT = (k.dram(n, [KC, 128, L], F32) for n in ("xA32", "xB32", "zT"))
        xA16, xB16, mixT16 = (k.dram(n, [KC, 128, L], BF16) for n in ("xA16", "xB16", "mixT16"))
        ident32 = k.sb([128, 128], F32, "ident0")
        k.dma("sp", ident32[:], P["ident32"].h[:, :], reads=[P["ident32"]], writes=[ident32])
        phase_prepass(k, x_in, xA32, xA16, L, ident32)
        qT, kT = k.dram("qT", [8, 128, L], F32), k.dram("kT", [8, 128, L], F32)
        lrT, hyT = k.dram("lrT", [32, L], F32), k.dram("hyT", [48, 128, L], F32)
        k_tok, v_tok, og_tok = k.dram("k_tok", [L, 1024], F32), k.dram("v_tok", [L, 2048], BF16), k.dram("og_tok", [L, 2048], F32)
        of_tok = k.dram("of_tok", [L, 2048], F32)
        phase_even_inproj(k, xA16, L, P["ev_fm"], P["ev_lr"], P["ev_tm"], qT, kT, lrT, hyT, k_tok, v_tok, og_tok)
        phase_gla(k, L, qT, kT, k_tok, v_tok, og_tok, lrT, P["gla_wg"], P["gla_gn"], P["gla_cst"], P["ident32"], of_tok, mixT16)
        phase_hyena(k, L, hyT, P, hy_scratch(k, L), mixT16)
        k.psr = k.psr4
        phase_outproj_ln(k, mixT16, L, P["ev_w_out"], xA32, xB32, xB16, zT, P["ones32"], P["gtab"], P["btab"], 0)
        phase_ffn(k, xB16, L, P["w_up0"], P["w_dn0"], P["ffn_cw"], xB32, xA32, xA16, zT, P["ones32"], P["gtab"], P["btab"], 1, 0)
        k.psr = k.psr8
        xbcT, dtT = k.dram("xbcT", [24, 128, L], F32), k.dram("dtT", [64, L], F32)
        ginT, rinT, z_tok = k.dram("ginT", [16, 128, L], F32), k.dram("rinT", [16, 128, L], F32), k.dram("z_tok", [L, 2048], F32)
        xs_tok, bm_tok = k.dram("xs_tok", [L, 2048], F32), k.dram("bm_tok", [L, 512], F32)
        bmT16, cmT16, yf_tok = k.dram("bmT16", [4, 128, L], BF16), k.dram("cmT16", [4, 128, L], BF16), k.dram("yf_tok", [L, 2048], F32)
        phase_odd_inproj(k, xA16, L, P["od_fm"], P["od_dt"], P["od_tm"], xbcT, dtT, ginT, rinT, z_tok)
        phase_ssd_prep(k, L, xbcT, P["ssd_scw"], P["ident32"], xs_tok, bm_tok, bmT16, cmT16)
        phase_ssd(k, L, dtT, z_tok, xs_tok, bm_tok, bmT16, cmT16, yf_tok, P, mixT16)
        phase_lru(k, L, rinT, ginT, P["lru_cw"], P["lru_wax"], P["lru_b"], P["lru_lam"], mixT16)
        k.psr = k.psr4
        phase_outproj_ln(k, mixT16, L, P["od_w_out"], xA32, xB32, xB16, zT, P["ones32"], P["gtab"], P["btab"], 2)
        phase_ffn(k, xB16, L, P["w_up1"], P["w_dn1"], P["ffn_cw"], xB32, None, None, zT, P["ones32"], P["gtab"], P["btab"], 3, 1,
                  out_tok=y_out, ident32_d=P["ident32"])
        k.finish([y_out])
        stats = dict(k.ninst)
        stats["nsem"] = k.nsem
    return nc, stats


def kernel(**inputs):
    L = L_FULL
    xp = np.asarray(inputs["x_prompt"], np.float32)
    xs = np.asarray(inputs["x_sample"], np.float32)
    seqs = [xp[0], xp[1], xs[0], xs[1], xs[2], xs[3]]
    P = host_params(inputs)
    nc, stats = build_program(L, {n: a.shape for n, a in P.items()})
    in_maps = []
    for c in range(6):
        m = dict(P)
        m["x"] = np.ascontiguousarray(seqs[c])
        in_maps.append(m)
    res = run_bass_kernel_spmd(nc, in_maps, core_ids=list(range(6)))
    outs = [np.asarray(res.results[c]["y"], np.float32) for c in range(6)]
    return (np.stack(outs[0:2]), np.stack(outs[2:6]))
```

```python
import contextlib
import numpy as np
import concourse.bass as bass
import concourse.mybir as mybir
from concourse.bass_utils import run_bass_kernel_spmd

F32 = mybir.dt.float32
BF16 = mybir.dt.bfloat16
AF = mybir.ActivationFunctionType
ALU = mybir.AluOpType

D = 4096
KC = 32
TB = 512
FF = 11008
FC = 86
NCORES = 8
ALPHA = 4.0 ** 0.25
LN_EPS = 1e-5
RMS_EPS = 1e-6


class Tk:
    __slots__ = ("h", "w", "r", "pr")

    def __init__(self, h):
        self.h = h
        self.w = {}
        self.r = {}
        self.pr = {}

    def __getitem__(self, idx):
        return self.h[idx]


class KB:
    SEM_LIMIT = 30000
    ND = 6

    def __init__(self, nc, es):
        self.nc = nc
        self.es = es
        self.eng = {"pe": nc.tensor, "act": nc.scalar, "dve": nc.vector, "pool": nc.gpsimd, "sp": nc.sync}
        self.sem = {}
        self.cnt = {}
        self.pe_sems = set()
        self.known = {e: {} for e in self.eng}
        self.nsem = 0
        self.prev = {}
        self.es2 = None
        for e in ("pe", "act", "dve", "pool"):
            self._new_sem(e)
        self.dq = {}
        for q in ("sp", "act", "pool"):
            self.dq[q] = {"sems": [self._alloc_sem() for _ in range(self.ND)], "vals": [0] * self.ND, "i": 0}
        self.ninst = {e: 0 for e in self.eng}
        self.uid = 0

    def _alloc_sem(self):
        self.nsem += 1
        return self.es.enter_context(self.nc.semaphore("s%d" % self.nsem))

    def _new_sem(self, e):
        if e in self.sem:
            self.prev.setdefault(e, []).append((self.sem[e], self.cnt[e]))
        s = self._alloc_sem()
        self.sem[e] = s
        self.cnt[e] = 0
        if e == "pe":
            self.pe_sems.add(s)

    def sb(self, shape, dt, name=None):
        self.uid += 1
        es = self.es2 if getattr(self, "es2", None) is not None else self.es
        h = es.enter_context(self.nc.sbuf_tensor("%s%d" % (name or "sb", self.uid), list(shape), dt))
        return Tk(h)

    def ps(self, shape=(128, 512), dt=F32, name=None):
        self.uid += 1
        h = self.es.enter_context(self.nc.psum_tensor("%s%d" % (name or "ps", self.uid), list(shape), dt))
        return Tk(h)

    def dram(self, name, shape, dt, kind="Internal"):
        h = self.nc.dram_tensor(name, list(shape), dt, kind=kind).ap()
        return Tk(h)

    def _wait(self, e, deps):
        kn = self.known[e]
        eo = self.eng[e]
        for s, v in deps.items():
            if kn.get(s, 0) < v:
                eo.wait_ge(s, v)
                kn[s] = v
                self.ninst[e] += 1

    @staticmethod
    def _merge(d, o):
        for s, v in o.items():
            if d.get(s, 0) < v:
                d[s] = v

    def op(self, e, fn, reads=(), writes=(), acc=False, last=True):
        deps = {}
        for t in reads:
            self._merge(deps, t.w)
        for t in writes:
            self._merge(deps, t.r)
            if not acc:
                self._merge(deps, t.w)
            else:
                self._merge(deps, t.pr)
        if e == "pe":
            for s in list(deps):
                if s in self.pe_sems:
                    del deps[s]
        self._wait(e, deps)
        ins = fn(self.eng[e])
        self.ninst[e] += 1
        if e == "pe" and not last:
            tok = (self.sem[e], self.cnt[e] + 1)
        else:
            ins.then_inc(self.sem[e], 1)
            self.cnt[e] += 1
            tok = (self.sem[e], self.cnt[e])
            if self.cnt[e] >= self.SEM_LIMIT:
                self._new_sem(e)
        s, v = tok
        for t in reads:
            if t.r.get(s, 0) < v:
                t.r[s] = v
        for t in writes:
            if acc:
                if t.w.get(s, 0) < v:
                    t.w[s] = v
            else:
                pr = dict(t.w)
                self._merge(pr, t.r)
                t.pr = pr
                t.w = {s: v}
                t.r = {}
        return ins

    def dma(self, q, out_ap, in_ap, reads=(), writes=(), acc=False, **kw):
        dq = self.dq[q]
        slot = dq["i"] % self.ND
        dq["i"] += 1
        if dq["vals"][slot] >= 60000:
            dq["sems"][slot] = self._alloc_sem()
            dq["vals"][slot] = 0
        s = dq["sems"][slot]
        deps = {}
        if dq["vals"][slot] > 0:
            deps[s] = dq["vals"][slot]
        for t in reads:
            self._merge(deps, t.w)
        for t in writes:
            self._merge(deps, t.r)
            if not acc:
                self._merge(deps, t.w)
            else:
                self._merge(deps, t.pr)
        self._wait(q, deps)
        ins = self.eng[q].dma_start(out=out_ap, in_=in_ap, **kw)
        ins.then_inc(s, 16)
        self.ninst[q] += 1
        dq["vals"][slot] += 16
        v = dq["vals"][slot]
        for t in reads:
            if t.r.get(s, 0) < v:
                t.r[s] = v
        for t in writes:
            if acc:
                if t.w.get(s, 0) < v:
                    t.w[s] = v
            else:
                pr = dict(t.w)
                self._merge(pr, t.r)
                t.pr = pr
                t.w = {s: v}
                t.r = {}
        return ins

    def barrier(self):
        deps = {}
        for e in ("pe", "act", "dve", "pool"):
            if self.cnt[e] > 0:
                deps[self.sem[e]] = self.cnt[e]
            elif self.prev.get(e):
                ps_, pv_ = self.prev[e][-1]
                deps[ps_] = pv_
        for q in self.dq:
            for s, v in zip(self.dq[q]["sems"], self.dq[q]["vals"]):
                if v > 0:
                    deps[s] = v
        for e in self.eng:
            self._wait(e, dict(deps))

    @contextlib.contextmanager
    def scope(self):
        old = self.es
        with contextlib.ExitStack() as es:
            self.es2 = es
            yield
            self.barrier()
        self.es2 = None

    def finish(self, outs):
        deps = {}
        for t in outs:
            self._merge(deps, t.w)
        self._wait("sp", deps)
        for q in self.dq:
            dd = {}
            for s, v in zip(self.dq[q]["sems"], self.dq[q]["vals"]):
                if v > 0:
                    dd[s] = v
            self._wait("sp", dd)


class Ring:
    def __init__(self, items):
        self.items = items
        self.i = 0

    def next(self):
        t = self.items[self.i % len(self.items)]
        self.i += 1
        return t


def sl(i, n=128):
    return slice(i * n, (i + 1) * n)


def phase_prepass(k, x_in, xT32, xT16, L, ident32):
    nb = L // TB
    with k.scope():
        xin = Ring([[k.sb([128, D], F32, "xin") for _ in range(4)] for _ in range(2)])
        st32 = Ring([k.sb([128, TB], F32, "st32") for _ in range(3)])
        st16 = Ring([k.sb([128, TB], BF16, "st16") for _ in range(3)])
        for b in range(nb):
            tiles = xin.next()
            for tt in range(4):
                t0 = b * TB + tt * 128
                k.dma("sp" if tt % 2 == 0 else "act", tiles[tt][:], x_in[t0:t0 + 128, :], reads=[x_in], writes=[tiles[tt]])
            for dc in range(KC):
                pst = k.psr.next()
                for tt in range(4):
                    k.op("pe", lambda e, tt=tt: e.transpose(pst[:, sl(tt)], tiles[tt][:, sl(dc)], ident32[:]),
                         reads=[tiles[tt], ident32], writes=[pst], acc=(tt > 0), last=(tt == 3))
                s32 = st32.next()
                s16 = st16.next()
                k.op("dve", lambda e: e.tensor_copy(s32[:], pst[:]), reads=[pst], writes=[s32])
                k.op("act", lambda e: e.copy(s16[:], s32[:]), reads=[s32], writes=[s16])
                k.dma("sp", xT32[dc, :, sl(b, TB)], s32[:], reads=[s32], writes=[xT32], acc=True)
                k.dma("act", xT16[dc, :, sl(b, TB)], s16[:], reads=[s16], writes=[xT16], acc=True)


def load_X(k, X, xT16, b, L):
    nb = L // TB
    lo = b * TB - 1
    hi = b * TB + TB + 1
    c0, c1 = 0, TB + 2
    if b == 0:
        lo, c0 = 0, 1
    if b == nb - 1:
        hi, c1 = L, TB + 1
    for g in range(4):
        if b == 0:
            k.op("pool", lambda e: e.memset(X[g][:, :, 0:1], 0.0), writes=[X[g]])
        if b == nb - 1:
            k.op("pool", lambda e: e.memset(X[g][:, :, TB + 1:TB + 2], 0.0), writes=[X[g]], acc=(b == 0))
        src = xT16.h[g * 8:(g + 1) * 8, :, lo:hi].rearrange("c p t -> p c t")
        k.dma("sp" if g % 2 == 0 else "act", X[g][:, :, c0:c1], src, reads=[xT16], writes=[X[g]],
              acc=(b == 0 or b == nb - 1))


def dense_fm(k, X, wring, w_ap, m, epi, reads_w, after_mm=None):
    wt = wring.next()
    k.dma("pool", wt[:, :, 0:m], w_ap, reads=reads_w, writes=[wt], max_dma_last_dim=4096)
    pst = k.psr.next()
    for kc in range(KC):
        k.op("pe", lambda e, kc=kc: e.matmul(pst[0:m, :], wt[:, kc, 0:m], X[kc // 8][:, kc % 8, 1:TB + 1],
                                             start=(kc == 0), stop=(kc == KC - 1)),
             reads=[wt, X[kc // 8]], writes=[pst], acc=(kc > 0), last=(kc == KC - 1))
    if after_mm is not None:
        after_mm()
    epi(pst)


def dense_tm(k, X, wring, w_ap, epi, reads_w, n=512):
    wt = wring.next()
    k.dma("pool", wt[:, 0:16, 0:n], w_ap[:, 0:16, :], reads=reads_w, writes=[wt], max_dma_last_dim=4096)
    k.dma("pool", wt[:, 16:32, 0:n], w_ap[:, 16:32, :], reads=reads_w, writes=[wt], acc=True, max_dma_last_dim=4096)
    for tt in range(4):
        pst = k.psr.next()
        for kc in range(KC):
            k.op("pe", lambda e, kc=kc: e.matmul(pst[:, 0:n], X[kc // 8][:, kc % 8, 1 + tt * 128:1 + (tt + 1) * 128], wt[:, kc, 0:n],
                                                 start=(kc == 0), stop=(kc == KC - 1)),
                 reads=[wt, X[kc // 8]], writes=[pst], acc=(kc > 0), last=(kc == KC - 1))
        epi(pst, tt)


def evac_to_dram(k, pst, m, ncol, stage_ring, dst_ap, dst_t, i, in_ap=None):
    st = stage_ring.next()
    src = pst[0:m, 0:ncol] if in_ap is None else in_ap
    if i % 2 == 0:
        k.op("act", lambda e: e.copy(st[0:m, 0:ncol], src), reads=[pst], writes=[st])
    else:
        k.op("dve", lambda e: e.tensor_copy(st[0:m, 0:ncol], src), reads=[pst], writes=[st])
    k.dma("sp" if i % 2 == 0 else "act", dst_ap, st[0:m, 0:ncol], reads=[st], writes=[dst_t], acc=True)


def phase_even_inproj(k, xT16, L, w_fm, w_lr, w_tm, qT, kT, lrT, hyT, k_tok, v_tok, og_tok):
    nb = L // TB
    with k.scope():
        X = [k.sb([128, 8, TB + 2], BF16, "X") for _ in range(4)]
        wr_fm = Ring([k.sb([128, KC, 128], BF16, "wfm") for _ in range(3)])
        wr_tm = Ring([k.sb([128, KC, 512], BF16, "wtm") for _ in range(2)])
        st32 = Ring([k.sb([128, TB], F32, "st32") for _ in range(4)])
        st16 = Ring([k.sb([128, TB], BF16, "st16") for _ in range(2)])
        for b in range(nb):
            load_X(k, X, xT16, b, L)
            cnt = [0]
            for c in range(64):
                if c < 8:
                    dst_t, dst = qT, qT.h[c, :, sl(b, TB)]
                elif c < 16:
                    dst_t, dst = kT, kT.h[c - 8, :, sl(b, TB)]
                else:
                    dst_t, dst = hyT, hyT.h[c - 16, :, sl(b, TB)]

                def epi(pst, dst=dst, dst_t=dst_t):
                    evac_to_dram(k, pst, 128, TB, st32, dst, dst_t, cnt[0])
                    cnt[0] += 1
                dense_fm(k, X, wr_fm, w_fm.h[c], 128, epi, [w_fm])

            def epi_lr(pst):
                evac_to_dram(k, pst, 32, TB, st32, lrT.h[:, sl(b, TB)], lrT, 0)
            dense_fm(k, X, wr_fm, w_lr.h, 32, epi_lr, [w_lr])
            for n in range(10):
                if n < 2:
                    dst_t, col, ring = k_tok, n * 512, st32
                elif n < 6:
                    dst_t, col, ring = v_tok, (n - 2) * 512, st16
                else:
                    dst_t, col, ring = og_tok, (n - 6) * 512, st32

                def epi(pst, tt, dst_t=dst_t, col=col, ring=ring):
                    t0 = b * TB + tt * 128
                    evac_to_dram(k, pst, 128, 512, ring, dst_t.h[t0:t0 + 128, col:col + 512], dst_t, cnt[0])
                    cnt[0] += 1
                dense_tm(k, X, wr_tm, w_tm.h[n], epi, [w_tm])


def tile_fm(W):
    K, N = W.shape
    return np.ascontiguousarray(W.reshape(K // 128, 128, N // 128, 128).transpose(2, 1, 0, 3))


def tile_tm(W, n=512):
    K, N = W.shape
    return np.ascontiguousarray(W.reshape(K // 128, 128, N // n, n).transpose(2, 1, 0, 3))


def make_consts():
    c = {}
    c["ident32"] = np.eye(128, dtype=np.float32)
    return c


class LNBlock:
    def __init__(self, k, G, S, ones32, eps_t, ps_sum, ps_sq, zT, gtab, btab, ln_i):
        self.k, self.G, self.S = k, G, S
        self.ones32, self.eps_t = ones32, eps_t
        self.ps_sum, self.ps_sq = ps_sum, ps_sq
        self.zT, self.gtab, self.btab, self.ln_i = zT, gtab, btab, ln_i
        self.pending = None

    def prefetch_res(self, xT32_old, dc, b):
        res = self.G.next()
        self.k.dma("sp", res[:, 0:TB], xT32_old.h[dc, :, sl(b, TB)], reads=[xT32_old], writes=[res])
        return res

    def flush(self):
        if self.pending is not None:
            self.pending()
            self.pending = None

    def chunk(self, pst, res, dc, b):
        k = self.k
        z = self.G.next()
        zsq = self.G.next()
        k.op("dve", lambda e: e.scalar_tensor_tensor(out=z[:, 0:TB], in0=res[:, 0:TB], scalar=ALPHA, in1=pst[:, :],
                                                     op0=ALU.mult, op1=ALU.add), reads=[res, pst], writes=[z])
        k.op("act", lambda e: e.activation(out=zsq[:, 0:TB], in_=z[:, 0:TB], func=AF.Square), reads=[z], writes=[zsq])
        k.dma("act", self.zT.h[dc, :, sl(b, TB)], z[:, 0:TB], reads=[z], writes=[self.zT], acc=True)

        def stats(z=z, zsq=zsq, dc=dc):
            k.op("pe", lambda e: e.matmul(self.ps_sum[:, :], self.ones32[:], z[:, 0:TB], start=(dc == 0), stop=(dc == KC - 1)),
                 reads=[self.ones32, z], writes=[self.ps_sum], acc=(dc > 0), last=True)
            k.op("pe", lambda e: e.matmul(self.ps_sq[:, :], self.ones32[:], zsq[:, 0:TB], start=(dc == 0), stop=(dc == KC - 1)),
                 reads=[self.ones32, zsq], writes=[self.ps_sq], acc=(dc > 0), last=True)
        self.pending = stats

    def finish(self, b, xT32_new, xT16_new, out_tok=None, ident32=None):
        k = self.k
        self.flush()
        mean, msq, rstd, nmr = self.S
        k.op("dve", lambda e: e.tensor_scalar(out=mean[:], in0=self.ps_sum[:, :], scalar1=1.0 / D, scalar2=None, op0=ALU.mult),
             reads=[self.ps_sum], writes=[mean])
        k.op("dve", lambda e: e.tensor_tensor(out=msq[:], in0=mean[:], in1=mean[:], op=ALU.mult), reads=[mean], writes=[msq])
        k.op("dve", lambda e: e.scalar_tensor_tensor(out=msq[:], in0=self.ps_sq[:, :], scalar=1.0 / D, in1=msq[:],
                                                     op0=ALU.mult, op1=ALU.subtract), reads=[self.ps_sq, msq], writes=[msq])
        k.op("act", lambda e: e.activation(out=rstd[:], in_=msq[:], func=AF.Sqrt, bias=self.eps_t[:, 0:1], scale=1.0),
             reads=[msq, self.eps_t], writes=[rstd])
        k.op("dve", lambda e: e.reciprocal(out=rstd[:], in_=rstd[:]), reads=[rstd], writes=[rstd])
        k.op("dve", lambda e: e.scalar_tensor_tensor(out=nmr[:], in0=mean[:], scalar=-1.0, in1=rstd[:], op0=ALU.mult, op1=ALU.mult),
             reads=[mean, rstd], writes=[nmr])
        gi = self.ln_i * KC
        for dc in range(KC):
            zl = self.G.next()
            k.dma("sp", zl[:, 0:TB], self.zT.h[dc, :, sl(b, TB)], reads=[self.zT], writes=[zl])
            t1 = self.G.next()
            k.op("dve", lambda e: e.tensor_tensor(out=t1[:, 0:TB], in0=zl[:, 0:TB], in1=rstd[:], op=ALU.mult), reads=[zl, rstd], writes=[t1])
            k.op("pool", lambda e: e.tensor_tensor(out=t1[:, 0:TB], in0=t1[:, 0:TB], in1=nmr[:], op=ALU.add), reads=[t1, nmr], writes=[t1])
            x32 = self.G.next()
            k.op("act", lambda e: e.activation(out=x32[:, 0:TB], in_=t1[:, 0:TB], func=AF.Identity,
                                               bias=self.btab[:, gi + dc:gi + dc + 1], scale=self.gtab[:, gi + dc:gi + dc + 1]),
                 reads=[t1, self.gtab, self.btab], writes=[x32])
            if xT32_new is not None:
                k.dma("sp", xT32_new.h[dc, :, sl(b, TB)], x32[:, 0:TB], reads=[x32], writes=[xT32_new], acc=True)
                x16 = self.G.next()
                x16v = x16.h[:, 0:TB // 2].bitcast(BF16)
                k.op("pool", lambda e: e.tensor_copy(x16v, x32[:, 0:TB]), reads=[x32], writes=[x16])
                k.dma("act", xT16_new.h[dc, :, sl(b, TB)], x16v, reads=[x16], writes=[xT16_new], acc=True)
            if out_tok is not None:
                pst = k.psr.next()
                for tt in range(4):
                    k.op("pe", lambda e, tt=tt: e.transpose(pst[:, sl(tt)], x32[:, sl(tt)], ident32[:]),
                         reads=[x32, ident32], writes=[pst], acc=(tt > 0), last=(tt == 3))
                ot = self.G.next()
                k.op("dve", lambda e: e.tensor_copy(ot[:, 0:TB], pst[:, :]), reads=[pst], writes=[ot])
                for tt in range(4):
                    t0 = b * TB + tt * 128
                    k.dma("sp" if tt % 2 == 0 else "act", out_tok.h[t0:t0 + 128, sl(dc)], ot[:, sl(tt)], reads=[ot], writes=[out_tok], acc=True)


def ln_consts(k, ones32_d, gtab_d, btab_d):
    ones32 = k.sb([128, 128], F32, "ones")
    k.dma("sp", ones32[:], ones32_d.h[:, :], reads=[ones32_d], writes=[ones32])
    eps_t = k.sb([128, 1], F32, "eps")
    k.op("dve", lambda e: e.memset(eps_t[:], LN_EPS), writes=[eps_t])
    gtab = k.sb([128, 4 * KC], F32, "gtab")
    btab = k.sb([128, 4 * KC], F32, "btab")
    k.dma("sp", gtab[:], gtab_d.h[:, :], reads=[gtab_d], writes=[gtab])
    k.dma("sp", btab[:], btab_d.h[:, :], reads=[btab_d], writes=[btab])
    return ones32, eps_t, gtab, btab


def phase_outproj_ln(k, mixT16, L, w_out, xT32_old, xT32_new, xT16_new, zT, ones32_d, gtab_d, btab_d, ln_i):
    nb = L // TB
    with k.scope():
        ones32, eps_t, gtab, btab = ln_consts(k, ones32_d, gtab_d, btab_d)
        X = [k.sb([128, 8, TB + 2], BF16, "X") for _ in range(4)]
        wr = Ring([k.sb([128, KC, 128], BF16, "wfm") for _ in range(3)])
        G = Ring([k.sb([128, TB + 2], F32, "G") for _ in range(12)])
        S = [k.sb([128, TB], F32, "S") for _ in range(4)]
        for b in range(nb):
            load_X(k, X, mixT16, b, L)
            ln = LNBlock(k, G, S, ones32, eps_t, k.ps_sum, k.ps_sq, zT, gtab, btab, ln_i)
            for dc in range(KC):
                res = ln.prefetch_res(xT32_old, dc, b)
                dense_fm(k, X, wr, w_out.h[dc], 128, lambda pst, res=res, dc=dc: ln.chunk(pst, res, dc, b), [w_out], after_mm=ln.flush)
            ln.finish(b, xT32_new, xT16_new)


def phase_ffn(k, xT16, L, w_up, w_dn, cw_d, xT32_old, xT32_new, xT16_new, zT, ones32_d, gtab_d, btab_d, ln_i, layer,
              wc_up, wc_dn, out_tok=None, ident32_d=None):
    nb = L // TB
    NH = 2 * (nb - 1)
    with k.scope():
        ones32, eps_t, gtab, btab = ln_consts(k, ones32_d, gtab_d, btab_d)
        ident32 = None
        if out_tok is not None:
            ident32 = k.sb([128, 128], F32, "ident")
            k.dma("sp", ident32[:], ident32_d.h[:, :], reads=[ident32_d], writes=[ident32])
        cw = k.sb([128, 2 * FC, 4], F32, "cw")
        k.dma("sp", cw[:], cw_d.h[:, layer], reads=[cw_d], writes=[cw])
        X = [k.sb([128, 8, TB + 2], BF16, "X") for _ in range(4)]
        act = [k.sb([128, TB], BF16, "act") for _ in range(FC)]
        wr = Ring([k.sb([128, 43 * 128], BF16, "w") for _ in range(4)])
        G = Ring([k.sb([128, TB + 2], F32, "G") for _ in range(10)])
        S = [k.sb([128, TB], F32, "S") for _ in range(4)]
        hh = k.sb([128, 2 * FC, 16], F32, "hh")
        Xh = k.sb([128, KC, 16], BF16, "Xh")
        k.op("pool", lambda e: e.memset(Xh[:], 0.0), writes=[Xh])
        for j in range(1, nb):
            k.dma("sp" if j % 2 else "act", Xh[:, :, 2 * (j - 1):2 * j], xT16.h[:, :, TB * j - 1:TB * j + 1].rearrange("c p t -> p c t"),
                  reads=[xT16], writes=[Xh], acc=True)
        if NH == 0:
            k.op("pool", lambda e: e.memset(hh[:], 0.0), writes=[hh])
        for fi in range(2 * FC):
            wt = wr.next()
            wv = wt.h[:, 0:KC * 128].rearrange("p (c j) -> p c j", j=128)
            k.dma("pool", wv, w_up.h[fi], reads=[w_up], writes=[wt], max_dma_last_dim=4096)
            k.dma("sp" if fi % 2 else "act", wc_up.h[fi], wt[:, 0:KC * 128], reads=[wt], writes=[wc_up], acc=True)
            if NH > 0:
                ph = k.psr.next()
                for kc in range(KC):
                    k.op("pe", lambda e, kc=kc: e.matmul(ph[:, 0:NH], wv[:, kc, :], Xh[:, kc, 0:NH], start=(kc == 0), stop=(kc == KC - 1)),
                         reads=[wt, Xh], writes=[ph], acc=(kc > 0), last=(kc == KC - 1))
                k.op("act", lambda e: e.copy(hh[:, fi, 0:NH], ph[:, 0:NH]), reads=[ph], writes=[hh], acc=(fi > 0))
        for b in range(nb):
            load_X(k, X, xT16, b, L)
            for f in range(FC):
                cs = []
                for half in range(2):
                    fi = half * FC + f
                    wt = wr.next()
                    wv = wt.h[:, 0:KC * 128].rearrange("p (c j) -> p c j", j=128)
                    k.dma("sp" if half == 0 else "act", wt[:, 0:KC * 128], wc_up.h[fi], reads=[wc_up], writes=[wt])
                    pst = k.psr.next()
                    for kc in range(KC):
                        xt = X[kc // 8]
                        k.op("pe", lambda e, kc=kc, xt=xt: e.matmul(pst[:, :], wv[:, kc, :], xt[:, kc % 8, 1:TB + 1],
                                                                   start=(kc == 0), stop=(kc == KC - 1)),
                             reads=[wt, xt], writes=[pst], acc=(kc > 0), last=(kc == KC - 1))
                    hb = G.next()
                    k.op("act", lambda e: e.copy(hb[:, 1:TB + 1], pst[:, :]), reads=[pst], writes=[hb])
                    if b > 0:
                        k.op("pool", lambda e: e.tensor_copy(hb[:, 0:1], hh[:, fi, 2 * (b - 1):2 * (b - 1) + 1]), reads=[hh], writes=[hb], acc=True)
                    else:
                        k.op("pool", lambda e: e.memset(hb[:, 0:1], 0.0), writes=[hb], acc=True)
                    if b < nb - 1:
                        k.op("pool", lambda e: e.tensor_copy(hb[:, TB + 1:TB + 2], hh[:, fi, 2 * b + 1:2 * b + 2]), reads=[hh], writes=[hb], acc=True)
                    else:
                        k.op("pool", lambda e: e.memset(hb[:, TB + 1:TB + 2], 0.0), writes=[hb], acc=True)
                    c = G.next()
                    k.op("dve", lambda e: e.tensor_scalar(out=c[:, 0:TB], in0=hb[:, 0:TB], scalar1=cw[:, fi, 0:1], scalar2=cw[:, fi, 3:4],
                                                          op0=ALU.mult, op1=ALU.add), reads=[hb, cw], writes=[c])
                    k.op("dve", lambda e: e.scalar_tensor_tensor(out=c[:, 0:TB], in0=hb[:, 1:TB + 1], scalar=cw[:, fi, 1:2], in1=c[:, 0:TB],
                                                                 op0=ALU.mult, op1=ALU.add), reads=[hb, cw, c], writes=[c])
                    k.op("dve", lambda e: e.scalar_tensor_tensor(out=c[:, 0:TB], in0=hb[:, 2:TB + 2], scalar=cw[:, fi, 2:3], in1=c[:, 0:TB],
                                                                 op0=ALU.mult, op1=ALU.add), reads=[hb, cw, c], writes=[c])
                    cs.append(c)
                sg = G.next()
                k.op("act", lambda e: e.activation(out=sg[:, 0:TB], in_=cs[0][:, 0:TB], func=AF.Silu), reads=[cs[0]], writes=[sg])
                k.op("pool", lambda e: e.tensor_tensor(out=act[f][:], in0=sg[:, 0:TB], in1=cs[1][:, 0:TB], op=ALU.mult),
                     reads=[sg, cs[1]], writes=[act[f]])
            ln = LNBlock(k, G, S, ones32, eps_t, k.ps_sum, k.ps_sq, zT, gtab, btab, ln_i)
            for dc in range(KC):
                res = ln.prefetch_res(xT32_old, dc, b)
                wts = []
                for h in range(2):
                    wt = wr.next()
                    cs_ = slice(h * 43 * 128, (h + 1) * 43 * 128)
                    if b == 0:
                        k.dma("pool", wt[:, :], w_dn.h[dc, :, cs_], reads=[w_dn], writes=[wt], max_dma_last_dim=4096)
                        k.dma("sp" if h else "act", wc_dn.h[dc, :, cs_], wt[:, :], reads=[wt], writes=[wc_dn], acc=True)
                    else:
                        k.dma("sp" if h else "act", wt[:, :], wc_dn.h[dc, :, cs_], reads=[wc_dn], writes=[wt])
                    wts.append(wt)
                pst = k.psr.next()
                for fc in range(FC):
                    wt = wts[fc // 43]
                    o = (fc % 43) * 128
                    k.op("pe", lambda e, wt=wt, o=o, fc=fc: e.matmul(pst[:, :], wt[:, o:o + 128], act[fc][:], start=(fc == 0), stop=(fc == FC - 1)),
                         reads=[wt, act[fc]], writes=[pst], acc=(fc > 0), last=(fc == FC - 1))
                ln.flush()
                ln.chunk(pst, res, dc, b)
            ln.finish(b, xT32_new, xT16_new, out_tok=out_tok, ident32=ident32)


def phase_gla(k, L, qT, kT, k_tok, v_tok, og_tok, lrT, wg_d, gnorm_d, cst_d, ident32_d, of_tok, mixT16):
    NCH = L // 128
    with k.scope():
        cst = k.sb([128, 4, 128], F32, "cst")
        k.dma("sp", cst[:], cst_d.h.rearrange("a p c -> p a c"), reads=[cst_d], writes=[cst])
        ident32 = k.sb([128, 128], F32, "ident")
        k.dma("sp", ident32[:], ident32_d.h[:, :], reads=[ident32_d], writes=[ident32])
        gn = k.sb([128, 512], F32, "gn")
        k.dma("sp", gn[:], gnorm_d.h[:, :], reads=[gnorm_d], writes=[gn])
        wg = [k.sb([17, 1024], F32, "wg") for _ in range(2)]
        lra = [k.sb([17, L], F32, "lra") for _ in range(2)]
        for d in range(2):
            k.dma("sp", wg[d][:], wg_d.h[d], reads=[wg_d], writes=[wg[d]])
            k.op("dve", lambda e, d=d: e.memset(lra[d][:], 1.0), writes=[lra[d]])
            k.dma("sp", lra[d][0:16, :], lrT.h[16 * d:16 * d + 16, :], reads=[lrT], writes=[lra[d]])
        epsr = k.sb([128, 1], F32, "epsr")
        k.op("dve", lambda e: e.memset(epsr[:], RMS_EPS), writes=[epsr])
        S32 = [k.sb([128, 512], F32, "S32") for _ in range(2)]
        S16 = [k.sb([128, 512], BF16, "S16") for _ in range(2)]
        qr = Ring([k.sb([128, 2, 128], F32, "q") for _ in range(2)])
        kr = Ring([k.sb([128, 2, 128], F32, "kk") for _ in range(2)])
        ktr = Ring([k.sb([128, 256], F32, "kt") for _ in range(2)])
        vr = Ring([k.sb([128, 512], BF16, "v") for _ in range(2)])
        ogr = Ring([k.sb([128, 512], F32, "og") for _ in range(2)])
        ofr = Ring([k.sb([128, 512], F32, "of") for _ in range(2)])
        A = Ring([k.sb([128, 256], F32, "A") for _ in range(8)])
        Bq = Ring([k.sb([128, 2, 128], BF16, "Bq") for _ in range(4)])
        Bk = Ring([k.sb([128, 256], BF16, "Bk") for _ in range(2)])
        Pr = Ring([k.sb([128, 128], BF16, "P") for _ in range(2)])
        O = Ring([k.sb([128, 512], F32, "O") for _ in range(6)])
        sm = Ring([k.sb([128, 1], F32, "sm") for _ in range(6)])
        tr16 = Ring([k.sb([128, 4, 128], BF16, "tr") for _ in range(2)])
        for h in range(4):
            for d in range(2):
                tri = cst[:, 0 + d, :]
                ust = cst[:, 2 + d, :]
                for j in range(2):
                    k.op("dve", lambda e, j=j: e.memset(S32[j][:], 0.0), writes=[S32[j]])
                    k.op("pool", lambda e, j=j: e.memset(S16[j][:], 0.0), writes=[S16[j]])
                order = range(NCH) if d == 0 else range(NCH - 1, -1, -1)
                for n in order:
                    ts = slice(n * 128, (n + 1) * 128)
                    qt, kt, ktt, vt = qr.next(), kr.next(), ktr.next(), vr.next()
                    k.dma("sp", qt[:], qT.h[2 * h:2 * h + 2, :, ts].rearrange("j p c -> p j c"), reads=[qT], writes=[qt])
                    k.dma("act", kt[:], kT.h[2 * h:2 * h + 2, :, ts].rearrange("j p c -> p j c"), reads=[kT], writes=[kt])
                    k.dma("sp", ktt[:], k_tok.h[ts, h * 256:(h + 1) * 256], reads=[k_tok], writes=[ktt])
                    k.dma("act", vt[:], v_tok.h[ts, h * 512:(h + 1) * 512], reads=[v_tok], writes=[vt])
                    p1 = k.psr.next()
                    k.op("pe", lambda e: e.matmul(p1[:, 0:256], lra[d][0:17, ts], wg[d][0:17, h * 256:(h + 1) * 256], start=True, stop=True),
                         reads=[lra[d], wg[d]], writes=[p1])
                    ex = A.next()
                    k.op("act", lambda e: e.activation(out=ex[:], in_=p1[:, 0:256], func=AF.Exp, scale=-1.0), reads=[p1], writes=[ex])
                    la = A.next()
                    k.op("act", lambda e: e.activation(out=la[:], in_=ex[:], func=AF.Ln, bias=1.0, scale=1.0), reads=[ex], writes=[la])
                    p2 = k.psr.next()
                    k.op("pe", lambda e: e.matmul(p2[:, 0:256], ust, la[:], start=True, stop=True), reads=[cst, la], writes=[p2])
                    kd = A.next()
                    k.op("act", lambda e: e.activation(out=kd[:], in_=p2[:, 0:256], func=AF.Exp, scale=-1.0 / 16), reads=[p2], writes=[kd])
                    kdec = Bk.next()
                    k.op("dve", lambda e: e.tensor_tensor(out=kdec[:], in0=ktt[:], in1=kd[:], op=ALU.mult), reads=[ktt, kd], writes=[kdec])
                    p3 = k.psr.next()
                    for j in range(2):
                        k.op("pe", lambda e, j=j: e.matmul(p3[:, j * 128:(j + 1) * 128], la[:, j * 128:(j + 1) * 128], tri, start=True, stop=True),
                             reads=[la, cst], writes=[p3], acc=(j > 0), last=(j == 1))
                    eb = A.next()
                    ei = A.next()
                    k.op("act", lambda e: e.activation(out=eb[:], in_=p3[:, 0:256], func=AF.Exp, scale=-1.0 / 16), reads=[p3], writes=[eb])
                    k.op("act", lambda e: e.activation(out=ei[:], in_=p3[:, 0:256], func=AF.Exp, scale=1.0 / 16), reads=[p3], writes=[ei])
                    qd = Bq.next()
                    ki = Bq.next()
                    k.op("dve", lambda e: e.scalar_tensor_tensor(out=qd.h[:].rearrange("p j c -> p (j c)"), in0=qt.h[:].rearrange("p j c -> p (j c)"),
                                                                 scalar=1.0 / 16, in1=eb[:], op0=ALU.mult, op1=ALU.mult),
                         reads=[qt, eb], writes=[qd])
                    k.op("pool", lambda e: e.tensor_tensor(out=ki.h[:].rearrange("p j c -> p (j c)"), in0=kt.h[:].rearrange("p j c -> p (j c)"),
                                                           in1=ei[:], op=ALU.mult), reads=[kt, ei], writes=[ki])
                    p4 = k.psr.next()
                    for j in range(2):
                        k.op("pe", lambda e, j=j: e.matmul(p4[:, 0:128], ki[:, j, :], qd[:, j, :], start=(j == 0), stop=(j == 1)),
                             reads=[ki, qd], writes=[p4], acc=(j > 0), last=(j == 1))
                    P = Pr.next()
                    k.op("dve", lambda e: e.tensor_tensor(out=P[:], in0=p4[:, 0:128], in1=tri, op=ALU.mult), reads=[p4, cst], writes=[P])
                    p5 = k.psr.next()
                    k.op("pe", lambda e: e.matmul(p5[:, :], P[:], vt[:], start=True, stop=False), reads=[P, vt], writes=[p5], last=False)
                    for j in range(2):
                        k.op("pe", lambda e, j=j: e.matmul(p5[:, :], qd[:, j, :], S16[j][:], start=False, stop=(j == 1)),
                             reads=[qd, S16[j]], writes=[p5], acc=True, last=(j == 1))
                    lastc = 127 if d == 0 else 0
                    for j in range(2):
                        p6 = k.psr.next()
                        k.op("pe", lambda e, j=j: e.matmul(p6[:, :], kdec[:, j * 128:(j + 1) * 128], vt[:], start=True, stop=True),
                             reads=[kdec, vt], writes=[p6])
                        k.op("dve", lambda e, j=j, p6=p6: e.scalar_tensor_tensor(out=S32[j][:], in0=S32[j][:], scalar=eb[:, j * 128 + lastc:j * 128 + lastc + 1],
                                                                                 in1=p6[:, :], op0=ALU.mult, op1=ALU.add),
                             reads=[S32[j], eb, p6], writes=[S32[j]])
                        k.op("act", lambda e, j=j: e.copy(S16[j][:], S32[j][:]), reads=[S32[j]], writes=[S16[j]])
                    if d == 0:
                        o = O.next()
                        k.op("act", lambda e: e.copy(o[:], p5[:, :]), reads=[p5], writes=[o])
                        k.dma("sp", of_tok.h[ts, h * 512:(h + 1) * 512], o[:], reads=[o], writes=[of_tok], acc=True)
                    else:
                        oft, ogt = ofr.next(), ogr.next()
                        k.dma("sp", oft[:], of_tok.h[ts, h * 512:(h + 1) * 512], reads=[of_tok], writes=[oft])
                        k.dma("act", ogt[:], og_tok.h[ts, h * 512:(h + 1) * 512], reads=[og_tok], writes=[ogt])
                        o = O.next()
                        k.op("dve", lambda e: e.tensor_tensor(out=o[:], in0=oft[:], in1=p5[:, :], op=ALU.add), reads=[oft, p5], writes=[o])
                        sq = O.next()
                        ss = sm.next()
                        k.op("act", lambda e: e.activation(out=sq[:], in_=o[:], func=AF.Square, accum_out=ss[:]), reads=[o], writes=[sq, ss])
                        rs = sm.next()
                        k.op("act", lambda e: e.activation(out=rs[:], in_=ss[:], func=AF.Sqrt, bias=epsr[:, 0:1], scale=1.0 / 512),
                             reads=[ss, epsr], writes=[rs])
                        k.op("dve", lambda e: e.reciprocal(out=rs[:], in_=rs[:]), reads=[rs], writes=[rs])
                        on = O.next()
                        k.op("dve", lambda e: e.scalar_tensor_tensor(out=on[:], in0=o[:], scalar=rs[:, 0:1], in1=gn[:], op0=ALU.mult, op1=ALU.mult),
                             reads=[o, rs, gn], writes=[on])
                        sg = O.next()
                        k.op("act", lambda e: e.activation(out=sg[:], in_=ogt[:], func=AF.Silu), reads=[ogt], writes=[sg])
                        k.op("pool", lambda e: e.tensor_tensor(out=on[:], in0=on[:], in1=sg[:], op=ALU.mult), reads=[on, sg], writes=[on])
                        p7 = k.psr.next()
                        for j in range(4):
                            k.op("pe", lambda e, j=j: e.transpose(p7[:, sl(j)], on[:, sl(j)], ident32[:]),
                                 reads=[on, ident32], writes=[p7], acc=(j > 0), last=(j == 3))
                        t16 = tr16.next()
                        k.op("dve", lambda e: e.tensor_copy(t16.h[:].rearrange("p j c -> p (j c)"), p7[:, :]), reads=[p7], writes=[t16])
                        k.dma("sp", mixT16.h[4 * h:4 * h + 4, :, ts].rearrange("j p c -> p j c"), t16[:], reads=[t16], writes=[mixT16], acc=True)


def gla_consts():
    i = np.arange(128)
    tri = (i[:, None] <= i[None, :]).astype(np.float32)
    ust = (i[:, None] > i[None, :]).astype(np.float32)
    return np.stack([tri, tri.T.copy(), ust, ust.T.copy()])


MAGIC = 12582912.0
TWO_PI = 6.283185307179586


def hy_consts(L):
    N = 2 * L
    N1 = N // 128
    T1 = L // 128
    t = np.linspace(0.0, 1.0, L, dtype=np.float32)[:, None]
    bands = np.linspace(1e-4, 15.0, 16, dtype=np.float32)
    w = (2.0 * np.pi * np.arange(L, dtype=np.float32)[:, None] / L).astype(np.float32)
    z = np.concatenate([t, np.cos(bands * w), -np.sin(bands * w), np.ones((L, 1), np.float32)], -1).astype(np.float32)
    c = {}
    c["hy_zT"] = np.ascontiguousarray(z.T)
    max_decay = np.log(1e-2) / 0.3
    min_decay = np.log(1e-2) / 1.5
    deltas = np.linspace(min_decay, max_decay, 2048, dtype=np.float32)
    c["hy_absd"] = np.tile(np.abs(deltas)[None, :], (128, 1)).astype(np.float32)
    tau = (np.arange(L).reshape(T1, 128).T).astype(np.float64)
    c["hy_negt"] = np.ascontiguousarray((-(tau / (L - 1))).astype(np.float32))
    t1 = np.arange(N1)[:, None, None]
    t2 = np.arange(128)[None, :, None]
    f1 = np.arange(N1)[None, None, :]
    ang = 2.0 * np.pi * ((f1 * (128 * t1 + t2)) % N) / N
    fw = np.stack([np.cos(ang), -np.sin(ang)], 2).astype(np.float32)
    c["hy_fw"] = np.ascontiguousarray(fw[:T1])
    iv = np.stack([np.cos(ang), -np.sin(ang)], 2) / N
    c["hy_iv"] = np.ascontiguousarray(iv[:T1].transpose(3, 1, 2, 0)).astype(np.float32)
    a = 2.0 * np.pi * ((np.arange(128)[:, None] * np.arange(128)[None, :]) % 128) / 128
    c["hy_cs"] = np.stack([np.cos(a), np.sin(a), -np.sin(a)]).astype(np.float32)
    return c


def hy_sin(k, G, ps, bias_ap, bias_t, out_ap, out_t, m):
    xs, n1, xr = G.next(), G.next(), G.next()
    if bias_ap is None:
        k.op("dve", lambda e: e.tensor_copy(xs[0:m, :], ps[0:m, :]), reads=[ps], writes=[xs])
    else:
        k.op("dve", lambda e: e.tensor_scalar(out=xs[0:m, :], in0=ps[0:m, :], scalar1=bias_ap, scalar2=None, op0=ALU.add),
             reads=[ps, bias_t], writes=[xs])
    k.op("dve", lambda e: e.tensor_scalar(out=n1[0:m, :], in0=xs[0:m, :], scalar1=1.0 / TWO_PI, scalar2=MAGIC, op0=ALU.mult, op1=ALU.add),
         reads=[xs], writes=[n1])
    k.op("dve", lambda e: e.tensor_scalar(out=n1[0:m, :], in0=n1[0:m, :], scalar1=-MAGIC, scalar2=None, op0=ALU.add), reads=[n1], writes=[n1])
    k.op("dve", lambda e: e.scalar_tensor_tensor(out=xr[0:m, :], in0=n1[0:m, :], scalar=-TWO_PI, in1=xs[0:m, :], op0=ALU.mult, op1=ALU.add),
         reads=[n1, xs], writes=[xr])
    k.op("act", lambda e: e.activation(out=out_ap, in_=xr[0:m, :], func=AF.Sin), reads=[xr], writes=[out_t])


def hy_filters(k, L, zT_d, w1a_d, w2_d, b2_d, w3_d, absd_d, negt_d, kf_tok, kb_tok):
    T1 = L // 128
    with k.scope():
        zT = k.sb([34, L], F32, "zT")
        k.dma("sp", zT[:], zT_d.h[:, :], reads=[zT_d], writes=[zT])
        w1a = k.sb([34, 64], F32, "w1a")
        k.dma("sp", w1a[:], w1a_d.h[:, :], reads=[w1a_d], writes=[w1a])
        w2 = k.sb([64, 64], F32, "w2")
        k.dma("sp", w2[:], w2_d.h[:, :], reads=[w2_d], writes=[w2])
        b2 = k.sb([64, 1], F32, "b2")
        k.dma("sp", b2[:], b2_d.h[:, :], reads=[b2_d], writes=[b2])
        w3 = k.sb([64, 4096], F32, "w3")
        k.dma("act", w3[:], w3_d.h[:, :], reads=[w3_d], writes=[w3])
        absd = k.sb([128, 2048], F32, "absd")
        k.dma("act", absd[:], absd_d.h[:, :], reads=[absd_d], writes=[absd])
        negt = k.sb([128, T1], F32, "negt")
        k.dma("sp", negt[:], negt_d.h[:, :], reads=[negt_d], writes=[negt])
        h2T = k.sb([64, L], F32, "h2T")
        G = Ring([k.sb([128, 512], F32, "G") for _ in range(10)])
        h1r = Ring([k.sb([64, 512], F32, "h1") for _ in range(2)])
        for b in range(L // 512):
            p = k.psr.next()
            k.op("pe", lambda e: e.matmul(p[0:64, :], w1a[:, :], zT[:, sl(b, 512)], start=True, stop=True), reads=[w1a, zT], writes=[p])
            h1 = h1r.next()
            hy_sin(k, G, p, None, None, h1[:, :], h1, 64)
            p2 = k.psr.next()
            k.op("pe", lambda e: e.matmul(p2[0:64, :], w2[:, :], h1[:, :], start=True, stop=True), reads=[w2, h1], writes=[p2])
            hy_sin(k, G, p2, b2[:, 0:1], b2, h2T[:, sl(b, 512)], h2T, 64)
        i = 0
        for n in range(T1):
            for c4 in range(4):
                win = G.next()
                k.op("act", lambda e: e.activation(out=win[:], in_=absd[:, sl(c4, 512)], func=AF.Exp, scale=negt[:, n:n + 1]),
                     reads=[absd, negt], writes=[win])
                for fb in range(2):
                    p = k.psr.next()
                    col = fb * 2048 + c4 * 512
                    k.op("pe", lambda e: e.matmul(p[:, :], h2T[:, sl(n)], w3[:, col:col + 512], start=True, stop=True), reads=[h2T, w3], writes=[p])
                    st = G.next()
                    k.op("dve", lambda e: e.tensor_tensor(out=st[:], in0=p[:, :], in1=win[:], op=ALU.mult), reads=[p, win], writes=[st])
                    dst = kf_tok if fb == 0 else kb_tok
                    if fb == 1 and n == 0:
                        k.op("dve", lambda e: e.memset(st[0:1, :], 0.0), writes=[st])
                    k.dma("sp" if i % 2 == 0 else "act", dst.h[sl(n), sl(c4, 512)], st[:], reads=[st], writes=[dst], acc=True)
                    i += 1


def hy_prep_u(k, L, hyT, hcw_d, ident32_d, uT, u_tok):
    with k.scope():
        hcw = k.sb([128, 48, 4], F32, "hcw")
        k.dma("sp", hcw[:], hcw_d.h[:, :, :], reads=[hcw_d], writes=[hcw])
        ident32 = k.sb([128, 128], F32, "ident")
        k.dma("sp", ident32[:], ident32_d.h[:, :], reads=[ident32_d], writes=[ident32])
        inr = Ring([k.sb([128, L + 2], F32, "in") for _ in range(3)])
        cr = Ring([k.sb([128, L], F32, "c") for _ in range(3)])
        st = Ring([k.sb([128, 4, 128], F32, "st") for _ in range(3)])
        for cc in range(16):
            cs = []
            for which in (16, 32):
                ch = which + cc
                t = inr.next()
                k.op("pool", lambda e: e.memset(t[:, 0:L + 2:L + 1], 0.0), writes=[t])
                k.dma("sp" if which == 16 else "act", t[:, 1:L + 1], hyT.h[ch], reads=[hyT], writes=[t], acc=True)
                c = cr.next()
                hy_conv(k, c, t, hcw, ch, L)
                cs.append(c)
            u = cs[0]
            k.op("pool", lambda e: e.tensor_tensor(out=u[:], in0=cs[0][:], in1=cs[1][:], op=ALU.mult), reads=[cs[0], cs[1]], writes=[u])
            k.dma("sp", uT.h[cc], u[:], reads=[u], writes=[uT], acc=True)
            for g in range(L // 512):
                p = k.psr.next()
                for j in range(4):
                    k.op("pe", lambda e, j=j: e.transpose(p[:, sl(j)], u[:, g * 512 + j * 128:g * 512 + (j + 1) * 128], ident32[:]),
                         reads=[u, ident32], writes=[p], acc=(j > 0), last=(j == 3))
                s = st.next()
                if g % 2 == 0:
                    k.op("act", lambda e: e.copy(s.h[:].rearrange("p a c -> p (a c)"), p[:, :]), reads=[p], writes=[s])
                else:
                    k.op("dve", lambda e: e.tensor_copy(s.h[:].rearrange("p a c -> p (a c)"), p[:, :]), reads=[p], writes=[s])
                k.dma("act" if g % 2 == 0 else "sp", u_tok.h[g * 512:(g + 1) * 512, sl(cc)].rearrange("(a p) c -> p a c", p=128), s[:],
                      reads=[s], writes=[u_tok], acc=True)


def hy_conv(k, c, t, hcw, ch, L):
    k.op("dve", lambda e: e.tensor_scalar(out=c[:, 0:L], in0=t[:, 0:L], scalar1=hcw[:, ch, 0:1], scalar2=hcw[:, ch, 3:4], op0=ALU.mult, op1=ALU.add),
         reads=[t, hcw], writes=[c])
    k.op("dve", lambda e: e.scalar_tensor_tensor(out=c[:, 0:L], in0=t[:, 1:L + 1], scalar=hcw[:, ch, 1:2], in1=c[:, 0:L], op0=ALU.mult, op1=ALU.add),
         reads=[t, hcw, c], writes=[c])
    k.op("dve", lambda e: e.scalar_tensor_tensor(out=c[:, 0:L], in0=t[:, 2:L + 2], scalar=hcw[:, ch, 2:3], in1=c[:, 0:L], op0=ALU.mult, op1=ALU.add),
         reads=[t, hcw, c], writes=[c])


def hy_fft_a(k, L, src_tok, fw_d, scrA):
    N1, T1 = 2 * L // 128, L // 128
    with k.scope():
        fw = k.sb([T1, 128, 2, N1], F32, "fw")
        k.dma("sp", fw[:, 0:64], fw_d.h[:, 0:64], reads=[fw_d], writes=[fw])
        k.dma("act", fw[:, 64:128], fw_d.h[:, 64:128], reads=[fw_d], writes=[fw], acc=True)
        sr = Ring([k.sb([T1, 2048], F32, "s") for _ in range(3)])
        st = Ring([k.sb([N1, 2048], F32, "st") for _ in range(4)])
        src_v = src_tok.h.rearrange("(a p) c -> p a c", p=128)
        i = 0
        for t2 in range(128):
            s = sr.next()
            k.dma("sp" if t2 % 2 == 0 else "act", s[:], src_v[t2], reads=[src_tok], writes=[s])
            for ri in range(2):
                so = st.next()
                for ct in range(4):
                    p = k.psr.next()
                    k.op("pe", lambda e: e.matmul(p[0:N1, :], fw[:, t2, ri, :], s[:, sl(ct, 512)], start=True, stop=True), reads=[fw, s], writes=[p])
                    if i % 2 == 0:
                        k.op("act", lambda e: e.copy(so[:, sl(ct, 512)], p[0:N1, :]), reads=[p], writes=[so], acc=(ct > 0))
                    else:
                        k.op("dve", lambda e: e.tensor_copy(so[:, sl(ct, 512)], p[0:N1, :]), reads=[p], writes=[so], acc=(ct > 0))
                    i += 1
                k.dma("sp" if ri == 0 else "act", scrA.h[ri, :, t2, :], so[:], reads=[so], writes=[scrA], acc=True)


def hy_fft_b(k, L, scrA, cs_d, epi, mk_extra):
    N1 = 2 * L // 128
    with k.scope():
        cs = k.sb([128, 3, 128], F32, "cs")
        k.dma("sp", cs[:], cs_d.h.rearrange("a p c -> p a c"), reads=[cs_d], writes=[cs])
        ar = Ring([k.sb([128, 2, 2048], F32, "a") for _ in range(2)])
        ctx = mk_extra(cs)
        for f1 in range(N1):
            a = ar.next()
            k.dma("sp", a[:, 0, :], scrA.h[0, f1], reads=[scrA], writes=[a])
            k.dma("act", a[:, 1, :], scrA.h[1, f1], reads=[scrA], writes=[a], acc=True)
            for ct in range(4):
                pr, pi = k.psr.next(), k.psr.next()
                c = slice(ct * 512, (ct + 1) * 512)
                k.op("pe", lambda e: e.matmul(pr[:, :], cs[:, 0, :], a[:, 0, c], start=True, stop=False), reads=[cs, a], writes=[pr], last=False)
                k.op("pe", lambda e: e.matmul(pr[:, :], cs[:, 1, :], a[:, 1, c], start=False, stop=True), reads=[cs, a], writes=[pr], acc=True)
                k.op("pe", lambda e: e.matmul(pi[:, :], cs[:, 0, :], a[:, 1, c], start=True, stop=False), reads=[cs, a], writes=[pi], last=False)
                k.op("pe", lambda e: e.matmul(pi[:, :], cs[:, 2, :], a[:, 0, c], start=False, stop=True), reads=[cs, a], writes=[pi], acc=True)
                epi(f1, ct, pr, pi, ctx)


def hy_spectrum_store(k, scrK, combine):
    def mk(cs):
        return {"st": Ring([k.sb([128, 2, 512], F32, "kst") for _ in range(3)]),
                "ld": Ring([k.sb([128, 2, 512], F32, "kld") for _ in range(3)])}

    def epi(f1, ct, pr, pi, ctx):
        c = slice(ct * 512, (ct + 1) * 512)
        s = ctx["st"].next()
        if not combine:
            k.op("act", lambda e: e.copy(s[:, 0, :], pr[:, :]), reads=[pr], writes=[s])
            k.op("dve", lambda e: e.tensor_copy(s[:, 1, :], pi[:, :]), reads=[pi], writes=[s], acc=True)
        else:
            ld = ctx["ld"].next()
            k.dma("sp", ld[:], scrK.h[:, f1, :, c].rearrange("r p c -> p r c"), reads=[scrK], writes=[ld])
            k.op("dve", lambda e: e.tensor_tensor(out=s[:, 0, :], in0=ld[:, 0, :], in1=pr[:, :], op=ALU.add), reads=[ld, pr], writes=[s])
            k.op("dve", lambda e: e.tensor_tensor(out=s[:, 1, :], in0=ld[:, 1, :], in1=pi[:, :], op=ALU.subtract), reads=[ld, pi], writes=[s], acc=True)
        k.dma("act", scrK.h[:, f1, :, c].rearrange("r p c -> p r c"), s[:], reads=[s], writes=[scrK], acc=True)
    return mk, epi


def hy_mul_inv(k, scrK, scrG):
    def mk(cs):
        return {"cs": cs, "ld": Ring([k.sb([128, 2, 512], F32, "kld") for _ in range(3)]),
                "y": Ring([k.sb([128, 2, 512], F32, "y") for _ in range(2)]),
                "t": Ring([k.sb([128, 512], F32, "t") for _ in range(4)]),
                "g": Ring([k.sb([128, 2, 512], F32, "g") for _ in range(3)])}

    def epi(f1, ct, pr, pi, ctx):
        cs = ctx["cs"]
        c = slice(ct * 512, (ct + 1) * 512)
        ld = ctx["ld"].next()
        k.dma("sp", ld[:], scrK.h[:, f1, :, c].rearrange("r p c -> p r c"), reads=[scrK], writes=[ld])
        y = ctx["y"].next()
        ta, tb = ctx["t"].next(), ctx["t"].next()
        k.op("dve", lambda e: e.tensor_tensor(out=ta[:], in0=pr[:, :], in1=ld[:, 0, :], op=ALU.mult), reads=[pr, ld], writes=[ta])
        k.op("act", lambda e: e.copy(tb[:], pi[:, :]), reads=[pi], writes=[tb])
        k.op("dve", lambda e: e.tensor_tensor(out=y[:, 1, :], in0=pr[:, :], in1=ld[:, 1, :], op=ALU.mult), reads=[pr, ld], writes=[y])
        tc_ = ctx["t"].next()
        k.op("pool", lambda e: e.tensor_tensor(out=tc_[:], in0=tb[:], in1=ld[:, 1, :], op=ALU.mult), reads=[tb, ld], writes=[tc_])
        k.op("pool", lambda e: e.tensor_tensor(out=y[:, 0, :], in0=ta[:], in1=tc_[:], op=ALU.subtract), reads=[ta, tc_], writes=[y], acc=True)
        td = ctx["t"].next()
        k.op("dve", lambda e: e.tensor_tensor(out=td[:], in0=tb[:], in1=ld[:, 0, :], op=ALU.mult), reads=[tb, ld], writes=[td])
        k.op("dve", lambda e: e.tensor_tensor(out=y[:, 1, :], in0=y[:, 1, :], in1=td[:], op=ALU.add), reads=[y, td], writes=[y], acc=True)
        gr, gi = k.psr.next(), k.psr.next()
        k.op("pe", lambda e: e.matmul(gr[:, :], cs[:, 0, :], y[:, 0, :], start=True, stop=False), reads=[cs, y], writes=[gr], last=False)
        k.op("pe", lambda e: e.matmul(gr[:, :], cs[:, 2, :], y[:, 1, :], start=False, stop=True), reads=[cs, y], writes=[gr], acc=True)
        k.op("pe", lambda e: e.matmul(gi[:, :], cs[:, 0, :], y[:, 1, :], start=True, stop=False), reads=[cs, y], writes=[gi], last=False)
        k.op("pe", lambda e: e.matmul(gi[:, :], cs[:, 1, :], y[:, 0, :], start=False, stop=True), reads=[cs, y], writes=[gi], acc=True)
        g = ctx["g"].next()
        k.op("act", lambda e: e.copy(g[:, 0, :], gr[:, :]), reads=[gr], writes=[g])
        k.op("dve", lambda e: e.tensor_copy(g[:, 1, :], gi[:, :]), reads=[gi], writes=[g], acc=True)
        k.dma("act", scrG.h[:, :, f1, c].rearrange("r p c -> p r c"), g[:], reads=[g], writes=[scrG], acc=True)
    return mk, epi


def hy_fft_c(k, L, scrG, iv_d, hyT, uT, hcw_d, hbias_d, mixT16):
    N1, T1 = 2 * L // 128, L // 128
    TPB = 512 // T1
    with k.scope():
        iv = k.sb([N1, 128, 2, T1], F32, "iv")
        k.dma("sp", iv[:, 0:64], iv_d.h[:, 0:64], reads=[iv_d], writes=[iv])
        k.dma("act", iv[:, 64:128], iv_d.h[:, 64:128], reads=[iv_d], writes=[iv], acc=True)
        hcw = k.sb([128, 48, 4], F32, "hcw")
        k.dma("sp", hcw[:], hcw_d.h[:, :, :], reads=[hcw_d], writes=[hcw])
        hb = k.sb([128, 16], F32, "hb")
        k.dma("sp", hb[:], hbias_d.h[:, :], reads=[hbias_d], writes=[hb])
        gr = Ring([k.sb([N1, 2, 512], F32, "g") for _ in range(3)])
        yT = [k.sb([128, L], F32, "yT") for _ in range(4)]
        x0r = Ring([k.sb([128, L + 2], F32, "x0") for _ in range(1)])
        ur = Ring([k.sb([128, L], F32, "u") for _ in range(1)])
        cr = Ring([k.sb([128, L], F32, "c") for _ in range(1)])
        o16 = Ring([k.sb([128, L], BF16, "o16") for _ in range(2)])
        allps = Ring(k.allps)
        for cg in range(4):
            banks = None
            for t2 in range(128):
                if t2 % TPB == 0:
                    banks = [allps.next() for _ in range(4)]
                g = gr.next()
                k.dma("sp" if t2 % 2 == 0 else "act", g[:], scrG.h[:, t2, :, sl(cg, 512)].rearrange("r f c -> f r c"), reads=[scrG], writes=[g])
                o = (t2 % TPB) * T1
                for j in range(4):
                    k.op("pe", lambda e, j=j: e.matmul(banks[j][:, o:o + T1], g[:, 0, sl(j)], iv[:, t2, 0, :], start=True, stop=False),
                         reads=[g, iv], writes=[banks[j]], acc=True, last=False)
                    k.op("pe", lambda e, j=j: e.matmul(banks[j][:, o:o + T1], g[:, 1, sl(j)], iv[:, t2, 1, :], start=False, stop=True),
                         reads=[g, iv], writes=[banks[j]], acc=True, last=True)
                if t2 % TPB == TPB - 1:
                    t2a = t2 - (TPB - 1)
                    for j in range(4):
                        dst = yT[j].h[:, :].rearrange("p (a b) -> p b a", b=128)[:, t2a:t2a + TPB, :]
                        srcp = banks[j].h[:, 0:TPB * T1].rearrange("p (b a) -> p b a", a=T1)
                        if j % 2 == 0:
                            k.op("act", lambda e: e.copy(dst, srcp), reads=[banks[j]], writes=[yT[j]], acc=True)
                        else:
                            k.op("dve", lambda e: e.tensor_copy(dst, srcp), reads=[banks[j]], writes=[yT[j]], acc=True)
            for j in range(4):
                cc = cg * 4 + j
                u = ur.next()
                k.dma("sp", u[:], uT.h[cc], reads=[uT], writes=[u])
                t = x0r.next()
                k.op("pool", lambda e: e.memset(t[:, 0:L + 2:L + 1], 0.0), writes=[t])
                k.dma("act", t[:, 1:L + 1], hyT.h[cc], reads=[hyT], writes=[t], acc=True)
                c = cr.next()
                hy_conv(k, c, t, hcw, cc, L)
                k.op("dve", lambda e: e.scalar_tensor_tensor(out=u[:], in0=u[:], scalar=hb[:, cc:cc + 1], in1=yT[j][:], op0=ALU.mult, op1=ALU.add),
                     reads=[u, hb, yT[j]], writes=[u])
                o = o16.next()
                k.op("pool", lambda e: e.tensor_tensor(out=o[:], in0=u[:], in1=c[:], op=ALU.mult), reads=[u, c], writes=[o])
                k.dma("sp", mixT16.h[16 + cc], o[:], reads=[o], writes=[mixT16], acc=True)


def phase_hyena(k, L, hyT, P, S, mixT16):
    hy_filters(k, L, P["hy_zT"], P["hy_w1a"], P["hy_w2"], P["hy_b2"], P["hy_w3"], P["hy_absd"], P["hy_negt"], S["kf_tok"], S["kb_tok"])
    hy_prep_u(k, L, hyT, P["hy_cw"], P["ident32"], S["uT"], S["u_tok"])
    hy_fft_a(k, L, S["kf_tok"], P["hy_fw"], S["scrA"])
    mk, epi = hy_spectrum_store(k, S["scrK"], False)
    hy_fft_b(k, L, S["scrA"], P["hy_cs"], epi, mk)
    hy_fft_a(k, L, S["kb_tok"], P["hy_fw"], S["scrA"])
    mk, epi = hy_spectrum_store(k, S["scrK"], True)
    hy_fft_b(k, L, S["scrA"], P["hy_cs"], epi, mk)
    hy_fft_a(k, L, S["u_tok"], P["hy_fw"], S["scrA"])
    mk, epi = hy_mul_inv(k, S["scrK"], S["scrG"])
    hy_fft_b(k, L, S["scrA"], P["hy_cs"], epi, mk)
    hy_fft_c(k, L, S["scrG"], P["hy_iv"], hyT, S["uT"], P["hy_cw"], P["hy_bias"], mixT16)


def hy_scratch(k, L, kind="Internal"):
    N1 = 2 * L // 128
    return {"kf_tok": k.dram("kf_tok", [L, 2048], F32, kind=kind), "kb_tok": k.dram("kb_tok", [L, 2048], F32, kind=kind),
            "uT": k.dram("uT", [16, 128, L], F32, kind=kind), "u_tok": k.dram("u_tok", [L, 2048], F32, kind=kind),
            "scrA": k.dram("scrA", [2, N1, 128, 2048], F32, kind=kind), "scrK": k.dram("scrK", [2, N1, 128, 2048], F32, kind=kind),
            "scrG": k.dram("scrG", [2, 128, N1, 2048], F32, kind=kind)}


def phase_odd_inproj(k, xT16, L, w_fm, w_dt, w_tm, xbcT, dtT, ginT, rinT, z_tok):
    nb = L // TB
    with k.scope():
        X = [k.sb([128, 8, TB + 2], BF16, "X") for _ in range(4)]
        wr_fm = Ring([k.sb([128, KC, 128], BF16, "wfm") for _ in range(3)])
        wr_tm = Ring([k.sb([128, KC, 512], BF16, "wtm") for _ in range(2)])
        st32 = Ring([k.sb([128, TB], F32, "st32") for _ in range(4)])
        for b in range(nb):
            load_X(k, X, xT16, b, L)
            cnt = [0]
            for c in range(56):
                if c < 24:
                    dst_t, dst = xbcT, xbcT.h[c, :, sl(b, TB)]
                elif c < 40:
                    dst_t, dst = ginT, ginT.h[c - 24, :, sl(b, TB)]
                else:
                    dst_t, dst = rinT, rinT.h[c - 40, :, sl(b, TB)]

                def epi(pst, dst=dst, dst_t=dst_t):
                    evac_to_dram(k, pst, 128, TB, st32, dst, dst_t, cnt[0])
                    cnt[0] += 1
                dense_fm(k, X, wr_fm, w_fm.h[c], 128, epi, [w_fm])

            def epi_dt(pst):
                evac_to_dram(k, pst, 64, TB, st32, dtT.h[:, sl(b, TB)], dtT, 0)
            dense_fm(k, X, wr_fm, w_dt.h, 64, epi_dt, [w_dt])
            for n in range(4):
                def epi(pst, tt, n=n):
                    t0 = b * TB + tt * 128
                    evac_to_dram(k, pst, 128, 512, st32, z_tok.h[t0:t0 + 128, n * 512:(n + 1) * 512], z_tok, cnt[0])
                    cnt[0] += 1
                dense_tm(k, X, wr_tm, w_tm.h[n], epi, [w_tm])


def conv4(k, c, t, cw, ch, L):
    k.op("dve", lambda e: e.tensor_scalar(out=c[:, 0:L], in0=t[:, 0:L], scalar1=cw[:, ch, 0:1], scalar2=cw[:, ch, 4:5], op0=ALU.mult, op1=ALU.add),
         reads=[t, cw], writes=[c])
    for j in (1, 2, 3):
        k.op("dve", lambda e, j=j: e.scalar_tensor_tensor(out=c[:, 0:L], in0=t[:, j:L + j], scalar=cw[:, ch, j:j + 1], in1=c[:, 0:L],
                                                          op0=ALU.mult, op1=ALU.add), reads=[t, cw, c], writes=[c])


def phase_lru(k, L, rinT, ginT, lcw_d, wax_d, lb_d, lam_d, mixT16):
    NT = L // 512
    with k.scope():
        lcw = k.sb([128, 16, 5], F32, "lcw")
        k.dma("sp", lcw[:], lcw_d.h[:, :, :], reads=[lcw_d], writes=[lcw])
        lb = k.sb([128, 2, 2, 16], F32, "lb")
        k.dma("sp", lb[:], lb_d.h[:, :, :, :], reads=[lb_d], writes=[lb])
        lam = k.sb([128, 32], F32, "lam")
        k.dma("sp", lam[:], lam_d.h.rearrange("p a b -> p (a b)"), reads=[lam_d], writes=[lam])
        nc8 = k.sb([128, 32], F32, "nc8")
        k.op("act", lambda e: e.activation(out=nc8[:], in_=lam[:], func=AF.Exp, scale=-1.0), reads=[lam], writes=[nc8])
        k.op("act", lambda e: e.activation(out=nc8[:], in_=nc8[:], func=AF.Ln, bias=1.0, scale=1.0), reads=[nc8], writes=[nc8])
        k.op("dve", lambda e: e.tensor_scalar(out=nc8[:], in0=nc8[:], scalar1=-8.0, scalar2=None, op0=ALU.mult), reads=[nc8], writes=[nc8])
        wr = Ring([k.sb([128, 2, 128], F32, "w") for _ in range(4)])
        padr = Ring([k.sb([128, L + 3], F32, "pad") for _ in range(2)])
        xcr = Ring([k.sb([128, L], F32, "xc") for _ in range(2)])
        hr = Ring([k.sb([128, L], F32, "h") for _ in range(3)])
        gr = Ring([k.sb([128, L], F32, "g") for _ in range(2)])
        o16 = Ring([k.sb([128, L], BF16, "o16") for _ in range(2)])
        G = Ring([k.sb([128, 512], F32, "G") for _ in range(12)])
        for cc in range(16):
            t = padr.next()
            k.op("pool", lambda e: e.memset(t[:, 0:1], 0.0), writes=[t])
            k.op("pool", lambda e: e.memset(t[:, L + 1:L + 3], 0.0), writes=[t], acc=True)
            k.dma("sp", t[:, 1:L + 1], rinT.h[cc], reads=[rinT], writes=[t], acc=True)
            xc = xcr.next()
            conv4(k, xc, t, lcw, cc, L)
            hs = []
            for d in range(2):
                w = wr.next()
                k.dma("act", w[:], wax_d.h[d, :, cc].rearrange("a p c -> p a c"), reads=[wax_d], writes=[w])
                h = hr.next()
                tiles = range(NT) if d == 0 else range(NT - 1, -1, -1)
                prev = None
                for ti in tiles:
                    c = slice(ti * 512, (ti + 1) * 512)
                    pr, pi = k.psr.next(), k.psr.next()
                    k.op("pe", lambda e: e.matmul(pr[:, :], w[:, 0, :], xc[:, c], start=True, stop=True), reads=[w, xc], writes=[pr])
                    k.op("pe", lambda e: e.matmul(pi[:, :], w[:, 1, :], xc[:, c], start=True, stop=True), reads=[w, xc], writes=[pi])
                    r, ig = G.next(), G.next()
                    k.op("act", lambda e: e.activation(out=r[:], in_=pr[:, :], func=AF.Sigmoid, bias=lb[:, d, 0, cc:cc + 1], scale=1.0), reads=[pr, lb], writes=[r])
                    k.op("act", lambda e: e.activation(out=ig[:], in_=pi[:, :], func=AF.Sigmoid, bias=lb[:, d, 1, cc:cc + 1], scale=1.0), reads=[pi, lb], writes=[ig])
                    a = G.next()
                    k.op("act", lambda e: e.activation(out=a[:], in_=r[:], func=AF.Exp, scale=nc8[:, d * 16 + cc:d * 16 + cc + 1]), reads=[r, nc8], writes=[a])
                    om = G.next()
                    k.op("dve", lambda e: e.tensor_tensor(out=om[:], in0=a[:], in1=a[:], op=ALU.mult), reads=[a], writes=[om])
                    k.op("dve", lambda e: e.tensor_scalar(out=om[:], in0=om[:], scalar1=-1.0, scalar2=1.0, op0=ALU.mult, op1=ALU.add), reads=[om], writes=[om])
                    k.op("act", lambda e: e.activation(out=om[:], in_=om[:], func=AF.Sqrt), reads=[om], writes=[om])
                    k.op("pool", lambda e: e.tensor_tensor(out=ig[:], in0=ig[:], in1=xc[:, c], op=ALU.mult), reads=[ig, xc], writes=[ig])
                    k.op("pool", lambda e: e.tensor_tensor(out=om[:], in0=om[:], in1=ig[:], op=ALU.mult), reads=[om, ig], writes=[om])
                    if d == 0:
                        init = 0.0 if prev is None else h[:, prev * 512 + 511:prev * 512 + 512]
                        k.op("dve", lambda e: e.tensor_tensor_scan(out=h[:, c], data0=a[:], data1=om[:], initial=init, op0=ALU.mult, op1=ALU.add),
                             reads=[a, om, h], writes=[h], acc=(prev is not None))
                    else:
                        init = 0.0 if prev is None else h[:, prev * 512:prev * 512 + 1]
                        lo = ti * 512
                        rs = slice(lo + 511, lo - 1 if lo > 0 else None, -1)
                        k.op("dve", lambda e: e.tensor_tensor_scan(out=h[:, rs], data0=a[:, ::-1], data1=om[:, ::-1], initial=init, op0=ALU.mult, op1=ALU.add),
                             reads=[a, om, h], writes=[h], acc=(prev is not None))
                    prev = ti
                hs.append(h)
            g = gr.next()
            k.dma("sp", g[:], ginT.h[cc], reads=[ginT], writes=[g])
            u = hr.next()
            k.op("pool", lambda e: e.tensor_tensor(out=u[:], in0=g[:], in1=g[:], op=ALU.mult), reads=[g], writes=[u])
            k.op("dve", lambda e: e.tensor_scalar(out=u[:], in0=u[:], scalar1=0.044715, scalar2=1.0, op0=ALU.mult, op1=ALU.add), reads=[u], writes=[u])
            k.op("dve", lambda e: e.tensor_tensor(out=u[:], in0=u[:], in1=g[:], op=ALU.mult), reads=[u, g], writes=[u])
            k.op("act", lambda e: e.activation(out=u[:], in_=u[:], func=AF.Sigmoid, scale=1.5957691216057308), reads=[u], writes=[u])
            k.op("pool", lambda e: e.tensor_tensor(out=u[:], in0=u[:], in1=g[:], op=ALU.mult), reads=[u, g], writes=[u])
            k.op("dve", lambda e: e.tensor_tensor(out=hs[0][:], in0=hs[0][:], in1=hs[1][:], op=ALU.add), reads=[hs[0], hs[1]], writes=[hs[0]])
            o = o16.next()
            k.op("dve", lambda e: e.tensor_tensor(out=o[:], in0=hs[0][:], in1=u[:], op=ALU.mult), reads=[hs[0], u], writes=[o])
            k.dma("act", mixT16.h[16 + cc], o[:], reads=[o], writes=[mixT16], acc=True)


def ssd_consts(L):
    c = {}
    sel = np.zeros((64, 64, 128), np.float32)
    for h in range(64):
        sel[h, h, :] = 1.0
    c["ssd_sel"] = sel
    c["ssd_ones64"] = np.ones((64, 128), np.float32)
    t = np.arange(L)
    m = np.ones((64, L), np.float32)
    m[0:32, t % 128 == 0] = 0.0
    m[32:64, t % 128 == 127] = 0.0
    c["ssd_smask"] = m
    return c


def phase_ssd_prep(k, L, xbcT, scw_d, ident32_d, xs_tok, bm_tok, bmT16, cmT16):
    with k.scope():
        scw = k.sb([128, 24, 5], F32, "scw")
        k.dma("sp", scw[:], scw_d.h[:, :, :], reads=[scw_d], writes=[scw])
        ident32 = k.sb([128, 128], F32, "ident")
        k.dma("sp", ident32[:], ident32_d.h[:, :], reads=[ident32_d], writes=[ident32])
        padr = Ring([k.sb([128, L + 3], F32, "pad") for _ in range(2)])
        cr = Ring([k.sb([128, L], F32, "c") for _ in range(2)])
        c16 = Ring([k.sb([128, L], BF16, "c16") for _ in range(2)])
        st = Ring([k.sb([128, 4, 128], F32, "st") for _ in range(3)])
        for ch in range(24):
            t = padr.next()
            k.op("pool", lambda e: e.memset(t[:, 0:1], 0.0), writes=[t])
            k.op("pool", lambda e: e.memset(t[:, L + 1:L + 3], 0.0), writes=[t], acc=True)
            k.dma("sp" if ch % 2 == 0 else "act", t[:, 1:L + 1], xbcT.h[ch], reads=[xbcT], writes=[t], acc=True)
            c = cr.next()
            conv4(k, c, t, scw, ch, L)
            k.op("act", lambda e: e.activation(out=c[:], in_=c[:], func=AF.Silu), reads=[c], writes=[c])
            if ch >= 16:
                s16 = c16.next()
                k.op("pool", lambda e: e.tensor_copy(s16[:], c[:]), reads=[c], writes=[s16])
                dst = bmT16 if ch < 20 else cmT16
                k.dma("act", dst.h[(ch - 16) % 4], s16[:], reads=[s16], writes=[dst], acc=True)
            if ch < 20:
                dst, col = (xs_tok, ch * 128) if ch < 16 else (bm_tok, (ch - 16) * 128)
                for g in range(L // 512):
                    p = k.psr.next()
                    for j in range(4):
                        k.op("pe", lambda e, j=j: e.transpose(p[:, sl(j)], c[:, g * 512 + j * 128:g * 512 + (j + 1) * 128], ident32[:]),
                             reads=[c, ident32], writes=[p], acc=(j > 0), last=(j == 3))
                    s = st.next()
                    if g % 2 == 0:
                        k.op("act", lambda e: e.copy(s.h[:].rearrange("p a c -> p (a c)"), p[:, :]), reads=[p], writes=[s])
                    else:
                        k.op("dve", lambda e: e.tensor_copy(s.h[:].rearrange("p a c -> p (a c)"), p[:, :]), reads=[p], writes=[s])
                    k.dma("act" if g % 2 == 0 else "sp", dst.h[g * 512:(g + 1) * 512, col:col + 128].rearrange("(a p) c -> p a c", p=128), s[:],
                          reads=[s], writes=[dst], acc=True)


def phase_ssd(k, L, dtT, z_tok, xs_tok, bm_tok, bmT16, cmT16, yf_tok, P, mixT16):
    NCH = L // 128
    with k.scope():
        def ld(name, shape, q="sp", src=None):
            t = k.sb(shape, F32, name)
            k.dma(q, t[:], (P[name].h if src is None else src), reads=[P[name]], writes=[t])
            return t
        sel = ld("ssd_sel", [64, 64, 128])
        ones64 = ld("ssd_ones64", [64, 128])
        smask = ld("ssd_smask", [64, L], "act")
        cst = ld("gla_cst", [128, 4, 128], "act", P["gla_cst"].h.rearrange("a p c -> p a c"))
        ident32 = ld("ident32", [128, 128])
        dtb = ld("ssd_dtb", [64, 1])
        alog = ld("ssd_alog", [64, 1])
        ng = ld("ssd_ng", [128, 2048], "act")
        dsk0 = ld("ssd_dsk", [128, 2048], "sp", P["ssd_dsk"].h[0])
        dsk1 = ld("ssd_dsk", [128, 2048], "act", P["ssd_dsk"].h[1])
        k.op("pool", lambda e: e.tensor_tensor(out=dsk0[:], in0=dsk0[:], in1=dsk1[:], op=ALU.add), reads=[dsk0, dsk1], writes=[dsk0])
        epsr = k.sb([128, 1], F32, "epsr")
        k.op("dve", lambda e: e.memset(epsr[:], RMS_EPS), writes=[epsr])
        dt = k.sb([64, L], F32, "dt")
        k.dma("sp", dt[:], dtT.h[:, :], reads=[dtT], writes=[dt])
        k.op("act", lambda e: e.activation(out=dt[:], in_=dt[:], func=AF.Exp, bias=dtb[:, 0:1], scale=1.0), reads=[dt, dtb], writes=[dt])
        k.op("act", lambda e: e.activation(out=dt[:], in_=dt[:], func=AF.Ln, bias=1.0, scale=1.0), reads=[dt], writes=[dt])
        negA = k.sb([64, 1], F32, "negA")
        k.op("act", lambda e: e.activation(out=negA[:], in_=alog[:], func=AF.Exp), reads=[alog], writes=[negA])
        k.op("dve", lambda e: e.tensor_scalar(out=negA[:], in0=negA[:], scalar1=-1.0, scalar2=None, op0=ALU.mult), reads=[negA], writes=[negA])
        acT = k.sb([64, L], F32, "acT")
        nacT = k.sb([64, L], F32, "nacT")
        k.op("dve", lambda e: e.tensor_scalar(out=nacT[:], in0=dt[:], scalar1=negA[:, 0:1], scalar2=None, op0=ALU.mult), reads=[dt, negA], writes=[nacT])
        k.op("dve", lambda e: e.tensor_tensor_scan(out=acT[0:32, :], data0=smask[0:32, :], data1=nacT[0:32, :], initial=0.0, op0=ALU.mult, op1=ALU.add),
             reads=[smask, nacT], writes=[acT])
        k.op("dve", lambda e: e.tensor_tensor_scan(out=acT[32:64, ::-1], data0=smask[32:64, ::-1], data1=nacT[32:64, ::-1], initial=0.0,
                                                   op0=ALU.mult, op1=ALU.add), reads=[smask, nacT], writes=[acT], acc=True)
        k.op("dve", lambda e: e.tensor_scalar(out=nacT[:], in0=acT[:], scalar1=-1.0, scalar2=None, op0=ALU.mult), reads=[acT], writes=[nacT])
        ac_tok = k.sb([128, NCH, 64], F32, "ac_tok")
        dt_tok = k.sb([128, NCH, 64], F32, "dt_tok")
        eac = k.sb([128, NCH, 64], F32, "eac")
        for src, dst in ((acT, ac_tok), (dt, dt_tok)):
            for n0 in range(0, NCH, 4):
                p = k.psr.next()
                nn = min(4, NCH - n0)
                for j in range(nn):
                    k.op("pe", lambda e, j=j: e.transpose(p[:, j * 64:(j + 1) * 64], src[:, sl(n0 + j)], ident32[0:64, 0:64]),
                         reads=[src, ident32], writes=[p], acc=(j > 0), last=(j == nn - 1))
                k.op("dve", lambda e: e.tensor_copy(dst.h[:, n0:n0 + nn, :].rearrange("p a c -> p (a c)"), p[:, 0:nn * 64]), reads=[p], writes=[dst], acc=(n0 > 0))
        k.op("act", lambda e: e.activation(out=eac.h[:].rearrange("p a c -> p (a c)"), in_=ac_tok.h[:].rearrange("p a c -> p (a c)"), func=AF.Exp),
             reads=[ac_tok], writes=[eac])
        S32 = k.sb([128, 512], F32, "S32")
        S16 = k.sb([128, 512], BF16, "S16")
        xsr = Ring([k.sb([128, 512], F32, "xs") for _ in range(2)])
        bmr = Ring([k.sb([128, 128], F32, "bm") for _ in range(2)])
        bm16r = Ring([k.sb([128, 128], BF16, "bm16") for _ in range(2)])
        bTr = Ring([k.sb([128, 128], BF16, "bT") for _ in range(2)])
        cTr = Ring([k.sb([128, 128], BF16, "cT") for _ in range(2)])
        cbr = Ring([k.sb([128, 128], F32, "cbm") for _ in range(2)])
        Dr = Ring([k.sb([128, 512], F32, "Dm") for _ in range(2)])
        lmr = Ring([k.sb([128, 4, 128], F32, "lm") for _ in range(2)])
        Mr = Ring([k.sb([128, 4, 128], BF16, "M") for _ in range(2)])
        xdtr = Ring([k.sb([128, 512], BF16, "xdt") for _ in range(2)])
        xddr = Ring([k.sb([128, 512], BF16, "xdd") for _ in range(2)])
        O = Ring([k.sb([128, 512], F32, "O") for _ in range(6)])
        sm = Ring([k.sb([128, 8], F32, "sm") for _ in range(6)])
        Xr = Ring([k.sb([64, 8], F32, "X") for _ in range(2)])
        tr16 = Ring([k.sb([128, 4, 128], BF16, "tr") for _ in range(2)])
        zr = Ring([k.sb([128, 512], F32, "z") for _ in range(2)])
        yfr = Ring([k.sb([128, 512], F32, "yf") for _ in range(2)])
        for g in range(4):
            gc = slice(g * 512, (g + 1) * 512)
            for d in range(2):
                row0 = d * 32 + g * 8
                mask = cst[:, d, :]
                lastc = 127 if d == 0 else 0
                k.op("dve", lambda e: e.memset(S32[:], 0.0), writes=[S32])
                k.op("pool", lambda e: e.memset(S16[:], 0.0), writes=[S16])
                order = range(NCH) if d == 0 else range(NCH - 1, -1, -1)
                for n in order:
                    ts = slice(n * 128, (n + 1) * 128)
                    xs, bmt, bT, cT = xsr.next(), bmr.next(), bTr.next(), cTr.next()
                    k.dma("sp", xs[:], xs_tok.h[ts, gc], reads=[xs_tok], writes=[xs])
                    k.dma("act", bmt[:], bm_tok.h[ts, sl(g)], reads=[bm_tok], writes=[bmt])
                    k.dma("sp", bT[:], bmT16.h[g, :, ts], reads=[bmT16], writes=[bT])
                    k.dma("act", cT[:], cmT16.h[g, :, ts], reads=[cmT16], writes=[cT])
                    bm16 = bm16r.next()
                    k.op("pool", lambda e: e.tensor_copy(bm16[:], bmt[:]), reads=[bmt], writes=[bm16])
                    p = k.psr.next()
                    k.op("pe", lambda e: e.matmul(p[:, 0:128], bT[:], cT[:], start=True, stop=True), reads=[bT, cT], writes=[p])
                    cbm = cbr.next()
                    k.op("dve", lambda e: e.tensor_tensor(out=cbm[:], in0=p[:, 0:128], in1=mask, op=ALU.mult), reads=[p, cst], writes=[cbm])
                    lms, Ms = [], []
                    for half in range(2):
                        pD = k.psr.next()
                        for q in range(4):
                            row = row0 + half * 4 + q
                            k.op("pe", lambda e, q=q, row=row: e.matmul(pD[:, sl(q)], sel[:, row, :], acT[:, ts], start=True, stop=False),
                                 reads=[sel, acT], writes=[pD], acc=True, last=False)
                            k.op("pe", lambda e, q=q, row=row: e.matmul(pD[:, sl(q)], nacT[:, ts], sel[:, row, :], start=False, stop=True),
                                 reads=[sel, nacT], writes=[pD], acc=True, last=(q == 3))
                        Dm = Dr.next()
                        k.op("dve", lambda e: e.tensor_scalar(out=Dm[:], in0=pD[:, :], scalar1=0.0, scalar2=None, op0=ALU.min), reads=[pD], writes=[Dm])
                        lm = lmr.next()
                        k.op("act", lambda e: e.activation(out=lm.h[:].rearrange("p a c -> p (a c)"), in_=Dm[:], func=AF.Exp), reads=[Dm], writes=[lm])
                        M = Mr.next()
                        for q in range(4):
                            k.op("pool" if q % 2 else "dve", lambda e, q=q: e.tensor_tensor(out=M[:, q, :], in0=lm[:, q, :], in1=cbm[:], op=ALU.mult),
                                 reads=[lm, cbm], writes=[M], acc=(q > 0))
                        lms.append(lm)
                        Ms.append(M)
                    xdt, xdd = xdtr.next(), xddr.next()
                    for h in range(8):
                        hc = slice(h * 64, (h + 1) * 64)
                        dts = dt_tok[:, n, row0 + h:row0 + h + 1]
                        k.op("pool", lambda e, hc=hc, dts=dts: e.tensor_scalar(out=xdt[:, hc], in0=xs[:, hc], scalar1=dts, scalar2=1.0, op0=ALU.mult, op1=ALU.mult),
                             reads=[xs, dt_tok], writes=[xdt], acc=(h > 0))
                        dss = lms[h // 4][:, h % 4, lastc:lastc + 1]
                        k.op("dve", lambda e, hc=hc, dts=dts, dss=dss: e.tensor_scalar(out=xdd[:, hc], in0=xs[:, hc], scalar1=dts, scalar2=dss, op0=ALU.mult, op1=ALU.mult),
                             reads=[xs, dt_tok, lms[h // 4]], writes=[xdd], acc=(h > 0))
                    pY = k.psr.next()
                    for h in range(8):
                        k.op("pe", lambda e, h=h: e.matmul(pY[:, h * 64:(h + 1) * 64], Ms[h // 4][:, h % 4, :], xdt[:, h * 64:(h + 1) * 64], start=True, stop=True),
                             reads=[Ms[h // 4], xdt], writes=[pY], acc=True, last=(h == 7))
                    pO = k.psr.next()
                    k.op("pe", lambda e: e.matmul(pO[:, :], cT[:], S16[:], start=True, stop=True), reads=[cT, S16], writes=[pO])
                    pS = k.psr.next()
                    k.op("pe", lambda e: e.matmul(pS[:, :], bm16[:], xdd[:], start=True, stop=True), reads=[bm16, xdd], writes=[pS])
                    tl = n * 128 + lastc
                    X = Xr.next()
                    k.op("pool", lambda e: e.tensor_scalar(out=X[:], in0=sel[:, row0:row0 + 8, 0], scalar1=acT[:, tl:tl + 1], scalar2=1.0, op0=ALU.mult, op1=ALU.mult),
                         reads=[sel, acT], writes=[X])
                    pC = k.psr.next()
                    k.op("pe", lambda e: e.matmul(pC[:, 0:8], ones64[:], X[:], start=True, stop=True), reads=[ones64, X], writes=[pC])
                    cd = sm.next()
                    k.op("act", lambda e: e.activation(out=cd[:], in_=pC[:, 0:8], func=AF.Exp), reads=[pC], writes=[cd])
                    yd = O.next()
                    k.op("act", lambda e: e.copy(yd[:], pY[:, :]), reads=[pY], writes=[yd])
                    y = O.next()
                    for h in range(8):
                        hc = slice(h * 64, (h + 1) * 64)
                        k.op("dve", lambda e, hc=hc, h=h: e.scalar_tensor_tensor(out=y[:, hc], in0=pO[:, hc], scalar=eac[:, n, row0 + h:row0 + h + 1], in1=yd[:, hc],
                                                                              op0=ALU.mult, op1=ALU.add), reads=[pO, eac, yd], writes=[y], acc=(h > 0))
                    for h in range(8):
                        hc = slice(h * 64, (h + 1) * 64)
                        k.op("dve", lambda e, hc=hc, h=h: e.scalar_tensor_tensor(out=S32[:, hc], in0=S32[:, hc], scalar=cd[:, h:h + 1], in1=pS[:, hc],
                                                                              op0=ALU.mult, op1=ALU.add), reads=[S32, cd, pS], writes=[S32], acc=(h > 0))
                    k.op("act", lambda e: e.copy(S16[:], S32[:]), reads=[S32], writes=[S16])
                    if d == 0:
                        k.dma("sp", yf_tok.h[ts, gc], y[:], reads=[y], writes=[yf_tok], acc=True)
                    else:
                        yf, zt = yfr.next(), zr.next()
                        k.dma("sp", yf[:], yf_tok.h[ts, gc], reads=[yf_tok], writes=[yf])
                        k.dma("act", zt[:], z_tok.h[ts, gc], reads=[z_tok], writes=[zt])
                        k.op("pool", lambda e: e.tensor_tensor(out=y[:], in0=y[:], in1=yf[:], op=ALU.add), reads=[y, yf], writes=[y])
                        t2 = O.next()
                        k.op("pool", lambda e: e.tensor_tensor(out=t2[:], in0=xs[:], in1=dsk0[:, gc], op=ALU.mult), reads=[xs, dsk0], writes=[t2])
                        k.op("dve", lambda e: e.tensor_tensor(out=y[:], in0=y[:], in1=t2[:], op=ALU.add), reads=[y, t2], writes=[y])
                        sz = O.next()
                        k.op("act", lambda e: e.activation(out=sz[:], in_=zt[:], func=AF.Silu), reads=[zt], writes=[sz])
                        k.op("pool", lambda e: e.tensor_tensor(out=y[:], in0=y[:], in1=sz[:], op=ALU.mult), reads=[y, sz], writes=[y])
                        ss = sm.next()
                        k.op("act", lambda e: e.activation(out=sz[:], in_=y[:], func=AF.Square, accum_out=ss[:, 0:1]), reads=[y], writes=[sz, ss])
                        k.op("act", lambda e: e.activation(out=ss[:, 0:1], in_=ss[:, 0:1], func=AF.Sqrt, bias=epsr[:, 0:1], scale=1.0 / 512), reads=[ss, epsr], writes=[ss])
                        k.op("dve", lambda e: e.reciprocal(out=ss[:, 0:1], in_=ss[:, 0:1]), reads=[ss], writes=[ss])
                        on = O.next()
                        k.op("dve", lambda e: e.scalar_tensor_tensor(out=on[:], in0=y[:], scalar=ss[:, 0:1], in1=ng[:, gc], op0=ALU.mult, op1=ALU.mult),
                             reads=[y, ss, ng], writes=[on])
                        p7 = k.psr.next()
                        for j in range(4):
                            k.op("pe", lambda e, j=j: e.transpose(p7[:, sl(j)], on[:, sl(j)], ident32[:]), reads=[on, ident32], writes=[p7], acc=(j > 0), last=(j == 3))
                        t16 = tr16.next()
                        k.op("dve", lambda e: e.tensor_copy(t16.h[:].rearrange("p j c -> p (j c)"), p7[:, :]), reads=[p7], writes=[t16])
                        k.dma("sp", mixT16.h[4 * g:4 * g + 4, :, ts].rearrange("j p c -> p j c"), t16[:], reads=[t16], writes=[mixT16], acc=True)


L_FULL = 4096


def host_params(inp, L=None):
    L_FULL = L or 4096
    f = lambda a: np.ascontiguousarray(np.asarray(a, dtype=np.float32))
    P = {}
    P["ident32"] = np.eye(128, dtype=np.float32)
    P["ones32"] = np.ones((128, 128), np.float32)
    P["gla_cst"] = gla_consts()
    P.update(hy_consts(L_FULL))
    P.update(ssd_consts(L_FULL))
    W = np.asarray(inp["ev_w_in"][0], np.float32)
    Wq, Wk, Wv, Wog, Wlr, Why = W[:, :1024], W[:, 1024:2048], W[:, 2048:4096], W[:, 4096:6144], W[:, 6144:6176], W[:, 6176:]
    P["ev_fm"] = tile_fm(np.concatenate([Wq, Wk, Why], 1))
    P["ev_lr"] = f(Wlr.reshape(KC, 128, 32).transpose(1, 0, 2))
    P["ev_tm"] = tile_tm(np.concatenate([Wk, Wv, Wog], 1))
    P["gla_wg"] = f(np.stack([np.concatenate([inp["ev_gla_wg_f"][0], inp["ev_gla_bg_f"][0][None]], 0),
                              np.concatenate([inp["ev_gla_wg_b"][0], inp["ev_gla_bg_b"][0][None]], 0)]))
    P["gla_gn"] = f(np.tile(np.asarray(inp["ev_gla_norm"][0])[None], (128, 1)))
    P["hy_w1a"] = f(np.concatenate([inp["ev_hy_w1"][0], inp["ev_hy_b1"][0][None]], 0))
    P["hy_w2"] = f(inp["ev_hy_w2"][0])
    P["hy_b2"] = f(np.asarray(inp["ev_hy_b2"][0])[:, None])
    P["hy_w3"] = f(inp["ev_hy_w3"][0])
    hcw = np.zeros((128, 48, 4), np.float32)
    hcw[:, :, 0:3] = np.asarray(inp["ev_hy_conv_w"][0]).T.reshape(48, 128, 3).transpose(1, 0, 2)
    hcw[:, :, 3] = np.asarray(inp["ev_hy_conv_b"][0]).reshape(48, 128).T
    P["hy_cw"] = hcw
    P["hy_bias"] = f(np.asarray(inp["ev_hy_bias"][0]).reshape(16, 128).T)
    P["ev_w_out"] = tile_fm(np.asarray(inp["ev_w_out"][0], np.float32))
    W = np.asarray(inp["od_w_in"][0], np.float32)
    P["od_fm"] = tile_fm(np.concatenate([W[:, 2048:5120], W[:, 5184:7232], W[:, 7232:9280]], 1))
    P["od_dt"] = f(W[:, 5120:5184].reshape(KC, 128, 64).transpose(1, 0, 2))
    P["od_tm"] = tile_tm(W[:, 0:2048])
    scw = np.zeros((128, 24, 5), np.float32)
    scw[:, :, 0:4] = np.asarray(inp["od_ssd_conv_w"][0]).T.reshape(24, 128, 4).transpose(1, 0, 2)
    scw[:, :, 4] = np.asarray(inp["od_ssd_conv_b"][0]).reshape(24, 128).T
    P["ssd_scw"] = scw
    P["ssd_dtb"] = f(np.asarray(inp["od_ssd_dt_bias"][0]).reshape(64, 1))
    P["ssd_alog"] = f(np.asarray(inp["od_ssd_a_log"][0]).reshape(64, 1))
    P["ssd_dsk"] = f(np.tile(np.repeat(np.asarray(inp["od_ssd_d"][0]), 64, axis=1)[:, None, :], (1, 128, 1)))
    P["ssd_ng"] = f(np.tile(np.asarray(inp["od_ssd_norm"][0])[None], (128, 1)))
    lcw = np.zeros((128, 16, 5), np.float32)
    lcw[:, :, 0:4] = np.asarray(inp["od_lru_conv_w"][0]).T.reshape(16, 128, 4).transpose(1, 0, 2)
    lcw[:, :, 4] = np.asarray(inp["od_lru_conv_b"][0]).reshape(16, 128).T
    P["lru_cw"] = lcw
    P["lru_wax"] = f(np.stack([inp["od_lru_wa"][0], inp["od_lru_wx"][0]], 1))
    P["lru_b"] = f(np.stack([np.asarray(inp["od_lru_ba"][0]).reshape(2, 16, 128), np.asarray(inp["od_lru_bx"][0]).reshape(2, 16, 128)], 1).transpose(3, 0, 1, 2))
    P["lru_lam"] = f(np.asarray(inp["od_lru_lambda"][0]).reshape(2, 16, 128).transpose(2, 0, 1))
    P["od_w_out"] = tile_fm(np.asarray(inp["od_w_out"][0], np.float32))
    cw = np.zeros((128, 2, 2 * FC, 4), np.float32)
    for i in range(2):
        P["w_up%d" % i] = tile_fm(np.asarray(inp["ffn_w_up"][i], np.float32))
        P["w_dn%d" % i] = f(np.asarray(inp["ffn_w_down"][i], np.float32).reshape(FC, 128, KC, 128).transpose(2, 1, 0, 3).reshape(KC, 128, FC * 128))
        cw[:, i, :, 0:3] = np.asarray(inp["ffn_conv_w"][i]).T.reshape(2 * FC, 128, 3).transpose(1, 0, 2)
        cw[:, i, :, 3] = np.asarray(inp["ffn_conv_b"][i]).reshape(2 * FC, 128).T
    P["ffn_cw"] = cw
    g = np.stack([inp["ln1_g"][0], inp["ln2_g"][0], inp["ln1_g"][1], inp["ln2_g"][1]])
    b = np.stack([inp["ln1_b"][0], inp["ln2_b"][0], inp["ln1_b"][1], inp["ln2_b"][1]])
    P["gtab"] = f(np.asarray(g).reshape(4, KC, 128).transpose(2, 0, 1).reshape(128, 4 * KC))
    P["btab"] = f(np.asarray(b).reshape(4, KC, 128).transpose(2, 0, 1).reshape(128, 4 * KC))
    return {n: f(a) for n, a in P.items()}


def build_program(L, pshapes):
    nc = bass.Bass("TRN2", target_bir_lowering=False)
    with contextlib.ExitStack() as es:
        k = KB(nc, es)
        k.allps = [k.ps() for _ in range(8)]
        k.psr8 = Ring(k.allps)
        k.psr4 = Ring(k.allps[0:6])
        k.ps_sum, k.ps_sq = k.allps[6], k.allps[7]
        k.psr = k.psr8
        x_in = k.dram("x", [L, D], F32, kind="ExternalInput")
        y_out = k.dram("y", [L, D], F32, kind="ExternalOutput")
        P = {n: k.dram(n, list(s), F32, kind="ExternalInput") for n, s in pshapes.items()}
        xA32, xB32, zT = (k.dram(n, [KC, 128, L], F32) for n in ("xA32", "xB32", "zT"))
        xA16, xB16, mixT16 = (k.dram(n, [KC, 128, L], BF16) for n in ("xA16", "xB16", "mixT16"))
        wc_up = k.dram("wc_up", [2 * FC, 128, KC * 128], BF16)
        wc_dn = k.dram("wc_dn", [KC, 128, FC * 128], BF16)
        ident32 = k.sb([128, 128], F32, "ident0")
        k.dma("sp", ident32[:], P["ident32"].h[:, :], reads=[P["ident32"]], writes=[ident32])
        phase_prepass(k, x_in, xA32, xA16, L, ident32)
        qT, kT = k.dram("qT", [8, 128, L], F32), k.dram("kT", [8, 128, L], F32)
        lrT, hyT = k.dram("lrT", [32, L], F32), k.dram("hyT", [48, 128, L], F32)
        k_tok, v_tok, og_tok = k.dram("k_tok", [L, 1024], F32), k.dram("v_tok", [L, 2048], BF16), k.dram("og_tok", [L, 2048], F32)
        of_tok = k.dram("of_tok", [L, 2048], F32)
        phase_even_inproj(k, xA16, L, P["ev_fm"], P["ev_lr"], P["ev_tm"], qT, kT, lrT, hyT, k_tok, v_tok, og_tok)
        phase_gla(k, L, qT, kT, k_tok, v_tok, og_tok, lrT, P["gla_wg"], P["gla_gn"], P["gla_cst"], P["ident32"], of_tok, mixT16)
        hyS = hy_scratch(k, L)
        phase_hyena(k, L, hyT, P, hyS, mixT16)
        k.psr = k.psr4
        phase_outproj_ln(k, mixT16, L, P["ev_w_out"], xA32, xB32, xB16, zT, P["ones32"], P["gtab"], P["btab"], 0)
        phase_ffn(k, xB16, L, P["w_up0"], P["w_dn0"], P["ffn_cw"], xB32, xA32, xA16, zT, P["ones32"], P["gtab"], P["btab"], 1, 0, wc_up, wc_dn)
        k.psr = k.psr8
        xbcT, ginT, dtT = Tk(hyT.h[0:24]), Tk(hyT.h[24:40]), k.dram("dtT", [64, L], F32)
        rinT, z_tok = Tk(hyS["uT"].h), Tk(og_tok.h)
        xs_tok, bm_tok = Tk(of_tok.h), Tk(k_tok.h[:, 0:512])
        bmT16, cmT16, yf_tok = k.dram("bmT16", [4, 128, L], BF16), k.dram("cmT16", [4, 128, L], BF16), Tk(hyS["u_tok"].h)
        phase_odd_inproj(k, xA16, L, P["od_fm"], P["od_dt"], P["od_tm"], xbcT, dtT, ginT, rinT, z_tok)
        phase_ssd_prep(k, L, xbcT, P["ssd_scw"], P["ident32"], xs_tok, bm_tok, bmT16, cmT16)
        phase_ssd(k, L, dtT, z_tok, xs_tok, bm_tok, bmT16, cmT16, yf_tok, P, mixT16)
        phase_lru(k, L, rinT, ginT, P["lru_cw"], P["lru_wax"], P["lru_b"], P["lru_lam"], mixT16)
        k.psr = k.psr4
        phase_outproj_ln(k, mixT16, L, P["od_w_out"], xA32, xB32, xB16, zT, P["ones32"], P["gtab"], P["btab"], 2)
        phase_ffn(k, xB16, L, P["w_up1"], P["w_dn1"], P["ffn_cw"], xB32, None, None, zT, P["ones32"], P["gtab"], P["btab"], 3, 1, wc_up, wc_dn,
                  out_tok=y_out, ident32_d=P["ident32"])
        k.finish([y_out])
        stats = dict(k.ninst)
        stats["nsem"] = k.nsem
    return nc, stats


def kernel(**inputs):
    L = L_FULL
    xp = np.asarray(inputs["x_prompt"], np.float32)
    xs = np.asarray(inputs["x_sample"], np.float32)
    seqs = [xp[0], xp[1], xs[0], xs[1], xs[2], xs[3]]
    P = host_params(inputs)
    nc, stats = build_program(L, {n: a.shape for n, a in P.items()})
    in_maps = []
    for c in range(6):
        m = dict(P)
        m["x"] = np.ascontiguousarray(seqs[c])
        in_maps.append(m)
    res = run_bass_kernel_spmd(nc, in_maps, core_ids=list(range(6)))
    outs = [np.asarray(res.results[c]["y"], np.float32) for c in range(6)]
    return (np.stack(outs[0:2]), np.stack(outs[2:6]))
```

```python
import contextlib
import numpy as np
import concourse.bass as bass
import concourse.mybir as mybir
from concourse.bass_utils import run_bass_kernel_spmd

F32 = mybir.dt.float32
BF16 = mybir.dt.bfloat16
AF = mybir.ActivationFunctionType
ALU = mybir.AluOpType

D = 4096
KC = 32
TB = 512
FF = 11008
FC = 86
NCORES = 8
ALPHA = 4.0 ** 0.25
LN_EPS = 1e-5
RMS_EPS = 1e-6


class Tk:
    __slots__ = ("h", "w", "r", "pr")

    def __init__(self, h):
        self.h = h
        self.w = {}
        self.r = {}
        self.pr = {}

    def __getitem__(self, idx):
        return self.h[idx]


class KB:
    SEM_LIMIT = 30000
    ND = 6

    def __init__(self, nc, es):
        self.nc = nc
        self.es = es
        self.eng = {"pe": nc.tensor, "act": nc.scalar, "dve": nc.vector, "pool": nc.gpsimd, "sp": nc.sync}
        self.sem = {}
        self.cnt = {}
        self.pe_sems = set()
        self.known = {e: {} for e in self.eng}
        self.nsem = 0
        self.prev = {}
        self.es2 = None
        for e in ("pe", "act", "dve", "pool"):
            self._new_sem(e)
        self.dq = {}
        for q in ("sp", "act", "pool"):
            self.dq[q] = {"sems": [self._alloc_sem() for _ in range(self.ND)], "vals": [0] * self.ND, "i": 0}
        self.ninst = {e: 0 for e in self.eng}
        self.uid = 0

    def _alloc_sem(self):
        self.nsem += 1
        return self.es.enter_context(self.nc.semaphore("s%d" % self.nsem))

    def _new_sem(self, e):
        if e in self.sem:
            self.prev.setdefault(e, []).append((self.sem[e], self.cnt[e]))
        s = self._alloc_sem()
        self.sem[e] = s
        self.cnt[e] = 0
        if e == "pe":
            self.pe_sems.add(s)

    def sb(self, shape, dt, name=None):
        self.uid += 1
        es = self.es2 if getattr(self, "es2", None) is not None else self.es
        h = es.enter_context(self.nc.sbuf_tensor("%s%d" % (name or "sb", self.uid), list(shape), dt))
        return Tk(h)

    def ps(self, shape=(128, 512), dt=F32, name=None):
        self.uid += 1
        h = self.es.enter_context(self.nc.psum_tensor("%s%d" % (name or "ps", self.uid), list(shape), dt))
        return Tk(h)

    def dram(self, name, shape, dt, kind="Internal"):
        h = self.nc.dram_tensor(name, list(shape), dt, kind=kind).ap()
        return Tk(h)

    def _wait(self, e, deps):
        kn = self.known[e]
        eo = self.eng[e]
        for s, v in deps.items():
            if kn.get(s, 0) < v:
                eo.wait_ge(s, v)
                kn[s] = v
                self.ninst[e] += 1

    @staticmethod
    def _merge(d, o):
        for s, v in o.items():
            if d.get(s, 0) < v:
                d[s] = v

    def op(self, e, fn, reads=(), writes=(), acc=False, last=True):
        deps = {}
        for t in reads:
            self._merge(deps, t.w)
        for t in writes:
            self._merge(deps, t.r)
            if not acc:
                self._merge(deps, t.w)
            else:
                self._merge(deps, t.pr)
        if e == "pe":
            for s in list(deps):
                if s in self.pe_sems:
                    del deps[s]
        self._wait(e, deps)
        ins = fn(self.eng[e])
        self.ninst[e] += 1
        if e == "pe" and not last:
            tok = (self.sem[e], self.cnt[e] + 1)
        else:
            ins.then_inc(self.sem[e], 1)
            self.cnt[e] += 1
            tok = (self.sem[e], self.cnt[e])
            if self.cnt[e] >= self.SEM_LIMIT:
                self._new_sem(e)
        s, v = tok
        for t in reads:
            if t.r.get(s, 0) < v:
                t.r[s] = v
        for t in writes:
            if acc:
                if t.w.get(s, 0) < v:
                    t.w[s] = v
            else:
                pr = dict(t.w)
                self._merge(pr, t.r)
                t.pr = pr
                t.w = {s: v}
                t.r = {}
        return ins

    def dma(self, q, out_ap, in_ap, reads=(), writes=(), acc=False, **kw):
        dq = self.dq[q]
        slot = dq["i"] % self.ND
        dq["i"] += 1
        if dq["vals"][slot] >= 60000:
            dq["sems"][slot] = self._alloc_sem()
            dq["vals"][slot] = 0
        s = dq["sems"][slot]
        deps = {}
        if dq["vals"][slot] > 0:
            deps[s] = dq["vals"][slot]
        for t in reads:
            self._merge(deps, t.w)
        for t in writes:
            self._merge(deps, t.r)
            if not acc:
                self._merge(deps, t.w)
            else:
                self._merge(deps, t.pr)
        self._wait(q, deps)
        ins = self.eng[q].dma_start(out=out_ap, in_=in_ap, **kw)
        ins.then_inc(s, 16)
        self.ninst[q] += 1
        dq["vals"][slot] += 16
        v = dq["vals"][slot]
        for t in reads:
            if t.r.get(s, 0) < v:
                t.r[s] = v
        for t in writes:
            if acc:
                if t.w.get(s, 0) < v:
                    t.w[s] = v
            else:
                pr = dict(t.w)
                self._merge(pr, t.r)
                t.pr = pr
                t.w = {s: v}
                t.r = {}
        return ins

    def barrier(self):
        deps = {}
        for e in ("pe", "act", "dve", "pool"):
            if self.cnt[e] > 0:
                deps[self.sem[e]] = self.cnt[e]
            elif self.prev.get(e):
                ps_, pv_ = self.prev[e][-1]
                deps[ps_] = pv_
        for q in self.dq:
            for s, v in zip(self.dq[q]["sems"], self.dq[q]["vals"]):
                if v > 0:
                    deps[s] = v
        for e in self.eng:
            self._wait(e, dict(deps))

    @contextlib.contextmanager
    def scope(self):
        old = self.es
        with contextlib.ExitStack() as es:
            self.es2 = es
            yield
            self.barrier()
        self.es2 = None

    def finish(self, outs):
        deps = {}
        for t in outs:
            self._merge(deps, t.w)
        self._wait("sp", deps)
        for q in self.dq:
            dd = {}
            for s, v in zip(self.dq[q]["sems"], self.dq[q]["vals"]):
                if v > 0:
                    dd[s] = v
            self._wait("sp", dd)


class Ring:
    def __init__(self, items):
        self.items = items
        self.i = 0

    def next(self):
        t = self.items[self.i % len(self.items)]
        self.i += 1
        return t


def sl(i, n=128):
    return slice(i * n, (i + 1) * n)


def phase_prepass(k, x_in, xT32, xT16, L, ident32):
    nb = L // TB
    with k.scope():
        xin = Ring([[k.sb([128, D], F32, "xin") for _ in range(4)] for _ in range(2)])
        st32 = Ring([k.sb([128, TB], F32, "st32") for _ in range(3)])
        st16 = Ring([k.sb([128, TB], BF16, "st16") for _ in range(3)])
        for b in range(nb):
            tiles = xin.next()
            for tt in range(4):
                t0 = b * TB + tt * 128
                k.dma("sp", tiles[tt][:], x_in[t0:t0 + 128, :], reads=[x_in], writes=[tiles[tt]])
            for dc in range(KC):
                pst = k.psr.next()
                for tt in range(4):
                    k.op("pe", lambda e, tt=tt: e.transpose(pst[:, sl(tt)], tiles[tt][:, sl(dc)], ident32[:]),
                         reads=[tiles[tt], ident32], writes=[pst], acc=(tt > 0), last=(tt == 3))
                s32 = st32.next()
                s16 = st16.next()
                k.op("dve", lambda e: e.tensor_copy(s32[:], pst[:]), reads=[pst], writes=[s32])
                k.op("act", lambda e: e.copy(s16[:], s32[:]), reads=[s32], writes=[s16])
                k.dma("act", xT32[dc, :, sl(b, TB)], s32[:], reads=[s32], writes=[xT32], acc=True)
                k.dma("act", xT16[dc, :, sl(b, TB)], s16[:], reads=[s16], writes=[xT16], acc=True)


def load_X(k, X, xT16, b, L):
    nb = L // TB
    lo = b * TB - 1
    hi = b * TB + TB + 1
    c0, c1 = 0, TB + 2
    if b == 0:
        lo, c0 = 0, 1
    if b == nb - 1:
        hi, c1 = L, TB + 1
    for g in range(4):
        if b == 0:
            k.op("pool", lambda e: e.memset(X[g][:, :, 0:1], 0.0), writes=[X[g]])
        if b == nb - 1:
            k.op("pool", lambda e: e.memset(X[g][:, :, TB + 1:TB + 2], 0.0), writes=[X[g]], acc=(b == 0))
        src = xT16.h[g * 8:(g + 1) * 8, :, lo:hi].rearrange("c p t -> p c t")
        k.dma("sp", X[g][:, :, c0:c1], src, reads=[xT16], writes=[X[g]],
              acc=(b == 0 or b == nb - 1))


def dense_fm(k, X, wring, w_ap, m, epi, reads_w, after_mm=None):
    wt = wring.next()
    k.dma("pool", wt[:, :, 0:m], w_ap, reads=reads_w, writes=[wt], max_dma_last_dim=4096)
    pst = k.psr.next()
    for kc in range(KC):
        k.op("pe", lambda e, kc=kc: e.matmul(pst[0:m, :], wt[:, kc, 0:m], X[kc // 8][:, kc % 8, 1:TB + 1],
                                             start=(kc == 0), stop=(kc == KC - 1)),
             reads=[wt, X[kc // 8]], writes=[pst], acc=(kc > 0), last=(kc == KC - 1))
    if after_mm is not None:
        after_mm()
    epi(pst)


def dense_tm(k, X, wring, w_ap, epi, reads_w, n=512):
    wt = wring.next()
    k.dma("pool", wt[:, 0:16, 0:n], w_ap[:, 0:16, :], reads=reads_w, writes=[wt], max_dma_last_dim=4096)
    k.dma("pool", wt[:, 16:32, 0:n], w_ap[:, 16:32, :], reads=reads_w, writes=[wt], acc=True, max_dma_last_dim=4096)
    for tt in range(4):
        pst = k.psr.next()
        for kc in range(KC):
            k.op("pe", lambda e, kc=kc: e.matmul(pst[:, 0:n], X[kc // 8][:, kc % 8, 1 + tt * 128:1 + (tt + 1) * 128], wt[:, kc, 0:n],
                                                 start=(kc == 0), stop=(kc == KC - 1)),
                 reads=[wt, X[kc // 8]], writes=[pst], acc=(kc > 0), last=(kc == KC - 1))
        epi(pst, tt)


def evac_to_dram(k, pst, m, ncol, stage_ring, dst_ap, dst_t, i, in_ap=None):
    st = stage_ring.next()
    src = pst[0:m, 0:ncol] if in_ap is None else in_ap
    if i % 2 == 0:
        k.op("act", lambda e: e.copy(st[0:m, 0:ncol], src), reads=[pst], writes=[st])
    else:
        k.op("dve", lambda e: e.tensor_copy(st[0:m, 0:ncol], src), reads=[pst], writes=[st])
    k.dma("sp" if i % 2 == 0 else "act", dst_ap, st[0:m, 0:ncol], reads=[st], writes=[dst_t], acc=True)


def phase_even_inproj(k, xT16, L, w_fm, w_lr, w_tm, qT, kT, lrT, hyT, k_tok, v_tok, og_tok):
    nb = L // TB
    with k.scope():
        X = [k.sb([128, 8, TB + 2], BF16, "X") for _ in range(4)]
        wr_fm = Ring([k.sb([128, KC, 128], BF16, "wfm") for _ in range(3)])
        wr_tm = Ring([k.sb([128, KC, 512], BF16, "wtm") for _ in range(2)])
        st32 = Ring([k.sb([128, TB], F32, "st32") for _ in range(4)])
        st16 = Ring([k.sb([128, TB], BF16, "st16") for _ in range(2)])
        for b in range(nb):
            load_X(k, X, xT16, b, L)
            cnt = [0]
            for c in range(64):
                if c < 8:
                    dst_t, dst = qT, qT.h[c, :, sl(b, TB)]
                elif c < 16:
                    dst_t, dst = kT, kT.h[c - 8, :, sl(b, TB)]
                else:
                    dst_t, dst = hyT, hyT.h[c - 16, :, sl(b, TB)]

                def epi(pst, dst=dst, dst_t=dst_t):
                    evac_to_dram(k, pst, 128, TB, st32, dst, dst_t, cnt[0])
                    cnt[0] += 1
                dense_fm(k, X, wr_fm, w_fm.h[c], 128, epi, [w_fm])

            def epi_lr(pst):
                evac_to_dram(k, pst, 32, TB, st32, lrT.h[:, sl(b, TB)], lrT, 0)
            dense_fm(k, X, wr_fm, w_lr.h, 32, epi_lr, [w_lr])
            for n in range(10):
                if n < 2:
                    dst_t, col, ring = k_tok, n * 512, st32
                elif n < 6:
                    dst_t, col, ring = v_tok, (n - 2) * 512, st16
                else:
                    dst_t, col, ring = og_tok, (n - 6) * 512, st32

                def epi(pst, tt, dst_t=dst_t, col=col, ring=ring):
                    t0 = b * TB + tt * 128
                    evac_to_dram(k, pst, 128, 512, ring, dst_t.h[t0:t0 + 128, col:col + 512], dst_t, cnt[0])
                    cnt[0] += 1
                dense_tm(k, X, wr_tm, w_tm.h[n], epi, [w_tm])


def tile_fm(W):
    K, N = W.shape
    return np.ascontiguousarray(W.reshape(K // 128, 128, N // 128, 128).transpose(2, 1, 0, 3))


def tile_tm(W, n=512):
    K, N = W.shape
    return np.ascontiguousarray(W.reshape(K // 128, 128, N // n, n).transpose(2, 1, 0, 3))


def make_consts():
    c = {}
    c["ident32"] = np.eye(128, dtype=np.float32)
    return c


class LNBlock:
    def __init__(self, k, G, S, ones32, eps_t, ps_sum, ps_sq, zT, gtab, btab, ln_i):
        self.k, self.G, self.S = k, G, S
        self.ones32, self.eps_t = ones32, eps_t
        self.ps_sum, self.ps_sq = ps_sum, ps_sq
        self.zT, self.gtab, self.btab, self.ln_i = zT, gtab, btab, ln_i
        self.pending = None

    def prefetch_res(self, xT32_old, dc, b):
        res = self.G.next()
        self.k.dma("sp", res[:, 0:TB], xT32_old.h[dc, :, sl(b, TB)], reads=[xT32_old], writes=[res])
        return res

    def flush(self):
        if self.pending is not None:
            self.pending()
            self.pending = None

    def chunk(self, pst, res, dc, b):
        k = self.k
        z = self.G.next()
        zsq = self.G.next()
        k.op("dve", lambda e: e.scalar_tensor_tensor(out=z[:, 0:TB], in0=res[:, 0:TB], scalar=ALPHA, in1=pst[:, :],
                                                     op0=ALU.mult, op1=ALU.add), reads=[res, pst], writes=[z])
        k.op("act", lambda e: e.activation(out=zsq[:, 0:TB], in_=z[:, 0:TB], func=AF.Square), reads=[z], writes=[zsq])
        k.dma("act", self.zT.h[dc, :, sl(b, TB)], z[:, 0:TB], reads=[z], writes=[self.zT], acc=True)

        def stats(z=z, zsq=zsq, dc=dc):
            k.op("pe", lambda e: e.matmul(self.ps_sum[:, :], self.ones32[:], z[:, 0:TB], start=(dc == 0), stop=(dc == KC - 1)),
                 reads=[self.ones32, z], writes=[self.ps_sum], acc=(dc > 0), last=True)
            k.op("pe", lambda e: e.matmul(self.ps_sq[:, :], self.ones32[:], zsq[:, 0:TB], start=(dc == 0), stop=(dc == KC - 1)),
                 reads=[self.ones32, zsq], writes=[self.ps_sq], acc=(dc > 0), last=True)
        self.pending = stats

    def finish(self, b, xT32_new, xT16_new, out_tok=None, ident32=None):
        k = self.k
        self.flush()
        mean, msq, rstd, nmr = self.S
        k.op("dve", lambda e: e.tensor_scalar(out=mean[:], in0=self.ps_sum[:, :], scalar1=1.0 / D, scalar2=None, op0=ALU.mult),
             reads=[self.ps_sum], writes=[mean])
        k.op("dve", lambda e: e.tensor_tensor(out=msq[:], in0=mean[:], in1=mean[:], op=ALU.mult), reads=[mean], writes=[msq])
        k.op("dve", lambda e: e.scalar_tensor_tensor(out=msq[:], in0=self.ps_sq[:, :], scalar=1.0 / D, in1=msq[:],
                                                     op0=ALU.mult, op1=ALU.subtract), reads=[self.ps_sq, msq], writes=[msq])
        k.op("act", lambda e: e.activation(out=rstd[:], in_=msq[:], func=AF.Sqrt, bias=self.eps_t[:, 0:1], scale=1.0),
             reads=[msq, self.eps_t], writes=[rstd])
        k.op("dve", lambda e: e.reciprocal(out=rstd[:], in_=rstd[:]), reads=[rstd], writes=[rstd])
        k.op("dve", lambda e: e.scalar_tensor_tensor(out=nmr[:], in0=mean[:], scalar=-1.0, in1=rstd[:], op0=ALU.mult, op1=ALU.mult),
             reads=[mean, rstd], writes=[nmr])
        gi = self.ln_i * KC
        for dc in range(KC):
            zl = self.G.next()
            k.dma("sp", zl[:, 0:TB], self.zT.h[dc, :, sl(b, TB)], reads=[self.zT], writes=[zl])
            t1 = self.G.next()
            k.op("dve", lambda e: e.tensor_tensor(out=t1[:, 0:TB], in0=zl[:, 0:TB], in1=rstd[:], op=ALU.mult), reads=[zl, rstd], writes=[t1])
            k.op("pool", lambda e: e.tensor_tensor(out=t1[:, 0:TB], in0=t1[:, 0:TB], in1=nmr[:], op=ALU.add), reads=[t1, nmr], writes=[t1])
            x32 = self.G.next()
            k.op("act", lambda e: e.activation(out=x32[:, 0:TB], in_=t1[:, 0:TB], func=AF.Identity,
                                               bias=self.btab[:, gi + dc:gi + dc + 1], scale=self.gtab[:, gi + dc:gi + dc + 1]),
                 reads=[t1, self.gtab, self.btab], writes=[x32])
            if xT32_new is not None:
                k.dma("act", xT32_new.h[dc, :, sl(b, TB)], x32[:, 0:TB], reads=[x32], writes=[xT32_new], acc=True)
                x16 = self.G.next()
                x16v = x16.h[:, 0:TB // 2].bitcast(BF16)
                k.op("pool", lambda e: e.tensor_copy(x16v, x32[:, 0:TB]), reads=[x32], writes=[x16])
                k.dma("act", xT16_new.h[dc, :, sl(b, TB)], x16v, reads=[x16], writes=[xT16_new], acc=True)
            if out_tok is not None:
                pst = k.psr.next()
                for tt in range(4):
                    k.op("pe", lambda e, tt=tt: e.transpose(pst[:, sl(tt)], x32[:, sl(tt)], ident32[:]),
                         reads=[x32, ident32], writes=[pst], acc=(tt > 0), last=(tt == 3))
                ot = self.G.next()
                k.op("dve", lambda e: e.tensor_copy(ot[:, 0:TB], pst[:, :]), reads=[pst], writes=[ot])
                for tt in range(4):
                    t0 = b * TB + tt * 128
                    k.dma("act", out_tok.h[t0:t0 + 128, sl(dc)], ot[:, sl(tt)], reads=[ot], writes=[out_tok], acc=True)


def ln_consts(k, ones32_d, gtab_d, btab_d):
    ones32 = k.sb([128, 128], F32, "ones")
    k.dma("sp", ones32[:], ones32_d.h[:, :], reads=[ones32_d], writes=[ones32])
    eps_t = k.sb([128, 1], F32, "eps")
    k.op("dve", lambda e: e.memset(eps_t[:], LN_EPS), writes=[eps_t])
    gtab = k.sb([128, 4 * KC], F32, "gtab")
    btab = k.sb([128, 4 * KC], F32, "btab")
    k.dma("sp", gtab[:], gtab_d.h[:, :], reads=[gtab_d], writes=[gtab])
    k.dma("sp", btab[:], btab_d.h[:, :], reads=[btab_d], writes=[btab])
    return ones32, eps_t, gtab, btab


def phase_outproj_ln(k, mixT16, L, w_out, xT32_old, xT32_new, xT16_new, zT, ones32_d, gtab_d, btab_d, ln_i):
    nb = L // TB
    with k.scope():
        ones32, eps_t, gtab, btab = ln_consts(k, ones32_d, gtab_d, btab_d)
        X = [k.sb([128, 8, TB + 2], BF16, "X") for _ in range(4)]
        wr = Ring([k.sb([128, KC, 128], BF16, "wfm") for _ in range(3)])
        G = Ring([k.sb([128, TB + 2], F32, "G") for _ in range(12)])
        S = [k.sb([128, TB], F32, "S") for _ in range(4)]
        for b in range(nb):
            load_X(k, X, mixT16, b, L)
            ln = LNBlock(k, G, S, ones32, eps_t, k.ps_sum, k.ps_sq, zT, gtab, btab, ln_i)
            for dc in range(KC):
                res = ln.prefetch_res(xT32_old, dc, b)
                dense_fm(k, X, wr, w_out.h[dc], 128, lambda pst, res=res, dc=dc: ln.chunk(pst, res, dc, b), [w_out], after_mm=ln.flush)
            ln.finish(b, xT32_new, xT16_new)


def phase_ffn(k, xT16, L, w_up, w_dn, cw_d, xT32_old, xT32_new, xT16_new, zT, ones32_d, gtab_d, btab_d, ln_i, layer,
              wc_up, wc_dn, out_tok=None, ident32_d=None):
    nb = L // TB
    NH = 2 * (nb - 1)
    with k.scope():
        ones32, eps_t, gtab, btab = ln_consts(k, ones32_d, gtab_d, btab_d)
        ident32 = None
        if out_tok is not None:
            ident32 = k.sb([128, 128], F32, "ident")
            k.dma("sp", ident32[:], ident32_d.h[:, :], reads=[ident32_d], writes=[ident32])
        cw = k.sb([128, 2 * FC, 4], F32, "cw")
        k.dma("sp", cw[:], cw_d.h[:, layer], reads=[cw_d], writes=[cw])
        X = [k.sb([128, 8, TB + 2], BF16, "X") for _ in range(4)]
        act = [k.sb([128, TB], BF16, "act") for _ in range(FC)]
        wr = Ring([k.sb([128, 43 * 128], BF16, "w") for _ in range(4)])
        G = Ring([k.sb([128, TB + 2], F32, "G") for _ in range(10)])
        S = [k.sb([128, TB], F32, "S") for _ in range(4)]
        hh = k.sb([128, 2 * FC, 16], F32, "hh")
        Xh = k.sb([128, KC, 16], BF16, "Xh")
        k.op("pool", lambda e: e.memset(Xh[:], 0.0), writes=[Xh])
        for j in range(1, nb):
            k.dma("sp" if j % 2 else "act", Xh[:, :, 2 * (j - 1):2 * j], xT16.h[:, :, TB * j - 1:TB * j + 1].rearrange("c p t -> p c t"),
                  reads=[xT16], writes=[Xh], acc=True)
        if NH == 0:
            k.op("pool", lambda e: e.memset(hh[:], 0.0), writes=[hh])
        for fi in range(2 * FC):
            wt = wr.next()
            wv = wt.h[:, 0:KC * 128].rearrange("p (c j) -> p c j", j=128)
            k.dma("pool", wv, w_up.h[fi], reads=[w_up], writes=[wt], max_dma_last_dim=4096)
            k.dma("sp" if fi % 2 else "act", wc_up.h[fi], wt[:, 0:KC * 128], reads=[wt], writes=[wc_up], acc=True)
            if NH > 0:
                ph = k.psr.next()
                for kc in range(KC):
                    k.op("pe", lambda e, kc=kc: e.matmul(ph[:, 0:NH], wv[:, kc, :], Xh[:, kc, 0:NH], start=(kc == 0), stop=(kc == KC - 1)),
                         reads=[wt, Xh], writes=[ph], acc=(kc > 0), last=(kc == KC - 1))
                k.op("act", lambda e: e.copy(hh[:, fi, 0:NH], ph[:, 0:NH]), reads=[ph], writes=[hh], acc=(fi > 0))
        for b in range(nb):
            load_X(k, X, xT16, b, L)
            for f in range(FC):
                cs = []
                for half in range(2):
                    fi = half * FC + f
                    wt = wr.next()
                    wv = wt.h[:, 0:KC * 128].rearrange("p (c j) -> p c j", j=128)
                    k.dma("sp", wt[:, 0:KC * 128], wc_up.h[fi], reads=[wc_up], writes=[wt])
                    pst = k.psr.next()
                    for kc in range(KC):
                        xt = X[kc // 8]
                        k.op("pe", lambda e, kc=kc, xt=xt: e.matmul(pst[:, :], wv[:, kc, :], xt[:, kc % 8, 1:TB + 1],
                                                                   start=(kc == 0), stop=(kc == KC - 1)),
                             reads=[wt, xt], writes=[pst], acc=(kc > 0), last=(kc == KC - 1))
                    hb = G.next()
                    k.op("act", lambda e: e.copy(hb[:, 1:TB + 1], pst[:, :]), reads=[pst], writes=[hb])
                    if b > 0:
                        k.op("pool", lambda e: e.tensor_copy(hb[:, 0:1], hh[:, fi, 2 * (b - 1):2 * (b - 1) + 1]), reads=[hh], writes=[hb], acc=True)
                    else:
                        k.op("pool", lambda e: e.memset(hb[:, 0:1], 0.0), writes=[hb], acc=True)
                    if b < nb - 1:
                        k.op("pool", lambda e: e.tensor_copy(hb[:, TB + 1:TB + 2], hh[:, fi, 2 * b + 1:2 * b + 2]), reads=[hh], writes=[hb], acc=True)
                    else:
                        k.op("pool", lambda e: e.memset(hb[:, TB + 1:TB + 2], 0.0), writes=[hb], acc=True)
                    c = G.next()
                    k.op("dve", lambda e: e.tensor_scalar(out=c[:, 0:TB], in0=hb[:, 0:TB], scalar1=cw[:, fi, 0:1], scalar2=cw[:, fi, 3:4],
                                                          op0=ALU.mult, op1=ALU.add), reads=[hb, cw], writes=[c])
                    k.op("dve", lambda e: e.scalar_tensor_tensor(out=c[:, 0:TB], in0=hb[:, 1:TB + 1], scalar=cw[:, fi, 1:2], in1=c[:, 0:TB],
                                                                 op0=ALU.mult, op1=ALU.add), reads=[hb, cw, c], writes=[c])
                    k.op("dve", lambda e: e.scalar_tensor_tensor(out=c[:, 0:TB], in0=hb[:, 2:TB + 2], scalar=cw[:, fi, 2:3], in1=c[:, 0:TB],
                                                                 op0=ALU.mult, op1=ALU.add), reads=[hb, cw, c], writes=[c])
                    cs.append(c)
                sg = G.next()
                k.op("act", lambda e: e.activation(out=sg[:, 0:TB], in_=cs[0][:, 0:TB], func=AF.Silu), reads=[cs[0]], writes=[sg])
                k.op("pool", lambda e: e.tensor_tensor(out=act[f][:], in0=sg[:, 0:TB], in1=cs[1][:, 0:TB], op=ALU.mult),
                     reads=[sg, cs[1]], writes=[act[f]])
            ln = LNBlock(k, G, S, ones32, eps_t, k.ps_sum, k.ps_sq, zT, gtab, btab, ln_i)
            for dc in range(KC):
                res = ln.prefetch_res(xT32_old, dc, b)
                wts = []
                for h in range(2):
                    wt = wr.next()
                    cs_ = slice(h * 43 * 128, (h + 1) * 43 * 128)
                    if b == 0:
                        k.dma("pool", wt[:, :], w_dn.h[dc, :, cs_], reads=[w_dn], writes=[wt], max_dma_last_dim=4096)
                        k.dma("pool", wc_dn.h[dc, :, cs_], wt[:, :], reads=[wt], writes=[wc_dn], acc=True)
                    else:
                        k.dma("sp", wt[:, :], wc_dn.h[dc, :, cs_], reads=[wc_dn], writes=[wt])
                    wts.append(wt)
                pst = k.psr.next()
                for fc in range(FC):
                    wt = wts[fc // 43]
                    o = (fc % 43) * 128
                    k.op("pe", lambda e, wt=wt, o=o, fc=fc: e.matmul(pst[:, :], wt[:, o:o + 128], act[fc][:], start=(fc == 0), stop=(fc == FC - 1)),
                         reads=[wt, act[fc]], writes=[pst], acc=(fc > 0), last=(fc == FC - 1))
                ln.flush()
                ln.chunk(pst, res, dc, b)
            ln.finish(b, xT32_new, xT16_new, out_tok=out_tok, ident32=ident32)


def phase_gla(k, L, qT, kT, k_tok, v_tok, og_tok, lrT, wg_d, gnorm_d, cst_d, ident32_d, of_tok, mixT16):
    NCH = L // 128
    with k.scope():
        cst = k.sb([128, 4, 128], F32, "cst")
        k.dma("sp", cst[:], cst_d.h.rearrange("a p c -> p a c"), reads=[cst_d], writes=[cst])
        ident32 = k.sb([128, 128], F32, "ident")
        k.dma("sp", ident32[:], ident32_d.h[:, :], reads=[ident32_d], writes=[ident32])
        gn = k.sb([128, 512], F32, "gn")
        k.dma("sp", gn[:], gnorm_d.h[:, :], reads=[gnorm_d], writes=[gn])
        wg = [k.sb([17, 1024], F32, "wg") for _ in range(2)]
        lra = [k.sb([17, L], F32, "lra") for _ in range(2)]
        for d in range(2):
            k.dma("sp", wg[d][:], wg_d.h[d], reads=[wg_d], writes=[wg[d]])
            k.op("dve", lambda e, d=d: e.memset(lra[d][:], 1.0), writes=[lra[d]])
            k.dma("sp", lra[d][0:16, :], lrT.h[16 * d:16 * d + 16, :], reads=[lrT], writes=[lra[d]])
        epsr = k.sb([128, 1], F32, "epsr")
        k.op("dve", lambda e: e.memset(epsr[:], RMS_EPS), writes=[epsr])
        S32 = [k.sb([128, 512], F32, "S32") for _ in range(2)]
        S16 = [k.sb([128, 512], BF16, "S16") for _ in range(2)]
        qr = Ring([k.sb([128, 2, 128], F32, "q") for _ in range(2)])
        kr = Ring([k.sb([128, 2, 128], F32, "kk") for _ in range(2)])
        ktr = Ring([k.sb([128, 256], F32, "kt") for _ in range(2)])
        vr = Ring([k.sb([128, 512], BF16, "v") for _ in range(2)])
        ogr = Ring([k.sb([128, 512], F32, "og") for _ in range(2)])
        ofr = Ring([k.sb([128, 512], F32, "of") for _ in range(2)])
        A = Ring([k.sb([128, 256], F32, "A") for _ in range(8)])
        Bq = Ring([k.sb([128, 2, 128], BF16, "Bq") for _ in range(4)])
        Bk = Ring([k.sb([128, 256], BF16, "Bk") for _ in range(2)])
        Pr = Ring([k.sb([128, 128], BF16, "P") for _ in range(2)])
        O = Ring([k.sb([128, 512], F32, "O") for _ in range(6)])
        sm = Ring([k.sb([128, 1], F32, "sm") for _ in range(6)])
        tr16 = Ring([k.sb([128, 4, 128], BF16, "tr") for _ in range(2)])
        for h in range(4):
            for d in range(2):
                tri = cst[:, 0 + d, :]
                ust = cst[:, 2 + d, :]
                for j in range(2):
                    k.op("dve", lambda e, j=j: e.memset(S32[j][:], 0.0), writes=[S32[j]])
                    k.op("pool", lambda e, j=j: e.memset(S16[j][:], 0.0), writes=[S16[j]])
                order = range(NCH) if d == 0 else range(NCH - 1, -1, -1)
                for n in order:
                    ts = slice(n * 128, (n + 1) * 128)
                    qt, kt, ktt, vt = qr.next(), kr.next(), ktr.next(), vr.next()
                    k.dma("sp", qt[:], qT.h[2 * h:2 * h + 2, :, ts].rearrange("j p c -> p j c"), reads=[qT], writes=[qt])
                    k.dma("sp", kt[:], kT.h[2 * h:2 * h + 2, :, ts].rearrange("j p c -> p j c"), reads=[kT], writes=[kt])
                    k.dma("sp", ktt[:], k_tok.h[ts, h * 256:(h + 1) * 256], reads=[k_tok], writes=[ktt])
                    k.dma("sp", vt[:], v_tok.h[ts, h * 512:(h + 1) * 512], reads=[v_tok], writes=[vt])
                    p1 = k.psr.next()
                    k.op("pe", lambda e: e.matmul(p1[:, 0:256], lra[d][0:17, ts], wg[d][0:17, h * 256:(h + 1) * 256], start=True, stop=True),
                         reads=[lra[d], wg[d]], writes=[p1])
                    ex = A.next()
                    k.op("act", lambda e: e.activation(out=ex[:], in_=p1[:, 0:256], func=AF.Exp, scale=-1.0), reads=[p1], writes=[ex])
                    la = A.next()
                    k.op("act", lambda e: e.activation(out=la[:], in_=ex[:], func=AF.Ln, bias=1.0, scale=1.0), reads=[ex], writes=[la])
                    p2 = k.psr.next()
                    k.op("pe", lambda e: e.matmul(p2[:, 0:256], ust, la[:], start=True, stop=True), reads=[cst, la], writes=[p2])
                    kd = A.next()
                    k.op("act", lambda e: e.activation(out=kd[:], in_=p2[:, 0:256], func=AF.Exp, scale=-1.0 / 16), reads=[p2], writes=[kd])
                    kdec = Bk.next()
                    k.op("dve", lambda e: e.tensor_tensor(out=kdec[:], in0=ktt[:], in1=kd[:], op=ALU.mult), reads=[ktt, kd], writes=[kdec])
                    p3 = k.psr.next()
                    for j in range(2):
                        k.op("pe", lambda e, j=j: e.matmul(p3[:, j * 128:(j + 1) * 128], la[:, j * 128:(j + 1) * 128], tri, start=True, stop=True),
                             reads=[la, cst], writes=[p3], acc=(j > 0), last=(j == 1))
                    eb = A.next()
                    ei = A.next()
                    k.op("act", lambda e: e.activation(out=eb[:], in_=p3[:, 0:256], func=AF.Exp, scale=-1.0 / 16), reads=[p3], writes=[eb])
                    k.op("act", lambda e: e.activation(out=ei[:], in_=p3[:, 0:256], func=AF.Exp, scale=1.0 / 16), reads=[p3], writes=[ei])
                    qd = Bq.next()
                    ki = Bq.next()
                    k.op("dve", lambda e: e.scalar_tensor_tensor(out=qd.h[:].rearrange("p j c -> p (j c)"), in0=qt.h[:].rearrange("p j c -> p (j c)"),
                                                                 scalar=1.0 / 16, in1=eb[:], op0=ALU.mult, op1=ALU.mult),
                         reads=[qt, eb], writes=[qd])
                    k.op("pool", lambda e: e.tensor_tensor(out=ki.h[:].rearrange("p j c -> p (j c)"), in0=kt.h[:].rearrange("p j c -> p (j c)"),
                                                           in1=ei[:], op=ALU.mult), reads=[kt, ei], writes=[ki])
                    p4 = k.psr.next()
                    for j in range(2):
                        k.op("pe", lambda e, j=j: e.matmul(p4[:, 0:128], ki[:, j, :], qd[:, j, :], start=(j == 0), stop=(j == 1)),
                             reads=[ki, qd], writes=[p4], acc=(j > 0), last=(j == 1))
                    P = Pr.next()
                    k.op("dve", lambda e: e.tensor_tensor(out=P[:], in0=p4[:, 0:128], in1=tri, op=ALU.mult), reads=[p4, cst], writes=[P])
                    p5 = k.psr.next()
                    k.op("pe", lambda e: e.matmul(p5[:, :], P[:], vt[:], start=True, stop=False), reads=[P, vt], writes=[p5], last=False)
                    for j in range(2):
                        k.op("pe", lambda e, j=j: e.matmul(p5[:, :], qd[:, j, :], S16[j][:], start=False, stop=(j == 1)),
                             reads=[qd, S16[j]], writes=[p5], acc=True, last=(j == 1))
                    lastc = 127 if d == 0 else 0
                    for j in range(2):
                        p6 = k.psr.next()
                        k.op("pe", lambda e, j=j: e.matmul(p6[:, :], kdec[:, j * 128:(j + 1) * 128], vt[:], start=True, stop=True),
                             reads=[kdec, vt], writes=[p6])
                        k.op("dve", lambda e, j=j, p6=p6: e.scalar_tensor_tensor(out=S32[j][:], in0=S32[j][:], scalar=eb[:, j * 128 + lastc:j * 128 + lastc + 1],
                                                                                 in1=p6[:, :], op0=ALU.mult, op1=ALU.add),
                             reads=[S32[j], eb, p6], writes=[S32[j]])
                        k.op("act", lambda e, j=j: e.copy(S16[j][:], S32[j][:]), reads=[S32[j]], writes=[S16[j]])
                    if d == 0:
                        o = O.next()
                        k.op("act", lambda e: e.copy(o[:], p5[:, :]), reads=[p5], writes=[o])
                        k.dma("act", of_tok.h[ts, h * 512:(h + 1) * 512], o[:], reads=[o], writes=[of_tok], acc=True)
                    else:
                        oft, ogt = ofr.next(), ogr.next()
                        k.dma("sp", oft[:], of_tok.h[ts, h * 512:(h + 1) * 512], reads=[of_tok], writes=[oft])
                        k.dma("sp", ogt[:], og_tok.h[ts, h * 512:(h + 1) * 512], reads=[og_tok], writes=[ogt])
                        o = O.next()
                        k.op("dve", lambda e: e.tensor_tensor(out=o[:], in0=oft[:], in1=p5[:, :], op=ALU.add), reads=[oft, p5], writes=[o])
                        sq = O.next()
                        ss = sm.next()
                        k.op("act", lambda e: e.activation(out=sq[:], in_=o[:], func=AF.Square, accum_out=ss[:]), reads=[o], writes=[sq, ss])
                        rs = sm.next()
                        k.op("act", lambda e: e.activation(out=rs[:], in_=ss[:], func=AF.Sqrt, bias=epsr[:, 0:1], scale=1.0 / 512),
                             reads=[ss, epsr], writes=[rs])
                        k.op("dve", lambda e: e.reciprocal(out=rs[:], in_=rs[:]), reads=[rs], writes=[rs])
                        on = O.next()
                        k.op("dve", lambda e: e.scalar_tensor_tensor(out=on[:], in0=o[:], scalar=rs[:, 0:1], in1=gn[:], op0=ALU.mult, op1=ALU.mult),
                             reads=[o, rs, gn], writes=[on])
                        sg = O.next()
                        k.op("act", lambda e: e.activation(out=sg[:], in_=ogt[:], func=AF.Silu), reads=[ogt], writes=[sg])
                        k.op("pool", lambda e: e.tensor_tensor(out=on[:], in0=on[:], in1=sg[:], op=ALU.mult), reads=[on, sg], writes=[on])
                        p7 = k.psr.next()
                        for j in range(4):
                            k.op("pe", lambda e, j=j: e.transpose(p7[:, sl(j)], on[:, sl(j)], ident32[:]),
                                 reads=[on, ident32], writes=[p7], acc=(j > 0), last=(j == 3))
                        t16 = tr16.next()
                        k.op("dve", lambda e: e.tensor_copy(t16.h[:].rearrange("p j c -> p (j c)"), p7[:, :]), reads=[p7], writes=[t16])
                        k.dma("act", mixT16.h[4 * h:4 * h + 4, :, ts].rearrange("j p c -> p j c"), t16[:], reads=[t16], writes=[mixT16], acc=True)


def gla_consts():
    i = np.arange(128)
    tri = (i[:, None] <= i[None, :]).astype(np.float32)
    ust = (i[:, None] > i[None, :]).astype(np.float32)
    return np.stack([tri, tri.T.copy(), ust, ust.T.copy()])


MAGIC = 12582912.0
TWO_PI = 6.283185307179586


def hy_consts(L):
    N = 2 * L
    N1 = N // 128
    T1 = L // 128
    t = np.linspace(0.0, 1.0, L, dtype=np.float32)[:, None]
    bands = np.linspace(1e-4, 15.0, 16, dtype=np.float32)
    w = (2.0 * np.pi * np.arange(L, dtype=np.float32)[:, None] / L).astype(np.float32)
    z = np.concatenate([t, np.cos(bands * w), -np.sin(bands * w), np.ones((L, 1), np.float32)], -1).astype(np.float32)
    c = {}
    c["hy_zT"] = np.ascontiguousarray(z.T)
    max_decay = np.log(1e-2) / 0.3
    min_decay = np.log(1e-2) / 1.5
    deltas = np.linspace(min_decay, max_decay, 2048, dtype=np.float32)
    c["hy_absd"] = np.tile(np.abs(deltas)[None, :], (128, 1)).astype(np.float32)
    tau = (np.arange(L).reshape(T1, 128).T).astype(np.float64)
    c["hy_negt"] = np.ascontiguousarray((-(tau / (L - 1))).astype(np.float32))
    t1 = np.arange(N1)[:, None, None]
    t2 = np.arange(128)[None, :, None]
    f1 = np.arange(N1)[None, None, :]
    ang = 2.0 * np.pi * ((f1 * (128 * t1 + t2)) % N) / N
    fw = np.stack([np.cos(ang), -np.sin(ang)], 2).astype(np.float32)
    c["hy_fw"] = np.ascontiguousarray(fw[:T1])
    iv = np.stack([np.cos(ang), -np.sin(ang)], 2) / N
    c["hy_iv"] = np.ascontiguousarray(iv[:T1].transpose(3, 1, 2, 0)).astype(np.float32)
    a = 2.0 * np.pi * ((np.arange(128)[:, None] * np.arange(128)[None, :]) % 128) / 128
    c["hy_cs"] = np.stack([np.cos(a), np.sin(a), -np.sin(a)]).astype(np.float32)
    return c


def hy_sin(k, G, ps, bias_ap, bias_t, out_ap, out_t, m):
    xs, n1, xr = G.next(), G.next(), G.next()
    if bias_ap is None:
        k.op("dve", lambda e: e.tensor_copy(xs[0:m, :], ps[0:m, :]), reads=[ps], writes=[xs])
    else:
        k.op("dve", lambda e: e.tensor_scalar(out=xs[0:m, :], in0=ps[0:m, :], scalar1=bias_ap, scalar2=None, op0=ALU.add),
             reads=[ps, bias_t], writes=[xs])
    k.op("dve", lambda e: e.tensor_scalar(out=n1[0:m, :], in0=xs[0:m, :], scalar1=1.0 / TWO_PI, scalar2=MAGIC, op0=ALU.mult, op1=ALU.add),
         reads=[xs], writes=[n1])
    k.op("dve", lambda e: e.tensor_scalar(out=n1[0:m, :], in0=n1[0:m, :], scalar1=-MAGIC, scalar2=None, op0=ALU.add), reads=[n1], writes=[n1])
    k.op("dve", lambda e: e.scalar_tensor_tensor(out=xr[0:m, :], in0=n1[0:m, :], scalar=-TWO_PI, in1=xs[0:m, :], op0=ALU.mult, op1=ALU.add),
         reads=[n1, xs], writes=[xr])
    k.op("act", lambda e: e.activation(out=out_ap, in_=xr[0:m, :], func=AF.Sin), reads=[xr], writes=[out_t])


def hy_filters(k, L, zT_d, w1a_d, w2_d, b2_d, w3_d, absd_d, negt_d, kf_tok, kb_tok):
    T1 = L // 128
    with k.scope():
        zT = k.sb([34, L], F32, "zT")
        k.dma("sp", zT[:], zT_d.h[:, :], reads=[zT_d], writes=[zT])
        w1a = k.sb([34, 64], F32, "w1a")
        k.dma("sp", w1a[:], w1a_d.h[:, :], reads=[w1a_d], writes=[w1a])
        w2 = k.sb([64, 64], F32, "w2")
        k.dma("sp", w2[:], w2_d.h[:, :], reads=[w2_d], writes=[w2])
        b2 = k.sb([64, 1], F32, "b2")
        k.dma("sp", b2[:], b2_d.h[:, :], reads=[b2_d], writes=[b2])
        w3 = k.sb([64, 4096], F32, "w3")
        k.dma("act", w3[:], w3_d.h[:, :], reads=[w3_d], writes=[w3])
        absd = k.sb([128, 2048], F32, "absd")
        k.dma("act", absd[:], absd_d.h[:, :], reads=[absd_d], writes=[absd])
        negt = k.sb([128, T1], F32, "negt")
        k.dma("sp", negt[:], negt_d.h[:, :], reads=[negt_d], writes=[negt])
        h2T = k.sb([64, L], F32, "h2T")
        G = Ring([k.sb([128, 512], F32, "G") for _ in range(10)])
        h1r = Ring([k.sb([64, 512], F32, "h1") for _ in range(2)])
        for b in range(L // 512):
            p = k.psr.next()
            k.op("pe", lambda e: e.matmul(p[0:64, :], w1a[:, :], zT[:, sl(b, 512)], start=True, stop=True), reads=[w1a, zT], writes=[p])
            h1 = h1r.next()
            hy_sin(k, G, p, None, None, h1[:, :], h1, 64)
            p2 = k.psr.next()
            k.op("pe", lambda e: e.matmul(p2[0:64, :], w2[:, :], h1[:, :], start=True, stop=True), reads=[w2, h1], writes=[p2])
            hy_sin(k, G, p2, b2[:, 0:1], b2, h2T[:, sl(b, 512)], h2T, 64)
        i = 0
        for n in range(T1):
            for c4 in range(4):
                win = G.next()
                k.op("act", lambda e: e.activation(out=win[:], in_=absd[:, sl(c4, 512)], func=AF.Exp, scale=negt[:, n:n + 1]),
                     reads=[absd, negt], writes=[win])
                for fb in range(2):
                    p = k.psr.next()
                    col = fb * 2048 + c4 * 512
                    k.op("pe", lambda e: e.matmul(p[:, :], h2T[:, sl(n)], w3[:, col:col + 512], start=True, stop=True), reads=[h2T, w3], writes=[p])
                    st = G.next()
                    k.op("dve", lambda e: e.tensor_tensor(out=st[:], in0=p[:, :], in1=win[:], op=ALU.mult), reads=[p, win], writes=[st])
                    dst = kf_tok if fb == 0 else kb_tok
                    if fb == 1 and n == 0:
                        k.op("dve", lambda e: e.memset(st[0:1, :], 0.0), writes=[st])
                    k.dma("act", dst.h[sl(n), sl(c4, 512)], st[:], reads=[st], writes=[dst], acc=True)
                    i += 1


def hy_prep_u(k, L, hyT, hcw_d, ident32_d, uT, u_tok):
    with k.scope():
        hcw = k.sb([128, 48, 4], F32, "hcw")
        k.dma("sp", hcw[:], hcw_d.h[:, :, :], reads=[hcw_d], writes=[hcw])
        ident32 = k.sb([128, 128], F32, "ident")
        k.dma("sp", ident32[:], ident32_d.h[:, :], reads=[ident32_d], writes=[ident32])
        inr = Ring([k.sb([128, L + 2], F32, "in") for _ in range(3)])
        cr = Ring([k.sb([128, L], F32, "c") for _ in range(3)])
        st = Ring([k.sb([128, 4, 128], F32, "st") for _ in range(3)])
        for cc in range(16):
            cs = []
            for which in (16, 32):
                ch = which + cc
                t = inr.next()
                k.op("pool", lambda e: e.memset(t[:, 0:L + 2:L + 1], 0.0), writes=[t])
                k.dma("sp", t[:, 1:L + 1], hyT.h[ch], reads=[hyT], writes=[t], acc=True)
                c = cr.next()
                hy_conv(k, c, t, hcw, ch, L)
                cs.append(c)
            u = cs[0]
            k.op("pool", lambda e: e.tensor_tensor(out=u[:], in0=cs[0][:], in1=cs[1][:], op=ALU.mult), reads=[cs[0], cs[1]], writes=[u])
            k.dma("act", uT.h[cc], u[:], reads=[u], writes=[uT], acc=True)
            for g in range(L // 512):
                p = k.psr.next()
                for j in range(4):
                    k.op("pe", lambda e, j=j: e.transpose(p[:, sl(j)], u[:, g * 512 + j * 128:g * 512 + (j + 1) * 128], ident32[:]),
                         reads=[u, ident32], writes=[p], acc=(j > 0), last=(j == 3))
                s = st.next()
                if g % 2 == 0:
                    k.op("act", lambda e: e.copy(s.h[:].rearrange("p a c -> p (a c)"), p[:, :]), reads=[p], writes=[s])
                else:
                    k.op("dve", lambda e: e.tensor_copy(s.h[:].rearrange("p a c -> p (a c)"), p[:, :]), reads=[p], writes=[s])
                k.dma("act", u_tok.h[g * 512:(g + 1) * 512, sl(cc)].rearrange("(a p) c -> p a c", p=128), s[:],
                      reads=[s], writes=[u_tok], acc=True)


def hy_conv(k, c, t, hcw, ch, L):
    k.op("dve", lambda e: e.tensor_scalar(out=c[:, 0:L], in0=t[:, 0:L], scalar1=hcw[:, ch, 0:1], scalar2=hcw[:, ch, 3:4], op0=ALU.mult, op1=ALU.add),
         reads=[t, hcw], writes=[c])
    k.op("dve", lambda e: e.scalar_tensor_tensor(out=c[:, 0:L], in0=t[:, 1:L + 1], scalar=hcw[:, ch, 1:2], in1=c[:, 0:L], op0=ALU.mult, op1=ALU.add),
         reads=[t, hcw, c], writes=[c])
    k.op("dve", lambda e: e.scalar_tensor_tensor(out=c[:, 0:L], in0=t[:, 2:L + 2], scalar=hcw[:, ch, 2:3], in1=c[:, 0:L], op0=ALU.mult, op1=ALU.add),
         reads=[t, hcw, c], writes=[c])


def hy_fft_a(k, L, src_tok, fw_d, scrA):
    N1, T1 = 2 * L // 128, L // 128
    with k.scope():
        fw = k.sb([T1, 128, 2, N1], F32, "fw")
        k.dma("sp", fw[:, 0:64], fw_d.h[:, 0:64], reads=[fw_d], writes=[fw])
        k.dma("act", fw[:, 64:128], fw_d.h[:, 64:128], reads=[fw_d], writes=[fw], acc=True)
        sr = Ring([k.sb([T1, 2048], F32, "s") for _ in range(3)])
        st = Ring([k.sb([N1, 2048], F32, "st") for _ in range(4)])
        src_v = src_tok.h.rearrange("(a p) c -> p a c", p=128)
        i = 0
        for t2 in range(128):
            s = sr.next()
            k.dma("sp", s[:], src_v[t2], reads=[src_tok], writes=[s])
            for ri in range(2):
                so = st.next()
                for ct in range(4):
                    p = k.psr.next()
                    k.op("pe", lambda e: e.matmul(p[0:N1, :], fw[:, t2, ri, :], s[:, sl(ct, 512)], start=True, stop=True), reads=[fw, s], writes=[p])
                    if i % 2 == 0:
                        k.op("act", lambda e: e.copy(so[:, sl(ct, 512)], p[0:N1, :]), reads=[p], writes=[so], acc=(ct > 0))
                    else:
                        k.op("dve", lambda e: e.tensor_copy(so[:, sl(ct, 512)], p[0:N1, :]), reads=[p], writes=[so], acc=(ct > 0))
                    i += 1
                k.dma("act", scrA.h[ri, :, t2, :], so[:], reads=[so], writes=[scrA], acc=True)


def hy_fft_b(k, L, scrA, cs_d, epi, mk_extra):
    N1 = 2 * L // 128
    with k.scope():
        cs = k.sb([128, 3, 128], F32, "cs")
        k.dma("sp", cs[:], cs_d.h.rearrange("a p c -> p a c"), reads=[cs_d], writes=[cs])
        ar = Ring([k.sb([128, 2, 2048], F32, "a") for _ in range(2)])
        ctx = mk_extra(cs)
        for f1 in range(N1):
            a = ar.next()
            k.dma("sp", a[:, 0, :], scrA.h[0, f1], reads=[scrA], writes=[a])
            k.dma("sp", a[:, 1, :], scrA.h[1, f1], reads=[scrA], writes=[a], acc=True)
            for ct in range(4):
                pr, pi = k.psr.next(), k.psr.next()
                c = slice(ct * 512, (ct + 1) * 512)
                k.op("pe", lambda e: e.matmul(pr[:, :], cs[:, 0, :], a[:, 0, c], start=True, stop=False), reads=[cs, a], writes=[pr], last=False)
                k.op("pe", lambda e: e.matmul(pr[:, :], cs[:, 1, :], a[:, 1, c], start=False, stop=True), reads=[cs, a], writes=[pr], acc=True)
                k.op("pe", lambda e: e.matmul(pi[:, :], cs[:, 0, :], a[:, 1, c], start=True, stop=False), reads=[cs, a], writes=[pi], last=False)
                k.op("pe", lambda e: e.matmul(pi[:, :], cs[:, 2, :], a[:, 0, c], start=False, stop=True), reads=[cs, a], writes=[pi], acc=True)
                epi(f1, ct, pr, pi, ctx)


def hy_spectrum_store(k, scrK, combine):
    def mk(cs):
        return {"st": Ring([k.sb([128, 2, 512], F32, "kst") for _ in range(3)]),
                "ld": Ring([k.sb([128, 2, 512], F32, "kld") for _ in range(3)])}

    def epi(f1, ct, pr, pi, ctx):
        c = slice(ct * 512, (ct + 1) * 512)
        s = ctx["st"].next()
        if not combine:
            k.op("act", lambda e: e.copy(s[:, 0, :], pr[:, :]), reads=[pr], writes=[s])
            k.op("dve", lambda e: e.tensor_copy(s[:, 1, :], pi[:, :]), reads=[pi], writes=[s], acc=True)
        else:
            ld = ctx["ld"].next()
            k.dma("sp", ld[:], scrK.h[:, f1, :, c].rearrange("r p c -> p r c"), reads=[scrK], writes=[ld])
            k.op("dve", lambda e: e.tensor_tensor(out=s[:, 0, :], in0=ld[:, 0, :], in1=pr[:, :], op=ALU.add), reads=[ld, pr], writes=[s])
            k.op("dve", lambda e: e.tensor_tensor(out=s[:, 1, :], in0=ld[:, 1, :], in1=pi[:, :], op=ALU.subtract), reads=[ld, pi], writes=[s], acc=True)
        k.dma("act", scrK.h[:, f1, :, c].rearrange("r p c -> p r c"), s[:], reads=[s], writes=[scrK], acc=True)
    return mk, epi


def hy_mul_inv(k, scrK, scrG):
    def mk(cs):
        return {"cs": cs, "ld": Ring([k.sb([128, 2, 512], F32, "kld") for _ in range(3)]),
                "y": Ring([k.sb([128, 2, 512], F32, "y") for _ in range(2)]),
                "t": Ring([k.sb([128, 512], F32, "t") for _ in range(4)]),
                "g": Ring([k.sb([128, 2, 512], F32, "g") for _ in range(3)])}

    def epi(f1, ct, pr, pi, ctx):
        cs = ctx["cs"]
        c = slice(ct * 512, (ct + 1) * 512)
        ld = ctx["ld"].next()
        k.dma("sp", ld[:], scrK.h[:, f1, :, c].rearrange("r p c -> p r c"), reads=[scrK], writes=[ld])
        y = ctx["y"].next()
        ta, tb = ctx["t"].next(), ctx["t"].next()
        k.op("dve", lambda e: e.tensor_tensor(out=ta[:], in0=pr[:, :], in1=ld[:, 0, :], op=ALU.mult), reads=[pr, ld], writes=[ta])
        k.op("act", lambda e: e.copy(tb[:], pi[:, :]), reads=[pi], writes=[tb])
        k.op("dve", lambda e: e.tensor_tensor(out=y[:, 1, :], in0=pr[:, :], in1=ld[:, 1, :], op=ALU.mult), reads=[pr, ld], writes=[y])
        tc_ = ctx["t"].next()
        k.op("pool", lambda e: e.tensor_tensor(out=tc_[:], in0=tb[:], in1=ld[:, 1, :], op=ALU.mult), reads=[tb, ld], writes=[tc_])
        k.op("pool", lambda e: e.tensor_tensor(out=y[:, 0, :], in0=ta[:], in1=tc_[:], op=ALU.subtract), reads=[ta, tc_], writes=[y], acc=True)
        td = ctx["t"].next()
        k.op("dve", lambda e: e.tensor_tensor(out=td[:], in0=tb[:], in1=ld[:, 0, :], op=ALU.mult), reads=[tb, ld], writes=[td])
        k.op("dve", lambda e: e.tensor_tensor(out=y[:, 1, :], in0=y[:, 1, :], in1=td[:], op=ALU.add), reads=[y, td], writes=[y], acc=True)
        gr, gi = k.psr.next(), k.psr.next()
        k.op("pe", lambda e: e.matmul(gr[:, :], cs[:, 0, :], y[:, 0, :], start=True, stop=False), reads=[cs, y], writes=[gr], last=False)
        k.op("pe", lambda e: e.matmul(gr[:, :], cs[:, 2, :], y[:, 1, :], start=False, stop=True), reads=[cs, y], writes=[gr], acc=True)
        k.op("pe", lambda e: e.matmul(gi[:, :], cs[:, 0, :], y[:, 1, :], start=True, stop=False), reads=[cs, y], writes=[gi], last=False)
        k.op("pe", lambda e: e.matmul(gi[:, :], cs[:, 1, :], y[:, 0, :], start=False, stop=True), reads=[cs, y], writes=[gi], acc=True)
        g = ctx["g"].next()
        k.op("act", lambda e: e.copy(g[:, 0, :], gr[:, :]), reads=[gr], writes=[g])
        k.op("dve", lambda e: e.tensor_copy(g[:, 1, :], gi[:, :]), reads=[gi], writes=[g], acc=True)
        k.dma("act", scrG.h[:, :, f1, c].rearrange("r p c -> p r c"), g[:], reads=[g], writes=[scrG], acc=True)
    return mk, epi


def hy_fft_c(k, L, scrG, iv_d, hyT, uT, hcw_d, hbias_d, mixT16):
    N1, T1 = 2 * L // 128, L // 128
    TPB = 512 // T1
    with k.scope():
        iv = k.sb([N1, 128, 2, T1], F32, "iv")
        k.dma("sp", iv[:, 0:64], iv_d.h[:, 0:64], reads=[iv_d], writes=[iv])
        k.dma("act", iv[:, 64:128], iv_d.h[:, 64:128], reads=[iv_d], writes=[iv], acc=True)
        hcw = k.sb([128, 48, 4], F32, "hcw")
        k.dma("sp", hcw[:], hcw_d.h[:, :, :], reads=[hcw_d], writes=[hcw])
        hb = k.sb([128, 16], F32, "hb")
        k.dma("sp", hb[:], hbias_d.h[:, :], reads=[hbias_d], writes=[hb])
        gr = Ring([k.sb([N1, 2, 512], F32, "g") for _ in range(3)])
        yT = [k.sb([128, L], F32, "yT") for _ in range(4)]
        x0r = Ring([k.sb([128, L + 2], F32, "x0") for _ in range(1)])
        ur = Ring([k.sb([128, L], F32, "u") for _ in range(1)])
        cr = Ring([k.sb([128, L], F32, "c") for _ in range(1)])
        o16 = Ring([k.sb([128, L], BF16, "o16") for _ in range(2)])
        allps = Ring(k.allps)
        for cg in range(4):
            banks = None
            for t2 in range(128):
                if t2 % TPB == 0:
                    banks = [allps.next() for _ in range(4)]
                g = gr.next()
                k.dma("sp", g[:], scrG.h[:, t2, :, sl(cg, 512)].rearrange("r f c -> f r c"), reads=[scrG], writes=[g])
                o = (t2 % TPB) * T1
                for j in range(4):
                    k.op("pe", lambda e, j=j: e.matmul(banks[j][:, o:o + T1], g[:, 0, sl(j)], iv[:, t2, 0, :], start=True, stop=False),
                         reads=[g, iv], writes=[banks[j]], acc=True, last=False)
                    k.op("pe", lambda e, j=j: e.matmul(banks[j][:, o:o + T1], g[:, 1, sl(j)], iv[:, t2, 1, :], start=False, stop=True),
                         reads=[g, iv], writes=[banks[j]], acc=True, last=True)
                if t2 % TPB == TPB - 1:
                    t2a = t2 - (TPB - 1)
                    for j in range(4):
                        dst = yT[j].h[:, :].rearrange("p (a b) -> p b a", b=128)[:, t2a:t2a + TPB, :]
                        srcp = banks[j].h[:, 0:TPB * T1].rearrange("p (b a) -> p b a", a=T1)
                        if j % 2 == 0:
                            k.op("act", lambda e: e.copy(dst, srcp), reads=[banks[j]], writes=[yT[j]], acc=True)
                        else:
                            k.op("dve", lambda e: e.tensor_copy(dst, srcp), reads=[banks[j]], writes=[yT[j]], acc=True)
            for j in range(4):
                cc = cg * 4 + j
                u = ur.next()
                k.dma("sp", u[:], uT.h[cc], reads=[uT], writes=[u])
                t = x0r.next()
                k.op("pool", lambda e: e.memset(t[:, 0:L + 2:L + 1], 0.0), writes=[t])
                k.dma("sp", t[:, 1:L + 1], hyT.h[cc], reads=[hyT], writes=[t], acc=True)
                c = cr.next()
                hy_conv(k, c, t, hcw, cc, L)
                k.op("dve", lambda e: e.scalar_tensor_tensor(out=u[:], in0=u[:], scalar=hb[:, cc:cc + 1], in1=yT[j][:], op0=ALU.mult, op1=ALU.add),
                     reads=[u, hb, yT[j]], writes=[u])
                o = o16.next()
                k.op("pool", lambda e: e.tensor_tensor(out=o[:], in0=u[:], in1=c[:], op=ALU.mult), reads=[u, c], writes=[o])
                k.dma("act", mixT16.h[16 + cc], o[:], reads=[o], writes=[mixT16], acc=True)


def phase_hyena(k, L, hyT, P, S, mixT16):
    hy_filters(k, L, P["hy_zT"], P["hy_w1a"], P["hy_w2"], P["hy_b2"], P["hy_w3"], P["hy_absd"], P["hy_negt"], S["kf_tok"], S["kb_tok"])
    hy_prep_u(k, L, hyT, P["hy_cw"], P["ident32"], S["uT"], S["u_tok"])
    hy_fft_a(k, L, S["kf_tok"], P["hy_fw"], S["scrA"])
    mk, epi = hy_spectrum_store(k, S["scrK"], False)
    hy_fft_b(k, L, S["scrA"], P["hy_cs"], epi, mk)
    hy_fft_a(k, L, S["kb_tok"], P["hy_fw"], S["scrA"])
    mk, epi = hy_spectrum_store(k, S["scrK"], True)
    hy_fft_b(k, L, S["scrA"], P["hy_cs"], epi, mk)
    hy_fft_a(k, L, S["u_tok"], P["hy_fw"], S["scrA"])
    mk, epi = hy_mul_inv(k, S["scrK"], S["scrG"])
    hy_fft_b(k, L, S["scrA"], P["hy_cs"], epi, mk)
    hy_fft_c(k, L, S["scrG"], P["hy_iv"], hyT, S["uT"], P["hy_cw"], P["hy_bias"], mixT16)


def hy_scratch(k, L, kind="Internal"):
    N1 = 2 * L // 128
    return {"kf_tok": k.dram("kf_tok", [L, 2048], F32, kind=kind), "kb_tok": k.dram("kb_tok", [L, 2048], F32, kind=kind),
            "uT": k.dram("uT", [16, 128, L], F32, kind=kind), "u_tok": k.dram("u_tok", [L, 2048], F32, kind=kind),
            "scrA": k.dram("scrA", [2, N1, 128, 2048], F32, kind=kind), "scrK": k.dram("scrK", [2, N1, 128, 2048], F32, kind=kind),
            "scrG": k.dram("scrG", [2, 128, N1, 2048], F32, kind=kind)}


def phase_odd_inproj(k, xT16, L, w_fm, w_dt, w_tm, xbcT, dtT, ginT, rinT, z_tok):
    nb = L // TB
    with k.scope():
        X = [k.sb([128, 8, TB + 2], BF16, "X") for _ in range(4)]
        wr_fm = Ring([k.sb([128, KC, 128], BF16, "wfm") for _ in range(3)])
        wr_tm = Ring([k.sb([128, KC, 512], BF16, "wtm") for _ in range(2)])
        st32 = Ring([k.sb([128, TB], F32, "st32") for _ in range(4)])
        for b in range(nb):
            load_X(k, X, xT16, b, L)
            cnt = [0]
            for c in range(56):
                if c < 24:
                    dst_t, dst = xbcT, xbcT.h[c, :, sl(b, TB)]
                elif c < 40:
                    dst_t, dst = ginT, ginT.h[c - 24, :, sl(b, TB)]
                else:
                    dst_t, dst = rinT, rinT.h[c - 40, :, sl(b, TB)]

                def epi(pst, dst=dst, dst_t=dst_t):
                    evac_to_dram(k, pst, 128, TB, st32, dst, dst_t, cnt[0])
                    cnt[0] += 1
                dense_fm(k, X, wr_fm, w_fm.h[c], 128, epi, [w_fm])

            def epi_dt(pst):
                evac_to_dram(k, pst, 64, TB, st32, dtT.h[:, sl(b, TB)], dtT, 0)
            dense_fm(k, X, wr_fm, w_dt.h, 64, epi_dt, [w_dt])
            for n in range(4):
                def epi(pst, tt, n=n):
                    t0 = b * TB + tt * 128
                    evac_to_dram(k, pst, 128, 512, st32, z_tok.h[t0:t0 + 128, n * 512:(n + 1) * 512], z_tok, cnt[0])
                    cnt[0] += 1
                dense_tm(k, X, wr_tm, w_tm.h[n], epi, [w_tm])


def conv4(k, c, t, cw, ch, L):
    k.op("dve", lambda e: e.tensor_scalar(out=c[:, 0:L], in0=t[:, 0:L], scalar1=cw[:, ch, 0:1], scalar2=cw[:, ch, 4:5], op0=ALU.mult, op1=ALU.add),
         reads=[t, cw], writes=[c])
    for j in (1, 2, 3):
        k.op("dve", lambda e, j=j: e.scalar_tensor_tensor(out=c[:, 0:L], in0=t[:, j:L + j], scalar=cw[:, ch, j:j + 1], in1=c[:, 0:L],
                                                          op0=ALU.mult, op1=ALU.add), reads=[t, cw, c], writes=[c])


def phase_lru(k, L, rinT, ginT, lcw_d, wax_d, lb_d, lam_d, mixT16):
    NT = L // 512
    with k.scope():
        lcw = k.sb([128, 16, 5], F32, "lcw")
        k.dma("sp", lcw[:], lcw_d.h[:, :, :], reads=[lcw_d], writes=[lcw])
        lb = k.sb([128, 2, 2, 16], F32, "lb")
        k.dma("sp", lb[:], lb_d.h[:, :, :, :], reads=[lb_d], writes=[lb])
        lam = k.sb([128, 32], F32, "lam")
        k.dma("sp", lam[:], lam_d.h.rearrange("p a b -> p (a b)"), reads=[lam_d], writes=[lam])
        nc8 = k.sb([128, 32], F32, "nc8")
        k.op("act", lambda e: e.activation(out=nc8[:], in_=lam[:], func=AF.Exp, scale=-1.0), reads=[lam], writes=[nc8])
        k.op("act", lambda e: e.activation(out=nc8[:], in_=nc8[:], func=AF.Ln, bias=1.0, scale=1.0), reads=[nc8], writes=[nc8])
        k.op("dve", lambda e: e.tensor_scalar(out=nc8[:], in0=nc8[:], scalar1=-8.0, scalar2=None, op0=ALU.mult), reads=[nc8], writes=[nc8])
        wr = Ring([k.sb([128, 2, 128], F32, "w") for _ in range(4)])
        padr = Ring([k.sb([128, L + 3], F32, "pad") for _ in range(2)])
        xcr = Ring([k.sb([128, L], F32, "xc") for _ in range(2)])
        hr = Ring([k.sb([128, L], F32, "h") for _ in range(3)])
        gr = Ring([k.sb([128, L], F32, "g") for _ in range(2)])
        o16 = Ring([k.sb([128, L], BF16, "o16") for _ in range(2)])
        G = Ring([k.sb([128, 512], F32, "G") for _ in range(12)])
        for cc in range(16):
            t = padr.next()
            k.op("pool", lambda e: e.memset(t[:, 0:1], 0.0), writes=[t])
            k.op("pool", lambda e: e.memset(t[:, L + 1:L + 3], 0.0), writes=[t], acc=True)
            k.dma("sp", t[:, 1:L + 1], rinT.h[cc], reads=[rinT], writes=[t], acc=True)
            xc = xcr.next()
            conv4(k, xc, t, lcw, cc, L)
            hs = []
            for d in range(2):
                w = wr.next()
                k.dma("sp", w[:], wax_d.h[d, :, cc].rearrange("a p c -> p a c"), reads=[wax_d], writes=[w])
                h = hr.next()
                tiles = range(NT) if d == 0 else range(NT - 1, -1, -1)
                prev = None
                for ti in tiles:
                    c = slice(ti * 512, (ti + 1) * 512)
                    pr, pi = k.psr.next(), k.psr.next()
                    k.op("pe", lambda e: e.matmul(pr[:, :], w[:, 0, :], xc[:, c], start=True, stop=True), reads=[w, xc], writes=[pr])
                    k.op("pe", lambda e: e.matmul(pi[:, :], w[:, 1, :], xc[:, c], start=True, stop=True), reads=[w, xc], writes=[pi])
                    r, ig = G.next(), G.next()
                    k.op("act", lambda e: e.activation(out=r[:], in_=pr[:, :], func=AF.Sigmoid, bias=lb[:, d, 0, cc:cc + 1], scale=1.0), reads=[pr, lb], writes=[r])
                    k.op("act", lambda e: e.activation(out=ig[:], in_=pi[:, :], func=AF.Sigmoid, bias=lb[:, d, 1, cc:cc + 1], scale=1.0), reads=[pi, lb], writes=[ig])
                    a = G.next()
                    k.op("act", lambda e: e.activation(out=a[:], in_=r[:], func=AF.Exp, scale=nc8[:, d * 16 + cc:d * 16 + cc + 1]), reads=[r, nc8], writes=[a])
                    om = G.next()
                    k.op("dve", lambda e: e.tensor_tensor(out=om[:], in0=a[:], in1=a[:], op=ALU.mult), reads=[a], writes=[om])
                    k.op("dve", lambda e: e.tensor_scalar(out=om[:], in0=om[:], scalar1=-1.0, scalar2=1.0, op0=ALU.mult, op1=ALU.add), reads=[om], writes=[om])
                    k.op("act", lambda e: e.activation(out=om[:], in_=om[:], func=AF.Sqrt), reads=[om], writes=[om])
                    k.op("pool", lambda e: e.tensor_tensor(out=ig[:], in0=ig[:], in1=xc[:, c], op=ALU.mult), reads=[ig, xc], writes=[ig])
                    k.op("pool", lambda e: e.tensor_tensor(out=om[:], in0=om[:], in1=ig[:], op=ALU.mult), reads=[om, ig], writes=[om])
                    if d == 0:
                        init = 0.0 if prev is None else h[:, prev * 512 + 511:prev * 512 + 512]
                        k.op("dve", lambda e: e.tensor_tensor_scan(out=h[:, c], data0=a[:], data1=om[:], initial=init, op0=ALU.mult, op1=ALU.add),
                             reads=[a, om, h], writes=[h], acc=(prev is not None))
                    else:
                        init = 0.0 if prev is None else h[:, prev * 512:prev * 512 + 1]
                        lo = ti * 512
                        rs = slice(lo + 511, lo - 1 if lo > 0 else None, -1)
                        k.op("dve", lambda e: e.tensor_tensor_scan(out=h[:, rs], data0=a[:, ::-1], data1=om[:, ::-1], initial=init, op0=ALU.mult, op1=ALU.add),
                             reads=[a, om, h], writes=[h], acc=(prev is not None))
                    prev = ti
                hs.append(h)
            g = gr.next()
            k.dma("sp", g[:], ginT.h[cc], reads=[ginT], writes=[g])
            u = hr.next()
            k.op("pool", lambda e: e.tensor_tensor(out=u[:], in0=g[:], in1=g[:], op=ALU.mult), reads=[g], writes=[u])
            k.op("dve", lambda e: e.tensor_scalar(out=u[:], in0=u[:], scalar1=0.044715, scalar2=1.0, op0=ALU.mult, op1=ALU.add), reads=[u], writes=[u])
            k.op("dve", lambda e: e.tensor_tensor(out=u[:], in0=u[:], in1=g[:], op=ALU.mult), reads=[u, g], writes=[u])
            k.op("act", lambda e: e.activation(out=u[:], in_=u[:], func=AF.Sigmoid, scale=1.5957691216057308), reads=[u], writes=[u])
            k.op("pool", lambda e: e.tensor_tensor(out=u[:], in0=u[:], in1=g[:], op=ALU.mult), reads=[u, g], writes=[u])
            k.op("dve", lambda e: e.tensor_tensor(out=hs[0][:], in0=hs[0][:], in1=hs[1][:], op=ALU.add), reads=[hs[0], hs[1]], writes=[hs[0]])
            o = o16.next()
            k.op("dve", lambda e: e.tensor_tensor(out=o[:], in0=hs[0][:], in1=u[:], op=ALU.mult), reads=[hs[0], u], writes=[o])
            k.dma("act", mixT16.h[16 + cc], o[:], reads=[o], writes=[mixT16], acc=True)


def ssd_consts(L):
    c = {}
    sel = np.zeros((64, 64, 128), np.float32)
    for h in range(64):
        sel[h, h, :] = 1.0
    c["ssd_sel"] = sel
    c["ssd_ones64"] = np.ones((64, 128), np.float32)
    t = np.arange(L)
    m = np.ones((64, L), np.float32)
    m[0:32, t % 128 == 0] = 0.0
    m[32:64, t % 128 == 127] = 0.0
    c["ssd_smask"] = m
    return c


def phase_ssd_prep(k, L, xbcT, scw_d, ident32_d, xs_tok, bm_tok, bmT16, cmT16):
    with k.scope():
        scw = k.sb([128, 24, 5], F32, "scw")
        k.dma("sp", scw[:], scw_d.h[:, :, :], reads=[scw_d], writes=[scw])
        ident32 = k.sb([128, 128], F32, "ident")
        k.dma("sp", ident32[:], ident32_d.h[:, :], reads=[ident32_d], writes=[ident32])
        padr = Ring([k.sb([128, L + 3], F32, "pad") for _ in range(2)])
        cr = Ring([k.sb([128, L], F32, "c") for _ in range(2)])
        c16 = Ring([k.sb([128, L], BF16, "c16") for _ in range(2)])
        st = Ring([k.sb([128, 4, 128], F32, "st") for _ in range(3)])
        for ch in range(24):
            t = padr.next()
            k.op("pool", lambda e: e.memset(t[:, 0:1], 0.0), writes=[t])
            k.op("pool", lambda e: e.memset(t[:, L + 1:L + 3], 0.0), writes=[t], acc=True)
            k.dma("sp", t[:, 1:L + 1], xbcT.h[ch], reads=[xbcT], writes=[t], acc=True)
            c = cr.next()
            conv4(k, c, t, scw, ch, L)
            k.op("act", lambda e: e.activation(out=c[:], in_=c[:], func=AF.Silu), reads=[c], writes=[c])
            if ch >= 16:
                s16 = c16.next()
                k.op("pool", lambda e: e.tensor_copy(s16[:], c[:]), reads=[c], writes=[s16])
                dst = bmT16 if ch < 20 else cmT16
                k.dma("act", dst.h[(ch - 16) % 4], s16[:], reads=[s16], writes=[dst], acc=True)
            if ch < 20:
                dst, col = (xs_tok, ch * 128) if ch < 16 else (bm_tok, (ch - 16) * 128)
                for g in range(L // 512):
                    p = k.psr.next()
                    for j in range(4):
                        k.op("pe", lambda e, j=j: e.transpose(p[:, sl(j)], c[:, g * 512 + j * 128:g * 512 + (j + 1) * 128], ident32[:]),
                             reads=[c, ident32], writes=[p], acc=(j > 0), last=(j == 3))
                    s = st.next()
                    if g % 2 == 0:
                        k.op("act", lambda e: e.copy(s.h[:].rearrange("p a c -> p (a c)"), p[:, :]), reads=[p], writes=[s])
                    else:
                        k.op("dve", lambda e: e.tensor_copy(s.h[:].rearrange("p a c -> p (a c)"), p[:, :]), reads=[p], writes=[s])
                    k.dma("act", dst.h[g * 512:(g + 1) * 512, col:col + 128].rearrange("(a p) c -> p a c", p=128), s[:],
                          reads=[s], writes=[dst], acc=True)


def phase_ssd(k, L, dtT, z_tok, xs_tok, bm_tok, bmT16, cmT16, yf_tok, P, mixT16):
    NCH = L // 128
    with k.scope():
        def ld(name, shape, q="sp", src=None):
            t = k.sb(shape, F32, name)
            k.dma(q, t[:], (P[name].h if src is None else src), reads=[P[name]], writes=[t])
            return t
        sel = ld("ssd_sel", [64, 64, 128])
        ones64 = ld("ssd_ones64", [64, 128])
        smask = ld("ssd_smask", [64, L], "act")
        cst = ld("gla_cst", [128, 4, 128], "act", P["gla_cst"].h.rearrange("a p c -> p a c"))
        ident32 = ld("ident32", [128, 128])
        dtb = ld("ssd_dtb", [64, 1])
        alog = ld("ssd_alog", [64, 1])
        ng = ld("ssd_ng", [128, 2048], "act")
        dsk0 = ld("ssd_dsk", [128, 2048], "sp", P["ssd_dsk"].h[0])
        dsk1 = ld("ssd_dsk", [128, 2048], "act", P["ssd_dsk"].h[1])
        k.op("pool", lambda e: e.tensor_tensor(out=dsk0[:], in0=dsk0[:], in1=dsk1[:], op=ALU.add), reads=[dsk0, dsk1], writes=[dsk0])
        epsr = k.sb([128, 1], F32, "epsr")
        k.op("dve", lambda e: e.memset(epsr[:], RMS_EPS), writes=[epsr])
        dt = k.sb([64, L], F32, "dt")
        k.dma("sp", dt[:], dtT.h[:, :], reads=[dtT], writes=[dt])
        k.op("act", lambda e: e.activation(out=dt[:], in_=dt[:], func=AF.Exp, bias=dtb[:, 0:1], scale=1.0), reads=[dt, dtb], writes=[dt])
        k.op("act", lambda e: e.activation(out=dt[:], in_=dt[:], func=AF.Ln, bias=1.0, scale=1.0), reads=[dt], writes=[dt])
        negA = k.sb([64, 1], F32, "negA")
        k.op("act", lambda e: e.activation(out=negA[:], in_=alog[:], func=AF.Exp), reads=[alog], writes=[negA])
        k.op("dve", lambda e: e.tensor_scalar(out=negA[:], in0=negA[:], scalar1=-1.0, scalar2=None, op0=ALU.mult), reads=[negA], writes=[negA])
        acT = k.sb([64, L], F32, "acT")
        nacT = k.sb([64, L], F32, "nacT")
        k.op("dve", lambda e: e.tensor_scalar(out=nacT[:], in0=dt[:], scalar1=negA[:, 0:1], scalar2=None, op0=ALU.mult), reads=[dt, negA], writes=[nacT])
        k.op("dve", lambda e: e.tensor_tensor_scan(out=acT[0:32, :], data0=smask[0:32, :], data1=nacT[0:32, :], initial=0.0, op0=ALU.mult, op1=ALU.add),
             reads=[smask, nacT], writes=[acT])
        k.op("dve", lambda e: e.tensor_tensor_scan(out=acT[32:64, ::-1], data0=smask[32:64, ::-1], data1=nacT[32:64, ::-1], initial=0.0,
                                                   op0=ALU.mult, op1=ALU.add), reads=[smask, nacT], writes=[acT], acc=True)
        k.op("dve", lambda e: e.tensor_scalar(out=nacT[:], in0=acT[:], scalar1=-1.0, scalar2=None, op0=ALU.mult), reads=[acT], writes=[nacT])
        ac_tok = k.sb([128, NCH, 64], F32, "ac_tok")
        dt_tok = k.sb([128, NCH, 64], F32, "dt_tok")
        eac = k.sb([128, NCH, 64], F32, "eac")
        for src, dst in ((acT, ac_tok), (dt, dt_tok)):
            for n0 in range(0, NCH, 4):
                p = k.psr.next()
                nn = min(4, NCH - n0)
                for j in range(nn):
                    k.op("pe", lambda e, j=j: e.transpose(p[:, j * 64:(j + 1) * 64], src[:, sl(n0 + j)], ident32[0:64, 0:64]),
                         reads=[src, ident32], writes=[p], acc=(j > 0), last=(j == nn - 1))
                k.op("dve", lambda e: e.tensor_copy(dst.h[:, n0:n0 + nn, :].rearrange("p a c -> p (a c)"), p[:, 0:nn * 64]), reads=[p], writes=[dst], acc=(n0 > 0))
        k.op("act", lambda e: e.activation(out=eac.h[:].rearrange("p a c -> p (a c)"), in_=ac_tok.h[:].rearrange("p a c -> p (a c)"), func=AF.Exp),
             reads=[ac_tok], writes=[eac])
        S32 = k.sb([128, 512], F32, "S32")
        S16 = k.sb([128, 512], BF16, "S16")
        xsr = Ring([k.sb([128, 512], F32, "xs") for _ in range(2)])
        bmr = Ring([k.sb([128, 128], F32, "bm") for _ in range(2)])
        bm16r = Ring([k.sb([128, 128], BF16, "bm16") for _ in range(2)])
        bTr = Ring([k.sb([128, 128], BF16, "bT") for _ in range(2)])
        cTr = Ring([k.sb([128, 128], BF16, "cT") for _ in range(2)])
        cbr = Ring([k.sb([128, 128], F32, "cbm") for _ in range(2)])
        Dr = Ring([k.sb([128, 512], F32, "Dm") for _ in range(2)])
        lmr = Ring([k.sb([128, 4, 128], F32, "lm") for _ in range(2)])
        Mr = Ring([k.sb([128, 4, 128], BF16, "M") for _ in range(2)])
        xdtr = Ring([k.sb([128, 512], BF16, "xdt") for _ in range(2)])
        xddr = Ring([k.sb([128, 512], BF16, "xdd") for _ in range(2)])
        O = Ring([k.sb([128, 512], F32, "O") for _ in range(6)])
        sm = Ring([k.sb([128, 8], F32, "sm") for _ in range(6)])
        Xr = Ring([k.sb([64, 8], F32, "X") for _ in range(2)])
        tr16 = Ring([k.sb([128, 4, 128], BF16, "tr") for _ in range(2)])
        zr = Ring([k.sb([128, 512], F32, "z") for _ in range(2)])
        yfr = Ring([k.sb([128, 512], F32, "yf") for _ in range(2)])
        for g in range(4):
            gc = slice(g * 512, (g + 1) * 512)
            for d in range(2):
                row0 = d * 32 + g * 8
                mask = cst[:, d, :]
                lastc = 127 if d == 0 else 0
                k.op("dve", lambda e: e.memset(S32[:], 0.0), writes=[S32])
                k.op("pool", lambda e: e.memset(S16[:], 0.0), writes=[S16])
                order = range(NCH) if d == 0 else range(NCH - 1, -1, -1)
                for n in order:
                    ts = slice(n * 128, (n + 1) * 128)
                    xs, bmt, bT, cT = xsr.next(), bmr.next(), bTr.next(), cTr.next()
                    k.dma("sp", xs[:], xs_tok.h[ts, gc], reads=[xs_tok], writes=[xs])
                    k.dma("sp", bmt[:], bm_tok.h[ts, sl(g)], reads=[bm_tok], writes=[bmt])
                    k.dma("sp", bT[:], bmT16.h[g, :, ts], reads=[bmT16], writes=[bT])
                    k.dma("sp", cT[:], cmT16.h[g, :, ts], reads=[cmT16], writes=[cT])
                    bm16 = bm16r.next()
                    k.op("pool", lambda e: e.tensor_copy(bm16[:], bmt[:]), reads=[bmt], writes=[bm16])
                    p = k.psr.next()
                    k.op("pe", lambda e: e.matmul(p[:, 0:128], bT[:], cT[:], start=True, stop=True), reads=[bT, cT], writes=[p])
                    cbm = cbr.next()
                    k.op("dve", lambda e: e.tensor_tensor(out=cbm[:], in0=p[:, 0:128], in1=mask, op=ALU.mult), reads=[p, cst], writes=[cbm])
                    lms, Ms = [], []
                    for half in range(2):
                        pD = k.psr.next()
                        for q in range(4):
                            row = row0 + half * 4 + q
                            k.op("pe", lambda e, q=q, row=row: e.matmul(pD[:, sl(q)], sel[:, row, :], acT[:, ts], start=True, stop=False),
                                 reads=[sel, acT], writes=[pD], acc=True, last=False)
                            k.op("pe", lambda e, q=q, row=row: e.matmul(pD[:, sl(q)], nacT[:, ts], sel[:, row, :], start=False, stop=True),
                                 reads=[sel, nacT], writes=[pD], acc=True, last=(q == 3))
                        Dm = Dr.next()
                        k.op("dve", lambda e: e.tensor_scalar(out=Dm[:], in0=pD[:, :], scalar1=0.0, scalar2=None, op0=ALU.min), reads=[pD], writes=[Dm])
                        lm = lmr.next()
                        k.op("act", lambda e: e.activation(out=lm.h[:].rearrange("p a c -> p (a c)"), in_=Dm[:], func=AF.Exp), reads=[Dm], writes=[lm])
                        M = Mr.next()
                        for q in range(4):
                            k.op("pool" if q % 2 else "dve", lambda e, q=q: e.tensor_tensor(out=M[:, q, :], in0=lm[:, q, :], in1=cbm[:], op=ALU.mult),
                                 reads=[lm, cbm], writes=[M], acc=(q > 0))
                        lms.append(lm)
                        Ms.append(M)
                    xdt, xdd = xdtr.next(), xddr.next()
                    for h in range(8):
                        hc = slice(h * 64, (h + 1) * 64)
                        dts = dt_tok[:, n, row0 + h:row0 + h + 1]
                        k.op("pool", lambda e, hc=hc, dts=dts: e.tensor_scalar(out=xdt[:, hc], in0=xs[:, hc], scalar1=dts, scalar2=1.0, op0=ALU.mult, op1=ALU.mult),
                             reads=[xs, dt_tok], writes=[xdt], acc=(h > 0))
                        dss = lms[h // 4][:, h % 4, lastc:lastc + 1]
                        k.op("dve", lambda e, hc=hc, dts=dts, dss=dss: e.tensor_scalar(out=xdd[:, hc], in0=xs[:, hc], scalar1=dts, scalar2=dss, op0=ALU.mult, op1=ALU.mult),
                             reads=[xs, dt_tok, lms[h // 4]], writes=[xdd], acc=(h > 0))
                    pY = k.psr.next()
                    for h in range(8):
                        k.op("pe", lambda e, h=h: e.matmul(pY[:, h * 64:(h + 1) * 64], Ms[h // 4][:, h % 4, :], xdt[:, h * 64:(h + 1) * 64], start=True, stop=True),
                             reads=[Ms[h // 4], xdt], writes=[pY], acc=True, last=(h == 7))
                    pO = k.psr.next()
                    k.op("pe", lambda e: e.matmul(pO[:, :], cT[:], S16[:], start=True, stop=True), reads=[cT, S16], writes=[pO])
                    pS = k.psr.next()
                    k.op("pe", lambda e: e.matmul(pS[:, :], bm16[:], xdd[:], start=True, stop=True), reads=[bm16, xdd], writes=[pS])
                    tl = n * 128 + lastc
                    X = Xr.next()
                    k.op("pool", lambda e: e.tensor_scalar(out=X[:], in0=sel[:, row0:row0 + 8, 0], scalar1=acT[:, tl:tl + 1], scalar2=1.0, op0=ALU.mult, op1=ALU.mult),
                         reads=[sel, acT], writes=[X])
                    pC = k.psr.next()
                    k.op("pe", lambda e: e.matmul(pC[:, 0:8], ones64[:], X[:], start=True, stop=True), reads=[ones64, X], writes=[pC])
                    cd = sm.next()
                    k.op("act", lambda e: e.activation(out=cd[:], in_=pC[:, 0:8], func=AF.Exp), reads=[pC], writes=[cd])
                    yd = O.next()
                    k.op("act", lambda e: e.copy(yd[:], pY[:, :]), reads=[pY], writes=[yd])
                    y = O.next()
                    for h in range(8):
                        hc = slice(h * 64, (h + 1) * 64)
                        k.op("dve", lambda e, hc=hc, h=h: e.scalar_tensor_tensor(out=y[:, hc], in0=pO[:, hc], scalar=eac[:, n, row0 + h:row0 + h + 1], in1=yd[:, hc],
                                                                              op0=ALU.mult, op1=ALU.add), reads=[pO, eac, yd], writes=[y], acc=(h > 0))
                    for h in range(8):
                        hc = slice(h * 64, (h + 1) * 64)
                        k.op("dve", lambda e, hc=hc, h=h: e.scalar_tensor_tensor(out=S32[:, hc], in0=S32[:, hc], scalar=cd[:, h:h + 1], in1=pS[:, hc],
                                                                              op0=ALU.mult, op1=ALU.add), reads=[S32, cd, pS], writes=[S32], acc=(h > 0))
                    k.op("act", lambda e: e.copy(S16[:], S32[:]), reads=[S32], writes=[S16])
                    if d == 0:
                        k.dma("act", yf_tok.h[ts, gc], y[:], reads=[y], writes=[yf_tok], acc=True)
                    else:
                        yf, zt = yfr.next(), zr.next()
                        k.dma("sp", yf[:], yf_tok.h[ts, gc], reads=[yf_tok], writes=[yf])
                        k.dma("sp", zt[:], z_tok.h[ts, gc], reads=[z_tok], writes=[zt])
                        k.op("pool", lambda e: e.tensor_tensor(out=y[:], in0=y[:], in1=yf[:], op=ALU.add), reads=[y, yf], writes=[y])
                        t2 = O.next()
                        k.op("pool", lambda e: e.tensor_tensor(out=t2[:], in0=xs[:], in1=dsk0[:, gc], op=ALU.mult), reads=[xs, dsk0], writes=[t2])
                        k.op("dve", lambda e: e.tensor_tensor(out=y[:], in0=y[:], in1=t2[:], op=ALU.add), reads=[y, t2], writes=[y])
                        sz = O.next()
                        k.op("act", lambda e: e.activation(out=sz[:], in_=zt[:], func=AF.Silu), reads=[zt], writes=[sz])
                        k.op("pool", lambda e: e.tensor_tensor(out=y[:], in0=y[:], in1=sz[:], op=ALU.mult), reads=[y, sz], writes=[y])
                        ss = sm.next()
                        k.op("act", lambda e: e.activation(out=sz[:], in_=y[:], func=AF.Square, accum_out=ss[:, 0:1]), reads=[y], writes=[sz, ss])
                        k.op("act", lambda e: e.activation(out=ss[:, 0:1], in_=ss[:, 0:1], func=AF.Sqrt, bias=epsr[:, 0:1], scale=1.0 / 512), reads=[ss, epsr], writes=[ss])
                        k.op("dve", lambda e: e.reciprocal(out=ss[:, 0:1], in_=ss[:, 0:1]), reads=[ss], writes=[ss])
                        on = O.next()
                        k.op("dve", lambda e: e.scalar_tensor_tensor(out=on[:], in0=y[:], scalar=ss[:, 0:1], in1=ng[:, gc], op0=ALU.mult, op1=ALU.mult),
                             reads=[y, ss, ng], writes=[on])
                        p7 = k.psr.next()
                        for j in range(4):
                            k.op("pe", lambda e, j=j: e.transpose(p7[:, sl(j)], on[:, sl(j)], ident32[:]), reads=[on, ident32], writes=[p7], acc=(j > 0), last=(j == 3))
                        t16 = tr16.next()
                        k.op("dve", lambda e: e.tensor_copy(t16.h[:].rearrange("p j c -> p (j c)"), p7[:, :]), reads=[p7], writes=[t16])
                        k.dma("act", mixT16.h[4 * g:4 * g + 4, :, ts].rearrange("j p c -> p j c"), t16[:], reads=[t16], writes=[mixT16], acc=True)


L_FULL = 4096


def host_params(inp, L=None):
    L_FULL = L or 4096
    f = lambda a: np.ascontiguousarray(np.asarray(a, dtype=np.float32))
    P = {}
    P["ident32"] = np.eye(128, dtype=np.float32)
    P["ones32"] = np.ones((128, 128), np.float32)
    P["gla_cst"] = gla_consts()
    P.update(hy_consts(L_FULL))
    P.update(ssd_consts(L_FULL))
    W = np.asarray(inp["ev_w_in"][0], np.float32)
    Wq, Wk, Wv, Wog, Wlr, Why = W[:, :1024], W[:, 1024:2048], W[:, 2048:4096], W[:, 4096:6144], W[:, 6144:6176], W[:, 6176:]
    P["ev_fm"] = tile_fm(np.concatenate([Wq, Wk, Why], 1))
    P["ev_lr"] = f(Wlr.reshape(KC, 128, 32).transpose(1, 0, 2))
    P["ev_tm"] = tile_tm(np.concatenate([Wk, Wv, Wog], 1))
    P["gla_wg"] = f(np.stack([np.concatenate([inp["ev_gla_wg_f"][0], inp["ev_gla_bg_f"][0][None]], 0),
                              np.concatenate([inp["ev_gla_wg_b"][0], inp["ev_gla_bg_b"][0][None]], 0)]))
    P["gla_gn"] = f(np.tile(np.asarray(inp["ev_gla_norm"][0])[None], (128, 1)))
    P["hy_w1a"] = f(np.concatenate([inp["ev_hy_w1"][0], inp["ev_hy_b1"][0][None]], 0))
    P["hy_w2"] = f(inp["ev_hy_w2"][0])
    P["hy_b2"] = f(np.asarray(inp["ev_hy_b2"][0])[:, None])
    P["hy_w3"] = f(inp["ev_hy_w3"][0])
    hcw = np.zeros((128, 48, 4), np.float32)
    hcw[:, :, 0:3] = np.asarray(inp["ev_hy_conv_w"][0]).T.reshape(48, 128, 3).transpose(1, 0, 2)
    hcw[:, :, 3] = np.asarray(inp["ev_hy_conv_b"][0]).reshape(48, 128).T
    P["hy_cw"] = hcw
    P["hy_bias"] = f(np.asarray(inp["ev_hy_bias"][0]).reshape(16, 128).T)
    P["ev_w_out"] = tile_fm(np.asarray(inp["ev_w_out"][0], np.float32))
    W = np.asarray(inp["od_w_in"][0], np.float32)
    P["od_fm"] = tile_fm(np.concatenate([W[:, 2048:5120], W[:, 5184:7232], W[:, 7232:9280]], 1))
    P["od_dt"] = f(W[:, 5120:5184].reshape(KC, 128, 64).transpose(1, 0, 2))
    P["od_tm"] = tile_tm(W[:, 0:2048])
    scw = np.zeros((128, 24, 5), np.float32)
    scw[:, :, 0:4] = np.asarray(inp["od_ssd_conv_w"][0]).T.reshape(24, 128, 4).transpose(1, 0, 2)
    scw[:, :, 4] = np.asarray(inp["od_ssd_conv_b"][0]).reshape(24, 128).T
    P["ssd_scw"] = scw
    P["ssd_dtb"] = f(np.asarray(inp["od_ssd_dt_bias"][0]).reshape(64, 1))
    P["ssd_alog"] = f(np.asarray(inp["od_ssd_a_log"][0]).reshape(64, 1))
    P["ssd_dsk"] = f(np.tile(np.repeat(np.asarray(inp["od_ssd_d"][0]), 64, axis=1)[:, None, :], (1, 128, 1)))
    P["ssd_ng"] = f(np.tile(np.asarray(inp["od_ssd_norm"][0])[None], (128, 1)))
    lcw = np.zeros((128, 16, 5), np.float32)
    lcw[:, :, 0:4] = np.asarray(inp["od_lru_conv_w"][0]).T.reshape(16, 128, 4).transpose(1, 0, 2)
    lcw[:, :, 4] = np.asarray(inp["od_lru_conv_b"][0]).reshape(16, 128).T
    P["lru_cw"] = lcw
    P["lru_wax"] = f(np.stack([inp["od_lru_wa"][0], inp["od_lru_wx"][0]], 1))
    P["lru_b"] = f(np.stack([np.asarray(inp["od_lru_ba"][0]).reshape(2, 16, 128), np.asarray(inp["od_lru_bx"][0]).reshape(2, 16, 128)], 1).transpose(3, 0, 1, 2))
    P["lru_lam"] = f(np.asarray(inp["od_lru_lambda"][0]).reshape(2, 16, 128).transpose(2, 0, 1))
    P["od_w_out"] = tile_fm(np.asarray(inp["od_w_out"][0], np.float32))
    cw = np.zeros((128, 2, 2 * FC, 4), np.float32)
    for i in range(2):
        P["w_up%d" % i] = tile_fm(np.asarray(inp["ffn_w_up"][i], np.float32))
        P["w_dn%d" % i] = f(np.asarray(inp["ffn_w_down"][i], np.float32).reshape(FC, 128, KC, 128).transpose(2, 1, 0, 3).reshape(KC, 128, FC * 128))
        cw[:, i, :, 0:3] = np.asarray(inp["ffn_conv_w"][i]).T.reshape(2 * FC, 128, 3).transpose(1, 0, 2)
        cw[:, i, :, 3] = np.asarray(inp["ffn_conv_b"][i]).reshape(2 * FC, 128).T
    P["ffn_cw"] = cw
    g = np.stack([inp["ln1_g"][0], inp["ln2_g"][0], inp["ln1_g"][1], inp["ln2_g"][1]])
    b = np.stack([inp["ln1_b"][0], inp["ln2_b"][0], inp["ln1_b"][1], inp["ln2_b"][1]])
    P["gtab"] = f(np.asarray(g).reshape(4, KC, 128).transpose(2, 0, 1).reshape(128, 4 * KC))
    P["btab"] = f(np.asarray(b).reshape(4, KC, 128).transpose(2, 0, 1).reshape(128, 4 * KC))
    return {n: f(a) for n, a in P.items()}


def build_program(L, pshapes):
    nc = bass.Bass("TRN2", target_bir_lowering=False)
    with contextlib.ExitStack() as es:
        k = KB(nc, es)
        k.allps = [k.ps() for _ in range(8)]
        k.psr8 = Ring(k.allps)
        k.psr4 = Ring(k.allps[0:6])
        k.ps_sum, k.ps_sq = k.allps[6], k.allps[7]
        k.psr = k.psr8
        x_in = k.dram("x", [L, D], F32, kind="ExternalInput")
        y_out = k.dram("y", [L, D], F32, kind="ExternalOutput")
        P = {n: k.dram(n, list(s), F32, kind="ExternalInput") for n, s in pshapes.items()}
        xA32, xB32, zT = (k.dram(n, [KC, 128, L], F32) for n in ("xA32", "xB32", "zT"))
        xA16, xB16, mixT16 = (k.dram(n, [KC, 128, L], BF16) for n in ("xA16", "xB16", "mixT16"))
        wc_up = k.dram("wc_up", [2 * FC, 128, KC * 128], BF16)
        wc_dn = k.dram("wc_dn", [KC, 128, FC * 128], BF16)
        ident32 = k.sb([128, 128], F32, "ident0")
        k.dma("sp", ident32[:], P["ident32"].h[:, :], reads=[P["ident32"]], writes=[ident32])
        phase_prepass(k, x_in, xA32, xA16, L, ident32)
        qT, kT = k.dram("qT", [8, 128, L], F32), k.dram("kT", [8, 128, L], F32)
        lrT, hyT = k.dram("lrT", [32, L], F32), k.dram("hyT", [48, 128, L], F32)
        k_tok, v_tok, og_tok = k.dram("k_tok", [L, 1024], F32), k.dram("v_tok", [L, 2048], BF16), k.dram("og_tok", [L, 2048], F32)
        of_tok = k.dram("of_tok", [L, 2048], F32)
        phase_even_inproj(k, xA16, L, P["ev_fm"], P["ev_lr"], P["ev_tm"], qT, kT, lrT, hyT, k_tok, v_tok, og_tok)
        phase_gla(k, L, qT, kT, k_tok, v_tok, og_tok, lrT, P["gla_wg"], P["gla_gn"], P["gla_cst"], P["ident32"], of_tok, mixT16)
        hyS = hy_scratch(k, L)
        phase_hyena(k, L, hyT, P, hyS, mixT16)
        k.psr = k.psr4
        phase_outproj_ln(k, mixT16, L, P["ev_w_out"], xA32, xB32, xB16, zT, P["ones32"], P["gtab"], P["btab"], 0)
        phase_ffn(k, xB16, L, P["w_up0"], P["w_dn0"], P["ffn_cw"], xB32, xA32, xA16, zT, P["ones32"], P["gtab"], P["btab"], 1, 0, wc_up, wc_dn)
        k.psr = k.psr8
        xbcT, ginT, dtT = Tk(hyT.h[0:24]), Tk(hyT.h[24:40]), k.dram("dtT", [64, L], F32)
        rinT, z_tok = Tk(hyS["uT"].h), Tk(og_tok.h)
        xs_tok, bm_tok = Tk(of_tok.h), Tk(k_tok.h[:, 0:512])
        bmT16, cmT16, yf_tok = k.dram("bmT16", [4, 128, L], BF16), k.dram("cmT16", [4, 128, L], BF16), Tk(hyS["u_tok"].h)
        phase_odd_inproj(k, xA16, L, P["od_fm"], P["od_dt"], P["od_tm"], xbcT, dtT, ginT, rinT, z_tok)
        phase_ssd_prep(k, L, xbcT, P["ssd_scw"], P["ident32"], xs_tok, bm_tok, bmT16, cmT16)
        phase_ssd(k, L, dtT, z_tok, xs_tok, bm_tok, bmT16, cmT16, yf_tok, P, mixT16)
        phase_lru(k, L, rinT, ginT, P["lru_cw"], P["lru_wax"], P["lru_b"], P["lru_lam"], mixT16)
        k.psr = k.psr4
        phase_outproj_ln(k, mixT16, L, P["od_w_out"], xA32, xB32, xB16, zT, P["ones32"], P["gtab"], P["btab"], 2)
        phase_ffn(k, xB16, L, P["w_up1"], P["w_dn1"], P["ffn_cw"], xB32, None, None, zT, P["ones32"], P["gtab"], P["btab"], 3, 1, wc_up, wc_dn,
                  out_tok=y_out, ident32_d=P["ident32"])
        k.finish([y_out])
        stats = dict(k.ninst)
        stats["nsem"] = k.nsem
    return nc, stats


def kernel(**inputs):
    L = L_FULL
    xp = np.asarray(inputs["x_prompt"], np.float32)
    xs = np.asarray(inputs["x_sample"], np.float32)
    seqs = [xp[0], xp[1], xs[0], xs[1], xs[2], xs[3]]
    P = host_params(inputs)
    nc, stats = build_program(L, {n: a.shape for n, a in P.items()})
    in_maps = []
    for c in range(6):
        m = dict(P)
        m["x"] = np.ascontiguousarray(seqs[c])
        in_maps.append(m)
    res = run_bass_kernel_spmd(nc, in_maps, core_ids=list(range(6)))
    outs = [np.asarray(res.results[c]["y"], np.float32) for c in range(6)]
    return (np.stack(outs[0:2]), np.stack(outs[2:6]))
```

```python
import contextlib
import numpy as np
import concourse.bass as bass
import concourse.mybir as mybir
from concourse.bass_utils import run_bass_kernel_spmd

F32 = mybir.dt.float32
BF16 = mybir.dt.bfloat16
AF = mybir.ActivationFunctionType
ALU = mybir.AluOpType

D = 4096
KC = 32
TB = 512
FF = 11008
FC = 86
NCORES = 8
ALPHA = 4.0 ** 0.25
LN_EPS = 1e-5
RMS_EPS = 1e-6


class Tk:
    __slots__ = ("h", "w", "r", "pr")

    def __init__(self, h):
        self.h = h
        self.w = {}
        self.r = {}
        self.pr = {}

    def __getitem__(self, idx):
        return self.h[idx]


class KB:
    SEM_LIMIT = 30000
    ND = 6

    def __init__(self, nc, es):
        self.nc = nc
        self.es = es
        self.eng = {"pe": nc.tensor, "act": nc.scalar, "dve": nc.vector, "pool": nc.gpsimd, "sp": nc.sync}
        self.sem = {}
        self.cnt = {}
        self.pe_sems = set()
        self.known = {e: {} for e in self.eng}
        self.nsem = 0
        self.prev = {}
        self.es2 = None
        for e in ("pe", "act", "dve", "pool"):
            self._new_sem(e)
        self.dq = {}
        for q in ("sp", "act", "pool"):
            self.dq[q] = {"sems": [self._alloc_sem() for _ in range(self.ND)], "vals": [0] * self.ND, "i": 0}
        self.ninst = {e: 0 for e in self.eng}
        self.uid = 0

    def _alloc_sem(self):
        self.nsem += 1
        return self.es.enter_context(self.nc.semaphore("s%d" % self.nsem))

    def _new_sem(self, e):
        if e in self.sem:
            self.prev.setdefault(e, []).append((self.sem[e], self.cnt[e]))
        s = self._alloc_sem()
        self.sem[e] = s
        self.cnt[e] = 0
        if e == "pe":
            self.pe_sems.add(s)

    def sb(self, shape, dt, name=None):
        self.uid += 1
        es = self.es2 if getattr(self, "es2", None) is not None else self.es
        h = es.enter_context(self.nc.sbuf_tensor("%s%d" % (name or "sb", self.uid), list(shape), dt))
        return Tk(h)

    def ps(self, shape=(128, 512), dt=F32, name=None):
        self.uid += 1
        h = self.es.enter_context(self.nc.psum_tensor("%s%d" % (name or "ps", self.uid), list(shape), dt))
        return Tk(h)

    def dram(self, name, shape, dt, kind="Internal"):
        h = self.nc.dram_tensor(name, list(shape), dt, kind=kind).ap()
        return Tk(h)

    def _wait(self, e, deps):
        kn = self.known[e]
        eo = self.eng[e]
        for s, v in deps.items():
            if kn.get(s, 0) < v:
                eo.wait_ge(s, v)
                kn[s] = v
                self.ninst[e] += 1

    @staticmethod
    def _merge(d, o):
        for s, v in o.items():
            if d.get(s, 0) < v:
                d[s] = v

    def op(self, e, fn, reads=(), writes=(), acc=False, last=True):
        deps = {}
        for t in reads:
            self._merge(deps, t.w)
        for t in writes:
            self._merge(deps, t.r)
            if not acc:
                self._merge(deps, t.w)
            else:
                self._merge(deps, t.pr)
        if e == "pe":
            for s in list(deps):
                if s in self.pe_sems:
                    del deps[s]
        self._wait(e, deps)
        ins = fn(self.eng[e])
        self.ninst[e] += 1
        if e == "pe" and not last:
            tok = (self.sem[e], self.cnt[e] + 1)
        else:
            ins.then_inc(self.sem[e], 1)
            self.cnt[e] += 1
            tok = (self.sem[e], self.cnt[e])
            if self.cnt[e] >= self.SEM_LIMIT:
                self._new_sem(e)
        s, v = tok
        for t in reads:
            if t.r.get(s, 0) < v:
                t.r[s] = v
        for t in writes:
            if acc:
                if t.w.get(s, 0) < v:
                    t.w[s] = v
            else:
                pr = dict(t.w)
                self._merge(pr, t.r)
                t.pr = pr
                t.w = {s: v}
                t.r = {}
        return ins

    def dma(self, q, out_ap, in_ap, reads=(), writes=(), acc=False, **kw):
        dq = self.dq[q]
        slot = dq["i"] % self.ND
        dq["i"] += 1
        if dq["vals"][slot] >= 60000:
            dq["sems"][slot] = self._alloc_sem()
            dq["vals"][slot] = 0
        s = dq["sems"][slot]
        deps = {}
        if dq["vals"][slot] > 0:
            deps[s] = dq["vals"][slot]
        for t in reads:
            self._merge(deps, t.w)
        for t in writes:
            self._merge(deps, t.r)
            if not acc:
                self._merge(deps, t.w)
            else:
                self._merge(deps, t.pr)
        self._wait(q, deps)
        ins = self.eng[q].dma_start(out=out_ap, in_=in_ap, **kw)
        ins.then_inc(s, 16)
        self.ninst[q] += 1
        dq["vals"][slot] += 16
        v = dq["vals"][slot]
        for t in reads:
            if t.r.get(s, 0) < v:
                t.r[s] = v
        for t in writes:
            if acc:
                if t.w.get(s, 0) < v:
                    t.w[s] = v
            else:
                pr = dict(t.w)
                self._merge(pr, t.r)
                t.pr = pr
                t.w = {s: v}
                t.r = {}
        return ins

    def barrier(self):
        deps = {}
        for e in ("pe", "act", "dve", "pool"):
            if self.cnt[e] > 0:
                deps[self.sem[e]] = self.cnt[e]
            elif self.prev.get(e):
                ps_, pv_ = self.prev[e][-1]
                deps[ps_] = pv_
        for q in self.dq:
            for s, v in zip(self.dq[q]["sems"], self.dq[q]["vals"]):
                if v > 0:
                    deps[s] = v
        for e in self.eng:
            self._wait(e, dict(deps))

    @contextlib.contextmanager
    def scope(self):
        old = self.es
        with contextlib.ExitStack() as es:
            self.es2 = es
            yield
            self.barrier()
        self.es2 = None

    def finish(self, outs):
        deps = {}
        for t in outs:
            self._merge(deps, t.w)
        self._wait("sp", deps)
        for q in self.dq:
            dd = {}
            for s, v in zip(self.dq[q]["sems"], self.dq[q]["vals"]):
                if v > 0:
                    dd[s] = v
            self._wait("sp", dd)


class Ring:
    def __init__(self, items):
        self.items = items
        self.i = 0

    def next(self):
        t = self.items[self.i % len(self.items)]
        self.i += 1
        return t


def sl(i, n=128):
    return slice(i * n, (i + 1) * n)


def phase_prepass(k, x_in, xT32, xT16, L, ident32):
    nb = L // TB
    with k.scope():
        xin = Ring([[k.sb([128, D], F32, "xin") for _ in range(4)] for _ in range(2)])
        st32 = Ring([k.sb([128, TB], F32, "st32") for _ in range(3)])
        st16 = Ring([k.sb([128, TB], BF16, "st16") for _ in range(3)])
        for b in range(nb):
            tiles = xin.next()
            for tt in range(4):
                t0 = b * TB + tt * 128
                k.dma("sp", tiles[tt][:], x_in[t0:t0 + 128, :], reads=[x_in], writes=[tiles[tt]])
            for dc in range(KC):
                pst = k.psr.next()
                for tt in range(4):
                    k.op("pe", lambda e, tt=tt: e.transpose(pst[:, sl(tt)], tiles[tt][:, sl(dc)], ident32[:]),
                         reads=[tiles[tt], ident32], writes=[pst], acc=(tt > 0), last=(tt == 3))
                s32 = st32.next()
                s16 = st16.next()
                k.op("dve", lambda e: e.tensor_copy(s32[:], pst[:]), reads=[pst], writes=[s32])
                k.op("act", lambda e: e.copy(s16[:], s32[:]), reads=[s32], writes=[s16])
                k.dma("act", xT32[dc, :, sl(b, TB)], s32[:], reads=[s32], writes=[xT32], acc=True)
                k.dma("act", xT16[dc, :, sl(b, TB)], s16[:], reads=[s16], writes=[xT16], acc=True)


def load_X(k, X, xT16, b, L):
    nb = L // TB
    lo = b * TB - 1
    hi = b * TB + TB + 1
    c0, c1 = 0, TB + 2
    if b == 0:
        lo, c0 = 0, 1
    if b == nb - 1:
        hi, c1 = L, TB + 1
    for g in range(4):
        if b == 0:
            k.op("pool", lambda e: e.memset(X[g][:, :, 0:1], 0.0), writes=[X[g]])
        if b == nb - 1:
            k.op("pool", lambda e: e.memset(X[g][:, :, TB + 1:TB + 2], 0.0), writes=[X[g]], acc=(b == 0))
        src = xT16.h[g * 8:(g + 1) * 8, :, lo:hi].rearrange("c p t -> p c t")
        k.dma("sp", X[g][:, :, c0:c1], src, reads=[xT16], writes=[X[g]],
              acc=(b == 0 or b == nb - 1))


def dense_fm(k, X, wring, w_ap, m, epi, reads_w, after_mm=None):
    wt = wring.next()
    k.dma("pool", wt[:, :, 0:m], w_ap, reads=reads_w, writes=[wt], max_dma_last_dim=4096)
    pst = k.psr.next()
    for kc in range(KC):
        k.op("pe", lambda e, kc=kc: e.matmul(pst[0:m, :], wt[:, kc, 0:m], X[kc // 8][:, kc % 8, 1:TB + 1],
                                             start=(kc == 0), stop=(kc == KC - 1)),
             reads=[wt, X[kc // 8]], writes=[pst], acc=(kc > 0), last=(kc == KC - 1))
    if after_mm is not None:
        after_mm()
    epi(pst)


def dense_tm(k, X, wring, w_ap, epi, reads_w, n=512):
    wt = wring.next()
    k.dma("pool", wt[:, 0:16, 0:n], w_ap[:, 0:16, :], reads=reads_w, writes=[wt], max_dma_last_dim=4096)
    k.dma("pool", wt[:, 16:32, 0:n], w_ap[:, 16:32, :], reads=reads_w, writes=[wt], acc=True, max_dma_last_dim=4096)
    for tt in range(4):
        pst = k.psr.next()
        for kc in range(KC):
            k.op("pe", lambda e, kc=kc: e.matmul(pst[:, 0:n], X[kc // 8][:, kc % 8, 1 + tt * 128:1 + (tt + 1) * 128], wt[:, kc, 0:n],
                                                 start=(kc == 0), stop=(kc == KC - 1)),
                 reads=[wt, X[kc // 8]], writes=[pst], acc=(kc > 0), last=(kc == KC - 1))
        epi(pst, tt)


def evac_to_dram(k, pst, m, ncol, stage_ring, dst_ap, dst_t, i, in_ap=None):
    st = stage_ring.next()
    src = pst[0:m, 0:ncol] if in_ap is None else in_ap
    if i % 2 == 0:
        k.op("act", lambda e: e.copy(st[0:m, 0:ncol], src), reads=[pst], writes=[st])
    else:
        k.op("dve", lambda e: e.tensor_copy(st[0:m, 0:ncol], src), reads=[pst], writes=[st])
    k.dma("sp" if i % 2 == 0 else "act", dst_ap, st[0:m, 0:ncol], reads=[st], writes=[dst_t], acc=True)


def phase_even_inproj(k, xT16, L, w_fm, w_lr, w_tm, qT, kT, lrT, hyT, k_tok, v_tok, og_tok):
    nb = L // TB
    with k.scope():
        X = [k.sb([128, 8, TB + 2], BF16, "X") for _ in range(4)]
        wr_fm = Ring([k.sb([128, KC, 128], BF16, "wfm") for _ in range(3)])
        wr_tm = Ring([k.sb([128, KC, 512], BF16, "wtm") for _ in range(2)])
        st32 = Ring([k.sb([128, TB], F32, "st32") for _ in range(4)])
        st16 = Ring([k.sb([128, TB], BF16, "st16") for _ in range(2)])
        for b in range(nb):
            load_X(k, X, xT16, b, L)
            cnt = [0]
            for c in range(64):
                if c < 8:
                    dst_t, dst = qT, qT.h[c, :, sl(b, TB)]
                elif c < 16:
                    dst_t, dst = kT, kT.h[c - 8, :, sl(b, TB)]
                else:
                    dst_t, dst = hyT, hyT.h[c - 16, :, sl(b, TB)]

                def epi(pst, dst=dst, dst_t=dst_t):
                    evac_to_dram(k, pst, 128, TB, st32, dst, dst_t, cnt[0])
                    cnt[0] += 1
                dense_fm(k, X, wr_fm, w_fm.h[c], 128, epi, [w_fm])

            def epi_lr(pst):
                evac_to_dram(k, pst, 32, TB, st32, lrT.h[:, sl(b, TB)], lrT, 0)
            dense_fm(k, X, wr_fm, w_lr.h, 32, epi_lr, [w_lr])
            for n in range(10):
                if n < 2:
                    dst_t, col, ring = k_tok, n * 512, st32
                elif n < 6:
                    dst_t, col, ring = v_tok, (n - 2) * 512, st16
                else:
                    dst_t, col, ring = og_tok, (n - 6) * 512, st32

                def epi(pst, tt, dst_t=dst_t, col=col, ring=ring):
                    t0 = b * TB + tt * 128
                    evac_to_dram(k, pst, 128, 512, ring, dst_t.h[t0:t0 + 128, col:col + 512], dst_t, cnt[0])
                    cnt[0] += 1
                dense_tm(k, X, wr_tm, w_tm.h[n], epi, [w_tm])


def tile_fm(W):
    K, N = W.shape
    return np.ascontiguousarray(W.reshape(K // 128, 128, N // 128, 128).transpose(2, 1, 0, 3))


def tile_tm(W, n=512):
    K, N = W.shape
    return np.ascontiguousarray(W.reshape(K // 128, 128, N // n, n).transpose(2, 1, 0, 3))


def make_consts():
    c = {}
    c["ident32"] = np.eye(128, dtype=np.float32)
    return c


class LNBlock:
    def __init__(self, k, G, S, ones32, eps_t, ps_sum, ps_sq, zT, gtab, btab, ln_i):
        self.k, self.G, self.S = k, G, S
        self.ones32, self.eps_t = ones32, eps_t
        self.ps_sum, self.ps_sq = ps_sum, ps_sq
        self.zT, self.gtab, self.btab, self.ln_i = zT, gtab, btab, ln_i
        self.pending = None

    def prefetch_res(self, xT32_old, dc, b):
        res = self.G.next()
        self.k.dma("sp", res[:, 0:TB], xT32_old.h[dc, :, sl(b, TB)], reads=[xT32_old], writes=[res])
        return res

    def flush(self):
        if self.pending is not None:
            self.pending()
            self.pending = None

    def chunk(self, pst, res, dc, b):
        k = self.k
        z = self.G.next()
        zsq = self.G.next()
        k.op("dve", lambda e: e.scalar_tensor_tensor(out=z[:, 0:TB], in0=res[:, 0:TB], scalar=ALPHA, in1=pst[:, :],
                                                     op0=ALU.mult, op1=ALU.add), reads=[res, pst], writes=[z])
        k.op("act", lambda e: e.activation(out=zsq[:, 0:TB], in_=z[:, 0:TB], func=AF.Square), reads=[z], writes=[zsq])
        k.dma("act", self.zT.h[dc, :, sl(b, TB)], z[:, 0:TB], reads=[z], writes=[self.zT], acc=True)

        def stats(z=z, zsq=zsq, dc=dc):
            k.op("pe", lambda e: e.matmul(self.ps_sum[:, :], self.ones32[:], z[:, 0:TB], start=(dc == 0), stop=(dc == KC - 1)),
                 reads=[self.ones32, z], writes=[self.ps_sum], acc=(dc > 0), last=True)
            k.op("pe", lambda e: e.matmul(self.ps_sq[:, :], self.ones32[:], zsq[:, 0:TB], start=(dc == 0), stop=(dc == KC - 1)),
                 reads=[self.ones32, zsq], writes=[self.ps_sq], acc=(dc > 0), last=True)
        self.pending = stats

    def finish(self, b, xT32_new, xT16_new, out_tok=None, ident32=None):
        k = self.k
        self.flush()
        mean, msq, rstd, nmr = self.S
        k.op("dve", lambda e: e.tensor_scalar(out=mean[:], in0=self.ps_sum[:, :], scalar1=1.0 / D, scalar2=None, op0=ALU.mult),
             reads=[self.ps_sum], writes=[mean])
        k.op("dve", lambda e: e.tensor_tensor(out=msq[:], in0=mean[:], in1=mean[:], op=ALU.mult), reads=[mean], writes=[msq])
        k.op("dve", lambda e: e.scalar_tensor_tensor(out=msq[:], in0=self.ps_sq[:, :], scalar=1.0 / D, in1=msq[:],
                                                     op0=ALU.mult, op1=ALU.subtract), reads=[self.ps_sq, msq], writes=[msq])
        k.op("act", lambda e: e.activation(out=rstd[:], in_=msq[:], func=AF.Sqrt, bias=self.eps_t[:, 0:1], scale=1.0),
             reads=[msq, self.eps_t], writes=[rstd])
        k.op("dve", lambda e: e.reciprocal(out=rstd[:], in_=rstd[:]), reads=[rstd], writes=[rstd])
        k.op("dve", lambda e: e.scalar_tensor_tensor(out=nmr[:], in0=mean[:], scalar=-1.0, in1=rstd[:], op0=ALU.mult, op1=ALU.mult),
             reads=[mean, rstd], writes=[nmr])
        gi = self.ln_i * KC
        for dc in range(KC):
            zl = self.G.next()
            k.dma("sp", zl[:, 0:TB], self.zT.h[dc, :, sl(b, TB)], reads=[self.zT], writes=[zl])
            t1 = self.G.next()
            k.op("dve", lambda e: e.tensor_tensor(out=t1[:, 0:TB], in0=zl[:, 0:TB], in1=rstd[:], op=ALU.mult), reads=[zl, rstd], writes=[t1])
            k.op("pool", lambda e: e.tensor_tensor(out=t1[:, 0:TB], in0=t1[:, 0:TB], in1=nmr[:], op=ALU.add), reads=[t1, nmr], writes=[t1])
            x32 = self.G.next()
            k.op("act", lambda e: e.activation(out=x32[:, 0:TB], in_=t1[:, 0:TB], func=AF.Identity,
                                               bias=self.btab[:, gi + dc:gi + dc + 1], scale=self.gtab[:, gi + dc:gi + dc + 1]),
                 reads=[t1, self.gtab, self.btab], writes=[x32])
            if xT32_new is not None:
                k.dma("act", xT32_new.h[dc, :, sl(b, TB)], x32[:, 0:TB], reads=[x32], writes=[xT32_new], acc=True)
                x16 = self.G.next()
                x16v = x16.h[:, 0:TB // 2].bitcast(BF16)
                k.op("pool", lambda e: e.tensor_copy(x16v, x32[:, 0:TB]), reads=[x32], writes=[x16])
                k.dma("act", xT16_new.h[dc, :, sl(b, TB)], x16v, reads=[x16], writes=[xT16_new], acc=True)
            if out_tok is not None:
                pst = k.psr.next()
                for tt in range(4):
                    k.op("pe", lambda e, tt=tt: e.transpose(pst[:, sl(tt)], x32[:, sl(tt)], ident32[:]),
                         reads=[x32, ident32], writes=[pst], acc=(tt > 0), last=(tt == 3))
                ot = self.G.next()
                k.op("dve", lambda e: e.tensor_copy(ot[:, 0:TB], pst[:, :]), reads=[pst], writes=[ot])
                for tt in range(4):
                    t0 = b * TB + tt * 128
                    k.dma("act", out_tok.h[t0:t0 + 128, sl(dc)], ot[:, sl(tt)], reads=[ot], writes=[out_tok], acc=True)


def ln_consts(k, ones32_d, gtab_d, btab_d):
    ones32 = k.sb([128, 128], F32, "ones")
    k.dma("sp", ones32[:], ones32_d.h[:, :], reads=[ones32_d], writes=[ones32])
    eps_t = k.sb([128, 1], F32, "eps")
    k.op("dve", lambda e: e.memset(eps_t[:], LN_EPS), writes=[eps_t])
    gtab = k.sb([128, 4 * KC], F32, "gtab")
    btab = k.sb([128, 4 * KC], F32, "btab")
    k.dma("sp", gtab[:], gtab_d.h[:, :], reads=[gtab_d], writes=[gtab])
    k.dma("sp", btab[:], btab_d.h[:, :], reads=[btab_d], writes=[btab])
    return ones32, eps_t, gtab, btab


def phase_outproj_ln(k, mixT16, L, w_out, xT32_old, xT32_new, xT16_new, zT, ones32_d, gtab_d, btab_d, ln_i):
    nb = L // TB
    with k.scope():
        ones32, eps_t, gtab, btab = ln_consts(k, ones32_d, gtab_d, btab_d)
        X = [k.sb([128, 8, TB + 2], BF16, "X") for _ in range(4)]
        wr = Ring([k.sb([128, KC, 128], BF16, "wfm") for _ in range(3)])
        G = Ring([k.sb([128, TB + 2], F32, "G") for _ in range(12)])
        S = [k.sb([128, TB], F32, "S") for _ in range(4)]
        for b in range(nb):
            load_X(k, X, mixT16, b, L)
            ln = LNBlock(k, G, S, ones32, eps_t, k.ps_sum, k.ps_sq, zT, gtab, btab, ln_i)
            for dc in range(KC):
                res = ln.prefetch_res(xT32_old, dc, b)
                dense_fm(k, X, wr, w_out.h[dc], 128, lambda pst, res=res, dc=dc: ln.chunk(pst, res, dc, b), [w_out], after_mm=ln.flush)
            ln.finish(b, xT32_new, xT16_new)


def phase_ffn(k, xT16, L, w_up, w_dn, cw_d, xT32_old, xT32_new, xT16_new, zT, ones32_d, gtab_d, btab_d, ln_i, layer,
              wc_up, wc_dn, out_tok=None, ident32_d=None):
    nb = L // TB
    NH = 2 * (nb - 1)
    with k.scope():
        ones32, eps_t, gtab, btab = ln_consts(k, ones32_d, gtab_d, btab_d)
        ident32 = None
        if out_tok is not None:
            ident32 = k.sb([128, 128], F32, "ident")
            k.dma("sp", ident32[:], ident32_d.h[:, :], reads=[ident32_d], writes=[ident32])
        cw = k.sb([128, 2 * FC, 4], F32, "cw")
        k.dma("sp", cw[:], cw_d.h[:, layer], reads=[cw_d], writes=[cw])
        X = [k.sb([128, 8, TB + 2], BF16, "X") for _ in range(4)]
        act = [k.sb([128, TB], BF16, "act") for _ in range(FC)]
        wr = Ring([k.sb([128, 43 * 128], BF16, "w") for _ in range(4)])
        G = Ring([k.sb([128, TB + 2], F32, "G") for _ in range(10)])
        S = [k.sb([128, TB], F32, "S") for _ in range(4)]
        hh = k.sb([128, 2 * FC, 16], F32, "hh")
        Xh = k.sb([128, KC, 16], BF16, "Xh")
        k.op("pool", lambda e: e.memset(Xh[:], 0.0), writes=[Xh])
        for j in range(1, nb):
            k.dma("sp" if j % 2 else "act", Xh[:, :, 2 * (j - 1):2 * j], xT16.h[:, :, TB * j - 1:TB * j + 1].rearrange("c p t -> p c t"),
                  reads=[xT16], writes=[Xh], acc=True)
        if NH == 0:
            k.op("pool", lambda e: e.memset(hh[:], 0.0), writes=[hh])
        for fi in range(2 * FC):
            wt = wr.next()
            wv = wt.h[:, 0:KC * 128].rearrange("p (c j) -> p c j", j=128)
            k.dma("pool", wv, w_up.h[fi], reads=[w_up], writes=[wt], max_dma_last_dim=4096)
            k.dma("sp" if fi % 2 else "act", wc_up.h[fi], wt[:, 0:KC * 128], reads=[wt], writes=[wc_up], acc=True)
            if NH > 0:
                ph = k.psr.next()
                for kc in range(KC):
                    k.op("pe", lambda e, kc=kc: e.matmul(ph[:, 0:NH], wv[:, kc, :], Xh[:, kc, 0:NH], start=(kc == 0), stop=(kc == KC - 1)),
                         reads=[wt, Xh], writes=[ph], acc=(kc > 0), last=(kc == KC - 1))
                k.op("act", lambda e: e.copy(hh[:, fi, 0:NH], ph[:, 0:NH]), reads=[ph], writes=[hh], acc=(fi > 0))
        for b in range(nb):
            load_X(k, X, xT16, b, L)
            for f in range(FC):
                cs = []
                for half in range(2):
                    fi = half * FC + f
                    wt = wr.next()
                    wv = wt.h[:, 0:KC * 128].rearrange("p (c j) -> p c j", j=128)
                    k.dma("sp", wt[:, 0:KC * 128], wc_up.h[fi], reads=[wc_up], writes=[wt])
                    pst = k.psr.next()
                    for kc in range(KC):
                        xt = X[kc // 8]
                        k.op("pe", lambda e, kc=kc, xt=xt: e.matmul(pst[:, :], wv[:, kc, :], xt[:, kc % 8, 1:TB + 1],
                                                                   start=(kc == 0), stop=(kc == KC - 1)),
                             reads=[wt, xt], writes=[pst], acc=(kc > 0), last=(kc == KC - 1))
                    hb = G.next()
                    k.op("act", lambda e: e.copy(hb[:, 1:TB + 1], pst[:, :]), reads=[pst], writes=[hb])
                    if b > 0:
                        k.op("pool", lambda e: e.tensor_copy(hb[:, 0:1], hh[:, fi, 2 * (b - 1):2 * (b - 1) + 1]), reads=[hh], writes=[hb], acc=True)
                    else:
                        k.op("pool", lambda e: e.memset(hb[:, 0:1], 0.0), writes=[hb], acc=True)
                    if b < nb - 1:
                        k.op("pool", lambda e: e.tensor_copy(hb[:, TB + 1:TB + 2], hh[:, fi, 2 * b + 1:2 * b + 2]), reads=[hh], writes=[hb], acc=True)
                    else:
                        k.op("pool", lambda e: e.memset(hb[:, TB + 1:TB + 2], 0.0), writes=[hb], acc=True)
                    c = G.next()
                    k.op("dve", lambda e: e.tensor_scalar(out=c[:, 0:TB], in0=hb[:, 0:TB], scalar1=cw[:, fi, 0:1], scalar2=cw[:, fi, 3:4],
                                                          op0=ALU.mult, op1=ALU.add), reads=[hb, cw], writes=[c])
                    k.op("dve", lambda e: e.scalar_tensor_tensor(out=c[:, 0:TB], in0=hb[:, 1:TB + 1], scalar=cw[:, fi, 1:2], in1=c[:, 0:TB],
                                                                 op0=ALU.mult, op1=ALU.add), reads=[hb, cw, c], writes=[c])
                    k.op("dve", lambda e: e.scalar_tensor_tensor(out=c[:, 0:TB], in0=hb[:, 2:TB + 2], scalar=cw[:, fi, 2:3], in1=c[:, 0:TB],
                                                                 op0=ALU.mult, op1=ALU.add), reads=[hb, cw, c], writes=[c])
                    cs.append(c)
                sg = G.next()
                k.op("act", lambda e: e.activation(out=sg[:, 0:TB], in_=cs[0][:, 0:TB], func=AF.Silu), reads=[cs[0]], writes=[sg])
                k.op("pool", lambda e: e.tensor_tensor(out=act[f][:], in0=sg[:, 0:TB], in1=cs[1][:, 0:TB], op=ALU.mult),
                     reads=[sg, cs[1]], writes=[act[f]])
            ln = LNBlock(k, G, S, ones32, eps_t, k.ps_sum, k.ps_sq, zT, gtab, btab, ln_i)
            for dc in range(KC):
                res = ln.prefetch_res(xT32_old, dc, b)
                wts = []
                for h in range(2):
                    wt = wr.next()
                    cs_ = slice(h * 43 * 128, (h + 1) * 43 * 128)
                    if b == 0:
                        k.dma("pool", wt[:, :], w_dn.h[dc, :, cs_], reads=[w_dn], writes=[wt], max_dma_last_dim=4096)
                        k.dma("pool", wc_dn.h[dc, :, cs_], wt[:, :], reads=[wt], writes=[wc_dn], acc=True)
                    else:
                        k.dma("sp", wt[:, :], wc_dn.h[dc, :, cs_], reads=[wc_dn], writes=[wt])
                    wts.append(wt)
                pst = k.psr.next()
                for fc in range(FC):
                    wt = wts[fc // 43]
                    o = (fc % 43) * 128
                    k.op("pe", lambda e, wt=wt, o=o, fc=fc: e.matmul(pst[:, :], wt[:, o:o + 128], act[fc][:], start=(fc == 0), stop=(fc == FC - 1)),
                         reads=[wt, act[fc]], writes=[pst], acc=(fc > 0), last=(fc == FC - 1))
                ln.flush()
                ln.chunk(pst, res, dc, b)
            ln.finish(b, xT32_new, xT16_new, out_tok=out_tok, ident32=ident32)


def phase_gla(k, L, qT, kT, k_tok, v_tok, og_tok, lrT, wg_d, gnorm_d, cst_d, ident32_d, of_tok, mixT16):
    NCH = L // 128
    with k.scope():
        cst = k.sb([128, 4, 128], F32, "cst")
        k.dma("sp", cst[:], cst_d.h.rearrange("a p c -> p a c"), reads=[cst_d], writes=[cst])
        ident32 = k.sb([128, 128], F32, "ident")
        k.dma("sp", ident32[:], ident32_d.h[:, :], reads=[ident32_d], writes=[ident32])
        gn = k.sb([128, 512], F32, "gn")
        k.dma("sp", gn[:], gnorm_d.h[:, :], reads=[gnorm_d], writes=[gn])
        wg = [k.sb([17, 1024], F32, "wg") for _ in range(2)]
        lra = [k.sb([17, L], F32, "lra") for _ in range(2)]
        for d in range(2):
            k.dma("sp", wg[d][:], wg_d.h[d], reads=[wg_d], writes=[wg[d]])
            k.op("dve", lambda e, d=d: e.memset(lra[d][:], 1.0), writes=[lra[d]])
            k.dma("sp", lra[d][0:16, :], lrT.h[16 * d:16 * d + 16, :], reads=[lrT], writes=[lra[d]])
        epsr = k.sb([128, 1], F32, "epsr")
        k.op("dve", lambda e: e.memset(epsr[:], RMS_EPS), writes=[epsr])
        S32 = [k.sb([128, 512], F32, "S32") for _ in range(2)]
        S16 = [k.sb([128, 512], BF16, "S16") for _ in range(2)]
        qr = Ring([k.sb([128, 2, 128], F32, "q") for _ in range(2)])
        kr = Ring([k.sb([128, 2, 128], F32, "kk") for _ in range(2)])
        ktr = Ring([k.sb([128, 256], F32, "kt") for _ in range(2)])
        vr = Ring([k.sb([128, 512], BF16, "v") for _ in range(2)])
        ogr = Ring([k.sb([128, 512], F32, "og") for _ in range(2)])
        ofr = Ring([k.sb([128, 512], F32, "of") for _ in range(2)])
        A = Ring([k.sb([128, 256], F32, "A") for _ in range(8)])
        Bq = Ring([k.sb([128, 2, 128], BF16, "Bq") for _ in range(4)])
        Bk = Ring([k.sb([128, 256], BF16, "Bk") for _ in range(2)])
        Pr = Ring([k.sb([128, 128], BF16, "P") for _ in range(2)])
        O = Ring([k.sb([128, 512], F32, "O") for _ in range(6)])
        sm = Ring([k.sb([128, 1], F32, "sm") for _ in range(6)])
        tr16 = Ring([k.sb([128, 4, 128], BF16, "tr") for _ in range(2)])
        for h in range(4):
            for d in range(2):
                tri = cst[:, 0 + d, :]
                ust = cst[:, 2 + d, :]
                for j in range(2):
                    k.op("dve", lambda e, j=j: e.memset(S32[j][:], 0.0), writes=[S32[j]])
                    k.op("pool", lambda e, j=j: e.memset(S16[j][:], 0.0), writes=[S16[j]])
                order = range(NCH) if d == 0 else range(NCH - 1, -1, -1)
                for n in order:
                    ts = slice(n * 128, (n + 1) * 128)
                    qt, kt, ktt, vt = qr.next(), kr.next(), ktr.next(), vr.next()
                    k.dma("sp", qt[:], qT.h[2 * h:2 * h + 2, :, ts].rearrange("j p c -> p j c"), reads=[qT], writes=[qt])
                    k.dma("sp", kt[:], kT.h[2 * h:2 * h + 2, :, ts].rearrange("j p c -> p j c"), reads=[kT], writes=[kt])
                    k.dma("sp", ktt[:], k_tok.h[ts, h * 256:(h + 1) * 256], reads=[k_tok], writes=[ktt])
                    k.dma("sp", vt[:], v_tok.h[ts, h * 512:(h + 1) * 512], reads=[v_tok], writes=[vt])
                    p1 = k.psr.next()
                    k.op("pe", lambda e: e.matmul(p1[:, 0:256], lra[d][0:17, ts], wg[d][0:17, h * 256:(h + 1) * 256], start=True, stop=True),
                         reads=[lra[d], wg[d]], writes=[p1])
                    ex = A.next()
                    k.op("act", lambda e: e.activation(out=ex[:], in_=p1[:, 0:256], func=AF.Exp, scale=-1.0), reads=[p1], writes=[ex])
                    la = A.next()
                    k.op("act", lambda e: e.activation(out=la[:], in_=ex[:], func=AF.Ln, bias=1.0, scale=1.0), reads=[ex], writes=[la])
                    p2 = k.psr.next()
                    k.op("pe", lambda e: e.matmul(p2[:, 0:256], ust, la[:], start=True, stop=True), reads=[cst, la], writes=[p2])
                    kd = A.next()
                    k.op("act", lambda e: e.activation(out=kd[:], in_=p2[:, 0:256], func=AF.Exp, scale=-1.0 / 16), reads=[p2], writes=[kd])
                    kdec = Bk.next()
                    k.op("dve", lambda e: e.tensor_tensor(out=kdec[:], in0=ktt[:], in1=kd[:], op=ALU.mult), reads=[ktt, kd], writes=[kdec])
                    p3 = k.psr.next()
                    for j in range(2):
                        k.op("pe", lambda e, j=j: e.matmul(p3[:, j * 128:(j + 1) * 128], la[:, j * 128:(j + 1) * 128], tri, start=True, stop=True),
                             reads=[la, cst], writes=[p3], acc=(j > 0), last=(j == 1))
                    eb = A.next()
                    ei = A.next()
                    k.op("act", lambda e: e.activation(out=eb[:], in_=p3[:, 0:256], func=AF.Exp, scale=-1.0 / 16), reads=[p3], writes=[eb])
                    k.op("act", lambda e: e.activation(out=ei[:], in_=p3[:, 0:256], func=AF.Exp, scale=1.0 / 16), reads=[p3], writes=[ei])
                    qd = Bq.next()
                    ki = Bq.next()
                    k.op("dve", lambda e: e.scalar_tensor_tensor(out=qd.h[:].rearrange("p j c -> p (j c)"), in0=qt.h[:].rearrange("p j c -> p (j c)"),
                                                                 scalar=1.0 / 16, in1=eb[:], op0=ALU.mult, op1=ALU.mult),
                         reads=[qt, eb], writes=[qd])
                    k.op("pool", lambda e: e.tensor_tensor(out=ki.h[:].rearrange("p j c -> p (j c)"), in0=kt.h[:].rearrange("p j c -> p (j c)"),
                                                           in1=ei[:], op=ALU.mult), reads=[kt, ei], writes=[ki])
                    p4 = k.psr.next()
                    for j in range(2):
                        k.op("pe", lambda e, j=j: e.matmul(p4[:, 0:128], ki[:, j, :], qd[:, j, :], start=(j == 0), stop=(j == 1)),
                             reads=[ki, qd], writes=[p4], acc=(j > 0), last=(j == 1))
                    P = Pr.next()
                    k.op("dve", lambda e: e.tensor_tensor(out=P[:], in0=p4[:, 0:128], in1=tri, op=ALU.mult), reads=[p4, cst], writes=[P])
                    p5 = k.psr.next()
                    k.op("pe", lambda e: e.matmul(p5[:, :], P[:], vt[:], start=True, stop=False), reads=[P, vt], writes=[p5], last=False)
                    for j in range(2):
                        k.op("pe", lambda e, j=j: e.matmul(p5[:, :], qd[:, j, :], S16[j][:], start=False, stop=(j == 1)),
                             reads=[qd, S16[j]], writes=[p5], acc=True, last=(j == 1))
                    lastc = 127 if d == 0 else 0
                    for j in range(2):
                        p6 = k.psr.next()
                        k.op("pe", lambda e, j=j: e.matmul(p6[:, :], kdec[:, j * 128:(j + 1) * 128], vt[:], start=True, stop=True),
                             reads=[kdec, vt], writes=[p6])
                        k.op("dve", lambda e, j=j, p6=p6: e.scalar_tensor_tensor(out=S32[j][:], in0=S32[j][:], scalar=eb[:, j * 128 + lastc:j * 128 + lastc + 1],
                                                                                 in1=p6[:, :], op0=ALU.mult, op1=ALU.add),
                             reads=[S32[j], eb, p6], writes=[S32[j]])
                        k.op("act", lambda e, j=j: e.copy(S16[j][:], S32[j][:]), reads=[S32[j]], writes=[S16[j]])
                    if d == 0:
                        o = O.next()
                        k.op("act", lambda e: e.copy(o[:], p5[:, :]), reads=[p5], writes=[o])
                        k.dma("act", of_tok.h[ts, h * 512:(h + 1) * 512], o[:], reads=[o], writes=[of_tok], acc=True)
                    else:
                        oft, ogt = ofr.next(), ogr.next()
                        k.dma("sp", oft[:], of_tok.h[ts, h * 512:(h + 1) * 512], reads=[of_tok], writes=[oft])
                        k.dma("sp", ogt[:], og_tok.h[ts, h * 512:(h + 1) * 512], reads=[og_tok], writes=[ogt])
                        o = O.next()
                        k.op("dve", lambda e: e.tensor_tensor(out=o[:], in0=oft[:], in1=p5[:, :], op=ALU.add), reads=[oft, p5], writes=[o])
                        sq = O.next()
                        ss = sm.next()
                        k.op("act", lambda e: e.activation(out=sq[:], in_=o[:], func=AF.Square, accum_out=ss[:]), reads=[o], writes=[sq, ss])
                        rs = sm.next()
                        k.op("act", lambda e: e.activation(out=rs[:], in_=ss[:], func=AF.Sqrt, bias=epsr[:, 0:1], scale=1.0 / 512),
                             reads=[ss, epsr], writes=[rs])
                        k.op("dve", lambda e: e.reciprocal(out=rs[:], in_=rs[:]), reads=[rs], writes=[rs])
                        on = O.next()
                        k.op("dve", lambda e: e.scalar_tensor_tensor(out=on[:], in0=o[:], scalar=rs[:, 0:1], in1=gn[:], op0=ALU.mult, op1=ALU.mult),
                             reads=[o, rs, gn], writes=[on])
                        sg = O.next()
                        k.op("act", lambda e: e.activation(out=sg[:], in_=ogt[:], func=AF.Silu), reads=[ogt], writes=[sg])
                        k.op("pool", lambda e: e.tensor_tensor(out=on[:], in0=on[:], in1=sg[:], op=ALU.mult), reads=[on, sg], writes=[on])
                        p7 = k.psr.next()
                        for j in range(4):
                            k.op("pe", lambda e, j=j: e.transpose(p7[:, sl(j)], on[:, sl(j)], ident32[:]),
                                 reads=[on, ident32], writes=[p7], acc=(j > 0), last=(j == 3))
                        t16 = tr16.next()
                        k.op("dve", lambda e: e.tensor_copy(t16.h[:].rearrange("p j c -> p (j c)"), p7[:, :]), reads=[p7], writes=[t16])
                        k.dma("act", mixT16.h[4 * h:4 * h + 4, :, ts].rearrange("j p c -> p j c"), t16[:], reads=[t16], writes=[mixT16], acc=True)


def gla_consts():
    i = np.arange(128)
    tri = (i[:, None] <= i[None, :]).astype(np.float32)
    ust = (i[:, None] > i[None, :]).astype(np.float32)
    return np.stack([tri, tri.T.copy(), ust, ust.T.copy()])


MAGIC = 12582912.0
TWO_PI = 6.283185307179586


def hy_consts(L):
    N = 2 * L
    N1 = N // 128
    T1 = L // 128
    t = np.linspace(0.0, 1.0, L, dtype=np.float32)[:, None]
    bands = np.linspace(1e-4, 15.0, 16, dtype=np.float32)
    w = (2.0 * np.pi * np.arange(L, dtype=np.float32)[:, None] / L).astype(np.float32)
    z = np.concatenate([t, np.cos(bands * w), -np.sin(bands * w), np.ones((L, 1), np.float32)], -1).astype(np.float32)
    c = {}
    c["hy_zT"] = np.ascontiguousarray(z.T)
    max_decay = np.log(1e-2) / 0.3
    min_decay = np.log(1e-2) / 1.5
    deltas = np.linspace(min_decay, max_decay, 2048, dtype=np.float32)
    c["hy_absd"] = np.tile(np.abs(deltas)[None, :], (128, 1)).astype(np.float32)
    tau = (np.arange(L).reshape(T1, 128).T).astype(np.float64)
    c["hy_negt"] = np.ascontiguousarray((-(tau / (L - 1))).astype(np.float32))
    t1 = np.arange(N1)[:, None, None]
    t2 = np.arange(128)[None, :, None]
    f1 = np.arange(N1)[None, None, :]
    ang = 2.0 * np.pi * ((f1 * (128 * t1 + t2)) % N) / N
    fw = np.stack([np.cos(ang), -np.sin(ang)], 2).astype(np.float32)
    c["hy_fw"] = np.ascontiguousarray(fw[:T1])
    iv = np.stack([np.cos(ang), -np.sin(ang)], 2) / N
    c["hy_iv"] = np.ascontiguousarray(iv[:T1].transpose(3, 1, 2, 0)).astype(np.float32)
    a = 2.0 * np.pi * ((np.arange(128)[:, None] * np.arange(128)[None, :]) % 128) / 128
    c["hy_cs"] = np.stack([np.cos(a), np.sin(a), -np.sin(a)]).astype(np.float32)
    return c


def hy_sin(k, G, ps, bias_ap, bias_t, out_ap, out_t, m):
    xs, n1, xr = G.next(), G.next(), G.next()
    if bias_ap is None:
        k.op("dve", lambda e: e.tensor_copy(xs[0:m, :], ps[0:m, :]), reads=[ps], writes=[xs])
    else:
        k.op("dve", lambda e: e.tensor_scalar(out=xs[0:m, :], in0=ps[0:m, :], scalar1=bias_ap, scalar2=None, op0=ALU.add),
             reads=[ps, bias_t], writes=[xs])
    k.op("dve", lambda e: e.tensor_scalar(out=n1[0:m, :], in0=xs[0:m, :], scalar1=1.0 / TWO_PI, scalar2=MAGIC, op0=ALU.mult, op1=ALU.add),
         reads=[xs], writes=[n1])
    k.op("dve", lambda e: e.tensor_scalar(out=n1[0:m, :], in0=n1[0:m, :], scalar1=-MAGIC, scalar2=None, op0=ALU.add), reads=[n1], writes=[n1])
    k.op("dve", lambda e: e.scalar_tensor_tensor(out=xr[0:m, :], in0=n1[0:m, :], scalar=-TWO_PI, in1=xs[0:m, :], op0=ALU.mult, op1=ALU.add),
         reads=[n1, xs], writes=[xr])
    k.op("act", lambda e: e.activation(out=out_ap, in_=xr[0:m, :], func=AF.Sin), reads=[xr], writes=[out_t])


def hy_filters(k, L, zT_d, w1a_d, w2_d, b2_d, w3_d, absd_d, negt_d, kf_tok, kb_tok):
    T1 = L // 128
    with k.scope():
        zT = k.sb([34, L], F32, "zT")
        k.dma("sp", zT[:], zT_d.h[:, :], reads=[zT_d], writes=[zT])
        w1a = k.sb([34, 64], F32, "w1a")
        k.dma("sp", w1a[:], w1a_d.h[:, :], reads=[w1a_d], writes=[w1a])
        w2 = k.sb([64, 64], F32, "w2")
        k.dma("sp", w2[:], w2_d.h[:, :], reads=[w2_d], writes=[w2])
        b2 = k.sb([64, 1], F32, "b2")
        k.dma("sp", b2[:], b2_d.h[:, :], reads=[b2_d], writes=[b2])
        w3 = k.sb([64, 4096], F32, "w3")
        k.dma("act", w3[:], w3_d.h[:, :], reads=[w3_d], writes=[w3])
        absd = k.sb([128, 2048], F32, "absd")
        k.dma("act", absd[:], absd_d.h[:, :], reads=[absd_d], writes=[absd])
        negt = k.sb([128, T1], F32, "negt")
        k.dma("sp", negt[:], negt_d.h[:, :], reads=[negt_d], writes=[negt])
        h2T = k.sb([64, L], F32, "h2T")
        G = Ring([k.sb([128, 512], F32, "G") for _ in range(10)])
        h1r = Ring([k.sb([64, 512], F32, "h1") for _ in range(2)])
        for b in range(L // 512):
            p = k.psr.next()
            k.op("pe", lambda e: e.matmul(p[0:64, :], w1a[:, :], zT[:, sl(b, 512)], start=True, stop=True), reads=[w1a, zT], writes=[p])
            h1 = h1r.next()
            hy_sin(k, G, p, None, None, h1[:, :], h1, 64)
            p2 = k.psr.next()
            k.op("pe", lambda e: e.matmul(p2[0:64, :], w2[:, :], h1[:, :], start=True, stop=True), reads=[w2, h1], writes=[p2])
            hy_sin(k, G, p2, b2[:, 0:1], b2, h2T[:, sl(b, 512)], h2T, 64)
        i = 0
        for n in range(T1):
            for c4 in range(4):
                win = G.next()
                k.op("act", lambda e: e.activation(out=win[:], in_=absd[:, sl(c4, 512)], func=AF.Exp, scale=negt[:, n:n + 1]),
                     reads=[absd, negt], writes=[win])
                for fb in range(2):
                    p = k.psr.next()
                    col = fb * 2048 + c4 * 512
                    k.op("pe", lambda e: e.matmul(p[:, :], h2T[:, sl(n)], w3[:, col:col + 512], start=True, stop=True), reads=[h2T, w3], writes=[p])
                    st = G.next()
                    k.op("dve", lambda e: e.tensor_tensor(out=st[:], in0=p[:, :], in1=win[:], op=ALU.mult), reads=[p, win], writes=[st])
                    dst = kf_tok if fb == 0 else kb_tok
                    if fb == 1 and n == 0:
                        k.op("dve", lambda e: e.memset(st[0:1, :], 0.0), writes=[st])
                    k.dma("act", dst.h[sl(n), sl(c4, 512)], st[:], reads=[st], writes=[dst], acc=True)
                    i += 1


def hy_prep_u(k, L, hyT, hcw_d, ident32_d, uT, u_tok):
    with k.scope():
        hcw = k.sb([128, 48, 4], F32, "hcw")
        k.dma("sp", hcw[:], hcw_d.h[:, :, :], reads=[hcw_d], writes=[hcw])
        ident32 = k.sb([128, 128], F32, "ident")
        k.dma("sp", ident32[:], ident32_d.h[:, :], reads=[ident32_d], writes=[ident32])
        inr = Ring([k.sb([128, L + 2], F32, "in") for _ in range(3)])
        cr = Ring([k.sb([128, L], F32, "c") for _ in range(3)])
        st = Ring([k.sb([128, 4, 128], F32, "st") for _ in range(3)])
        for cc in range(16):
            cs = []
            for which in (16, 32):
                ch = which + cc
                t = inr.next()
                k.op("pool", lambda e: e.memset(t[:, 0:L + 2:L + 1], 0.0), writes=[t])
                k.dma("sp", t[:, 1:L + 1], hyT.h[ch], reads=[hyT], writes=[t], acc=True)
                c = cr.next()
                hy_conv(k, c, t, hcw, ch, L)
                cs.append(c)
            u = cs[0]
            k.op("pool", lambda e: e.tensor_tensor(out=u[:], in0=cs[0][:], in1=cs[1][:], op=ALU.mult), reads=[cs[0], cs[1]], writes=[u])
            k.dma("act", uT.h[cc], u[:], reads=[u], writes=[uT], acc=True)
            for g in range(L // 512):
                p = k.psr.next()
                for j in range(4):
                    k.op("pe", lambda e, j=j: e.transpose(p[:, sl(j)], u[:, g * 512 + j * 128:g * 512 + (j + 1) * 128], ident32[:]),
                         reads=[u, ident32], writes=[p], acc=(j > 0), last=(j == 3))
                s = st.next()
                if g % 2 == 0:
                    k.op("act", lambda e: e.copy(s.h[:].rearrange("p a c -> p (a c)"), p[:, :]), reads=[p], writes=[s])
                else:
                    k.op("dve", lambda e: e.tensor_copy(s.h[:].rearrange("p a c -> p (a c)"), p[:, :]), reads=[p], writes=[s])
                k.dma("act", u_tok.h[g * 512:(g + 1) * 512, sl(cc)].rearrange("(a p) c -> p a c", p=128), s[:],
                      reads=[s], writes=[u_tok], acc=True)


def hy_conv(k, c, t, hcw, ch, L):
    k.op("dve", lambda e: e.tensor_scalar(out=c[:, 0:L], in0=t[:, 0:L], scalar1=hcw[:, ch, 0:1], scalar2=hcw[:, ch, 3:4], op0=ALU.mult, op1=ALU.add),
         reads=[t, hcw], writes=[c])
    k.op("dve", lambda e: e.scalar_tensor_tensor(out=c[:, 0:L], in0=t[:, 1:L + 1], scalar=hcw[:, ch, 1:2], in1=c[:, 0:L], op0=ALU.mult, op1=ALU.add),
         reads=[t, hcw, c], writes=[c])
    k.op("dve", lambda e: e.scalar_tensor_tensor(out=c[:, 0:L], in0=t[:, 2:L + 2], scalar=hcw[:, ch, 2:3], in1=c[:, 0:L], op0=ALU.mult, op1=ALU.add),
         reads=[t, hcw, c], writes=[c])


def hy_fft_a(k, L, src_tok, fw_d, scrA):
    N1, T1 = 2 * L // 128, L // 128
    with k.scope():
        fw = k.sb([T1, 128, 2, N1], BF16, "fw")
        k.dma("pool", fw[:, 0:64], fw_d.h[:, 0:64], reads=[fw_d], writes=[fw], max_dma_last_dim=4096)
        k.dma("pool", fw[:, 64:128], fw_d.h[:, 64:128], reads=[fw_d], writes=[fw], acc=True, max_dma_last_dim=4096)
        sr = Ring([k.sb([T1, 2048], BF16, "s") for _ in range(3)])
        st = Ring([k.sb([N1, 2048], F32, "st") for _ in range(4)])
        src_v = src_tok.h.rearrange("(a p) c -> p a c", p=128)
        i = 0
        for t2 in range(128):
            s = sr.next()
            k.dma("pool", s[:], src_v[t2], reads=[src_tok], writes=[s], max_dma_last_dim=4096)
            for ri in range(2):
                so = st.next()
                for ct in range(4):
                    p = k.psr.next()
                    k.op("pe", lambda e: e.matmul(p[0:N1, :], fw[:, t2, ri, :], s[:, sl(ct, 512)], start=True, stop=True), reads=[fw, s], writes=[p])
                    if i % 2 == 0:
                        k.op("act", lambda e: e.copy(so[:, sl(ct, 512)], p[0:N1, :]), reads=[p], writes=[so], acc=(ct > 0))
                    else:
                        k.op("dve", lambda e: e.tensor_copy(so[:, sl(ct, 512)], p[0:N1, :]), reads=[p], writes=[so], acc=(ct > 0))
                    i += 1
                k.dma("act", scrA.h[ri, :, t2, :], so[:], reads=[so], writes=[scrA], acc=True)


def hy_fft_b(k, L, scrA, cs_d, epi, mk_extra):
    N1 = 2 * L // 128
    with k.scope():
        cs = k.sb([128, 3, 128], BF16, "cs")
        k.dma("pool", cs[:], cs_d.h.rearrange("a p c -> p a c"), reads=[cs_d], writes=[cs])
        ar = Ring([k.sb([128, 2, 2048], BF16, "a") for _ in range(3)])
        ctx = mk_extra(cs)
        for f1 in range(N1):
            a = ar.next()
            k.dma("pool", a[:, 0, :], scrA.h[0, f1], reads=[scrA], writes=[a], max_dma_last_dim=4096)
            k.dma("pool", a[:, 1, :], scrA.h[1, f1], reads=[scrA], writes=[a], acc=True, max_dma_last_dim=4096)
            for ct in range(4):
                pr, pi = k.psr.next(), k.psr.next()
                c = slice(ct * 512, (ct + 1) * 512)
                k.op("pe", lambda e: e.matmul(pr[:, :], cs[:, 0, :], a[:, 0, c], start=True, stop=False), reads=[cs, a], writes=[pr], last=False)
                k.op("pe", lambda e: e.matmul(pr[:, :], cs[:, 1, :], a[:, 1, c], start=False, stop=True), reads=[cs, a], writes=[pr], acc=True)
                k.op("pe", lambda e: e.matmul(pi[:, :], cs[:, 0, :], a[:, 1, c], start=True, stop=False), reads=[cs, a], writes=[pi], last=False)
                k.op("pe", lambda e: e.matmul(pi[:, :], cs[:, 2, :], a[:, 0, c], start=False, stop=True), reads=[cs, a], writes=[pi], acc=True)
                epi(f1, ct, pr, pi, ctx)


def hy_spectrum_store(k, scrK, combine):
    def mk(cs):
        return {"st": Ring([k.sb([128, 2, 512], F32, "kst") for _ in range(3)]),
                "ld": Ring([k.sb([128, 2, 512], F32, "kld") for _ in range(3)])}

    def epi(f1, ct, pr, pi, ctx):
        c = slice(ct * 512, (ct + 1) * 512)
        s = ctx["st"].next()
        if not combine:
            k.op("act", lambda e: e.copy(s[:, 0, :], pr[:, :]), reads=[pr], writes=[s])
            k.op("dve", lambda e: e.tensor_copy(s[:, 1, :], pi[:, :]), reads=[pi], writes=[s], acc=True)
        else:
            ld = ctx["ld"].next()
            k.dma("sp", ld[:], scrK.h[:, f1, :, c].rearrange("r p c -> p r c"), reads=[scrK], writes=[ld])
            k.op("dve", lambda e: e.tensor_tensor(out=s[:, 0, :], in0=ld[:, 0, :], in1=pr[:, :], op=ALU.add), reads=[ld, pr], writes=[s])
            k.op("dve", lambda e: e.tensor_tensor(out=s[:, 1, :], in0=ld[:, 1, :], in1=pi[:, :], op=ALU.subtract), reads=[ld, pi], writes=[s], acc=True)
        k.dma("act", scrK.h[:, f1, :, c].rearrange("r p c -> p r c"), s[:], reads=[s], writes=[scrK], acc=True)
    return mk, epi


def hy_mul_inv(k, scrK, scrG):
    def mk(cs):
        return {"cs": cs, "ld": Ring([k.sb([128, 2, 512], F32, "kld") for _ in range(3)]),
                "y": Ring([k.sb([128, 2, 512], BF16, "y") for _ in range(2)]),
                "t": Ring([k.sb([128, 512], F32, "t") for _ in range(4)]),
                "g": Ring([k.sb([128, 2, 512], F32, "g") for _ in range(3)])}

    def epi(f1, ct, pr, pi, ctx):
        cs = ctx["cs"]
        c = slice(ct * 512, (ct + 1) * 512)
        ld = ctx["ld"].next()
        k.dma("sp", ld[:], scrK.h[:, f1, :, c].rearrange("r p c -> p r c"), reads=[scrK], writes=[ld])
        y = ctx["y"].next()
        ta, tb = ctx["t"].next(), ctx["t"].next()
        k.op("dve", lambda e: e.tensor_tensor(out=ta[:], in0=pr[:, :], in1=ld[:, 0, :], op=ALU.mult), reads=[pr, ld], writes=[ta])
        k.op("act", lambda e: e.copy(tb[:], pi[:, :]), reads=[pi], writes=[tb])
        k.op("dve", lambda e: e.tensor_tensor(out=y[:, 1, :], in0=pr[:, :], in1=ld[:, 1, :], op=ALU.mult), reads=[pr, ld], writes=[y])
        tc_ = ctx["t"].next()
        k.op("pool", lambda e: e.tensor_tensor(out=tc_[:], in0=tb[:], in1=ld[:, 1, :], op=ALU.mult), reads=[tb, ld], writes=[tc_])
        k.op("pool", lambda e: e.tensor_tensor(out=y[:, 0, :], in0=ta[:], in1=tc_[:], op=ALU.subtract), reads=[ta, tc_], writes=[y], acc=True)
        td = ctx["t"].next()
        k.op("dve", lambda e: e.tensor_tensor(out=td[:], in0=tb[:], in1=ld[:, 0, :], op=ALU.mult), reads=[tb, ld], writes=[td])
        k.op("dve", lambda e: e.tensor_tensor(out=y[:, 1, :], in0=y[:, 1, :], in1=td[:], op=ALU.add), reads=[y, td], writes=[y], acc=True)
        gr, gi = k.psr.next(), k.psr.next()
        k.op("pe", lambda e: e.matmul(gr[:, :], cs[:, 0, :], y[:, 0, :], start=True, stop=False), reads=[cs, y], writes=[gr], last=False)
        k.op("pe", lambda e: e.matmul(gr[:, :], cs[:, 2, :], y[:, 1, :], start=False, stop=True), reads=[cs, y], writes=[gr], acc=True)
        k.op("pe", lambda e: e.matmul(gi[:, :], cs[:, 0, :], y[:, 1, :], start=True, stop=False), reads=[cs, y], writes=[gi], last=False)
        k.op("pe", lambda e: e.matmul(gi[:, :], cs[:, 1, :], y[:, 0, :], start=False, stop=True), reads=[cs, y], writes=[gi], acc=True)
        g = ctx["g"].next()
        k.op("act", lambda e: e.copy(g[:, 0, :], gr[:, :]), reads=[gr], writes=[g])
        k.op("dve", lambda e: e.tensor_copy(g[:, 1, :], gi[:, :]), reads=[gi], writes=[g], acc=True)
        k.dma("act", scrG.h[:, :, f1, c].rearrange("r p c -> p r c"), g[:], reads=[g], writes=[scrG], acc=True)
    return mk, epi


def hy_fft_c(k, L, scrG, iv_d, hyT, uT, hcw_d, hbias_d, mixT16):
    N1, T1 = 2 * L // 128, L // 128
    TPB = 512 // T1
    with k.scope():
        iv = k.sb([N1, 128, 2, T1], F32, "iv")
        k.dma("sp", iv[:, 0:64], iv_d.h[:, 0:64], reads=[iv_d], writes=[iv])
        k.dma("act", iv[:, 64:128], iv_d.h[:, 64:128], reads=[iv_d], writes=[iv], acc=True)
        hcw = k.sb([128, 48, 4], F32, "hcw")
        k.dma("sp", hcw[:], hcw_d.h[:, :, :], reads=[hcw_d], writes=[hcw])
        hb = k.sb([128, 16], F32, "hb")
        k.dma("sp", hb[:], hbias_d.h[:, :], reads=[hbias_d], writes=[hb])
        gr = Ring([k.sb([N1, 2, 512], F32, "g") for _ in range(3)])
        yT = [k.sb([128, L], F32, "yT") for _ in range(4)]
        x0r = Ring([k.sb([128, L + 2], F32, "x0") for _ in range(1)])
        ur = Ring([k.sb([128, L], F32, "u") for _ in range(1)])
        cr = Ring([k.sb([128, L], F32, "c") for _ in range(1)])
        o16 = Ring([k.sb([128, L], BF16, "o16") for _ in range(2)])
        allps = Ring(k.allps)
        for cg in range(4):
            banks = None
            for t2 in range(128):
                if t2 % TPB == 0:
                    banks = [allps.next() for _ in range(4)]
                g = gr.next()
                k.dma("sp", g[:], scrG.h[:, t2, :, sl(cg, 512)].rearrange("r f c -> f r c"), reads=[scrG], writes=[g])
                o = (t2 % TPB) * T1
                for j in range(4):
                    k.op("pe", lambda e, j=j: e.matmul(banks[j][:, o:o + T1], g[:, 0, sl(j)], iv[:, t2, 0, :], start=True, stop=False),
                         reads=[g, iv], writes=[banks[j]], acc=True, last=False)
                    k.op("pe", lambda e, j=j: e.matmul(banks[j][:, o:o + T1], g[:, 1, sl(j)], iv[:, t2, 1, :], start=False, stop=True),
                         reads=[g, iv], writes=[banks[j]], acc=True, last=True)
                if t2 % TPB == TPB - 1:
                    t2a = t2 - (TPB - 1)
                    for j in range(4):
                        dst = yT[j].h[:, :].rearrange("p (a b) -> p b a", b=128)[:, t2a:t2a + TPB, :]
                        srcp = banks[j].h[:, 0:TPB * T1].rearrange("p (b a) -> p b a", a=T1)
                        if j % 2 == 0:
                            k.op("act", lambda e: e.copy(dst, srcp), reads=[banks[j]], writes=[yT[j]], acc=True)
                        else:
                            k.op("dve", lambda e: e.tensor_copy(dst, srcp), reads=[banks[j]], writes=[yT[j]], acc=True)
            for j in range(4):
                cc = cg * 4 + j
                u = ur.next()
                k.dma("sp", u[:], uT.h[cc], reads=[uT], writes=[u])
                t = x0r.next()
                k.op("pool", lambda e: e.memset(t[:, 0:L + 2:L + 1], 0.0), writes=[t])
                k.dma("sp", t[:, 1:L + 1], hyT.h[cc], reads=[hyT], writes=[t], acc=True)
                c = cr.next()
                hy_conv(k, c, t, hcw, cc, L)
                k.op("dve", lambda e: e.scalar_tensor_tensor(out=u[:], in0=u[:], scalar=hb[:, cc:cc + 1], in1=yT[j][:], op0=ALU.mult, op1=ALU.add),
                     reads=[u, hb, yT[j]], writes=[u])
                o = o16.next()
                k.op("pool", lambda e: e.tensor_tensor(out=o[:], in0=u[:], in1=c[:], op=ALU.mult), reads=[u, c], writes=[o])
                k.dma("act", mixT16.h[16 + cc], o[:], reads=[o], writes=[mixT16], acc=True)


def phase_hyena(k, L, hyT, P, S, mixT16):
    hy_filters(k, L, P["hy_zT"], P["hy_w1a"], P["hy_w2"], P["hy_b2"], P["hy_w3"], P["hy_absd"], P["hy_negt"], S["kf_tok"], S["kb_tok"])
    hy_prep_u(k, L, hyT, P["hy_cw"], P["ident32"], S["uT"], S["u_tok"])
    hy_fft_a(k, L, S["kf_tok"], P["hy_fw"], S["scrA"])
    mk, epi = hy_spectrum_store(k, S["scrK"], False)
    hy_fft_b(k, L, S["scrA"], P["hy_cs"], epi, mk)
    hy_fft_a(k, L, S["kb_tok"], P["hy_fw"], S["scrA"])
    mk, epi = hy_spectrum_store(k, S["scrK"], True)
    hy_fft_b(k, L, S["scrA"], P["hy_cs"], epi, mk)
    hy_fft_a(k, L, S["u_tok"], P["hy_fw"], S["scrA"])
    mk, epi = hy_mul_inv(k, S["scrK"], S["scrG"])
    hy_fft_b(k, L, S["scrA"], P["hy_cs"], epi, mk)
    hy_fft_c(k, L, S["scrG"], P["hy_iv"], hyT, S["uT"], P["hy_cw"], P["hy_bias"], mixT16)


def hy_scratch(k, L, kind="Internal"):
    N1 = 2 * L // 128
    return {"kf_tok": k.dram("kf_tok", [L, 2048], F32, kind=kind), "kb_tok": k.dram("kb_tok", [L, 2048], F32, kind=kind),
            "uT": k.dram("uT", [16, 128, L], F32, kind=kind), "u_tok": k.dram("u_tok", [L, 2048], F32, kind=kind),
            "scrA": k.dram("scrA", [2, N1, 128, 2048], F32, kind=kind), "scrK": k.dram("scrK", [2, N1, 128, 2048], F32, kind=kind),
            "scrG": k.dram("scrG", [2, 128, N1, 2048], F32, kind=kind)}


def phase_odd_inproj(k, xT16, L, w_fm, w_dt, w_tm, xbcT, dtT, ginT, rinT, z_tok):
    nb = L // TB
    with k.scope():
        X = [k.sb([128, 8, TB + 2], BF16, "X") for _ in range(4)]
        wr_fm = Ring([k.sb([128, KC, 128], BF16, "wfm") for _ in range(3)])
        wr_tm = Ring([k.sb([128, KC, 512], BF16, "wtm") for _ in range(2)])
        st32 = Ring([k.sb([128, TB], F32, "st32") for _ in range(4)])
        for b in range(nb):
            load_X(k, X, xT16, b, L)
            cnt = [0]
            for c in range(56):
                if c < 24:
                    dst_t, dst = xbcT, xbcT.h[c, :, sl(b, TB)]
                elif c < 40:
                    dst_t, dst = ginT, ginT.h[c - 24, :, sl(b, TB)]
                else:
                    dst_t, dst = rinT, rinT.h[c - 40, :, sl(b, TB)]

                def epi(pst, dst=dst, dst_t=dst_t):
                    evac_to_dram(k, pst, 128, TB, st32, dst, dst_t, cnt[0])
                    cnt[0] += 1
                dense_fm(k, X, wr_fm, w_fm.h[c], 128, epi, [w_fm])

            def epi_dt(pst):
                evac_to_dram(k, pst, 64, TB, st32, dtT.h[:, sl(b, TB)], dtT, 0)
            dense_fm(k, X, wr_fm, w_dt.h, 64, epi_dt, [w_dt])
            for n in range(4):
                def epi(pst, tt, n=n):
                    t0 = b * TB + tt * 128
                    evac_to_dram(k, pst, 128, 512, st32, z_tok.h[t0:t0 + 128, n * 512:(n + 1) * 512], z_tok, cnt[0])
                    cnt[0] += 1
                dense_tm(k, X, wr_tm, w_tm.h[n], epi, [w_tm])


def conv4(k, c, t, cw, ch, L):
    k.op("dve", lambda e: e.tensor_scalar(out=c[:, 0:L], in0=t[:, 0:L], scalar1=cw[:, ch, 0:1], scalar2=cw[:, ch, 4:5], op0=ALU.mult, op1=ALU.add),
         reads=[t, cw], writes=[c])
    for j in (1, 2, 3):
        k.op("dve", lambda e, j=j: e.scalar_tensor_tensor(out=c[:, 0:L], in0=t[:, j:L + j], scalar=cw[:, ch, j:j + 1], in1=c[:, 0:L],
                                                          op0=ALU.mult, op1=ALU.add), reads=[t, cw, c], writes=[c])


def phase_lru(k, L, rinT, ginT, lcw_d, wax_d, lb_d, lam_d, mixT16):
    NT = L // 512
    with k.scope():
        lcw = k.sb([128, 16, 5], F32, "lcw")
        k.dma("sp", lcw[:], lcw_d.h[:, :, :], reads=[lcw_d], writes=[lcw])
        lb = k.sb([128, 2, 2, 16], F32, "lb")
        k.dma("sp", lb[:], lb_d.h[:, :, :, :], reads=[lb_d], writes=[lb])
        lam = k.sb([128, 32], F32, "lam")
        k.dma("sp", lam[:], lam_d.h.rearrange("p a b -> p (a b)"), reads=[lam_d], writes=[lam])
        nc8 = k.sb([128, 32], F32, "nc8")
        k.op("act", lambda e: e.activation(out=nc8[:], in_=lam[:], func=AF.Exp, scale=-1.0), reads=[lam], writes=[nc8])
        k.op("act", lambda e: e.activation(out=nc8[:], in_=nc8[:], func=AF.Ln, bias=1.0, scale=1.0), reads=[nc8], writes=[nc8])
        k.op("dve", lambda e: e.tensor_scalar(out=nc8[:], in0=nc8[:], scalar1=-8.0, scalar2=None, op0=ALU.mult), reads=[nc8], writes=[nc8])
        wr = Ring([k.sb([128, 2, 128], F32, "w") for _ in range(4)])
        padr = Ring([k.sb([128, L + 3], F32, "pad") for _ in range(2)])
        xcr = Ring([k.sb([128, L], F32, "xc") for _ in range(2)])
        hr = Ring([k.sb([128, L], F32, "h") for _ in range(3)])
        gr = Ring([k.sb([128, L], F32, "g") for _ in range(2)])
        o16 = Ring([k.sb([128, L], BF16, "o16") for _ in range(2)])
        G = Ring([k.sb([128, 512], F32, "G") for _ in range(12)])
        for cc in range(16):
            t = padr.next()
            k.op("pool", lambda e: e.memset(t[:, 0:1], 0.0), writes=[t])
            k.op("pool", lambda e: e.memset(t[:, L + 1:L + 3], 0.0), writes=[t], acc=True)
            k.dma("sp", t[:, 1:L + 1], rinT.h[cc], reads=[rinT], writes=[t], acc=True)
            xc = xcr.next()
            conv4(k, xc, t, lcw, cc, L)
            hs = []
            for d in range(2):
                w = wr.next()
                k.dma("sp", w[:], wax_d.h[d, :, cc].rearrange("a p c -> p a c"), reads=[wax_d], writes=[w])
                h = hr.next()
                tiles = range(NT) if d == 0 else range(NT - 1, -1, -1)
                prev = None
                for ti in tiles:
                    c = slice(ti * 512, (ti + 1) * 512)
                    pr, pi = k.psr.next(), k.psr.next()
                    k.op("pe", lambda e: e.matmul(pr[:, :], w[:, 0, :], xc[:, c], start=True, stop=True), reads=[w, xc], writes=[pr])
                    k.op("pe", lambda e: e.matmul(pi[:, :], w[:, 1, :], xc[:, c], start=True, stop=True), reads=[w, xc], writes=[pi])
                    r, ig = G.next(), G.next()
                    k.op("act", lambda e: e.activation(out=r[:], in_=pr[:, :], func=AF.Sigmoid, bias=lb[:, d, 0, cc:cc + 1], scale=1.0), reads=[pr, lb], writes=[r])
                    k.op("act", lambda e: e.activation(out=ig[:], in_=pi[:, :], func=AF.Sigmoid, bias=lb[:, d, 1, cc:cc + 1], scale=1.0), reads=[pi, lb], writes=[ig])
                    a = G.next()
                    k.op("act", lambda e: e.activation(out=a[:], in_=r[:], func=AF.Exp, scale=nc8[:, d * 16 + cc:d * 16 + cc + 1]), reads=[r, nc8], writes=[a])
                    om = G.next()
                    k.op("dve", lambda e: e.tensor_tensor(out=om[:], in0=a[:], in1=a[:], op=ALU.mult), reads=[a], writes=[om])
                    k.op("dve", lambda e: e.tensor_scalar(out=om[:], in0=om[:], scalar1=-1.0, scalar2=1.0, op0=ALU.mult, op1=ALU.add), reads=[om], writes=[om])
                    k.op("act", lambda e: e.activation(out=om[:], in_=om[:], func=AF.Sqrt), reads=[om], writes=[om])
                    k.op("pool", lambda e: e.tensor_tensor(out=ig[:], in0=ig[:], in1=xc[:, c], op=ALU.mult), reads=[ig, xc], writes=[ig])
                    k.op("pool", lambda e: e.tensor_tensor(out=om[:], in0=om[:], in1=ig[:], op=ALU.mult), reads=[om, ig], writes=[om])
                    if d == 0:
                        init = 0.0 if prev is None else h[:, prev * 512 + 511:prev * 512 + 512]
                        k.op("dve", lambda e: e.tensor_tensor_scan(out=h[:, c], data0=a[:], data1=om[:], initial=init, op0=ALU.mult, op1=ALU.add),
                             reads=[a, om, h], writes=[h], acc=(prev is not None))
                    else:
                        init = 0.0 if prev is None else h[:, prev * 512:prev * 512 + 1]
                        lo = ti * 512
                        rs = slice(lo + 511, lo - 1 if lo > 0 else None, -1)
                        k.op("dve", lambda e: e.tensor_tensor_scan(out=h[:, rs], data0=a[:, ::-1], data1=om[:, ::-1], initial=init, op0=ALU.mult, op1=ALU.add),
                             reads=[a, om, h], writes=[h], acc=(prev is not None))
                    prev = ti
                hs.append(h)
            g = gr.next()
            k.dma("sp", g[:], ginT.h[cc], reads=[ginT], writes=[g])
            u = hr.next()
            k.op("pool", lambda e: e.tensor_tensor(out=u[:], in0=g[:], in1=g[:], op=ALU.mult), reads=[g], writes=[u])
            k.op("dve", lambda e: e.tensor_scalar(out=u[:], in0=u[:], scalar1=0.044715, scalar2=1.0, op0=ALU.mult, op1=ALU.add), reads=[u], writes=[u])
            k.op("dve", lambda e: e.tensor_tensor(out=u[:], in0=u[:], in1=g[:], op=ALU.mult), reads=[u, g], writes=[u])
            k.op("act", lambda e: e.activation(out=u[:], in_=u[:], func=AF.Sigmoid, scale=1.5957691216057308), reads=[u], writes=[u])
            k.op("pool", lambda e: e.tensor_tensor(out=u[:], in0=u[:], in1=g[:], op=ALU.mult), reads=[u, g], writes=[u])
            k.op("dve", lambda e: e.tensor_tensor(out=hs[0][:], in0=hs[0][:], in1=hs[1][:], op=ALU.add), reads=[hs[0], hs[1]], writes=[hs[0]])
            o = o16.next()
            k.op("dve", lambda e: e.tensor_tensor(out=o[:], in0=hs[0][:], in1=u[:], op=ALU.mult), reads=[hs[0], u], writes=[o])
            k.dma("act", mixT16.h[16 + cc], o[:], reads=[o], writes=[mixT16], acc=True)


def ssd_consts(L):
    c = {}
    sel = np.zeros((64, 64, 128), np.float32)
    for h in range(64):
        sel[h, h, :] = 1.0
    c["ssd_sel"] = sel
    c["ssd_ones64"] = np.ones((64, 128), np.float32)
    t = np.arange(L)
    m = np.ones((64, L), np.float32)
    m[0:32, t % 128 == 0] = 0.0
    m[32:64, t % 128 == 127] = 0.0
    c["ssd_smask"] = m
    return c


def phase_ssd_prep(k, L, xbcT, scw_d, ident32_d, xs_tok, bm_tok, bmT16, cmT16):
    with k.scope():
        scw = k.sb([128, 24, 5], F32, "scw")
        k.dma("sp", scw[:], scw_d.h[:, :, :], reads=[scw_d], writes=[scw])
        ident32 = k.sb([128, 128], F32, "ident")
        k.dma("sp", ident32[:], ident32_d.h[:, :], reads=[ident32_d], writes=[ident32])
        padr = Ring([k.sb([128, L + 3], F32, "pad") for _ in range(2)])
        cr = Ring([k.sb([128, L], F32, "c") for _ in range(2)])
        c16 = Ring([k.sb([128, L], BF16, "c16") for _ in range(2)])
        st = Ring([k.sb([128, 4, 128], F32, "st") for _ in range(3)])
        for ch in range(24):
            t = padr.next()
            k.op("pool", lambda e: e.memset(t[:, 0:1], 0.0), writes=[t])
            k.op("pool", lambda e: e.memset(t[:, L + 1:L + 3], 0.0), writes=[t], acc=True)
            k.dma("sp", t[:, 1:L + 1], xbcT.h[ch], reads=[xbcT], writes=[t], acc=True)
            c = cr.next()
            conv4(k, c, t, scw, ch, L)
            k.op("act", lambda e: e.activation(out=c[:], in_=c[:], func=AF.Silu), reads=[c], writes=[c])
            if ch >= 16:
                s16 = c16.next()
                k.op("pool", lambda e: e.tensor_copy(s16[:], c[:]), reads=[c], writes=[s16])
                dst = bmT16 if ch < 20 else cmT16
                k.dma("act", dst.h[(ch - 16) % 4], s16[:], reads=[s16], writes=[dst], acc=True)
            if ch < 20:
                dst, col = (xs_tok, ch * 128) if ch < 16 else (bm_tok, (ch - 16) * 128)
                for g in range(L // 512):
                    p = k.psr.next()
                    for j in range(4):
                        k.op("pe", lambda e, j=j: e.transpose(p[:, sl(j)], c[:, g * 512 + j * 128:g * 512 + (j + 1) * 128], ident32[:]),
                             reads=[c, ident32], writes=[p], acc=(j > 0), last=(j == 3))
                    s = st.next()
                    if g % 2 == 0:
                        k.op("act", lambda e: e.copy(s.h[:].rearrange("p a c -> p (a c)"), p[:, :]), reads=[p], writes=[s])
                    else:
                        k.op("dve", lambda e: e.tensor_copy(s.h[:].rearrange("p a c -> p (a c)"), p[:, :]), reads=[p], writes=[s])
                    k.dma("act", dst.h[g * 512:(g + 1) * 512, col:col + 128].rearrange("(a p) c -> p a c", p=128), s[:],
                          reads=[s], writes=[dst], acc=True)


def phase_ssd(k, L, dtT, z_tok, xs_tok, bm_tok, bmT16, cmT16, yf_tok, P, mixT16):
    NCH = L // 128
    with k.scope():
        def ld(name, shape, q="sp", src=None):
            t = k.sb(shape, F32, name)
            k.dma(q, t[:], (P[name].h if src is None else src), reads=[P[name]], writes=[t])
            return t
        sel = ld("ssd_sel", [64, 64, 128])
        ones64 = ld("ssd_ones64", [64, 128])
        smask = ld("ssd_smask", [64, L], "act")
        cst = ld("gla_cst", [128, 4, 128], "act", P["gla_cst"].h.rearrange("a p c -> p a c"))
        ident32 = ld("ident32", [128, 128])
        dtb = ld("ssd_dtb", [64, 1])
        alog = ld("ssd_alog", [64, 1])
        ng = ld("ssd_ng", [128, 2048], "act")
        dsk0 = ld("ssd_dsk", [128, 2048], "sp", P["ssd_dsk"].h[0])
        dsk1 = ld("ssd_dsk", [128, 2048], "act", P["ssd_dsk"].h[1])
        k.op("pool", lambda e: e.tensor_tensor(out=dsk0[:], in0=dsk0[:], in1=dsk1[:], op=ALU.add), reads=[dsk0, dsk1], writes=[dsk0])
        epsr = k.sb([128, 1], F32, "epsr")
        k.op("dve", lambda e: e.memset(epsr[:], RMS_EPS), writes=[epsr])
        dt = k.sb([64, L], F32, "dt")
        k.dma("sp", dt[:], dtT.h[:, :], reads=[dtT], writes=[dt])
        k.op("act", lambda e: e.activation(out=dt[:], in_=dt[:], func=AF.Exp, bias=dtb[:, 0:1], scale=1.0), reads=[dt, dtb], writes=[dt])
        k.op("act", lambda e: e.activation(out=dt[:], in_=dt[:], func=AF.Ln, bias=1.0, scale=1.0), reads=[dt], writes=[dt])
        negA = k.sb([64, 1], F32, "negA")
        k.op("act", lambda e: e.activation(out=negA[:], in_=alog[:], func=AF.Exp), reads=[alog], writes=[negA])
        k.op("dve", lambda e: e.tensor_scalar(out=negA[:], in0=negA[:], scalar1=-1.0, scalar2=None, op0=ALU.mult), reads=[negA], writes=[negA])
        acT = k.sb([64, L], F32, "acT")
        nacT = k.sb([64, L], F32, "nacT")
        k.op("dve", lambda e: e.tensor_scalar(out=nacT[:], in0=dt[:], scalar1=negA[:, 0:1], scalar2=None, op0=ALU.mult), reads=[dt, negA], writes=[nacT])
        k.op("dve", lambda e: e.tensor_tensor_scan(out=acT[0:32, :], data0=smask[0:32, :], data1=nacT[0:32, :], initial=0.0, op0=ALU.mult, op1=ALU.add),
             reads=[smask, nacT], writes=[acT])
        k.op("dve", lambda e: e.tensor_tensor_scan(out=acT[32:64, ::-1], data0=smask[32:64, ::-1], data1=nacT[32:64, ::-1], initial=0.0,
                                                   op0=ALU.mult, op1=ALU.add), reads=[smask, nacT], writes=[acT], acc=True)
        k.op("dve", lambda e: e.tensor_scalar(out=nacT[:], in0=acT[:], scalar1=-1.0, scalar2=None, op0=ALU.mult), reads=[acT], writes=[nacT])
        ac_tok = k.sb([128, NCH, 64], F32, "ac_tok")
        dt_tok = k.sb([128, NCH, 64], F32, "dt_tok")
        eac = k.sb([128, NCH, 64], F32, "eac")
        for src, dst in ((acT, ac_tok), (dt, dt_tok)):
            for n0 in range(0, NCH, 4):
                p = k.psr.next()
                nn = min(4, NCH - n0)
                for j in range(nn):
                    k.op("pe", lambda e, j=j: e.transpose(p[:, j * 64:(j + 1) * 64], src[:, sl(n0 + j)], ident32[0:64, 0:64]),
                         reads=[src, ident32], writes=[p], acc=(j > 0), last=(j == nn - 1))
                k.op("dve", lambda e: e.tensor_copy(dst.h[:, n0:n0 + nn, :].rearrange("p a c -> p (a c)"), p[:, 0:nn * 64]), reads=[p], writes=[dst], acc=(n0 > 0))
        k.op("act", lambda e: e.activation(out=eac.h[:].rearrange("p a c -> p (a c)"), in_=ac_tok.h[:].rearrange("p a c -> p (a c)"), func=AF.Exp),
             reads=[ac_tok], writes=[eac])
        S32 = k.sb([128, 512], F32, "S32")
        S16 = k.sb([128, 512], BF16, "S16")
        xsr = Ring([k.sb([128, 512], F32, "xs") for _ in range(2)])
        bmr = Ring([k.sb([128, 128], F32, "bm") for _ in range(2)])
        bm16r = Ring([k.sb([128, 128], BF16, "bm16") for _ in range(2)])
        bTr = Ring([k.sb([128, 128], BF16, "bT") for _ in range(2)])
        cTr = Ring([k.sb([128, 128], BF16, "cT") for _ in range(2)])
        cbr = Ring([k.sb([128, 128], F32, "cbm") for _ in range(2)])
        Dr = Ring([k.sb([128, 512], F32, "Dm") for _ in range(2)])
        lmr = Ring([k.sb([128, 4, 128], F32, "lm") for _ in range(2)])
        Mr = Ring([k.sb([128, 4, 128], BF16, "M") for _ in range(2)])
        xdtr = Ring([k.sb([128, 512], BF16, "xdt") for _ in range(2)])
        xddr = Ring([k.sb([128, 512], BF16, "xdd") for _ in range(2)])
        O = Ring([k.sb([128, 512], F32, "O") for _ in range(6)])
        sm = Ring([k.sb([128, 8], F32, "sm") for _ in range(6)])
        Xr = Ring([k.sb([64, 8], F32, "X") for _ in range(2)])
        tr16 = Ring([k.sb([128, 4, 128], BF16, "tr") for _ in range(2)])
        zr = Ring([k.sb([128, 512], F32, "z") for _ in range(2)])
        yfr = Ring([k.sb([128, 512], F32, "yf") for _ in range(2)])
        for g in range(4):
            gc = slice(g * 512, (g + 1) * 512)
            for d in range(2):
                row0 = d * 32 + g * 8
                mask = cst[:, d, :]
                lastc = 127 if d == 0 else 0
                k.op("dve", lambda e: e.memset(S32[:], 0.0), writes=[S32])
                k.op("pool", lambda e: e.memset(S16[:], 0.0), writes=[S16])
                order = range(NCH) if d == 0 else range(NCH - 1, -1, -1)
                for n in order:
                    ts = slice(n * 128, (n + 1) * 128)
                    xs, bmt, bT, cT = xsr.next(), bmr.next(), bTr.next(), cTr.next()
                    k.dma("sp", xs[:], xs_tok.h[ts, gc], reads=[xs_tok], writes=[xs])
                    k.dma("sp", bmt[:], bm_tok.h[ts, sl(g)], reads=[bm_tok], writes=[bmt])
                    k.dma("sp", bT[:], bmT16.h[g, :, ts], reads=[bmT16], writes=[bT])
                    k.dma("sp", cT[:], cmT16.h[g, :, ts], reads=[cmT16], writes=[cT])
                    bm16 = bm16r.next()
                    k.op("pool", lambda e: e.tensor_copy(bm16[:], bmt[:]), reads=[bmt], writes=[bm16])
                    p = k.psr.next()
                    k.op("pe", lambda e: e.matmul(p[:, 0:128], bT[:], cT[:], start=True, stop=True), reads=[bT, cT], writes=[p])
                    cbm = cbr.next()
                    k.op("dve", lambda e: e.tensor_tensor(out=cbm[:], in0=p[:, 0:128], in1=mask, op=ALU.mult), reads=[p, cst], writes=[cbm])
                    lms, Ms = [], []
                    for half in range(2):
                        pD = k.psr.next()
                        for q in range(4):
                            row = row0 + half * 4 + q
                            k.op("pe", lambda e, q=q, row=row: e.matmul(pD[:, sl(q)], sel[:, row, :], acT[:, ts], start=True, stop=False),
                                 reads=[sel, acT], writes=[pD], acc=True, last=False)
                            k.op("pe", lambda e, q=q, row=row: e.matmul(pD[:, sl(q)], nacT[:, ts], sel[:, row, :], start=False, stop=True),
                                 reads=[sel, nacT], writes=[pD], acc=True, last=(q == 3))
                        Dm = Dr.next()
                        k.op("dve", lambda e: e.tensor_scalar(out=Dm[:], in0=pD[:, :], scalar1=0.0, scalar2=None, op0=ALU.min), reads=[pD], writes=[Dm])
                        lm = lmr.next()
                        k.op("act", lambda e: e.activation(out=lm.h[:].rearrange("p a c -> p (a c)"), in_=Dm[:], func=AF.Exp), reads=[Dm], writes=[lm])
                        M = Mr.next()
                        for q in range(4):
                            k.op("pool" if q % 2 else "dve", lambda e, q=q: e.tensor_tensor(out=M[:, q, :], in0=lm[:, q, :], in1=cbm[:], op=ALU.mult),
                                 reads=[lm, cbm], writes=[M], acc=(q > 0))
                        lms.append(lm)
                        Ms.append(M)
                    xdt, xdd = xdtr.next(), xddr.next()
                    for h in range(8):
                        hc = slice(h * 64, (h + 1) * 64)
                        dts = dt_tok[:, n, row0 + h:row0 + h + 1]
                        k.op("pool", lambda e, hc=hc, dts=dts: e.tensor_scalar(out=xdt[:, hc], in0=xs[:, hc], scalar1=dts, scalar2=1.0, op0=ALU.mult, op1=ALU.mult),
                             reads=[xs, dt_tok], writes=[xdt], acc=(h > 0))
                        dss = lms[h // 4][:, h % 4, lastc:lastc + 1]
                        k.op("dve", lambda e, hc=hc, dts=dts, dss=dss: e.tensor_scalar(out=xdd[:, hc], in0=xs[:, hc], scalar1=dts, scalar2=dss, op0=ALU.mult, op1=ALU.mult),
                             reads=[xs, dt_tok, lms[h // 4]], writes=[xdd], acc=(h > 0))
                    pY = k.psr.next()
                    for h in range(8):
                        k.op("pe", lambda e, h=h: e.matmul(pY[:, h * 64:(h + 1) * 64], Ms[h // 4][:, h % 4, :], xdt[:, h * 64:(h + 1) * 64], start=True, stop=True),
                             reads=[Ms[h // 4], xdt], writes=[pY], acc=True, last=(h == 7))
                    pO = k.psr.next()
                    k.op("pe", lambda e: e.matmul(pO[:, :], cT[:], S16[:], start=True, stop=True), reads=[cT, S16], writes=[pO])
                    pS = k.psr.next()
                    k.op("pe", lambda e: e.matmul(pS[:, :], bm16[:], xdd[:], start=True, stop=True), reads=[bm16, xdd], writes=[pS])
                    tl = n * 128 + lastc
                    X = Xr.next()
                    k.op("pool", lambda e: e.tensor_scalar(out=X[:], in0=sel[:, row0:row0 + 8, 0], scalar1=acT[:, tl:tl + 1], scalar2=1.0, op0=ALU.mult, op1=ALU.mult),
                         reads=[sel, acT], writes=[X])
                    pC = k.psr.next()
                    k.op("pe", lambda e: e.matmul(pC[:, 0:8], ones64[:], X[:], start=True, stop=True), reads=[ones64, X], writes=[pC])
                    cd = sm.next()
                    k.op("act", lambda e: e.activation(out=cd[:], in_=pC[:, 0:8], func=AF.Exp), reads=[pC], writes=[cd])
                    yd = O.next()
                    k.op("act", lambda e: e.copy(yd[:], pY[:, :]), reads=[pY], writes=[yd])
                    y = O.next()
                    for h in range(8):
                        hc = slice(h * 64, (h + 1) * 64)
                        k.op("dve", lambda e, hc=hc, h=h: e.scalar_tensor_tensor(out=y[:, hc], in0=pO[:, hc], scalar=eac[:, n, row0 + h:row0 + h + 1], in1=yd[:, hc],
                                                                              op0=ALU.mult, op1=ALU.add), reads=[pO, eac, yd], writes=[y], acc=(h > 0))
                    for h in range(8):
                        hc = slice(h * 64, (h + 1) * 64)
                        k.op("dve", lambda e, hc=hc, h=h: e.scalar_tensor_tensor(out=S32[:, hc], in0=S32[:, hc], scalar=cd[:, h:h + 1], in1=pS[:, hc],
                                                                              op0=ALU.mult, op1=ALU.add), reads=[S32, cd, pS], writes=[S32], acc=(h > 0))
                    k.op("act", lambda e: e.copy(S16[:], S32[:]), reads=[S32], writes=[S16])
                    if d == 0:
                        k.dma("act", yf_tok.h[ts, gc], y[:], reads=[y], writes=[yf_tok], acc=True)
                    else:
                        yf, zt = yfr.next(), zr.next()
                        k.dma("sp", yf[:], yf_tok.h[ts, gc], reads=[yf_tok], writes=[yf])
                        k.dma("sp", zt[:], z_tok.h[ts, gc], reads=[z_tok], writes=[zt])
                        k.op("pool", lambda e: e.tensor_tensor(out=y[:], in0=y[:], in1=yf[:], op=ALU.add), reads=[y, yf], writes=[y])
                        t2 = O.next()
                        k.op("pool", lambda e: e.tensor_tensor(out=t2[:], in0=xs[:], in1=dsk0[:, gc], op=ALU.mult), reads=[xs, dsk0], writes=[t2])
                        k.op("dve", lambda e: e.tensor_tensor(out=y[:], in0=y[:], in1=t2[:], op=ALU.add), reads=[y, t2], writes=[y])
                        sz = O.next()
                        k.op("act", lambda e: e.activation(out=sz[:], in_=zt[:], func=AF.Silu), reads=[zt], writes=[sz])
                        k.op("pool", lambda e: e.tensor_tensor(out=y[:], in0=y[:], in1=sz[:], op=ALU.mult), reads=[y, sz], writes=[y])
                        ss = sm.next()
                        k.op("act", lambda e: e.activation(out=sz[:], in_=y[:], func=AF.Square, accum_out=ss[:, 0:1]), reads=[y], writes=[sz, ss])
                        k.op("act", lambda e: e.activation(out=ss[:, 0:1], in_=ss[:, 0:1], func=AF.Sqrt, bias=epsr[:, 0:1], scale=1.0 / 512), reads=[ss, epsr], writes=[ss])
                        k.op("dve", lambda e: e.reciprocal(out=ss[:, 0:1], in_=ss[:, 0:1]), reads=[ss], writes=[ss])
                        on = O.next()
                        k.op("dve", lambda e: e.scalar_tensor_tensor(out=on[:], in0=y[:], scalar=ss[:, 0:1], in1=ng[:, gc], op0=ALU.mult, op1=ALU.mult),
                             reads=[y, ss, ng], writes=[on])
                        p7 = k.psr.next()
                        for j in range(4):
                            k.op("pe", lambda e, j=j: e.transpose(p7[:, sl(j)], on[:, sl(j)], ident32[:]), reads=[on, ident32], writes=[p7], acc=(j > 0), last=(j == 3))
                        t16 = tr16.next()
                        k.op("dve", lambda e: e.tensor_copy(t16.h[:].rearrange("p j c -> p (j c)"), p7[:, :]), reads=[p7], writes=[t16])
                        k.dma("act", mixT16.h[4 * g:4 * g + 4, :, ts].rearrange("j p c -> p j c"), t16[:], reads=[t16], writes=[mixT16], acc=True)


L_FULL = 4096


def host_params(inp, L=None):
    L_FULL = L or 4096
    f = lambda a: np.ascontiguousarray(np.asarray(a, dtype=np.float32))
    P = {}
    P["ident32"] = np.eye(128, dtype=np.float32)
    P["ones32"] = np.ones((128, 128), np.float32)
    P["gla_cst"] = gla_consts()
    P.update(hy_consts(L_FULL))
    P.update(ssd_consts(L_FULL))
    W = np.asarray(inp["ev_w_in"][0], np.float32)
    Wq, Wk, Wv, Wog, Wlr, Why = W[:, :1024], W[:, 1024:2048], W[:, 2048:4096], W[:, 4096:6144], W[:, 6144:6176], W[:, 6176:]
    P["ev_fm"] = tile_fm(np.concatenate([Wq, Wk, Why], 1))
    P["ev_lr"] = f(Wlr.reshape(KC, 128, 32).transpose(1, 0, 2))
    P["ev_tm"] = tile_tm(np.concatenate([Wk, Wv, Wog], 1))
    P["gla_wg"] = f(np.stack([np.concatenate([inp["ev_gla_wg_f"][0], inp["ev_gla_bg_f"][0][None]], 0),
                              np.concatenate([inp["ev_gla_wg_b"][0], inp["ev_gla_bg_b"][0][None]], 0)]))
    P["gla_gn"] = f(np.tile(np.asarray(inp["ev_gla_norm"][0])[None], (128, 1)))
    P["hy_w1a"] = f(np.concatenate([inp["ev_hy_w1"][0], inp["ev_hy_b1"][0][None]], 0))
    P["hy_w2"] = f(inp["ev_hy_w2"][0])
    P["hy_b2"] = f(np.asarray(inp["ev_hy_b2"][0])[:, None])
    P["hy_w3"] = f(inp["ev_hy_w3"][0])
    hcw = np.zeros((128, 48, 4), np.float32)
    hcw[:, :, 0:3] = np.asarray(inp["ev_hy_conv_w"][0]).T.reshape(48, 128, 3).transpose(1, 0, 2)
    hcw[:, :, 3] = np.asarray(inp["ev_hy_conv_b"][0]).reshape(48, 128).T
    P["hy_cw"] = hcw
    P["hy_bias"] = f(np.asarray(inp["ev_hy_bias"][0]).reshape(16, 128).T)
    P["ev_w_out"] = tile_fm(np.asarray(inp["ev_w_out"][0], np.float32))
    W = np.asarray(inp["od_w_in"][0], np.float32)
    P["od_fm"] = tile_fm(np.concatenate([W[:, 2048:5120], W[:, 5184:7232], W[:, 7232:9280]], 1))
    P["od_dt"] = f(W[:, 5120:5184].reshape(KC, 128, 64).transpose(1, 0, 2))
    P["od_tm"] = tile_tm(W[:, 0:2048])
    scw = np.zeros((128, 24, 5), np.float32)
    scw[:, :, 0:4] = np.asarray(inp["od_ssd_conv_w"][0]).T.reshape(24, 128, 4).transpose(1, 0, 2)
    scw[:, :, 4] = np.asarray(inp["od_ssd_conv_b"][0]).reshape(24, 128).T
    P["ssd_scw"] = scw
    P["ssd_dtb"] = f(np.asarray(inp["od_ssd_dt_bias"][0]).reshape(64, 1))
    P["ssd_alog"] = f(np.asarray(inp["od_ssd_a_log"][0]).reshape(64, 1))
    P["ssd_dsk"] = f(np.tile(np.repeat(np.asarray(inp["od_ssd_d"][0]), 64, axis=1)[:, None, :], (1, 128, 1)))
    P["ssd_ng"] = f(np.tile(np.asarray(inp["od_ssd_norm"][0])[None], (128, 1)))
    lcw = np.zeros((128, 16, 5), np.float32)
    lcw[:, :, 0:4] = np.asarray(inp["od_lru_conv_w"][0]).T.reshape(16, 128, 4).transpose(1, 0, 2)
    lcw[:, :, 4] = np.asarray(inp["od_lru_conv_b"][0]).reshape(16, 128).T
    P["lru_cw"] = lcw
    P["lru_wax"] = f(np.stack([inp["od_lru_wa"][0], inp["od_lru_wx"][0]], 1))
    P["lru_b"] = f(np.stack([np.asarray(inp["od_lru_ba"][0]).reshape(2, 16, 128), np.asarray(inp["od_lru_bx"][0]).reshape(2, 16, 128)], 1).transpose(3, 0, 1, 2))
    P["lru_lam"] = f(np.asarray(inp["od_lru_lambda"][0]).reshape(2, 16, 128).transpose(2, 0, 1))
    P["od_w_out"] = tile_fm(np.asarray(inp["od_w_out"][0], np.float32))
    cw = np.zeros((128, 2, 2 * FC, 4), np.float32)
    for i in range(2):
        P["w_up%d" % i] = tile_fm(np.asarray(inp["ffn_w_up"][i], np.float32))
        P["w_dn%d" % i] = f(np.asarray(inp["ffn_w_down"][i], np.float32).reshape(FC, 128, KC, 128).transpose(2, 1, 0, 3).reshape(KC, 128, FC * 128))
        cw[:, i, :, 0:3] = np.asarray(inp["ffn_conv_w"][i]).T.reshape(2 * FC, 128, 3).transpose(1, 0, 2)
        cw[:, i, :, 3] = np.asarray(inp["ffn_conv_b"][i]).reshape(2 * FC, 128).T
    P["ffn_cw"] = cw
    g = np.stack([inp["ln1_g"][0], inp["ln2_g"][0], inp["ln1_g"][1], inp["ln2_g"][1]])
    b = np.stack([inp["ln1_b"][0], inp["ln2_b"][0], inp["ln1_b"][1], inp["ln2_b"][1]])
    P["gtab"] = f(np.asarray(g).reshape(4, KC, 128).transpose(2, 0, 1).reshape(128, 4 * KC))
    P["btab"] = f(np.asarray(b).reshape(4, KC, 128).transpose(2, 0, 1).reshape(128, 4 * KC))
    return {n: f(a) for n, a in P.items()}


def build_program(L, pshapes):
    nc = bass.Bass("TRN2", target_bir_lowering=False)
    with contextlib.ExitStack() as es:
        k = KB(nc, es)
        k.allps = [k.ps() for _ in range(8)]
        k.psr8 = Ring(k.allps)
        k.psr4 = Ring(k.allps[0:6])
        k.ps_sum, k.ps_sq = k.allps[6], k.allps[7]
        k.psr = k.psr8
        x_in = k.dram("x", [L, D], F32, kind="ExternalInput")
        y_out = k.dram("y", [L, D], F32, kind="ExternalOutput")
        P = {n: k.dram(n, list(s), F32, kind="ExternalInput") for n, s in pshapes.items()}
        xA32, xB32, zT = (k.dram(n, [KC, 128, L], F32) for n in ("xA32", "xB32", "zT"))
        xA16, xB16, mixT16 = (k.dram(n, [KC, 128, L], BF16) for n in ("xA16", "xB16", "mixT16"))
        wc_up = k.dram("wc_up", [2 * FC, 128, KC * 128], BF16)
        wc_dn = k.dram("wc_dn", [KC, 128, FC * 128], BF16)
        ident32 = k.sb([128, 128], F32, "ident0")
        k.dma("sp", ident32[:], P["ident32"].h[:, :], reads=[P["ident32"]], writes=[ident32])
        phase_prepass(k, x_in, xA32, xA16, L, ident32)
        qT, kT = k.dram("qT", [8, 128, L], F32), k.dram("kT", [8, 128, L], F32)
        lrT, hyT = k.dram("lrT", [32, L], F32), k.dram("hyT", [48, 128, L], F32)
        k_tok, v_tok, og_tok = k.dram("k_tok", [L, 1024], F32), k.dram("v_tok", [L, 2048], BF16), k.dram("og_tok", [L, 2048], F32)
        of_tok = k.dram("of_tok", [L, 2048], F32)
        phase_even_inproj(k, xA16, L, P["ev_fm"], P["ev_lr"], P["ev_tm"], qT, kT, lrT, hyT, k_tok, v_tok, og_tok)
        phase_gla(k, L, qT, kT, k_tok, v_tok, og_tok, lrT, P["gla_wg"], P["gla_gn"], P["gla_cst"], P["ident32"], of_tok, mixT16)
        hyS = hy_scratch(k, L)
        phase_hyena(k, L, hyT, P, hyS, mixT16)
        k.psr = k.psr4
        phase_outproj_ln(k, mixT16, L, P["ev_w_out"], xA32, xB32, xB16, zT, P["ones32"], P["gtab"], P["btab"], 0)
        phase_ffn(k, xB16, L, P["w_up0"], P["w_dn0"], P["ffn_cw"], xB32, xA32, xA16, zT, P["ones32"], P["gtab"], P["btab"], 1, 0, wc_up, wc_dn)
        k.psr = k.psr8
        xbcT, ginT, dtT = Tk(hyT.h[0:24]), Tk(hyT.h[24:40]), k.dram("dtT", [64, L], F32)
        rinT, z_tok = Tk(hyS["uT"].h), Tk(og_tok.h)
        xs_tok, bm_tok = Tk(of_tok.h), Tk(k_tok.h[:, 0:512])
        bmT16, cmT16, yf_tok = k.dram("bmT16", [4, 128, L], BF16), k.dram("cmT16", [4, 128, L], BF16), Tk(hyS["u_tok"].h)
        phase_odd_inproj(k, xA16, L, P["od_fm"], P["od_dt"], P["od_tm"], xbcT, dtT, ginT, rinT, z_tok)
        phase_ssd_prep(k, L, xbcT, P["ssd_scw"], P["ident32"], xs_tok, bm_tok, bmT16, cmT16)
        phase_ssd(k, L, dtT, z_tok, xs_tok, bm_tok, bmT16, cmT16, yf_tok, P, mixT16)
        phase_lru(k, L, rinT, ginT, P["lru_cw"], P["lru_wax"], P["lru_b"], P["lru_lam"], mixT16)
        k.psr = k.psr4
        phase_outproj_ln(k, mixT16, L, P["od_w_out"], xA32, xB32, xB16, zT, P["ones32"], P["gtab"], P["btab"], 2)
        phase_ffn(k, xB16, L, P["w_up1"], P["w_dn1"], P["ffn_cw"], xB32, None, None, zT, P["ones32"], P["gtab"], P["btab"], 3, 1, wc_up, wc_dn,
                  out_tok=y_out, ident32_d=P["ident32"])
        k.finish([y_out])
        stats = dict(k.ninst)
        stats["nsem"] = k.nsem
    return nc, stats


def kernel(**inputs):
    L = L_FULL
    xp = np.asarray(inputs["x_prompt"], np.float32)
    xs = np.asarray(inputs["x_sample"], np.float32)
    seqs = [xp[0], xp[1], xs[0], xs[1], xs[2], xs[3]]
    P = host_params(inputs)
    nc, stats = build_program(L, {n: a.shape for n, a in P.items()})
    in_maps = []
    for c in range(6):
        m = dict(P)
        m["x"] = np.ascontiguousarray(seqs[c])
        in_maps.append(m)
    res = run_bass_kernel_spmd(nc, in_maps, core_ids=list(range(6)))
    outs = [np.asarray(res.results[c]["y"], np.float32) for c in range(6)]
    return (np.stack(outs[0:2]), np.stack(outs[2:6]))
```
